# Optimizing a Trainium2 kernel written in Bass

```python
import math
import jax
import jax.numpy as jnp
from jax import lax
import numpy as np

D_MODEL = 1024
BATCH = 16
SEQ = 2048
DEPTH = 4
DEC_BATCH = 8
DEC_SEQ = 64
PAST_LEN = 1024

CHUNK = 64
D_INNER = 2 * D_MODEL
D_SSD = D_INNER // 2
D_S5 = D_INNER - D_SSD
SSD_HEADDIM = 64
SSD_HEADS = D_SSD // SSD_HEADDIM
SSD_GROUPS = 4
SSD_STATE = 128
SSD_CONV = 4
SSD_CONV_DIM = D_SSD + 2 * SSD_GROUPS * SSD_STATE
S5_GROUP_CH = 16
S5_GROUPS = D_S5 // S5_GROUP_CH
S5_STATE = 64
IN_COLS = D_SSD + SSD_CONV_DIM + SSD_HEADS + 2 * D_S5
IN_SPLITS = (D_SSD, D_SSD + SSD_CONV_DIM, D_SSD + SSD_CONV_DIM + SSD_HEADS,
             D_SSD + SSD_CONV_DIM + SSD_HEADS + D_S5)
EPS = 1e-6

kernel_name = 'hymba_ssd_s5_stream_step'


def rmsnorm(x, g):
    xf = x.astype(jnp.float32)
    y = xf * lax.rsqrt(jnp.mean(xf * xf, axis=-1, keepdims=True) + EPS)
    return (y * g.astype(jnp.float32)).astype(x.dtype)


def block_len(length):
    return CHUNK if length % CHUNK == 0 else length


def to_blocks(t, q):
    b, length = t.shape[0], t.shape[1]
    return jnp.moveaxis(t.reshape((b, length // q, q) + t.shape[2:]), 1, 0)


def from_blocks(t):
    t = jnp.moveaxis(t, 0, 1)
    return t.reshape((t.shape[0], t.shape[1] * t.shape[2]) + t.shape[3:])


def causal_dwconv(x, prev, w, bias):
    length = x.shape[1]
    xp = jnp.concatenate([prev.astype(x.dtype), x], axis=1)
    y = bias.astype(x.dtype)
    for k in range(SSD_CONV):
        y = y + w[k] * xp[:, k:k + length]
    return y, xp[:, xp.shape[1] - (SSD_CONV - 1):]


def segsum(a):
    t = a.shape[-1]
    ae = jnp.broadcast_to(a[..., :, None], a.shape + (t,))
    strict = jnp.tril(jnp.ones((t, t), dtype=bool), -1)
    cs = jnp.cumsum(jnp.where(strict, ae, 0.0), axis=-2)
    return jnp.where(jnp.tril(jnp.ones((t, t), dtype=bool)), cs, -jnp.inf)


def ssd_scan(xdt, a, bm, cm, h0):
    b, length, n_h, p = xdt.shape
    g, n = bm.shape[2], bm.shape[3]
    r = n_h // g
    q = block_len(length)
    xb = to_blocks(xdt.reshape(b, length, g, r, p), q)
    ab = to_blocks(a.reshape(b, length, g, r), q)
    bb = to_blocks(bm, q)
    cbk = to_blocks(cm, q)

    def block(h, inp):
        x_q, a_q, b_q, c_q = inp
        a_t = jnp.moveaxis(a_q, 1, -1)
        a_cs = jnp.cumsum(a_t, axis=-1)
        decay_in = jnp.exp(segsum(a_t))
        cb = jnp.einsum('blgn,bsgn->bgls', c_q, b_q)
        y = jnp.einsum('bgrls,bsgrp->blgrp', cb[:, :, None] * decay_in, x_q)
        from_h = jnp.moveaxis(jnp.exp(a_cs), -1, 1)[..., None]
        y = y + jnp.einsum('blgn,bgrpn->blgrp', c_q, h) * from_h
        to_end = jnp.moveaxis(jnp.exp(a_cs[..., -1:] - a_cs), -1, 1)[..., None]
        h_new = (h * jnp.exp(a_cs[..., -1])[..., None, None]
                 + jnp.einsum('bsgn,bsgrp->bgrpn', b_q, x_q * to_end))
        return h_new, y

    h_fin, ys = lax.scan(block, h0.reshape(b, g, r, p, n), (xb, ab, bb, cbk))
    return from_blocks(ys).reshape(b, length, n_h, p), h_fin.reshape(b, n_h, p, n)


def ssd_branch(z, xbc, dt_raw, conv_prev, h0, conv_w, conv_b, dt_bias, a_log, d_ssd, ssd_norm_g):
    f32 = jnp.float32
    b, length, _ = xbc.shape
    xbc_c, conv_new = causal_dwconv(xbc, conv_prev, conv_w, conv_b)
    xbc_c = jax.nn.silu(xbc_c.astype(f32))
    xs, bm, cm = jnp.split(xbc_c, (D_SSD, D_SSD + SSD_GROUPS * SSD_STATE), axis=-1)
    xs = xs.reshape(b, length, SSD_HEADS, SSD_HEADDIM)
    bm = bm.reshape(b, length, SSD_GROUPS, SSD_STATE)
    cm = cm.reshape(b, length, SSD_GROUPS, SSD_STATE)
    dt = jax.nn.softplus(dt_raw.astype(f32) + dt_bias.astype(f32))
    a_neg = -jnp.exp(a_log.astype(f32))
    y, h_new = ssd_scan(xs * dt[..., None], dt * a_neg, bm, cm, h0.astype(f32))
    y = y + d_ssd.astype(f32)[:, None] * xs
    y = y.reshape(b, length, D_SSD) * jax.nn.silu(z.astype(f32))
    y = rmsnorm(y.reshape(b, length, SSD_GROUPS, D_SSD // SSD_GROUPS),
                ssd_norm_g.reshape(SSD_GROUPS, D_SSD // SSD_GROUPS)).reshape(b, length, D_SSD)
    return y.astype(z.dtype), conv_new, h_new


def _lin_op(e1, e2):
    a1, b1 = e1
    a2, b2 = e2
    return a1 * a2, a2 * b1 + b2


def s5_branch(u, z, h0_re, h0_im, lam_re, lam_im, log_dt, b_re, b_im, c_re, c_im, d_s5, w_glu, b_glu):
    f32 = jnp.float32
    bsz, length, _ = u.shape
    uf = u.astype(f32).reshape(bsz, length, S5_GROUPS, S5_GROUP_CH)
    lam = lax.complex(lam_re.astype(f32), lam_im.astype(f32))
    lam_bar = jnp.exp(lam * jnp.exp(log_dt.astype(f32))[:, None])
    b_bar = ((lam_bar - 1.0) / lam)[..., None] * lax.complex(b_re.astype(f32), b_im.astype(f32))
    c_mat = lax.complex(c_re.astype(f32), c_im.astype(f32))
    q = block_len(length)
    a_blk = jnp.broadcast_to(lam_bar, (bsz, q, S5_GROUPS, S5_STATE))

    def block(h, u_q):
        bu = jnp.einsum('gpc,bqgc->bqgp', b_bar, u_q.astype(jnp.complex64))
        a_cum, h_loc = lax.associative_scan(_lin_op, (a_blk, bu), axis=1)
        hs = a_cum * h[:, None] + h_loc
        return hs[:, -1], jnp.real(jnp.einsum('gcp,bqgp->bqgc', c_mat, hs))

    h0 = lax.complex(h0_re.astype(f32), h0_im.astype(f32))
    h_fin, ys = lax.scan(block, h0, to_blocks(uf, q))
    y = from_blocks(ys) + d_s5.astype(f32).reshape(S5_GROUPS, S5_GROUP_CH) * uf
    y = jax.nn.gelu(y.reshape(bsz, length, D_S5))
    y = y * jax.nn.sigmoid(y @ w_glu.astype(f32) + b_glu.astype(f32))
    y = y * jax.nn.silu(z.astype(f32))
    return y.astype(u.dtype), jnp.real(h_fin), jnp.imag(h_fin)


def mixer_layer(x, conv_prev, ssd_h0, s5_h0_re, s5_h0_im,
                norm_g, w_in, conv_w, conv_b, dt_bias, a_log, d_ssd, ssd_norm_g,
                lam_re, lam_im, log_dt, b_re, b_im, c_re, c_im, d_s5, w_glu, b_glu, w_out):
    h = rmsnorm(x, norm_g)
    z_a, xbc, dt_raw, u_b, z_b = jnp.split(h @ w_in, IN_SPLITS, axis=-1)
    y_a, conv_new, ssd_new = ssd_branch(z_a, xbc, dt_raw, conv_prev, ssd_h0, conv_w, conv_b,
                                        dt_bias, a_log, d_ssd, ssd_norm_g)
    y_b, s5_re, s5_im = s5_branch(u_b, z_b, s5_h0_re, s5_h0_im, lam_re, lam_im, log_dt,
                                  b_re, b_im, c_re, c_im, d_s5, w_glu, b_glu)
    out = jnp.concatenate([y_a, y_b], axis=-1) @ w_out
    return x + out.astype(x.dtype), conv_new, ssd_new, s5_re, s5_im


def run_trunk(x, conv_s, ssd_s, s5re_s, s5im_s, layer_weights, final_norm_g):
    convs, ssds, res, ims = [], [], [], []
    for l in range(DEPTH):
        x, c, s, sr, si = mixer_layer(x, conv_s[l], ssd_s[l], s5re_s[l], s5im_s[l],
                                      *[w[l] for w in layer_weights])
        convs.append(c)
        ssds.append(s)
        res.append(sr)
        ims.append(si)
    return rmsnorm(x, final_norm_g), jnp.stack(convs), jnp.stack(ssds), jnp.stack(res), jnp.stack(ims)


def setup_inputs(seed: int = 0) -> dict:
    key = jax.random.key(seed)
    ks = jax.random.split(key, 26)
    f32 = jnp.float32

    def nrm(k, shape, scale):
        return scale * jax.random.normal(k, shape, f32)

    log_lo, log_hi = math.log(1e-3), math.log(1e-1)
    dt0 = jnp.exp(jax.random.uniform(ks[9], (DEPTH, SSD_HEADS), f32, log_lo, log_hi))
    return {
        'x_prompt': nrm(ks[0], (BATCH, SEQ, D_MODEL), 1.0),
        'x_sample': nrm(ks[1], (DEC_BATCH, DEC_SEQ, D_MODEL), 1.0),
        'state_ssd_conv': nrm(ks[2], (DEPTH, DEC_BATCH, SSD_CONV - 1, SSD_CONV_DIM), 1.0),
        'state_ssd': nrm(ks[3], (DEPTH, DEC_BATCH, SSD_HEADS, SSD_HEADDIM, SSD_STATE), 0.5),
        'state_s5_re': nrm(ks[4], (DEPTH, DEC_BATCH, S5_GROUPS, S5_STATE), 0.5),
        'state_s5_im': nrm(ks[5], (DEPTH, DEC_BATCH, S5_GROUPS, S5_STATE), 0.5),
        'norm_g': 1.0 + nrm(ks[6], (DEPTH, D_MODEL), 0.02),
        'w_in': nrm(ks[7], (DEPTH, D_MODEL, IN_COLS), D_MODEL ** -0.5),
        'conv_w': nrm(ks[8], (DEPTH, SSD_CONV, SSD_CONV_DIM), SSD_CONV ** -0.5),
        'conv_b': nrm(ks[10], (DEPTH, SSD_CONV_DIM), 0.02),
        'dt_bias': dt0 + jnp.log(-jnp.expm1(-dt0)),
        'a_log': jnp.log(jax.random.uniform(ks[11], (DEPTH, SSD_HEADS), f32, 1.0, 16.0)),
        'd_ssd': 1.0 + nrm(ks[12], (DEPTH, SSD_HEADS), 0.1),
        'ssd_norm_g': 1.0 + nrm(ks[13], (DEPTH, D_SSD), 0.02),
        'lam_re': -0.5 + nrm(ks[14], (DEPTH, S5_GROUPS, S5_STATE), 0.01),
        'lam_im': math.pi * jnp.arange(S5_STATE, dtype=f32) + nrm(ks[15], (DEPTH, S5_GROUPS, S5_STATE), 0.01),
        'log_dt': jax.random.uniform(ks[16], (DEPTH, S5_GROUPS), f32, log_lo, log_hi),
        'b_re': nrm(ks[17], (DEPTH, S5_GROUPS, S5_STATE, S5_GROUP_CH), (2 * S5_GROUP_CH) ** -0.5),
        'b_im': nrm(ks[18], (DEPTH, S5_GROUPS, S5_STATE, S5_GROUP_CH), (2 * S5_GROUP_CH) ** -0.5),
        'c_re': nrm(ks[19], (DEPTH, S5_GROUPS, S5_GROUP_CH, S5_STATE), S5_STATE ** -0.5),
        'c_im': nrm(ks[20], (DEPTH, S5_GROUPS, S5_GROUP_CH, S5_STATE), S5_STATE ** -0.5),
        'd_s5': nrm(ks[21], (DEPTH, D_S5), 1.0),
        'w_glu': nrm(ks[22], (DEPTH, D_S5, D_S5), D_S5 ** -0.5),
        'b_glu': nrm(ks[23], (DEPTH, D_S5), 0.02),
        'w_out': nrm(ks[24], (DEPTH, D_INNER, D_MODEL), D_INNER ** -0.5),
        'final_norm_g': 1.0 + nrm(ks[25], (D_MODEL,), 0.02),
    }


def reference(x_prompt, x_sample, state_ssd_conv, state_ssd, state_s5_re, state_s5_im,
              norm_g, w_in, conv_w, conv_b, dt_bias, a_log, d_ssd, ssd_norm_g,
              lam_re, lam_im, log_dt, b_re, b_im, c_re, c_im, d_s5, w_glu, b_glu, w_out,
              final_norm_g):
    layer_weights = (norm_g, w_in, conv_w, conv_b, dt_bias, a_log, d_ssd, ssd_norm_g,
                     lam_re, lam_im, log_dt, b_re, b_im, c_re, c_im, d_s5, w_glu, b_glu, w_out)
    bp = x_prompt.shape[0]
    zero_conv = jnp.zeros((DEPTH, bp, SSD_CONV - 1, SSD_CONV_DIM), x_prompt.dtype)
    zero_ssd = jnp.zeros((DEPTH, bp, SSD_HEADS, SSD_HEADDIM, SSD_STATE), jnp.float32)
    zero_s5 = jnp.zeros((DEPTH, bp, S5_GROUPS, S5_STATE), jnp.float32)
    y_prompt, conv_p, ssd_p, s5re_p, s5im_p = run_trunk(
        x_prompt, zero_conv, zero_ssd, zero_s5, zero_s5, layer_weights, final_norm_g)
    y_sample, conv_s, ssd_s, s5re_s, s5im_s = run_trunk(
        x_sample, state_ssd_conv, state_ssd, state_s5_re, state_s5_im, layer_weights, final_norm_g)
    return (y_prompt, y_sample, conv_p, ssd_p, s5re_p, s5im_p, conv_s, ssd_s, s5re_s, s5im_s)
```

```python
import math
from contextlib import ExitStack

import numpy as np
import concourse.bass as bass
import concourse.mybir as mybir
from concourse.bass_utils import run_bass_kernel_spmd

F32 = mybir.dt.float32
BF16 = mybir.dt.bfloat16
I32 = mybir.dt.int32
AF = mybir.ActivationFunctionType
ALU = mybir.AluOpType

NCORES = 8
DEPTH = 4
T = 4160
NCH = T // 8
EPS = 1e-6
IN_COLS = 5136
C_ZA, C_XBC, C_DT, C_U, C_ZB = 0, 1024, 3072, 3088, 4112
TILES = [(s * 2048 + i * 512, 512, s, i == 0, i == 3) for s in range(2) for i in range(4)]
TILES.append((4096, 64, 2, True, True))
import os as _os
if _os.environ.get("K_TILES"):
    TILES = [TILES[int(i)] for i in _os.environ["K_TILES"].split(",")]
K_SKIP = set(_os.environ.get("K_SKIP", "").split(","))

ENGS = ("pe", "dve", "act", "pool", "sp")


class Sync:
    def __init__(self, nc, stack, n_dma_sems=40, same_engine_sync=True):
        self.nc = nc
        self.esem = {e: stack.enter_context(nc.semaphore("s_" + e)) for e in ENGS}
        self.cnt = {e: 0 for e in ENGS}
        self.prog = {e: [] for e in ENGS}
        self.waited = {e: {} for e in ENGS}
        self.res = {}
        self.same = same_engine_sync
        self.dsems = [stack.enter_context(nc.semaphore("s_dma%d" % i)) for i in range(n_dma_sems)]
        self.dval = [0] * n_dma_sems
        self.dnext = 0
        self.sems = {}
        for e in ENGS:
            self.sems[("e", e)] = self.esem[e]
        for i, s in enumerate(self.dsems):
            self.sems[("d", i)] = s
        self.n_instr = 0

    def _need(self, e, reads, writes):
        need = {}

        def add(ev):
            if ev is None:
                return
            k, v = ev
            if need.get(k, 0) < v:
                need[k] = v

        for r in reads:
            st = self.res.get(r)
            if st is not None:
                add(st[0])
        for w in writes:
            st = self.res.get(w)
            if st is not None:
                add(st[0])
                for ev in st[1]:
                    add(ev)
        out = []
        for k, v in need.items():
            if k == ("e", e) and (e == "pe" or not self.same):
                continue
            if self.waited[e].get(k, 0) >= v:
                continue
            self.waited[e][k] = v
            out.append((k, v))
        return out

    def _commit(self, ev, reads, writes):
        for r in reads:
            st = self.res.setdefault(r, [None, []])
            st[1].append(ev)
            if len(st[1]) > 16:
                mx = {}
                for k, v in st[1]:
                    if mx.get(k, 0) < v:
                        mx[k] = v
                st[1] = list(mx.items())
        for w in writes:
            self.res[w] = [ev, []]

    def op(self, e, fn, reads=(), writes=()):
        waits = self._need(e, reads, writes)
        self.cnt[e] += 1
        ev = (("e", e), self.cnt[e])
        sem = self.esem[e]
        sems = self.sems

        def emit(eng, waits=waits, fn=fn, sem=sem):
            for k, v in waits:
                eng.wait_ge(sems[k], v)
            fn(eng).then_inc(sem, 1)

        self.prog[e].append(emit)
        self._commit(ev, reads, writes)
        self.n_instr += 1
        return ev

    def dma(self, e, fn, reads=(), writes=()):
        i = self.dnext
        self.dnext = (self.dnext + 1) % len(self.dsems)
        k = ("d", i)
        waits = self._need(e, reads, writes)
        if self.dval[i] > 0 and self.waited[e].get(k, 0) < self.dval[i]:
            self.waited[e][k] = self.dval[i]
            waits.append((k, self.dval[i]))
        self.dval[i] += 16
        ev = (k, self.dval[i])
        sem = self.dsems[i]
        sems = self.sems

        def emit(eng, waits=waits, fn=fn, sem=sem):
            for kk, v in waits:
                eng.wait_ge(sems[kk], v)
            fn(eng).then_inc(sem, 16)

        self.prog[e].append(emit)
        self._commit(ev, reads, writes)
        self.n_instr += 1
        return ev

    def barrier(self):
        evs = [(("e", e), self.cnt[e]) for e in ENGS if self.cnt[e] > 0]
        evs += [(("d", i), v) for i, v in enumerate(self.dval) if v > 0]
        sems = self.sems
        for e in ENGS:
            waits = []
            for k, v in evs:
                if k == ("e", e):
                    continue
                if self.waited[e].get(k, 0) >= v:
                    continue
                self.waited[e][k] = v
                waits.append((k, v))

            def emit(eng, waits=waits):
                for kk, v in waits:
                    eng.wait_ge(sems[kk], v)

            self.prog[e].append(emit)
        self.res = {}

    def finish(self, block):
        self.barrier()
        prog = self.prog

        @block.tensor
        def _(eng):
            for f in prog["pe"]:
                f(eng)

        @block.vector
        def _(eng):
            for f in prog["dve"]:
                f(eng)

        @block.scalar
        def _(eng):
            for f in prog["act"]:
                f(eng)

        @block.gpsimd
        def _(eng):
            for f in prog["pool"]:
                f(eng)

        @block.sync
        def _(eng):
            for f in prog["sp"]:
                f(eng)


W_NAMES = ["norm_g", "w_in", "conv_w", "conv_b", "dt_bias", "a_log", "d_ssd", "ssd_norm_g",
           "lam_re", "lam_im", "log_dt", "b_re", "b_im", "c_re", "c_im", "d_s5", "w_glu", "b_glu",
           "w_out", "final_norm_g"]
W_SHAPES = {
    "norm_g": [4, 1024], "w_in": [4, 1024, 5136], "conv_w": [4, 4, 2048], "conv_b": [4, 2048],
    "dt_bias": [4, 16], "a_log": [4, 16], "d_ssd": [4, 16], "ssd_norm_g": [4, 1024],
    "lam_re": [4, 64, 64], "lam_im": [4, 64, 64], "log_dt": [4, 64], "b_re": [4, 64, 64, 16],
    "b_im": [4, 64, 64, 16], "c_re": [4, 64, 16, 64], "c_im": [4, 64, 16, 64], "d_s5": [4, 1024],
    "w_glu": [4, 1024, 1024], "b_glu": [4, 1024], "w_out": [4, 2048, 1024], "final_norm_g": [1024],
}


def build_nc(n_layers=DEPTH, dbg=False, phases=(1, 2, 3, 4, 5), do_final=True):
    nc = bass.Bass("TRN2", target_bir_lowering=False)
    skind = "ExternalOutput" if dbg else "Internal"

    def din(name, shape, dt=F32):
        return nc.dram_tensor(name, shape, dt, kind="ExternalInput").ap()

    def dout(name, shape, dt=F32):
        return nc.dram_tensor(name, shape, dt, kind="ExternalOutput").ap()

    def dscr(name, shape, dt=F32):
        return nc.dram_tensor(name, shape, dt, kind=skind).ap()

    x_in = din("x_core", [T, 1024])
    conv0 = din("conv0", [4, 3, 2048])
    ssd0 = din("ssd0", [4, 16, 64, 128])
    s5r0 = din("s5r0", [4, 64, 64])
    s5i0 = din("s5i0", [4, 64, 64])
    Wd = {n: din(n, W_SHAPES[n]) for n in W_NAMES}

    y_out = dout("y_out", [T, 1024])
    conv_o = dout("conv_o", [4, 3, 3, 2048])
    ssd_o = dout("ssd_o", [4, 3, 16, 64, 128])
    s5r_o = dout("s5r_o", [4, 3, 64, 64])
    s5i_o = dout("s5i_o", [4, 3, 64, 64])

    XT = dscr("XT", [8, 128, T])
    SZA = dscr("SZA", [8, 128, T], BF16)
    XBC = dscr("XBC", [16, 128, T])
    DTR = dscr("DTR", [T, 16])
    UD = dscr("UD", [8, 16, 8, 8, NCH], BF16)
    UT = dscr("UT", [8, 16, 8, T], BF16)
    SZB = dscr("SZB", [8, 128, T], BF16)
    YA = dscr("YA", [8, 128, T], BF16)
    YD = dscr("YD", [16, 8, 8 * 8, NCH]).rearrange("c g (j s) n -> c g j s n", s=8)
    YB = dscr("YB", [8, 128, T], BF16)

    with ExitStack() as st:
        S = Sync(nc, st)
        st.enter_context(nc.allow_non_contiguous_dma(reason="small strided parameter/state loads"))

        uniq = [0]

        def sb(stack, name, shape, dt):
            uniq[0] += 1
            return stack.enter_context(nc.sbuf_tensor("%s_%d" % (name, uniq[0]), shape, dt))

        def ps(stack, name, shape, dt=F32):
            uniq[0] += 1
            return stack.enter_context(nc.psum_tensor("%s_%d" % (name, uniq[0]), shape, dt))


        def TT(eng, out, in0, in1, op, reads, writes):
            return S.op(eng, lambda e: e.tensor_tensor(out=out, in0=in0, in1=in1, op=op), reads, writes)

        def TS(eng, out, in0, s1, s2, op0, op1, reads, writes):
            return S.op(eng, lambda e: e.tensor_scalar(out=out, in0=in0, scalar1=s1, scalar2=s2, op0=op0, op1=op1), reads, writes)

        def TSS(eng, out, in_, scalar, op, reads, writes):
            return S.op(eng, lambda e: e.tensor_single_scalar(out=out, in_=in_, scalar=scalar, op=op), reads, writes)

        def STT(out, in0, scalar, in1, op0, op1, reads, writes):
            return S.op("dve", lambda e: e.scalar_tensor_tensor(out=out, in0=in0, scalar=scalar, in1=in1, op0=op0, op1=op1), reads, writes)

        def ACT(out, in_, func, reads, writes, bias=None, scale=None):
            kw = {}
            if bias is not None:
                kw["bias"] = bias
            if scale is not None:
                kw["scale"] = scale
            return S.op("act", lambda e: e.activation(out=out, in_=in_, func=func, **kw), reads, writes)

        def CP(eng, out, in_, reads, writes):
            if eng == "act":
                return ACT(out, in_, AF.Copy, reads, writes)
            return S.op(eng, lambda e: e.tensor_copy(out=out, in_=in_), reads, writes)

        def MM(out, lhsT, rhs, start, stop, reads, writes):
            return S.op("pe", lambda e: e.matmul(out, lhsT=lhsT, rhs=rhs, start=start, stop=stop), reads, writes)

        def TR(out, in_, ident, reads, writes):
            return S.op("pe", lambda e: e.transpose(out=out, in_=in_, identity=ident), reads, writes)

        def DMA(out, in_, reads=(), writes=(), q="sp"):
            return S.dma(q, lambda e: e.dma_start(out=out, in_=in_), reads, writes)

        def MSET(eng, ap, val, writes):
            return S.op(eng, lambda e: e.memset(ap, val), (), writes)

        ones_f = sb(st, "ones_f", [128, 128], F32)
        ident_f = sb(st, "ident_f", [128, 128], F32)
        ident_b = sb(st, "ident_b", [128, 128], BF16)
        ones_b = sb(st, "ones_b", [128, 128], BF16)
        tri_f = sb(st, "tri_f", [128, 128], F32)
        tri_b = sb(st, "tri_b", [128, 128], BF16)
        bmask = sb(st, "bmask", [128, 128], F32)
        block = st.enter_context(nc.Block())

        S.op("pool", lambda e: e.memset(ones_f[:], 1.0), writes=["ones_f"])
        S.op("pool", lambda e: e.affine_select(out=ident_f[:], in_=ones_f[:], pattern=[[1, 128]],
                                               compare_op=ALU.is_equal, fill=0.0, base=0, channel_multiplier=-1),
             reads=["ones_f"], writes=["ident_f"])
        S.op("pool", lambda e: e.affine_select(out=tri_f[:], in_=ones_f[:], pattern=[[1, 128]],
                                               compare_op=ALU.is_ge, fill=0.0, base=0, channel_multiplier=-1),
             reads=["ones_f"], writes=["tri_f"])
        S.op("pool", lambda e: e.affine_select(out=bmask[:], in_=ones_f[:], pattern=[[16, 8], [0, 16]],
                                               compare_op=ALU.is_ge, fill=0.0, base=15, channel_multiplier=-1),
             reads=["ones_f"], writes=["bmask"])
        S.op("dve", lambda e: e.tensor_copy(out=ident_b[:], in_=ident_f[:]), reads=["ident_f"], writes=["ident_b"])
        S.op("dve", lambda e: e.tensor_copy(out=ones_b[:], in_=ones_f[:]), reads=["ones_f"], writes=["ones_b"])
        S.op("dve", lambda e: e.tensor_copy(out=tri_b[:], in_=tri_f[:]), reads=["tri_f"], writes=["tri_b"])
        S.barrier()

        def xt_ap(t0, W):
            return XT[:, :, t0:t0 + W].rearrange("k p t -> p k t")

        def rms_rstd(ph_ps, src_sq, W, scale, rs, rs2, key):
            nk = len(src_sq)
            for i, a in enumerate(src_sq):
                MM(ph_ps[:, :W], ones_b[:], a, i == 0, i == nk - 1, [key + "_sq"], [key + "_ms"])
            ACT(rs[:, :W], ph_ps[:, :W], AF.Sqrt, [key + "_ms"], [key + "_rs"], bias=EPS, scale=scale)
            S.op("dve", lambda e: e.reciprocal(out=rs2[:, :W], in_=rs[:, :W]), [key + "_rs"], [key + "_rs2"])

        def phase0():
            with ExitStack() as ph:
                xs = sb(ph, "p0_xs", [128, 4, 1024], F32)
                xo = sb(ph, "p0_xo", [128, 8, 512], F32)
                tp = [ps(ph, "p0_tp%d" % i, [128, 512]) for i in range(2)]
                for (t0, W, seq, first, last) in TILES:
                    Pb = min(128, W)
                    nb = W // Pb
                    DMA(xs[:Pb, :nb, :], x_in[t0:t0 + W, :].rearrange("(b p) d -> p b d", p=Pb), [], ["xs"])
                    for kt in range(8):
                        tpk = tp[kt % 2]
                        for b in range(nb):
                            TR(tpk[:, b * Pb:(b + 1) * Pb], xs[:Pb, b, kt * 128:(kt + 1) * 128], ident_f[:Pb, :Pb], ["xs"], [("tp", kt % 2)])
                        CP("act" if kt % 2 else "dve", xo[:, kt, :W], tpk[:, :W], [("tp", kt % 2)], ["xo"])
                    DMA(xt_ap(t0, W), xo[:, :, :W], ["xo"], [("XT", t0)])
            S.barrier()

        def phase1(l):
            with ExitStack() as ph:
                w_sb = sb(ph, "p1_w", [128, 8, IN_COLS], BF16)
                g1 = sb(ph, "p1_g", [128, 8], F32)
                xT = sb(ph, "p1_xT", [128, 8, 512], F32)
                sq = sb(ph, "p1_sq", [128, 8, 512], BF16)
                hT = sb(ph, "p1_hT", [128, 8, 512], BF16)
                rs = sb(ph, "p1_rs", [128, 512], F32)
                rs2 = sb(ph, "p1_rs2", [128, 512], F32)
                st_za = sb(ph, "p1_za", [128, 8, 512], BF16)
                st_xbc = [sb(ph, "p1_xbc%d" % i, [128, 2, 512], F32) for i in range(2)]
                st_us = sb(ph, "p1_us", [128, 8, 8, 64], BF16)
                st_ut = sb(ph, "p1_ut", [128, 8, 512], BF16)
                us32 = [sb(ph, "p1_us32_%d" % i, [128, 8, 64], F32) for i in range(2)]
                st_zb = sb(ph, "p1_zb", [128, 8, 512], BF16)
                st_dt = sb(ph, "p1_dt", [128, 4, 16], F32)
                mm = [ps(ph, "p1_mm%d" % i, [128, 512]) for i in range(4)]
                msp = ps(ph, "p1_ms", [128, 512])
                dtp = ps(ph, "p1_dtp", [128, 4, 16])
                if "wload" in K_SKIP:
                    MSET("pool", w_sb[:], 0.0, ["w"])
                else:
                    for kt in range(8):
                        for (c0, cw_) in ((0, 2048), (2048, 2048), (4096, 1040)):
                            DMA(w_sb[:, kt, c0:c0 + cw_], Wd["w_in"][l, kt * 128:(kt + 1) * 128, c0:c0 + cw_], [], ["w"], q="pool")
                DMA(g1[:], Wd["norm_g"][l].rearrange("(k p) -> p k", p=128), [], ["g1"])
                for kt in range(0 if "perm" in K_SKIP else 8):
                    tmpu = st_ut[:, (kt % 2) * 2:(kt % 2) * 2 + 2, :].rearrange("p a t -> p (a t)")
                    CP("dve", tmpu, w_sb[:, kt, C_U:C_U + 1024], ["w"], [("tmpu", kt % 2), "st_ut"])
                    CP("act", w_sb[:, kt, C_U:C_U + 1024].rearrange("p (j c g) -> p j c g", c=16, g=8),
                       tmpu.rearrange("p (j g c) -> p j c g", g=8, c=16), [("tmpu", kt % 2), "st_ut"], ["w"])
                mcount = [0]
                for (t0, W, seq, first, last) in TILES:
                    Pb = min(128, W)
                    nb = W // Pb
                    nC = W // 8
                    n0 = t0 // 8
                    DMA(xT[:, :, :W], xt_ap(t0, W), [("XT", t0)], ["xT"])
                    ACT(sq[:, :, :W], xT[:, :, :W], AF.Square, ["xT"], ["n1_sq"])
                    rms_rstd(msp, [sq[:, kt, :W] for kt in range(8)], W, 1.0 / 1024, rs, rs2, "n1")
                    for kt in range(8):
                        STT(hT[:, kt, :W], xT[:, kt, :W], g1[:, kt:kt + 1], rs2[:, :W], ALU.mult, ALU.mult, ["xT", "g1", "n1_rs2"], ["hT"])

                    def mtile(lhs_fn, W=W):
                        bank = mm[mcount[0] % 4]
                        key = ("mm", mcount[0] % 4)
                        mcount[0] += 1
                        for kt in range(8):
                            MM(bank[:, :W], lhs_fn(kt), hT[:, kt, :W], kt == 0, kt == 7, ["w", "hT"], [key])
                        return bank, key

                    for m in range(0 if "za" in K_SKIP else 8):
                        bank, key = mtile(lambda kt, m=m: w_sb[:, kt, C_ZA + m * 128:C_ZA + (m + 1) * 128])
                        ACT(st_za[:, m, :W], bank[:, :W], AF.Silu, [key], ["st_za"])
                    if "za" not in K_SKIP:
                        DMA(SZA[:, :, t0:t0 + W].rearrange("m p t -> p m t"), st_za[:, :, :W], ["st_za"], [("SZA", t0)])
                    for m in range(0 if "xbc" in K_SKIP else 16):
                        bank, key = mtile(lambda kt, m=m: w_sb[:, kt, C_XBC + m * 128:C_XBC + (m + 1) * 128])
                        pr = m // 2
                        stx = st_xbc[pr % 2]
                        skey = ("st_xbc", pr % 2)
                        CP("act" if m % 2 == 0 else "dve", stx[:, m % 2, :W], bank[:, :W], [key], [skey])
                        if m % 2 == 1:
                            DMA(XBC[pr * 2:(pr + 1) * 2, :, t0:t0 + W].rearrange("m p t -> p m t"), stx[:, :, :W], [skey], [("XBC", t0, pr)])
                    for j in range(0 if "u" in K_SKIP else 8):
                        bank, key = mtile(lambda kt, j=j: w_sb[:, kt, C_U + j * 128:C_U + (j + 1) * 128])
                        u32 = us32[j % 2]
                        ACT(u32[:, :, :nC].rearrange("p s n -> p n s"), bank[:, :W].rearrange("p (n s) -> p n s", s=8), AF.Copy, [key], [("us32", j % 2), ("brd", key)])
                        CP("pool", st_us[:, j, :, :nC], u32[:, :, :nC], [("us32", j % 2)], ["st_us"])
                        CP("dve", st_ut[:, j, :W], bank[:, :W], [key, ("brd", key)], ["st_ut"])
                    for j in range(0 if "ud" in K_SKIP else 8):
                        DMA(UD[:, :, :, j, n0:n0 + nC].rearrange("s c g n -> (c g) s n"), st_us[:, j, :, :nC], ["st_us"], [("UD", t0)])
                    if "u" not in K_SKIP and "utdma" not in K_SKIP:
                        DMA(UT[:, :, :, t0:t0 + W].rearrange("j c g t -> (c g) j t"), st_ut[:, :, :W], ["st_ut"], [("UT", t0)])
                    for m in range(0 if "zb" in K_SKIP else 8):
                        bank, key = mtile(lambda kt, m=m: w_sb[:, kt, C_ZB + m * 128:C_ZB + (m + 1) * 128])
                        ACT(st_zb[:, m, :W], bank[:, :W], AF.Silu, [key], ["st_zb"])
                    if "zb" not in K_SKIP:
                        DMA(SZB[:, :, t0:t0 + W].rearrange("m p t -> p m t"), st_zb[:, :, :W], ["st_zb"], [("SZB", t0)])
                    for b in range(0 if "dt" in K_SKIP else nb):
                        for kt in range(8):
                            MM(dtp[:Pb, b, :], hT[:, kt, b * Pb:(b + 1) * Pb], w_sb[:, kt, C_DT:C_DT + 16], kt == 0, kt == 7, ["w", "hT"], ["dtp"])
                    if "dt" not in K_SKIP:
                        ACT(st_dt[:Pb, :nb, :], dtp[:Pb, :nb, :], AF.Copy, ["dtp"], ["st_dt"])
                        DMA(DTR[t0:t0 + W, :].rearrange("(b p) h -> p b h", p=Pb), st_dt[:Pb, :nb, :], ["st_dt"], [("DTR", t0)])
            S.barrier()

        def phase2(l):
            with ExitStack() as ph:
                cw = sb(ph, "p2_cw", [128, 16, 4], F32)
                cbias = sb(ph, "p2_cb", [128, 16], F32)
                dtb = sb(ph, "p2_dtb", [128, 16], F32)
                arow = sb(ph, "p2_arow", [128, 16], F32)
                dvec = sb(ph, "p2_dvec", [128, 8], F32)
                ng = sb(ph, "p2_ng", [128, 8], F32)
                tails = sb(ph, "p2_tails", [128, 16, 3], F32)
                hstate = sb(ph, "p2_hst", [128, 1024], F32)
                hbf = sb(ph, "p2_hbf", [128, 1024], BF16)
                hio = sb(ph, "p2_hio", [128, 8, 128], F32)
                raw = sb(ph, "p2_raw", [128, 4, 515], F32)
                acc = [sb(ph, "p2_acc%d" % i, [128, 512], F32) for i in range(2)]
                xc = sb(ph, "p2_xc", [128, 16, 512], BF16)
                sza = sb(ph, "p2_sza", [128, 8, 512], BF16)
                dtr = sb(ph, "p2_dtr", [128, 4, 16], F32)
                x1 = sb(ph, "p2_x1", [128, 4, 16], F32)
                tA = sb(ph, "p2_tA", [128, 4, 16], F32)
                tB = sb(ph, "p2_tB", [128, 4, 16], F32)
                dtv = sb(ph, "p2_dtv", [128, 4, 16], F32)
                av = sb(ph, "p2_av", [128, 4, 16], F32)
                a_hi = sb(ph, "p2_ahi", [128, 4, 16], BF16)
                a_lo = sb(ph, "p2_alo", [128, 4, 16], BF16)
                xdt = sb(ph, "p2_xdt", [128, 1024], BF16)
                xe = sb(ph, "p2_xe", [128, 1024], BF16)
                btok = sb(ph, "p2_btok", [128, 512], BF16)
                acs = sb(ph, "p2_acs", [128, 16], F32)
                tmd = sb(ph, "p2_tmd", [128, 16], F32)
                te = sb(ph, "p2_te", [128, 16], F32)
                etot = sb(ph, "p2_etot", [128, 16], F32)
                cbm = sb(ph, "p2_cbm", [128, 4, 128], BF16)
                E = [sb(ph, "p2_E%d" % i, [128, 128], BF16) for i in range(2)]
                arg = [sb(ph, "p2_arg%d" % i, [128, 128], F32) for i in range(2)]
                dec = [sb(ph, "p2_dec%d" % i, [128, 128], BF16) for i in range(2)]
                Mh = [sb(ph, "p2_M%d" % i, [128, 128], BF16) for i in range(2)]
                Cs = [sb(ph, "p2_Cs%d" % i, [128, 128], BF16) for i in range(2)]
                yg = sb(ph, "p2_yg", [128, 8, 512], F32)
                sq = sb(ph, "p2_sq", [128, 8, 512], BF16)
                rs = sb(ph, "p2_rs", [128, 512], F32)
                rs2 = sb(ph, "p2_rs2", [128, 512], F32)
                htmp = sb(ph, "p2_htmp", [128, 512], F32)
                ya = sb(ph, "p2_ya", [128, 8, 512], BF16)
                smallp = ps(ph, "p2_small", [128, 512])
                tpp = ps(ph, "p2_tp", [128, 1024], BF16)
                cbp = ps(ph, "p2_cbp", [128, 512])
                acsb = [ps(ph, "p2_acsb%d" % i, [128, 512]) for i in range(2)]
                yps = [ps(ph, "p2_yps%d" % i, [128, 512]) for i in range(2)]
                hps = ps(ph, "p2_hps", [128, 512])

                for k in range(4):
                    DMA(cw[:, :, k], Wd["conv_w"][l, k].rearrange("(m p) -> p m", p=128), [], ["cw"])
                DMA(cbias[:], Wd["conv_b"][l].rearrange("(m p) -> p m", p=128), [], ["cbias"])
                DMA(dtb[:], Wd["dt_bias"][l].partition_broadcast(128), [], ["dtb"])
                DMA(arow[:], Wd["a_log"][l].partition_broadcast(128), [], ["arow"])
                for h2 in range(2):
                    DMA(dvec[h2 * 64:(h2 + 1) * 64, :], Wd["d_ssd"][l].rearrange("(m h) -> h m", h=2)[h2].partition_broadcast(64), [], ["dvec"])
                DMA(ng[:], Wd["ssd_norm_g"][l].rearrange("(k p) -> p k", p=128), [], ["ng"])
                ACT(arow[:], arow[:], AF.Exp, ["arow"], ["arow"])
                TSS("dve", arow[:], arow[:], -1.0, ALU.mult, ["arow"], ["arow"])

                for (t0, W, seq, first, last) in TILES:
                    Qc = min(128, W)
                    nck = W // Qc
                    if first:
                        if seq < 2:
                            MSET("pool", tails[:], 0.0, ["tails"])
                            MSET("pool", hstate[:], 0.0, ["hstate"])
                            MSET("pool", hbf[:], 0.0, ["hbf"])
                        else:
                            for k in range(3):
                                DMA(tails[:, :, k], conv0[l, k].rearrange("(m p) -> p m", p=128), [], ["tails"])
                            DMA(hio[:], ssd0[l].rearrange("(m h) p n -> (h p) m n", h=2), [], ["hio"])
                            for m in range(8):
                                TR(hps[:, (m % 4) * 128:(m % 4 + 1) * 128], hio[:, m, :], ident_f[:], ["hio"], ["hps"])
                                if m % 4 == 3:
                                    hh = m // 4
                                    CP("dve", hstate[:, hh * 512:(hh + 1) * 512], hps[:], ["hps"], ["hstate"])
                            CP("act", hbf[:], hstate[:], ["hstate"], ["hbf"])
                    for gq in range(4):
                        DMA(raw[:, :, 3:3 + W], XBC[gq * 4:(gq + 1) * 4, :, t0:t0 + W].rearrange("m p t -> p m t"),
                            [("XBC", t0, 2 * gq), ("XBC", t0, 2 * gq + 1)], ["raw"])
                        CP("dve", raw[:, :, 0:3], tails[:, gq * 4:(gq + 1) * 4, :], ["tails"], ["raw"])
                        for mi in range(4):
                            m = gq * 4 + mi
                            ac = acc[m % 2]
                            akey = ("acc", m % 2)
                            TS("dve", ac[:, :W], raw[:, mi, 0:W], cw[:, m, 0:1], cbias[:, m:m + 1], ALU.mult, ALU.add, ["raw", "cw", "cbias"], [akey])
                            for k in range(1, 4):
                                STT(ac[:, :W], raw[:, mi, k:k + W], cw[:, m, k:k + 1], ac[:, :W], ALU.mult, ALU.add, ["raw", "cw", akey], [akey])
                            ACT(xc[:, m, :W], ac[:, :W], AF.Silu, [akey], [("xc", m)])
                        CP("dve", tails[:, gq * 4:(gq + 1) * 4, :], raw[:, :, W:W + 3], ["raw"], ["tails"])
                    DMA(sza[:, :, :W], SZA[:, :, t0:t0 + W].rearrange("m p t -> p m t"), [("SZA", t0)], ["sza"])
                    DMA(dtr[:Qc, :nck, :], DTR[t0:t0 + W, :].rearrange("(b p) h -> p b h", p=Qc), [("DTR", t0)], ["dtr"])
                    sl = lambda tl: tl[:Qc, :nck, :]
                    bc = lambda tl: tl[:Qc, :].unsqueeze(1).to_broadcast([Qc, nck, 16])
                    TT("dve", sl(x1), sl(dtr), bc(dtb), ALU.add, ["dtr", "dtb"], ["x1"])
                    STT(sl(tA), sl(x1), -1.0, sl(x1), ALU.mult, ALU.max, ["x1"], ["tA"])
                    ACT(sl(tA), sl(tA), AF.Exp, ["tA"], ["tA"], scale=-1.0)
                    ACT(sl(tA), sl(tA), AF.Ln, ["tA"], ["tA"], bias=1.0, scale=1.0)
                    TSS("dve", sl(tB), sl(x1), 0.0, ALU.max, ["x1"], ["tB"])
                    TT("dve", sl(dtv), sl(tA), sl(tB), ALU.add, ["tA", "tB"], ["dtv"])
                    TT("dve", sl(av), sl(dtv), bc(arow), ALU.mult, ["dtv", "arow"], ["av"])
                    CP("dve", sl(a_hi), sl(av), ["av"], ["a_hi"])
                    TT("dve", sl(a_lo), sl(av), sl(a_hi), ALU.subtract, ["av", "a_hi"], ["a_lo"])
                    for c in range(nck):
                        cs = c * Qc
                        hd = lambda ap: ap.rearrange("p (h d) -> p h d", d=64)
                        for mq in range(2):
                            for mi in range(4):
                                TR(tpp[:Qc, mq * 512 + mi * 128:mq * 512 + (mi + 1) * 128], xc[:, mq * 4 + mi, cs:cs + Qc], ident_b[:],
                                   [("xc", mq * 4 + mi)], ["tp"])
                            TT("dve", hd(xdt[:Qc, mq * 512:(mq + 1) * 512]), hd(tpp[:Qc, mq * 512:(mq + 1) * 512]),
                               dtv[:Qc, c, mq * 8:(mq + 1) * 8].unsqueeze(2).to_broadcast([Qc, 8, 64]), ALU.mult, ["tp", "dtv"], [("xdt", mq)])
                        for g in range(4):
                            TR(tpp[:Qc, g * 128:(g + 1) * 128], xc[:, 8 + g, cs:cs + Qc], ident_b[:], [("xc", 8 + g)], ["tp"])
                        CP("dve", btok[:Qc, :], tpp[:Qc, 0:512], ["tp"], ["btok"])
                        for i, aa in enumerate((a_hi, a_lo)):
                            MM(smallp[:Qc, 0:16], tri_b[:Qc, :Qc], aa[:Qc, c, :], i == 0, i == 1, ["a_hi", "a_lo", "tri_b"], ["small"])
                        for i, aa in enumerate((a_hi, a_lo)):
                            MM(smallp[:, 16:32], ones_b[:Qc, :], aa[:Qc, c, :], i == 0, i == 1, ["a_hi", "a_lo"], ["small"])
                        CP("act", acs[:Qc, :], smallp[:Qc, 0:16], ["small"], ["acs"])
                        ACT(etot[:], smallp[:, 16:32], AF.Exp, ["small"], ["etot"])
                        TT("dve", tmd[:Qc, :], smallp[:Qc, 16:32], acs[:Qc, :], ALU.subtract, ["small", "acs", "etot"], ["tmd"])
                        ACT(te[:Qc, :], tmd[:Qc, :], AF.Exp, ["tmd"], ["te"])
                        TT("pool", hd(xe[:Qc, :]), hd(xdt[:Qc, :]), te[:Qc, :].unsqueeze(2).to_broadcast([Qc, 16, 64]), ALU.mult,
                           [("xdt", 0), ("xdt", 1), "te"], ["xe"])
                        for g in range(4):
                            MM(cbp[:Qc, g * 128:g * 128 + Qc], xc[:, 8 + g, cs:cs + Qc], xc[:, 12 + g, cs:cs + Qc], True, True,
                               [("xc", 8 + g), ("xc", 12 + g)], ["cbp"])
                        TT("dve", cbm[:Qc, :, :Qc], cbp[:Qc, :].rearrange("p (g l) -> p g l", g=4)[:, :, :Qc],
                           tri_b[:Qc, :Qc].unsqueeze(1).to_broadcast([Qc, 4, Qc]), ALU.mult, ["cbp", "tri_b"], ["cbm"])
                        for h in range(16):
                            par = h % 2
                            g = h // 4
                            ab = acsb[par]
                            for i, aa in enumerate((a_hi, a_lo)):
                                MM(ab[:, :Qc], aa[:Qc, c, h:h + 1].to_broadcast([Qc, 128]), tri_b[:Qc, :Qc], i == 0, i == 1,
                                   ["a_hi", "a_lo"], [("acsb", par)])
                            ACT(E[par][:, :Qc], ab[:, :Qc], AF.Exp, [("acsb", par)], [("E", par)])
                            TS("dve", arg[par][:Qc, :Qc], ab[:Qc, :Qc], acs[:Qc, h:h + 1], 0.0, ALU.subtract, ALU.min,
                               [("acsb", par), "acs", ("E", par)], [("arg", par)])
                            ACT(dec[par][:Qc, :Qc], arg[par][:Qc, :Qc], AF.Exp, [("arg", par)], [("dec", par)])
                            TT("pool", Mh[par][:Qc, :Qc], cbm[:Qc, g, :Qc], dec[par][:Qc, :Qc], ALU.mult, ["cbm", ("dec", par)], [("Mh", par)])
                            TT("pool", Cs[par][:, :Qc], xc[:, 12 + g, cs:cs + Qc], E[par][:, :Qc], ALU.mult, [("xc", 12 + g), ("E", par)], [("Cs", par)])
                            yp = yps[(h // 2) % 2]
                            ykey = ("yps", (h // 2) % 2)
                            MM(yp[par * 64:(par + 1) * 64, :Qc], xdt[:Qc, h * 64:(h + 1) * 64], Mh[par][:Qc, :Qc], True, False,
                               [("xdt", h // 8), ("Mh", par)], [ykey])
                            MM(yp[par * 64:(par + 1) * 64, :Qc], hbf[:, h * 64:(h + 1) * 64], Cs[par][:, :Qc], False, True,
                               ["hbf", ("Cs", par)], [ykey])
                            if par == 1:
                                m = h // 2
                                STT(yg[:, m, cs:cs + Qc], xc[:, m, cs:cs + Qc], dvec[:, m:m + 1], yp[:, :Qc], ALU.mult, ALU.add,
                                    [("xc", m), "dvec", ykey], ["yg"])
                        for half in range(2):
                            for gi in range(2):
                                g = half * 2 + gi
                                MM(hps[:, gi * 256:(gi + 1) * 256], btok[:Qc, g * 128:(g + 1) * 128], xe[:Qc, g * 256:(g + 1) * 256], True, True,
                                   ["btok", "xe"], ["hps"])
                            TT("dve", hd(htmp[:]), hd(hstate[:, half * 512:(half + 1) * 512]),
                               etot[:, half * 8:(half + 1) * 8].unsqueeze(2).to_broadcast([128, 8, 64]), ALU.mult, ["hstate", "etot"], ["htmp"])
                            TT("dve", hstate[:, half * 512:(half + 1) * 512], htmp[:], hps[:], ALU.add, ["htmp", "hps"], ["hstate"])
                        CP("act", hbf[:], hstate[:], ["hstate"], ["hbf"])
                    TT("dve", yg[:, :, :W], yg[:, :, :W], sza[:, :, :W], ALU.mult, ["yg", "sza"], ["yg"])
                    ACT(sq[:, :, :W], yg[:, :, :W], AF.Square, ["yg"], ["n2_sq"])
                    for gg in range(4):
                        rms_rstd(cbp, [sq[:, 2 * gg, :W], sq[:, 2 * gg + 1, :W]], W, 1.0 / 256, rs, rs2, "n2")
                        for m in (2 * gg, 2 * gg + 1):
                            STT(ya[:, m, :W], yg[:, m, :W], ng[:, m:m + 1], rs2[:, :W], ALU.mult, ALU.mult, ["yg", "ng", "n2_rs2"], ["ya"])
                    DMA(YA[:, :, t0:t0 + W].rearrange("m p t -> p m t"), ya[:, :, :W], ["ya"], [("YA", t0)])
                    if last:
                        for k in range(3):
                            DMA(conv_o[l, seq, k].rearrange("(m p) -> p m", p=128), tails[:, :, k], ["tails"], [])
                        for m in range(8):
                            TR(hps[:, (m % 4) * 128:(m % 4 + 1) * 128], hstate[:, m * 128:(m + 1) * 128], ident_f[:], ["hstate"], ["hps"])
                            if m % 4 == 3:
                                hh = m // 4
                                CP("dve", hio[:, hh * 4:(hh + 1) * 4, :], hps[:].rearrange("p (m n) -> p m n", n=128), ["hps"], ["hio"])
                        DMA(ssd_o[l, seq].rearrange("(m h) p n -> (h p) m n", h=2), hio[:], ["hio"], [])
            S.barrier()

        def phase3(l):
            with ExitStack() as ph:
                W_sb = sb(ph, "p3_W", [128, 64, 128], BF16)
                T0_sb = sb(ph, "p3_T0", [128, 64, 128], BF16)
                Cm_sb = sb(ph, "p3_Cm", [128, 32, 2, 128], BF16)
                L8 = sb(ph, "p3_L8", [128, 3, 32], F32)
                Hcar = sb(ph, "p3_Hcar", [128, 2, 32], F32)
                pa = ps(ph, "p3_pa", [128, 512])
                pb = ps(ph, "p3_pb", [128, 512])
                pc = ps(ph, "p3_pc", [128, 512])
                pd = ps(ph, "p3_pd", [128, 512])
                with ExitStack() as pp:
                    f64 = lambda nm, shp, dt=F32: sb(pp, nm, shp, dt)
                    lamr, lami, dtg, zr, zi = [f64("q_" + n, [64, 64]) for n in ("lamr", "lami", "dtg", "zr", "zi")]
                    kr, ki, t1, t2, den = [f64("q_" + n, [64, 64]) for n in ("kr", "ki", "t1", "t2", "den")]
                    erow_i = f64("q_erowi", [64, 25], I32)
                    erow = f64("q_erow", [64, 25])
                    ang, mag, s4, s2, Pr, Pi = [f64("q_" + n, [64, 64, 25]) for n in ("ang", "mag", "s4", "s2", "Pr", "Pi")]
                    kqi = f64("q_kqi", [64, 64, 25], I32)
                    PKr, PKi, u1 = [f64("q_" + n, [64, 64, 8]) for n in ("PKr", "PKi", "u1")]
                    Btr, Bti, Ctr, Cti = [f64("q_" + n, [64, 64, 16]) for n in ("Btr", "Bti", "Ctr", "Cti")]
                    Cn = sb(pp, "q_Cn", [128, 8, 64], F32)
                    Wtr, Wti, v1, v2, Cxr, Cxn = [f64("q_" + n, [64, 8, 8, 16]) for n in ("Wtr", "Wti", "v1", "v2", "Cxr", "Cxn")]

                    DMA(lamr[:], Wd["lam_re"][l].rearrange("g p -> p g"), [], ["lamr"])
                    DMA(lami[:], Wd["lam_im"][l].rearrange("g p -> p g"), [], ["lami"])
                    DMA(dtg[:], Wd["log_dt"][l].partition_broadcast(64), [], ["dtg"])
                    DMA(Btr[:], Wd["b_re"][l].rearrange("g p c -> p g c"), [], ["Btr"])
                    DMA(Bti[:], Wd["b_im"][l].rearrange("g p c -> p g c"), [], ["Bti"])
                    for (src, dst, nm) in ((Wd["c_re"], Ctr, "Ctr"), (Wd["c_im"], Cti, "Cti")):
                        DMA(Cn[:], src[l].rearrange("(j g) c p -> (g c) j p", g=8), [], ["Cn"])
                        for j in range(8):
                            TR(pa[:64, (j % 4) * 128:(j % 4 + 1) * 128], Cn[:, j, :], ident_f[:], ["Cn"], ["pa"])
                            if j % 4 == 3:
                                jj = j // 4
                                CP("dve", dst[:, jj * 32:(jj + 1) * 32, :], pa[:64, :].rearrange("p (g c) -> p g c", c=16), ["pa"], [nm])
                    ACT(dtg[:], dtg[:], AF.Exp, ["dtg"], ["dtg"])
                    TT("dve", zr[:], lamr[:], dtg[:], ALU.mult, ["lamr", "dtg"], ["zr"])
                    TT("dve", zi[:], lami[:], dtg[:], ALU.mult, ["lami", "dtg"], ["zi"])
                    for (a, b_, pat, base) in ((0, 8, [[-1, 8]], 7), (8, 16, [[1, 8]], 1), (16, 24, [[1, 8]], -7), (24, 25, [[1, 1]], 8)):
                        oap = erow_i[:, a:b_]
                        S.op("pool", lambda e, oap=oap, pat=pat, base=base: e.iota(out=oap, pattern=pat, base=base, channel_multiplier=0), (), ["erow_i"])
                    CP("dve", erow[:], erow_i[:], ["erow_i"], ["erow"])
                    ebc = erow[:, :].unsqueeze(1).to_broadcast([64, 64, 25])
                    gbc = lambda tl: tl[:, :].unsqueeze(2).to_broadcast([64, 64, 25])
                    fl = lambda tl: tl[:].rearrange("p g m -> p (g m)")
                    TT("dve", ang[:], gbc(zi), ebc, ALU.mult, ["zi", "erow"], ["ang"])
                    TT("dve", mag[:], gbc(zr), ebc, ALU.mult, ["zr", "erow"], ["mag"])
                    ACT(mag[:], mag[:], AF.Exp, ["mag"], ["mag"])
                    TSS("dve", s4[:], ang[:], 1.0 / (2 * math.pi), ALU.mult, ["ang"], ["s4"])
                    CP("dve", kqi[:], s4[:], ["s4"], ["kqi"])
                    CP("dve", s4[:], kqi[:], ["kqi"], ["s4"])
                    STT(fl(ang), fl(s4), -2 * math.pi, fl(ang), ALU.mult, ALU.add, ["s4", "ang"], ["ang"])
                    ACT(s4[:], ang[:], AF.Sin, ["ang"], ["s4"], scale=0.25)
                    ACT(s2[:], ang[:], AF.Sin, ["ang"], ["s2"], scale=0.5)
                    TT("dve", s4[:], s4[:], s4[:], ALU.mult, ["s4"], ["s4"])
                    TS("dve", s4[:], s4[:], -2.0, 1.0, ALU.mult, ALU.add, ["s4"], ["s4"])
                    TT("dve", Pi[:], s2[:], s4[:], ALU.mult, ["s2", "s4"], ["Pi"])
                    STT(fl(Pi), fl(Pi), 2.0, fl(mag), ALU.mult, ALU.mult, ["Pi", "mag"], ["Pi"])
                    TT("dve", s2[:], s2[:], s2[:], ALU.mult, ["s2"], ["s2"])
                    TS("dve", s2[:], s2[:], -2.0, 1.0, ALU.mult, ALU.add, ["s2"], ["s2"])
                    TT("dve", Pr[:], s2[:], mag[:], ALU.mult, ["s2", "mag"], ["Pr"])
                    TSS("dve", t1[:], Pr[:, :, 6], -1.0, ALU.add, ["Pr"], ["t1"])
                    TT("dve", den[:], lamr[:], lamr[:], ALU.mult, ["lamr"], ["den"])
                    TT("dve", t2[:], lami[:], lami[:], ALU.mult, ["lami"], ["t2"])
                    TT("dve", den[:], den[:], t2[:], ALU.add, ["den", "t2"], ["den"])
                    S.op("dve", lambda e: e.reciprocal(out=den[:], in_=den[:]), ["den"], ["den"])
                    TT("dve", kr[:], t1[:], lamr[:], ALU.mult, ["t1", "lamr"], ["kr"])
                    TT("dve", t2[:], Pi[:, :, 6], lami[:], ALU.mult, ["Pi", "lami"], ["t2"])
                    TT("dve", kr[:], kr[:], t2[:], ALU.add, ["kr", "t2"], ["kr"])
                    TT("dve", kr[:], kr[:], den[:], ALU.mult, ["kr", "den"], ["kr"])
                    TT("dve", ki[:], Pi[:, :, 6], lamr[:], ALU.mult, ["Pi", "lamr"], ["ki"])
                    TT("dve", t2[:], t1[:], lami[:], ALU.mult, ["t1", "lami"], ["t2"])
                    TT("dve", ki[:], ki[:], t2[:], ALU.subtract, ["ki", "t2"], ["ki"])
                    TT("dve", ki[:], ki[:], den[:], ALU.mult, ["ki", "den"], ["ki"])
                    k8 = lambda tl: tl[:, :].unsqueeze(2).to_broadcast([64, 64, 8])
                    TT("dve", PKr[:], Pr[:, :, 0:8], k8(kr), ALU.mult, ["Pr", "kr"], ["PKr"])
                    TT("dve", u1[:], Pi[:, :, 0:8], k8(ki), ALU.mult, ["Pi", "ki"], ["u1"])
                    TT("dve", PKr[:], PKr[:], u1[:], ALU.subtract, ["PKr", "u1"], ["PKr"])
                    TT("dve", PKi[:], Pr[:, :, 0:8], k8(ki), ALU.mult, ["Pr", "ki"], ["PKi"])
                    TT("dve", u1[:], Pi[:, :, 0:8], k8(kr), ALU.mult, ["Pi", "kr"], ["u1"])
                    TT("dve", PKi[:], PKi[:], u1[:], ALU.add, ["PKi", "u1"], ["PKi"])
                    for (ti, src, sc) in ((0, Pr, 1.0), (1, Pi, 1.0), (2, Pi, -1.0)):
                        for gh in range(2):
                            ACT(L8[gh * 64:(gh + 1) * 64, ti, :].rearrange("p (g j) -> p g j", j=8),
                                src[:, :, 24].rearrange("p (j g) -> p g j", g=8)[:, gh * 4:(gh + 1) * 4, :], AF.Copy, ["Pr", "Pi"], ["L8"], scale=sc)

                    def cmul(outr, outi, PA_r, PA_i, Xr, Xi, jb, key, rd, neg):
                        pbc = lambda P_: P_.unsqueeze(3).to_broadcast([64, 8, 8, 16])
                        xbc = lambda X_: X_[:, jb * 8:(jb + 1) * 8, :].unsqueeze(2).to_broadcast([64, 8, 8, 16])
                        TT("dve", outr[:], pbc(PA_r), xbc(Xr), ALU.mult, rd, [key + "r"])
                        TT("pool", v1[:], pbc(PA_i), xbc(Xi), ALU.mult, rd, ["v1"])
                        TT("dve", outr[:], outr[:], v1[:], ALU.subtract, [key + "r", "v1"], [key + "r"])
                        TT("pool", outi[:], pbc(PA_r), xbc(Xi), ALU.mult, rd, [key + "i"])
                        TT("dve", v2[:], pbc(PA_i), xbc(Xr), ALU.mult, rd, ["v2"])
                        TT("pool", outi[:], outi[:], v2[:], ALU.add, [key + "i", "v2"], [key + "i"])
                        if neg:
                            TSS("dve", outi[:], outi[:], -1.0, ALU.mult, [key + "i"], [key + "i"])

                    base_rd = ["Pr", "Pi", "PKr", "PKi", "Btr", "Bti", "Ctr", "Cti"]
                    sc_ = lambda ap: ap.rearrange("p s c -> p (s c)")
                    for jb in range(8):
                        gs = slice(jb * 8, (jb + 1) * 8)
                        cmul(Wtr, Wti, PKr[:, gs, :], PKi[:, gs, :], Btr, Bti, jb, "Wt", base_rd, False)
                        for (ri, src, nm) in ((0, Wtr, "Wtr"), (1, Wti, "Wti")):
                            for g8 in range(8):
                                bank, bk = (pa, "pa") if g8 < 4 else (pb, "pb")
                                TR(bank[:, (g8 % 4) * 64:(g8 % 4 + 1) * 64], sc_(src[:, g8, :, :]), ident_f[:64, :64], [nm], [bk])
                            for hb, (bank, bk) in enumerate(((pa, "pa"), (pb, "pb"))):
                                ACT(W_sb[:, :, ri * 64:(ri + 1) * 64].rearrange("p (g j) x -> p g j x", j=8)[:, hb * 4:(hb + 1) * 4, jb, :],
                                    bank[:, 0:256].rearrange("p (g x) -> p g x", x=64), AF.Copy, [bk], ["W_sb"])
                        cmul(Cxr, Cxn, Pr[:, gs, 16:24], Pi[:, gs, 16:24], Ctr, Cti, jb, "Cx", base_rd, True)
                        for g8 in range(8):
                            bank, bk = (pc, "pc") if g8 < 4 else (pd, "pd")
                            o = bank[:, (g8 % 4) * 128:(g8 % 4 + 1) * 128]
                            MM(o, sc_(Wtr[:, g8, :, :]), sc_(Cxr[:, g8, :, :]), True, False, ["Wtr", "Cxr"], [bk])
                            MM(o, sc_(Wti[:, g8, :, :]), sc_(Cxn[:, g8, :, :]), False, True, ["Wti", "Cxi"], [bk])
                        for hb, (bank, bk) in enumerate(((pc, "pc"), (pd, "pd"))):
                            TT("dve", T0_sb[:].rearrange("p (g j) x -> p g j x", j=8)[:, hb * 4:(hb + 1) * 4, jb, :],
                               bank[:].rearrange("p (g x) -> p g x", x=128), bmask[:].unsqueeze(1).to_broadcast([128, 4, 128]), ALU.mult,
                               [bk, "bmask"], ["T0_sb"])
                        cmul(Cxr, Cxn, Pr[:, gs, 8:16], Pi[:, gs, 8:16], Ctr, Cti, jb, "Cx", base_rd, True)
                        for (ri, src, nm) in ((0, Cxr, "Cxr"), (1, Cxn, "Cxi")):
                            for gh in range(2):
                                ACT(Cm_sb[gh * 64:(gh + 1) * 64, :, ri, :].rearrange("p (g j) x -> p g j x", j=8)[:, :, jb, :],
                                    src[:, gh * 4:(gh + 1) * 4, :, :].rearrange("p g s c -> p g (s c)"), AF.Copy, [nm], ["Cm_sb"])
                S.barrier()
                with ExitStack() as pm:
                    Hb = sb(pm, "p3_Hb", [128, 2, 32, 65], F32)
                    U_sb = sb(pm, "p3_U", [128, 64, 64], BF16)
                    Hbf = sb(pm, "p3_Hbf", [128, 2, 32, 64], BF16)
                    Yst = sb(pm, "p3_Yst", [128, 64, 64], F32)
                    tA = sb(pm, "p3_tA", [128, 2, 32], F32)
                    tB = sb(pm, "p3_tB", [128, 2, 32], F32)
                    lbanks = [pa, pb]
                    ybanks = [pc, pd]
                    for (t0, W, seq, first, last) in TILES:
                        nC = W // 8
                        n0 = t0 // 8
                        if first:
                            if seq < 2:
                                MSET("pool", Hb[:, :, :, 0:1], 0.0, ["Hb"])
                            else:
                                for (ri, src) in ((0, s5r0), (1, s5i0)):
                                    for g8 in range(8):
                                        gh, g4 = g8 // 4, g8 % 4
                                        DMA(Hcar[gh * 64:(gh + 1) * 64, ri, g4 * 8:(g4 + 1) * 8],
                                            src[l].rearrange("(j g) p -> p g j", g=8)[:, g8, :], [], ["Hcar"])
                                CP("dve", Hb[:, :, :, 0], Hcar[:], ["Hcar"], ["Hb"])
                        DMA(U_sb[:, :, :nC], UD[:, :, :, :, n0:n0 + nC].rearrange("s c g j n -> (s c) (g j) n"), [("UD", t0)], ["U_sb"])
                        for gl4 in range(8):
                            bank = lbanks[gl4 % 2]
                            bk = ("lb", gl4 % 2)
                            bv = bank[:].rearrange("p (r g n) -> p r g n", r=2, g=4)
                            for gh in range(2):
                                for gi in range(4):
                                    gidx = gh * 32 + gl4 * 4 + gi
                                    for ri in range(2):
                                        MM(bv[gh * 64:(gh + 1) * 64, ri, gi, :nC], W_sb[:, gidx, ri * 64:(ri + 1) * 64], U_sb[:, gidx, :nC], True, True,
                                           ["U_sb"], [bk])
                            CP("act", Hb[:, :, gl4 * 4:(gl4 + 1) * 4, 1:1 + nC], bv[:, :, :, :nC], [bk], ["Hb"])
                        for n in range(1, nC + 1):
                            TT("dve", tA[:], Hb[:, :, :, n - 1], L8[:, 0, :].unsqueeze(1).to_broadcast([128, 2, 32]), ALU.mult, ["Hb"], ["tA"])
                            TT("pool", tB[:, 0, :], Hb[:, 1, :, n - 1], L8[:, 2, :], ALU.mult, ["Hb"], ["tB0"])
                            TT("pool", tB[:, 1, :], Hb[:, 0, :, n - 1], L8[:, 1, :], ALU.mult, ["Hb"], ["tB1"])
                            TT("dve", tA[:], tA[:], tB[:], ALU.add, ["tA", "tB0", "tB1"], ["tA"])
                            TT("dve", Hb[:, :, :, n], Hb[:, :, :, n], tA[:], ALU.add, ["tA", "Hb"], ["Hb"])
                        CP("act", Hbf[:, :, :, :nC], Hb[:, :, :, 0:nC], ["Hb"], ["Hbf"])
                        CP("dve", Hcar[:], Hb[:, :, :, nC], ["Hb"], ["Hcar"])
                        CP("dve", Hb[:, :, :, 0], Hcar[:], ["Hcar", "Hbf"], ["Hb"])
                        for g8b in range(8):
                            bank = ybanks[g8b % 2]
                            bk = ("yb", g8b % 2)
                            bv = bank[:].rearrange("p (g n) -> p g n", g=8)
                            for gi in range(8):
                                gidx = g8b * 8 + gi
                                gh, gl = gidx // 32, gidx % 32
                                MM(bv[:, gi, :nC], T0_sb[:, gidx, :], U_sb[:, gidx, :nC], True, False, ["U_sb"], [bk])
                                for ri in range(2):
                                    MM(bv[:, gi, :nC], Cm_sb[gh * 64:(gh + 1) * 64, gl, ri, :], Hbf[gh * 64:(gh + 1) * 64, ri, gl, :nC], False, ri == 1,
                                       ["Hbf"], [bk])
                            CP("act" if g8b % 2 else "dve", Yst[:, g8b * 8:(g8b + 1) * 8, :nC], bv[:, :, :nC], [bk], ["Yst"])
                        for s_ in range(8):
                            DMA(YD[:, :, :, s_, n0:n0 + nC].rearrange("c g j n -> c (g j) n"), Yst[s_ * 16:(s_ + 1) * 16, :, :nC], ["Yst"], [("YD", t0)])
                        if last:
                            for (ri, dst) in ((0, s5r_o), (1, s5i_o)):
                                for g8 in range(8):
                                    gh, g4 = g8 // 4, g8 % 4
                                    DMA(dst[l, seq].rearrange("(j g) p -> p g j", g=8)[:, g8, :],
                                        Hcar[gh * 64:(gh + 1) * 64, ri, g4 * 8:(g4 + 1) * 8], ["Hcar"], [])
            S.barrier()

        def phase4(l):
            with ExitStack() as ph:
                wg = sb(ph, "p4_w", [128, 8, 1024], BF16)
                ds5 = sb(ph, "p4_ds5", [128, 8], F32)
                bgl = sb(ph, "p4_bgl", [128, 8], F32)
                yp4 = sb(ph, "p4_yp", [128, 8, 8, 64], F32)
                ut4 = sb(ph, "p4_ut", [128, 8, 512], BF16)
                szb4 = sb(ph, "p4_szb", [128, 8, 512], BF16)
                yb = sb(ph, "p4_yb", [128, 8, 512], F32)
                gf = sb(ph, "p4_gf", [128, 8, 512], F32)
                gb = sb(ph, "p4_gb", [128, 8, 512], BF16)
                sg = [sb(ph, "p4_sg%d" % i, [128, 512], F32) for i in range(2)]
                ybst = sb(ph, "p4_ybst", [128, 8, 512], BF16)
                mm = [ps(ph, "p4_mm%d" % i, [128, 512]) for i in range(2)]
                for kt in range(8):
                    DMA(wg[:, kt, :], Wd["w_glu"][l, kt * 128:(kt + 1) * 128, :], [], ["wg"], q="pool")
                DMA(ds5[:], Wd["d_s5"][l].rearrange("(k p) -> p k", p=128), [], ["ds5"])
                DMA(bgl[:], Wd["b_glu"][l].rearrange("(k p) -> p k", p=128), [], ["bgl"])
                for (t0, W, seq, first, last) in TILES:
                    nC = W // 8
                    n0 = t0 // 8
                    for g8 in range(8):
                        DMA(yp4[g8 * 16:(g8 + 1) * 16, :, :, :nC].rearrange("c j s n -> c (j s) n"),
                            YD[:, g8, :, :, n0:n0 + nC].rearrange("c j s n -> c (j s) n"), [("YD", t0)], ["yp4"])
                        DMA(ut4[g8 * 16:(g8 + 1) * 16, :, :W], UT[:, :, g8, t0:t0 + W].rearrange("j c t -> c j t"), [("UT", t0)], ["ut4"])
                    DMA(szb4[:, :, :W], SZB[:, :, t0:t0 + W].rearrange("m p t -> p m t"), [("SZB", t0)], ["szb4"])
                    for j in range(8):
                        STT(yb[:, j, :W].rearrange("p (n s) -> p n s", s=8), ut4[:, j, :W].rearrange("p (n s) -> p n s", s=8), ds5[:, j:j + 1],
                            yp4[:, j, :, :nC].rearrange("p s n -> p n s"), ALU.mult, ALU.add, ["ut4", "yp4", "ds5"], [("yb", j)])
                        ACT(gf[:, j, :W], yb[:, j, :W], AF.Gelu_apprx_tanh, [("yb", j)], [("gf", j)])
                        CP("pool", gb[:, j, :W], gf[:, j, :W], [("gf", j)], ["gb"])
                    for jo in range(8):
                        bank = mm[jo % 2]
                        bk = ("mm", jo % 2)
                        for ji in range(8):
                            MM(bank[:, :W], wg[:, ji, jo * 128:(jo + 1) * 128], gb[:, ji, :W], ji == 0, ji == 7, ["wg", "gb"], [bk])
                        sgt = sg[jo % 2]
                        sk = ("sg", jo % 2)
                        ACT(sgt[:, :W], bank[:, :W], AF.Sigmoid, [bk, "bgl"], [sk], bias=bgl[:, jo:jo + 1], scale=1.0)
                        TT("dve", sgt[:, :W], sgt[:, :W], gf[:, jo, :W], ALU.mult, [sk, ("gf", jo)], [sk])
                        TT("dve", ybst[:, jo, :W], sgt[:, :W], szb4[:, jo, :W], ALU.mult, [sk, "szb4"], ["ybst"])
                    DMA(YB[:, :, t0:t0 + W].rearrange("m p t -> p m t"), ybst[:, :, :W], ["ybst"], [("YB", t0)])
            S.barrier()

        def phase5(l):
            with ExitStack() as ph:
                wo = sb(ph, "p5_w", [128, 16, 1024], BF16)
                y5 = sb(ph, "p5_y", [128, 16, 512], BF16)
                xT5 = sb(ph, "p5_x", [128, 8, 512], F32)
                xo = sb(ph, "p5_xo", [128, 8, 512], F32)
                mm = [ps(ph, "p5_mm%d" % i, [128, 512]) for i in range(2)]
                for kt in range(16):
                    DMA(wo[:, kt, :], Wd["w_out"][l, kt * 128:(kt + 1) * 128, :], [], ["wo"], q="pool")
                for (t0, W, seq, first, last) in TILES:
                    DMA(y5[:, 0:8, :W], YA[:, :, t0:t0 + W].rearrange("m p t -> p m t"), [("YA", t0)], ["y5a"])
                    DMA(y5[:, 8:16, :W], YB[:, :, t0:t0 + W].rearrange("m p t -> p m t"), [("YB", t0)], ["y5b"])
                    DMA(xT5[:, :, :W], xt_ap(t0, W), [("XT", t0)], ["xT5"])
                    for dm in range(8):
                        bank = mm[dm % 2]
                        bk = ("mm", dm % 2)
                        for kt in range(16):
                            MM(bank[:, :W], wo[:, kt, dm * 128:(dm + 1) * 128], y5[:, kt, :W], kt == 0, kt == 15, ["wo", "y5a", "y5b"], [bk])
                        TT("dve", xo[:, dm, :W], bank[:, :W], xT5[:, dm, :W], ALU.add, [bk, "xT5"], ["xo"])
                    DMA(xt_ap(t0, W), xo[:, :, :W], ["xo"], [("XT", t0)])
            S.barrier()

        def phase_final():
            with ExitStack() as ph:
                fg = sb(ph, "pf_g", [128, 8], F32)
                xT = sb(ph, "pf_xT", [128, 8, 512], F32)
                sq = sb(ph, "pf_sq", [128, 8, 512], BF16)
                hf = sb(ph, "pf_hf", [128, 8, 512], F32)
                rs = sb(ph, "pf_rs", [128, 512], F32)
                rs2 = sb(ph, "pf_rs2", [128, 512], F32)
                yst = sb(ph, "pf_yst", [128, 4, 1024], F32)
                msp = ps(ph, "pf_ms", [128, 512])
                tp = [ps(ph, "pf_tp%d" % i, [128, 512]) for i in range(4)]
                DMA(fg[:], Wd["final_norm_g"].rearrange("(k p) -> p k", p=128), [], ["fg"])
                for (t0, W, seq, first, last) in TILES:
                    Pb = min(128, W)
                    nb = W // Pb
                    DMA(xT[:, :, :W], xt_ap(t0, W), [("XT", t0)], ["xT"])
                    ACT(sq[:, :, :W], xT[:, :, :W], AF.Square, ["xT"], ["nf_sq"])
                    rms_rstd(msp, [sq[:, kt, :W] for kt in range(8)], W, 1.0 / 1024, rs, rs2, "nf")
                    for kt in range(8):
                        STT(hf[:, kt, :W], xT[:, kt, :W], fg[:, kt:kt + 1], rs2[:, :W], ALU.mult, ALU.mult, ["xT", "fg", "nf_rs2"], ["hf"])
                    for b in range(nb):
                        for half in range(2):
                            bank = tp[(b * 2 + half) % 4]
                            bk = ("tp", (b * 2 + half) % 4)
                            for k4 in range(4):
                                kt = half * 4 + k4
                                TR(bank[:Pb, k4 * 128:(k4 + 1) * 128], hf[:, kt, b * Pb:(b + 1) * Pb], ident_f[:], ["hf"], [bk])
                            CP("act" if half else "dve", yst[:Pb, b, half * 512:(half + 1) * 512], bank[:Pb, :], [bk], ["yst"])
                    DMA(y_out[t0:t0 + W, :].rearrange("(b p) d -> p b d", p=Pb), yst[:Pb, :nb, :], ["yst"], [])
            S.barrier()

        phase0()
        for l in range(n_layers):
            if 1 in phases:
                phase1(l)
            if 2 in phases:
                phase2(l)
            if 3 in phases:
                phase3(l)
            if 4 in phases:
                phase4(l)
            if 5 in phases:
                phase5(l)
        if do_final:
            phase_final()
        S.finish(block)
        nc._n_instr = S.n_instr
    return nc


def make_in_maps(inputs):
    xp = np.ascontiguousarray(inputs["x_prompt"], dtype=np.float32)
    xs = np.ascontiguousarray(inputs["x_sample"], dtype=np.float32)
    maps = []
    for c in range(NCORES):
        m = {}
        m["x_core"] = np.ascontiguousarray(np.concatenate([xp[2 * c], xp[2 * c + 1], xs[c]], axis=0))
        m["conv0"] = np.ascontiguousarray(inputs["state_ssd_conv"][:, c])
        m["ssd0"] = np.ascontiguousarray(inputs["state_ssd"][:, c])
        m["s5r0"] = np.ascontiguousarray(inputs["state_s5_re"][:, c])
        m["s5i0"] = np.ascontiguousarray(inputs["state_s5_im"][:, c])
        for n in W_NAMES:
            m[n] = np.ascontiguousarray(inputs[n], dtype=np.float32)
        maps.append(m)
    return maps


_NC_CACHE = {}


def kernel(**inputs):
    inputs = {k: np.asarray(v) for k, v in inputs.items()}
    if "nc" not in _NC_CACHE:
        _NC_CACHE["nc"] = build_nc()
    nc = _NC_CACHE["nc"]
    maps = make_in_maps(inputs)
    res = run_bass_kernel_spmd(nc, maps, core_ids=list(range(NCORES)))
    R = res.results
    y_prompt = np.zeros((16, 2048, 1024), np.float32)
    y_sample = np.zeros((8, 64, 1024), np.float32)
    conv_p = np.zeros((4, 16, 3, 2048), np.float32)
    ssd_p = np.zeros((4, 16, 16, 64, 128), np.float32)
    s5r_p = np.zeros((4, 16, 64, 64), np.float32)
    s5i_p = np.zeros((4, 16, 64, 64), np.float32)
    conv_s = np.zeros((4, 8, 3, 2048), np.float32)
    ssd_s = np.zeros((4, 8, 16, 64, 128), np.float32)
    s5r_s = np.zeros((4, 8, 64, 64), np.float32)
    s5i_s = np.zeros((4, 8, 64, 64), np.float32)
    for c in range(NCORES):
        r = R[c]
        y = r["y_out"]
        y_prompt[2 * c] = y[0:2048]
        y_prompt[2 * c + 1] = y[2048:4096]
        y_sample[c] = y[4096:4160]
        for i in range(2):
            conv_p[:, 2 * c + i] = r["conv_o"][:, i]
            ssd_p[:, 2 * c + i] = r["ssd_o"][:, i]
            s5r_p[:, 2 * c + i] = r["s5r_o"][:, i]
            s5i_p[:, 2 * c + i] = r["s5i_o"][:, i]
        conv_s[:, c] = r["conv_o"][:, 2]
        ssd_s[:, c] = r["ssd_o"][:, 2]
        s5r_s[:, c] = r["s5r_o"][:, 2]
        s5i_s[:, c] = r["s5i_o"][:, 2]
    return (y_prompt, y_sample, conv_p, ssd_p, s5r_p, s5i_p, conv_s, ssd_s, s5r_s, s5i_s)
```

```python
import math
from contextlib import ExitStack

import numpy as np
import concourse.bass as bass
import concourse.mybir as mybir
from concourse.bass_utils import run_bass_kernel_spmd

F32 = mybir.dt.float32
BF16 = mybir.dt.bfloat16
I32 = mybir.dt.int32
AF = mybir.ActivationFunctionType
ALU = mybir.AluOpType

NCORES = 8
DEPTH = 4
T = 4160
NCH = T // 8
EPS = 1e-6
IN_COLS = 5136
C_ZA, C_XBC, C_DT, C_U, C_ZB = 0, 1024, 3072, 3088, 4112
TILES = [(s * 2048 + i * 512, 512, s, i == 0, i == 3) for s in range(2) for i in range(4)]
TILES.append((4096, 64, 2, True, True))
import os as _os
if _os.environ.get("K_TILES"):
    TILES = [TILES[int(i)] for i in _os.environ["K_TILES"].split(",")]
K_SKIP = set(_os.environ.get("K_SKIP", "").split(","))

ENGS = ("pe", "dve", "act", "pool", "sp")


class Sync:
    def __init__(self, nc, stack, n_dma_sems=40, same_engine_sync=True):
        self.nc = nc
        self.esem = {e: stack.enter_context(nc.semaphore("s_" + e)) for e in ENGS}
        self.cnt = {e: 0 for e in ENGS}
        self.prog = {e: [] for e in ENGS}
        self.waited = {e: {} for e in ENGS}
        self.res = {}
        self.same = same_engine_sync
        self.dsems = [stack.enter_context(nc.semaphore("s_dma%d" % i)) for i in range(n_dma_sems)]
        self.dval = [0] * n_dma_sems
        self.dnext = 0
        self.sems = {}
        for e in ENGS:
            self.sems[("e", e)] = self.esem[e]
        for i, s in enumerate(self.dsems):
            self.sems[("d", i)] = s
        self.n_instr = 0

    def _need(self, e, reads, writes):
        need = {}

        def add(ev):
            if ev is None:
                return
            k, v = ev
            if need.get(k, 0) < v:
                need[k] = v

        for r in reads:
            st = self.res.get(r)
            if st is not None:
                add(st[0])
        for w in writes:
            st = self.res.get(w)
            if st is not None:
                add(st[0])
                for ev in st[1]:
                    add(ev)
        out = []
        for k, v in need.items():
            if k == ("e", e) and (e == "pe" or not self.same):
                continue
            if self.waited[e].get(k, 0) >= v:
                continue
            self.waited[e][k] = v
            out.append((k, v))
        return out

    def _commit(self, ev, reads, writes):
        for r in reads:
            st = self.res.setdefault(r, [None, []])
            st[1].append(ev)
            if len(st[1]) > 16:
                mx = {}
                for k, v in st[1]:
                    if mx.get(k, 0) < v:
                        mx[k] = v
                st[1] = list(mx.items())
        for w in writes:
            self.res[w] = [ev, []]

    def op(self, e, fn, reads=(), writes=()):
        waits = self._need(e, reads, writes)
        self.cnt[e] += 1
        ev = (("e", e), self.cnt[e])
        sem = self.esem[e]
        sems = self.sems

        def emit(eng, waits=waits, fn=fn, sem=sem):
            for k, v in waits:
                eng.wait_ge(sems[k], v)
            fn(eng).then_inc(sem, 1)

        self.prog[e].append(emit)
        self._commit(ev, reads, writes)
        self.n_instr += 1
        return ev

    def dma(self, e, fn, reads=(), writes=()):
        i = self.dnext
        self.dnext = (self.dnext + 1) % len(self.dsems)
        k = ("d", i)
        waits = self._need(e, reads, writes)
        if self.dval[i] > 0 and self.waited[e].get(k, 0) < self.dval[i]:
            self.waited[e][k] = self.dval[i]
            waits.append((k, self.dval[i]))
        self.dval[i] += 16
        ev = (k, self.dval[i])
        sem = self.dsems[i]
        sems = self.sems

        def emit(eng, waits=waits, fn=fn, sem=sem):
            for kk, v in waits:
                eng.wait_ge(sems[kk], v)
            fn(eng).then_inc(sem, 16)

        self.prog[e].append(emit)
        self._commit(ev, reads, writes)
        self.n_instr += 1
        return ev

    def barrier(self):
        evs = [(("e", e), self.cnt[e]) for e in ENGS if self.cnt[e] > 0]
        evs += [(("d", i), v) for i, v in enumerate(self.dval) if v > 0]
        sems = self.sems
        for e in ENGS:
            waits = []
            for k, v in evs:
                if k == ("e", e):
                    continue
                if self.waited[e].get(k, 0) >= v:
                    continue
                self.waited[e][k] = v
                waits.append((k, v))

            def emit(eng, waits=waits):
                for kk, v in waits:
                    eng.wait_ge(sems[kk], v)

            self.prog[e].append(emit)
        self.res = {}

    def finish(self, block):
        self.barrier()
        prog = self.prog

        @block.tensor
        def _(eng):
            for f in prog["pe"]:
                f(eng)

        @block.vector
        def _(eng):
            for f in prog["dve"]:
                f(eng)

        @block.scalar
        def _(eng):
            for f in prog["act"]:
                f(eng)

        @block.gpsimd
        def _(eng):
            for f in prog["pool"]:
                f(eng)

        @block.sync
        def _(eng):
            for f in prog["sp"]:
                f(eng)


W_NAMES = ["norm_g", "w_in", "conv_w", "conv_b", "dt_bias", "a_log", "d_ssd", "ssd_norm_g",
           "lam_re", "lam_im", "log_dt", "b_re", "b_im", "c_re", "c_im", "d_s5", "w_glu", "b_glu",
           "w_out", "final_norm_g"]
W_SHAPES = {
    "norm_g": [4, 1024], "w_in": [4, 1024, 5136], "conv_w": [4, 4, 2048], "conv_b": [4, 2048],
    "dt_bias": [4, 16], "a_log": [4, 16], "d_ssd": [4, 16], "ssd_norm_g": [4, 1024],
    "lam_re": [4, 64, 64], "lam_im": [4, 64, 64], "log_dt": [4, 64], "b_re": [4, 64, 64, 16],
    "b_im": [4, 64, 64, 16], "c_re": [4, 64, 16, 64], "c_im": [4, 64, 16, 64], "d_s5": [4, 1024],
    "w_glu": [4, 1024, 1024], "b_glu": [4, 1024], "w_out": [4, 2048, 1024], "final_norm_g": [1024],
}


def build_nc(n_layers=DEPTH, dbg=False, phases=(1, 2, 3, 4, 5), do_final=True):
    nc = bass.Bass("TRN2", target_bir_lowering=False)
    skind = "ExternalOutput" if dbg else "Internal"

    def din(name, shape, dt=F32):
        return nc.dram_tensor(name, shape, dt, kind="ExternalInput").ap()

    def dout(name, shape, dt=F32):
        return nc.dram_tensor(name, shape, dt, kind="ExternalOutput").ap()

    def dscr(name, shape, dt=F32):
        return nc.dram_tensor(name, shape, dt, kind=skind).ap()

    x_in = din("x_core", [T, 1024])
    conv0 = din("conv0", [4, 3, 2048])
    ssd0 = din("ssd0", [4, 16, 64, 128])
    s5r0 = din("s5r0", [4, 64, 64])
    s5i0 = din("s5i0", [4, 64, 64])
    Wd = {n: din(n, W_SHAPES[n]) for n in W_NAMES}

    y_out = dout("y_out", [T, 1024])
    conv_o = dout("conv_o", [4, 3, 3, 2048])
    ssd_o = dout("ssd_o", [4, 3, 16, 64, 128])
    s5r_o = dout("s5r_o", [4, 3, 64, 64])
    s5i_o = dout("s5i_o", [4, 3, 64, 64])

    XT = dscr("XT", [8, 128, T])
    SZA = dscr("SZA", [8, 128, T], BF16)
    XBC = dscr("XBC", [16, 128, T])
    DTR = dscr("DTR", [T, 16])
    UD = dscr("UD", [8, 16, 8, 8, NCH], BF16)
    UT = dscr("UT", [8, 16, 8, T], BF16)
    SZB = dscr("SZB", [8, 128, T], BF16)
    YA = dscr("YA", [8, 128, T], BF16)
    YD = dscr("YD", [16, 8, 8 * 8, NCH]).rearrange("c g (j s) n -> c g j s n", s=8)
    YB = dscr("YB", [8, 128, T], BF16)

    with ExitStack() as st:
        S = Sync(nc, st)
        st.enter_context(nc.allow_non_contiguous_dma(reason="small strided parameter/state loads"))

        uniq = [0]

        def sb(stack, name, shape, dt):
            uniq[0] += 1
            return stack.enter_context(nc.sbuf_tensor("%s_%d" % (name, uniq[0]), shape, dt))

        def ps(stack, name, shape, dt=F32):
            uniq[0] += 1
            return stack.enter_context(nc.psum_tensor("%s_%d" % (name, uniq[0]), shape, dt))


        def TT(eng, out, in0, in1, op, reads, writes):
            return S.op(eng, lambda e: e.tensor_tensor(out=out, in0=in0, in1=in1, op=op), reads, writes)

        def TS(eng, out, in0, s1, s2, op0, op1, reads, writes):
            return S.op(eng, lambda e: e.tensor_scalar(out=out, in0=in0, scalar1=s1, scalar2=s2, op0=op0, op1=op1), reads, writes)

        def TSS(eng, out, in_, scalar, op, reads, writes):
            return S.op(eng, lambda e: e.tensor_single_scalar(out=out, in_=in_, scalar=scalar, op=op), reads, writes)

        def STT(out, in0, scalar, in1, op0, op1, reads, writes):
            return S.op("dve", lambda e: e.scalar_tensor_tensor(out=out, in0=in0, scalar=scalar, in1=in1, op0=op0, op1=op1), reads, writes)

        def ACT(out, in_, func, reads, writes, bias=None, scale=None):
            kw = {}
            if bias is not None:
                kw["bias"] = bias
            if scale is not None:
                kw["scale"] = scale
            return S.op("act", lambda e: e.activation(out=out, in_=in_, func=func, **kw), reads, writes)

        def CP(eng, out, in_, reads, writes):
            if eng == "act":
                return ACT(out, in_, AF.Copy, reads, writes)
            return S.op(eng, lambda e: e.tensor_copy(out=out, in_=in_), reads, writes)

        def MM(out, lhsT, rhs, start, stop, reads, writes):
            return S.op("pe", lambda e: e.matmul(out, lhsT=lhsT, rhs=rhs, start=start, stop=stop), reads, writes)

        def TR(out, in_, ident, reads, writes):
            return S.op("pe", lambda e: e.transpose(out=out, in_=in_, identity=ident), reads, writes)

        def DMA(out, in_, reads=(), writes=(), q="sp"):
            return S.dma(q, lambda e: e.dma_start(out=out, in_=in_), reads, writes)

        def MSET(eng, ap, val, writes):
            return S.op(eng, lambda e: e.memset(ap, val), (), writes)

        ones_f = sb(st, "ones_f", [128, 128], F32)
        ident_f = sb(st, "ident_f", [128, 128], F32)
        ident_b = sb(st, "ident_b", [128, 128], BF16)
        ones_b = sb(st, "ones_b", [128, 128], BF16)
        tri_f = sb(st, "tri_f", [128, 128], F32)
        tri_b = sb(st, "tri_b", [128, 128], BF16)
        bmask = sb(st, "bmask", [128, 128], F32)
        block = st.enter_context(nc.Block())

        S.op("pool", lambda e: e.memset(ones_f[:], 1.0), writes=["ones_f"])
        S.op("pool", lambda e: e.affine_select(out=ident_f[:], in_=ones_f[:], pattern=[[1, 128]],
                                               compare_op=ALU.is_equal, fill=0.0, base=0, channel_multiplier=-1),
             reads=["ones_f"], writes=["ident_f"])
        S.op("pool", lambda e: e.affine_select(out=tri_f[:], in_=ones_f[:], pattern=[[1, 128]],
                                               compare_op=ALU.is_ge, fill=0.0, base=0, channel_multiplier=-1),
             reads=["ones_f"], writes=["tri_f"])
        S.op("pool", lambda e: e.affine_select(out=bmask[:], in_=ones_f[:], pattern=[[16, 8], [0, 16]],
                                               compare_op=ALU.is_ge, fill=0.0, base=15, channel_multiplier=-1),
             reads=["ones_f"], writes=["bmask"])
        S.op("dve", lambda e: e.tensor_copy(out=ident_b[:], in_=ident_f[:]), reads=["ident_f"], writes=["ident_b"])
        S.op("dve", lambda e: e.tensor_copy(out=ones_b[:], in_=ones_f[:]), reads=["ones_f"], writes=["ones_b"])
        S.op("dve", lambda e: e.tensor_copy(out=tri_b[:], in_=tri_f[:]), reads=["tri_f"], writes=["tri_b"])
        S.barrier()

        def xt_ap(t0, W):
            return XT[:, :, t0:t0 + W].rearrange("k p t -> p k t")

        def rms_rstd(ph_ps, src_sq, W, scale, rs, rs2, key):
            nk = len(src_sq)
            for i, a in enumerate(src_sq):
                MM(ph_ps[:, :W], ones_b[:], a, i == 0, i == nk - 1, [key + "_sq"], [key + "_ms"])
            ACT(rs[:, :W], ph_ps[:, :W], AF.Ln, [key + "_ms"], [key + "_rs"], bias=EPS, scale=scale)
            ACT(rs2[:, :W], rs[:, :W], AF.Exp, [key + "_rs"], [key + "_rs2"], scale=-0.5)

        def phase0():
            with ExitStack() as ph:
                xs = sb(ph, "p0_xs", [128, 4, 1024], F32)
                xo = sb(ph, "p0_xo", [128, 8, 512], F32)
                tp = [ps(ph, "p0_tp%d" % i, [128, 512]) for i in range(2)]
                for (t0, W, seq, first, last) in TILES:
                    Pb = min(128, W)
                    nb = W // Pb
                    DMA(xs[:Pb, :nb, :], x_in[t0:t0 + W, :].rearrange("(b p) d -> p b d", p=Pb), [], ["xs"])
                    for kt in range(8):
                        tpk = tp[kt % 2]
                        for b in range(nb):
                            TR(tpk[:, b * Pb:(b + 1) * Pb], xs[:Pb, b, kt * 128:(kt + 1) * 128], ident_f[:Pb, :Pb], ["xs"], [("tp", kt % 2)])
                        CP("act" if kt % 2 else "dve", xo[:, kt, :W], tpk[:, :W], [("tp", kt % 2)], ["xo"])
                    DMA(xt_ap(t0, W), xo[:, :, :W], ["xo"], [("XT", t0)])
            S.barrier()

        def load_w_in(l, w_sb):
            for kt in range(8):
                for (c0, cw_) in ((0, 2048), (2048, 2048), (4096, 1040)):
                    DMA(w_sb[:, kt, c0:c0 + cw_], Wd["w_in"][l, kt * 128:(kt + 1) * 128, c0:c0 + cw_], [], ["w"], q="pool")

        def phase1(l, w_sb):
            with ExitStack() as ph:
                g1 = sb(ph, "p1_g", [128, 8], F32)
                xT = sb(ph, "p1_xT", [128, 8, 512], F32)
                sq = sb(ph, "p1_sq", [128, 8, 512], BF16)
                hT = sb(ph, "p1_hT", [128, 8, 512], BF16)
                rs = sb(ph, "p1_rs", [128, 512], F32)
                rs2 = sb(ph, "p1_rs2", [128, 512], F32)
                st_za = sb(ph, "p1_za", [128, 8, 512], BF16)
                st_xbc = [sb(ph, "p1_xbc%d" % i, [128, 2, 512], F32) for i in range(2)]
                st_us = sb(ph, "p1_us", [128, 8, 8, 64], BF16)
                st_ut = sb(ph, "p1_ut", [128, 8, 512], BF16)
                us32 = [sb(ph, "p1_us32_%d" % i, [128, 8, 64], F32) for i in range(2)]
                st_zb = sb(ph, "p1_zb", [128, 8, 512], BF16)
                st_dt = sb(ph, "p1_dt", [128, 4, 16], F32)
                mm = [ps(ph, "p1_mm%d" % i, [128, 512]) for i in range(4)]
                msp = ps(ph, "p1_ms", [128, 512])
                dtp = ps(ph, "p1_dtp", [128, 4, 16])
                DMA(g1[:], Wd["norm_g"][l].rearrange("(k p) -> p k", p=128), [], ["g1"])
                for kt in range(0 if "perm" in K_SKIP else 8):
                    tmpu = st_ut[:, (kt % 2) * 2:(kt % 2) * 2 + 2, :].rearrange("p a t -> p (a t)")
                    CP("dve", tmpu, w_sb[:, kt, C_U:C_U + 1024], ["w"], [("tmpu", kt % 2), "st_ut"])
                    CP("act", w_sb[:, kt, C_U:C_U + 1024].rearrange("p (j c g) -> p j c g", c=16, g=8),
                       tmpu.rearrange("p (j g c) -> p j c g", g=8, c=16), [("tmpu", kt % 2), "st_ut"], ["w"])
                mcount = [0]
                def pro_load(ti):
                    (t0_, W_, _, _, _) = TILES[ti]
                    DMA(xT[:, :, :W_], xt_ap(t0_, W_), [("XT", t0_)], ["xT"])

                def pro_sq(ti):
                    W_ = TILES[ti][1]
                    ACT(sq[:, :, :W_], xT[:, :, :W_], AF.Square, ["xT"], ["n1_sq"])

                def pro_ms(ti):
                    W_ = TILES[ti][1]
                    rms_rstd(msp, [sq[:, kt, :W_] for kt in range(8)], W_, 1.0 / 1024, rs, rs2, "n1")

                pro_load(0)
                pro_sq(0)
                pro_ms(0)
                for ti, (t0, W, seq, first, last) in enumerate(TILES):
                    Pb = min(128, W)
                    nb = W // Pb
                    nC = W // 8
                    n0 = t0 // 8
                    nxt = ti + 1 if ti + 1 < len(TILES) else None
                    for kt in range(8):
                        STT(hT[:, kt, :W], xT[:, kt, :W], g1[:, kt:kt + 1], rs2[:, :W], ALU.mult, ALU.mult, ["xT", "g1", "n1_rs2"], ["hT"])
                    if nxt is not None:
                        pro_load(nxt)

                    def mtile(lhs_fn, W=W):
                        bank = mm[mcount[0] % 4]
                        key = ("mm", mcount[0] % 4)
                        mcount[0] += 1
                        for kt in range(8):
                            MM(bank[:, :W], lhs_fn(kt), hT[:, kt, :W], kt == 0, kt == 7, ["w", "hT"], [key])
                        return bank, key

                    for m in range(0 if "za" in K_SKIP else 8):
                        bank, key = mtile(lambda kt, m=m: w_sb[:, kt, C_ZA + m * 128:C_ZA + (m + 1) * 128])
                        ACT(st_za[:, m, :W], bank[:, :W], AF.Silu, [key], ["st_za"])
                    if "za" not in K_SKIP:
                        DMA(SZA[:, :, t0:t0 + W].rearrange("m p t -> p m t"), st_za[:, :, :W], ["st_za"], [("SZA", t0)])
                    if nxt is not None:
                        pro_sq(nxt)
                    for m in range(0 if "xbc" in K_SKIP else 16):
                        bank, key = mtile(lambda kt, m=m: w_sb[:, kt, C_XBC + m * 128:C_XBC + (m + 1) * 128])
                        pr = m // 2
                        stx = st_xbc[pr % 2]
                        skey = ("st_xbc", pr % 2)
                        CP("act" if m % 2 == 0 else "dve", stx[:, m % 2, :W], bank[:, :W], [key], [skey])
                        if m % 2 == 1:
                            DMA(XBC[pr * 2:(pr + 1) * 2, :, t0:t0 + W].rearrange("m p t -> p m t"), stx[:, :, :W], [skey], [("XBC", t0, pr)])
                    if nxt is not None:
                        pro_ms(nxt)
                    for j in range(0 if "u" in K_SKIP else 8):
                        bank, key = mtile(lambda kt, j=j: w_sb[:, kt, C_U + j * 128:C_U + (j + 1) * 128])
                        u32 = us32[j % 2]
                        ACT(u32[:, :, :nC].rearrange("p s n -> p n s"), bank[:, :W].rearrange("p (n s) -> p n s", s=8), AF.Copy, [key], [("us32", j % 2), ("brd", key)])
                        CP("pool", st_us[:, j, :, :nC], u32[:, :, :nC], [("us32", j % 2)], ["st_us"])
                        CP("dve", st_ut[:, j, :W], bank[:, :W], [key, ("brd", key)], ["st_ut"])
                    for j in range(0 if "ud" in K_SKIP else 8):
                        DMA(UD[:, :, :, j, n0:n0 + nC].rearrange("s c g n -> (c g) s n"), st_us[:, j, :, :nC], ["st_us"], [("UD", t0)])
                    if "u" not in K_SKIP and "utdma" not in K_SKIP:
                        DMA(UT[:, :, :, t0:t0 + W].rearrange("j c g t -> (c g) j t"), st_ut[:, :, :W], ["st_ut"], [("UT", t0)])
                    for m in range(0 if "zb" in K_SKIP else 8):
                        bank, key = mtile(lambda kt, m=m: w_sb[:, kt, C_ZB + m * 128:C_ZB + (m + 1) * 128])
                        ACT(st_zb[:, m, :W], bank[:, :W], AF.Silu, [key], ["st_zb"])
                    if "zb" not in K_SKIP:
                        DMA(SZB[:, :, t0:t0 + W].rearrange("m p t -> p m t"), st_zb[:, :, :W], ["st_zb"], [("SZB", t0)])
                    for b in range(0 if "dt" in K_SKIP else nb):
                        for kt in range(8):
                            MM(dtp[:Pb, b, :], hT[:, kt, b * Pb:(b + 1) * Pb], w_sb[:, kt, C_DT:C_DT + 16], kt == 0, kt == 7, ["w", "hT"], ["dtp"])
                    if "dt" not in K_SKIP:
                        ACT(st_dt[:Pb, :nb, :], dtp[:Pb, :nb, :], AF.Copy, ["dtp"], ["st_dt"])
                        DMA(DTR[t0:t0 + W, :].rearrange("(b p) h -> p b h", p=Pb), st_dt[:Pb, :nb, :], ["st_dt"], [("DTR", t0)])
            S.barrier()

        def phase2(l):
            with ExitStack() as ph:
                cw = sb(ph, "p2_cw", [128, 16, 4], F32)
                cbias = sb(ph, "p2_cb", [128, 16], F32)
                dtb = sb(ph, "p2_dtb", [128, 16], F32)
                arow = sb(ph, "p2_arow", [128, 16], F32)
                dvec = sb(ph, "p2_dvec", [128, 8], F32)
                ng = sb(ph, "p2_ng", [128, 8], F32)
                tails = sb(ph, "p2_tails", [128, 16, 3], F32)
                hstate = sb(ph, "p2_hst", [128, 1024], F32)
                hbf = sb(ph, "p2_hbf", [128, 1024], BF16)
                hio = sb(ph, "p2_hio", [128, 8, 128], F32)
                raw = sb(ph, "p2_raw", [128, 4, 515], F32)
                acc = [sb(ph, "p2_acc%d" % i, [128, 512], F32) for i in range(2)]
                xc = sb(ph, "p2_xc", [128, 16, 512], BF16)
                sza = sb(ph, "p2_sza", [128, 8, 512], BF16)
                dtr = sb(ph, "p2_dtr", [128, 4, 16], F32)
                x1 = sb(ph, "p2_x1", [128, 4, 16], F32)
                tA = sb(ph, "p2_tA", [128, 4, 16], F32)
                tB = sb(ph, "p2_tB", [128, 4, 16], F32)
                dtv = sb(ph, "p2_dtv", [128, 4, 16], F32)
                av = sb(ph, "p2_av", [128, 4, 16], F32)
                a_hi = sb(ph, "p2_ahi", [128, 4, 16], BF16)
                a_lo = sb(ph, "p2_alo", [128, 4, 16], BF16)
                xdt = sb(ph, "p2_xdt", [128, 1024], BF16)
                xe = sb(ph, "p2_xe", [128, 1024], BF16)
                btok = sb(ph, "p2_btok", [128, 512], BF16)
                acs = sb(ph, "p2_acs", [128, 16], F32)
                tmd = sb(ph, "p2_tmd", [128, 16], F32)
                te = sb(ph, "p2_te", [128, 16], F32)
                etot = sb(ph, "p2_etot", [128, 16], F32)
                cbm = sb(ph, "p2_cbm", [128, 4, 128], BF16)
                E = [sb(ph, "p2_E%d" % i, [128, 512], BF16) for i in range(2)]
                arg = [sb(ph, "p2_arg%d" % i, [128, 512], F32) for i in range(2)]
                dec = [sb(ph, "p2_dec%d" % i, [128, 512], BF16) for i in range(2)]
                Mh = [sb(ph, "p2_M%d" % i, [128, 512], BF16) for i in range(2)]
                Cs = [sb(ph, "p2_Cs%d" % i, [128, 512], BF16) for i in range(2)]
                yg = sb(ph, "p2_yg", [128, 8, 512], F32)
                sq = sb(ph, "p2_sq", [128, 8, 512], BF16)
                rs = sb(ph, "p2_rs", [128, 512], F32)
                rs2 = sb(ph, "p2_rs2", [128, 512], F32)
                htmp = sb(ph, "p2_htmp", [128, 512], F32)
                ya = sb(ph, "p2_ya", [128, 8, 512], BF16)
                smallp = ps(ph, "p2_small", [128, 512])
                tpp = ps(ph, "p2_tp", [128, 1024], BF16)
                cbp = ps(ph, "p2_cbp", [128, 512])
                acsb = [ps(ph, "p2_acsb%d" % i, [128, 512]) for i in range(2)]
                yps = [ps(ph, "p2_yps%d" % i, [128, 512]) for i in range(2)]
                hps = ps(ph, "p2_hps", [128, 512])

                for k in range(4):
                    DMA(cw[:, :, k], Wd["conv_w"][l, k].rearrange("(m p) -> p m", p=128), [], ["cw"])
                DMA(cbias[:], Wd["conv_b"][l].rearrange("(m p) -> p m", p=128), [], ["cbias"])
                DMA(dtb[:], Wd["dt_bias"][l].partition_broadcast(128), [], ["dtb"])
                DMA(arow[:], Wd["a_log"][l].partition_broadcast(128), [], ["arow"])
                for h2 in range(2):
                    DMA(dvec[h2 * 64:(h2 + 1) * 64, :], Wd["d_ssd"][l].rearrange("(m h) -> h m", h=2)[h2].partition_broadcast(64), [], ["dvec"])
                DMA(ng[:], Wd["ssd_norm_g"][l].rearrange("(k p) -> p k", p=128), [], ["ng"])
                ACT(arow[:], arow[:], AF.Exp, ["arow"], ["arow"])
                TSS("dve", arow[:], arow[:], -1.0, ALU.mult, ["arow"], ["arow"])

                for (t0, W, seq, first, last) in TILES:
                    Qc = min(128, W)
                    nck = W // Qc
                    if first:
                        if seq < 2:
                            MSET("pool", tails[:], 0.0, ["tails"])
                            MSET("pool", hstate[:], 0.0, ["hstate"])
                            MSET("pool", hbf[:], 0.0, ["hbf"])
                        else:
                            for k in range(3):
                                DMA(tails[:, :, k], conv0[l, k].rearrange("(m p) -> p m", p=128), [], ["tails"])
                            DMA(hio[:], ssd0[l].rearrange("(m h) p n -> (h p) m n", h=2), [], ["hio"])
                            for m in range(8):
                                TR(hps[:, (m % 4) * 128:(m % 4 + 1) * 128], hio[:, m, :], ident_f[:], ["hio"], ["hps"])
                                if m % 4 == 3:
                                    hh = m // 4
                                    CP("dve", hstate[:, hh * 512:(hh + 1) * 512], hps[:], ["hps"], ["hstate"])
                            CP("act", hbf[:], hstate[:], ["hstate"], ["hbf"])
                    for gq in range(4):
                        DMA(raw[:, :, 3:3 + W], XBC[gq * 4:(gq + 1) * 4, :, t0:t0 + W].rearrange("m p t -> p m t"),
                            [("XBC", t0, 2 * gq), ("XBC", t0, 2 * gq + 1)], ["raw"])
                        CP("dve", raw[:, :, 0:3], tails[:, gq * 4:(gq + 1) * 4, :], ["tails"], ["raw"])
                        for mi in range(4):
                            m = gq * 4 + mi
                            ac = acc[m % 2]
                            akey = ("acc", m % 2)
                            ACT(ac[:, :W], raw[:, mi, 0:W], AF.Identity, ["raw", "cw", "cbias"], [akey], bias=cbias[:, m:m + 1], scale=cw[:, m, 0:1])
                            for k in range(1, 4):
                                STT(ac[:, :W], raw[:, mi, k:k + W], cw[:, m, k:k + 1], ac[:, :W], ALU.mult, ALU.add, ["raw", "cw", akey], [akey])
                            ACT(xc[:, m, :W], ac[:, :W], AF.Silu, [akey], [("xc", m)])
                        CP("dve", tails[:, gq * 4:(gq + 1) * 4, :], raw[:, :, W:W + 3], ["raw"], ["tails"])
                    DMA(sza[:, :, :W], SZA[:, :, t0:t0 + W].rearrange("m p t -> p m t"), [("SZA", t0)], ["sza"])
                    DMA(dtr[:Qc, :nck, :], DTR[t0:t0 + W, :].rearrange("(b p) h -> p b h", p=Qc), [("DTR", t0)], ["dtr"])
                    sl = lambda tl: tl[:Qc, :nck, :]
                    bc = lambda tl: tl[:Qc, :].unsqueeze(1).to_broadcast([Qc, nck, 16])
                    TT("dve", sl(x1), sl(dtr), bc(dtb), ALU.add, ["dtr", "dtb"], ["x1"])
                    STT(sl(tA), sl(x1), -1.0, sl(x1), ALU.mult, ALU.max, ["x1"], ["tA"])
                    ACT(sl(tA), sl(tA), AF.Exp, ["tA"], ["tA"], scale=-1.0)
                    ACT(sl(tA), sl(tA), AF.Ln, ["tA"], ["tA"], bias=1.0, scale=1.0)
                    TSS("dve", sl(tB), sl(x1), 0.0, ALU.max, ["x1"], ["tB"])
                    TT("dve", sl(dtv), sl(tA), sl(tB), ALU.add, ["tA", "tB"], ["dtv"])
                    TT("dve", sl(av), sl(dtv), bc(arow), ALU.mult, ["dtv", "arow"], ["av"])
                    CP("dve", sl(a_hi), sl(av), ["av"], ["a_hi"])
                    TT("dve", sl(a_lo), sl(av), sl(a_hi), ALU.subtract, ["av", "a_hi"], ["a_lo"])
                    for c in range(nck):
                        cs = c * Qc
                        hd = lambda ap: ap.rearrange("p (h d) -> p h d", d=64)
                        for mq in range(2):
                            for mi in range(4):
                                TR(tpp[:Qc, mq * 512 + mi * 128:mq * 512 + (mi + 1) * 128], xc[:, mq * 4 + mi, cs:cs + Qc], ident_b[:],
                                   [("xc", mq * 4 + mi)], ["tp"])
                            TT("dve", hd(xdt[:Qc, mq * 512:(mq + 1) * 512]), hd(tpp[:Qc, mq * 512:(mq + 1) * 512]),
                               dtv[:Qc, c, mq * 8:(mq + 1) * 8].unsqueeze(2).to_broadcast([Qc, 8, 64]), ALU.mult, ["tp", "dtv"], [("xdt", mq)])
                        for g in range(4):
                            TR(tpp[:Qc, g * 128:(g + 1) * 128], xc[:, 8 + g, cs:cs + Qc], ident_b[:], [("xc", 8 + g)], ["tp"])
                        CP("dve", btok[:Qc, :], tpp[:Qc, 0:512], ["tp"], ["btok"])
                        for i, aa in enumerate((a_hi, a_lo)):
                            MM(smallp[:Qc, 0:16], tri_b[:Qc, :Qc], aa[:Qc, c, :], i == 0, i == 1, ["a_hi", "a_lo", "tri_b"], ["small"])
                        for i, aa in enumerate((a_hi, a_lo)):
                            MM(smallp[:, 16:32], ones_b[:Qc, :], aa[:Qc, c, :], i == 0, i == 1, ["a_hi", "a_lo"], ["small"])
                        CP("act", acs[:Qc, :], smallp[:Qc, 0:16], ["small"], ["acs"])
                        ACT(etot[:], smallp[:, 16:32], AF.Exp, ["small"], ["etot"])
                        TT("dve", tmd[:Qc, :], smallp[:Qc, 16:32], acs[:Qc, :], ALU.subtract, ["small", "acs", "etot"], ["tmd"])
                        ACT(te[:Qc, :], tmd[:Qc, :], AF.Exp, ["tmd"], ["te"])
                        TT("pool", hd(xe[:Qc, :]), hd(xdt[:Qc, :]), te[:Qc, :].unsqueeze(2).to_broadcast([Qc, 16, 64]), ALU.mult,
                           [("xdt", 0), ("xdt", 1), "te"], ["xe"])
                        for g in range(4):
                            MM(cbp[:Qc, g * 128:g * 128 + Qc], xc[:, 8 + g, cs:cs + Qc], xc[:, 12 + g, cs:cs + Qc], True, True,
                               [("xc", 8 + g), ("xc", 12 + g)], ["cbp"])
                        TT("dve", cbm[:Qc, :, :Qc], cbp[:Qc, :].rearrange("p (g l) -> p g l", g=4)[:, :, :Qc],
                           tri_b[:Qc, :Qc].unsqueeze(1).to_broadcast([Qc, 4, Qc]), ALU.mult, ["cbp", "tri_b"], ["cbm"])
                        for g in range(4):
                            par = g % 2
                            ab = acsb[par]
                            abv = ab[:].rearrange("p (h l) -> p h l", h=4)
                            for hh in range(4):
                                h = g * 4 + hh
                                for i, aa in enumerate((a_hi, a_lo)):
                                    MM(abv[:, hh, :Qc], aa[:Qc, c, h:h + 1].to_broadcast([Qc, 128]), tri_b[:Qc, :Qc], i == 0, i == 1,
                                       ["a_hi", "a_lo"], [("acsb", par)])
                            Ev = E[par][:].rearrange("p (h l) -> p h l", h=4)
                            ACT(Ev[:, :, :Qc], abv[:, :, :Qc], AF.Exp, [("acsb", par)], [("E", par)])
                            xv = arg[par][:].rearrange("p (h l) -> p h l", h=4)
                            TT("dve", xv[:Qc, :, :Qc], abv[:Qc, :, :Qc], acs[:Qc, g * 4:(g + 1) * 4].unsqueeze(2).to_broadcast([Qc, 4, Qc]), ALU.subtract,
                               [("acsb", par), "acs", ("E", par)], [("arg", par)])
                            dv = dec[par][:].rearrange("p (h l) -> p h l", h=4)
                            ACT(dv[:Qc, :, :Qc], xv[:Qc, :, :Qc], AF.Exp, [("arg", par)], [("dec", par)])
                            Mv = Mh[par][:].rearrange("p (h l) -> p h l", h=4)
                            for hh in range(4):
                                STT(Mv[:Qc, hh, :Qc], dv[:Qc, hh, :Qc], 1.0, cbm[:Qc, g, :Qc], ALU.min, ALU.mult, [("dec", par), "cbm"], [("Mh", par)])
                            Cv = Cs[par][:].rearrange("p (h l) -> p h l", h=4)
                            TT("pool", Cv[:, :, :Qc], xc[:, 12 + g, cs:cs + Qc].unsqueeze(1).to_broadcast([128, 4, Qc]), Ev[:, :, :Qc], ALU.mult,
                               [("xc", 12 + g), ("E", par)], [("Cs", par)])
                            yp = yps[par]
                            ykey = ("yps", par)
                            for hh in range(4):
                                h = g * 4 + hh
                                o = yp[(hh % 2) * 64:(hh % 2 + 1) * 64, (hh // 2) * 128:(hh // 2) * 128 + Qc]
                                MM(o, xdt[:Qc, h * 64:(h + 1) * 64], Mv[:Qc, hh, :Qc], True, False, [("xdt", h // 8), ("Mh", par)], [ykey])
                                MM(o, hbf[:, h * 64:(h + 1) * 64], Cv[:, hh, :Qc], False, True, ["hbf", ("Cs", par)], [ykey])
                            for mm_ in range(2):
                                m = 2 * g + mm_
                                STT(yg[:, m, cs:cs + Qc], xc[:, m, cs:cs + Qc], dvec[:, m:m + 1], yp[:, mm_ * 128:mm_ * 128 + Qc], ALU.mult, ALU.add,
                                    [("xc", m), "dvec", ykey], ["yg"])
                        for half in range(2):
                            for gi in range(2):
                                g = half * 2 + gi
                                MM(hps[:, gi * 256:(gi + 1) * 256], btok[:Qc, g * 128:(g + 1) * 128], xe[:Qc, g * 256:(g + 1) * 256], True, True,
                                   ["btok", "xe"], ["hps"])
                            TT("dve", hd(htmp[:]), hd(hstate[:, half * 512:(half + 1) * 512]),
                               etot[:, half * 8:(half + 1) * 8].unsqueeze(2).to_broadcast([128, 8, 64]), ALU.mult, ["hstate", "etot"], ["htmp"])
                            TT("dve", hstate[:, half * 512:(half + 1) * 512], htmp[:], hps[:], ALU.add, ["htmp", "hps"], ["hstate"])
                        CP("act", hbf[:], hstate[:], ["hstate"], ["hbf"])
                    TT("dve", yg[:, :, :W], yg[:, :, :W], sza[:, :, :W], ALU.mult, ["yg", "sza"], ["yg"])
                    ACT(sq[:, :, :W], yg[:, :, :W], AF.Square, ["yg"], ["n2_sq"])
                    for gg in range(4):
                        rms_rstd(cbp, [sq[:, 2 * gg, :W], sq[:, 2 * gg + 1, :W]], W, 1.0 / 256, rs, rs2, "n2")
                        for m in (2 * gg, 2 * gg + 1):
                            STT(ya[:, m, :W], yg[:, m, :W], ng[:, m:m + 1], rs2[:, :W], ALU.mult, ALU.mult, ["yg", "ng", "n2_rs2"], ["ya"])
                    DMA(YA[:, :, t0:t0 + W].rearrange("m p t -> p m t"), ya[:, :, :W], ["ya"], [("YA", t0)])
                    if last:
                        for k in range(3):
                            DMA(conv_o[l, seq, k].rearrange("(m p) -> p m", p=128), tails[:, :, k], ["tails"], [])
                        for m in range(8):
                            TR(hps[:, (m % 4) * 128:(m % 4 + 1) * 128], hstate[:, m * 128:(m + 1) * 128], ident_f[:], ["hstate"], ["hps"])
                            if m % 4 == 3:
                                hh = m // 4
                                CP("dve", hio[:, hh * 4:(hh + 1) * 4, :], hps[:].rearrange("p (m n) -> p m n", n=128), ["hps"], ["hio"])
                        DMA(ssd_o[l, seq].rearrange("(m h) p n -> (h p) m n", h=2), hio[:], ["hio"], [])
            S.barrier()

        def phase3(l):
            with ExitStack() as ph:
                W_sb = sb(ph, "p3_W", [128, 64, 128], BF16)
                T0_sb = sb(ph, "p3_T0", [128, 64, 128], BF16)
                Cm_sb = sb(ph, "p3_Cm", [128, 32, 2, 128], BF16)
                L8 = sb(ph, "p3_L8", [128, 3, 32], F32)
                Hcar = sb(ph, "p3_Hcar", [128, 2, 32], F32)
                pa = ps(ph, "p3_pa", [128, 512])
                pb = ps(ph, "p3_pb", [128, 512])
                pc = ps(ph, "p3_pc", [128, 512])
                pd = ps(ph, "p3_pd", [128, 512])
                with ExitStack() as pp:
                    f64 = lambda nm, shp, dt=F32: sb(pp, nm, shp, dt)
                    lamr, lami, dtg, zr, zi = [f64("q_" + n, [64, 64]) for n in ("lamr", "lami", "dtg", "zr", "zi")]
                    kr, ki, t1, t2, den = [f64("q_" + n, [64, 64]) for n in ("kr", "ki", "t1", "t2", "den")]
                    erow_i = f64("q_erowi", [64, 25], I32)
                    erow = f64("q_erow", [64, 25])
                    ang, mag, s4, s2, Pr, Pi = [f64("q_" + n, [64, 64, 25]) for n in ("ang", "mag", "s4", "s2", "Pr", "Pi")]
                    kqi = f64("q_kqi", [64, 64, 25], I32)
                    PKr, PKi, u1 = [f64("q_" + n, [64, 64, 8]) for n in ("PKr", "PKi", "u1")]
                    Btr, Bti, Ctr, Cti = [f64("q_" + n, [64, 64, 16]) for n in ("Btr", "Bti", "Ctr", "Cti")]
                    Cn = sb(pp, "q_Cn", [128, 8, 64], F32)
                    Wtr, Wti, v1, v2, Cxr, Cxn = [f64("q_" + n, [64, 8, 8, 16]) for n in ("Wtr", "Wti", "v1", "v2", "Cxr", "Cxn")]

                    DMA(lamr[:], Wd["lam_re"][l].rearrange("g p -> p g"), [], ["lamr"])
                    DMA(lami[:], Wd["lam_im"][l].rearrange("g p -> p g"), [], ["lami"])
                    DMA(dtg[:], Wd["log_dt"][l].partition_broadcast(64), [], ["dtg"])
                    DMA(Btr[:], Wd["b_re"][l].rearrange("g p c -> p g c"), [], ["Btr"])
                    DMA(Bti[:], Wd["b_im"][l].rearrange("g p c -> p g c"), [], ["Bti"])
                    for (src, dst, nm) in ((Wd["c_re"], Ctr, "Ctr"), (Wd["c_im"], Cti, "Cti")):
                        DMA(Cn[:], src[l].rearrange("(j g) c p -> (g c) j p", g=8), [], ["Cn"])
                        for j in range(8):
                            TR(pa[:64, (j % 4) * 128:(j % 4 + 1) * 128], Cn[:, j, :], ident_f[:], ["Cn"], ["pa"])
                            if j % 4 == 3:
                                jj = j // 4
                                CP("dve", dst[:, jj * 32:(jj + 1) * 32, :], pa[:64, :].rearrange("p (g c) -> p g c", c=16), ["pa"], [nm])
                    ACT(dtg[:], dtg[:], AF.Exp, ["dtg"], ["dtg"])
                    TT("dve", zr[:], lamr[:], dtg[:], ALU.mult, ["lamr", "dtg"], ["zr"])
                    TT("dve", zi[:], lami[:], dtg[:], ALU.mult, ["lami", "dtg"], ["zi"])
                    for (a, b_, pat, base) in ((0, 8, [[-1, 8]], 7), (8, 16, [[1, 8]], 1), (16, 24, [[1, 8]], -7), (24, 25, [[1, 1]], 8)):
                        oap = erow_i[:, a:b_]
                        S.op("pool", lambda e, oap=oap, pat=pat, base=base: e.iota(out=oap, pattern=pat, base=base, channel_multiplier=0), (), ["erow_i"])
                    CP("dve", erow[:], erow_i[:], ["erow_i"], ["erow"])
                    ebc = erow[:, :].unsqueeze(1).to_broadcast([64, 64, 25])
                    gbc = lambda tl: tl[:, :].unsqueeze(2).to_broadcast([64, 64, 25])
                    fl = lambda tl: tl[:].rearrange("p g m -> p (g m)")
                    TT("dve", ang[:], gbc(zi), ebc, ALU.mult, ["zi", "erow"], ["ang"])
                    TT("dve", mag[:], gbc(zr), ebc, ALU.mult, ["zr", "erow"], ["mag"])
                    ACT(mag[:], mag[:], AF.Exp, ["mag"], ["mag"])
                    TSS("dve", s4[:], ang[:], 1.0 / (2 * math.pi), ALU.mult, ["ang"], ["s4"])
                    CP("dve", kqi[:], s4[:], ["s4"], ["kqi"])
                    CP("dve", s4[:], kqi[:], ["kqi"], ["s4"])
                    STT(fl(ang), fl(s4), -2 * math.pi, fl(ang), ALU.mult, ALU.add, ["s4", "ang"], ["ang"])
                    ACT(s4[:], ang[:], AF.Sin, ["ang"], ["s4"], scale=0.25)
                    ACT(s2[:], ang[:], AF.Sin, ["ang"], ["s2"], scale=0.5)
                    TT("dve", s4[:], s4[:], s4[:], ALU.mult, ["s4"], ["s4"])
                    TS("dve", s4[:], s4[:], -2.0, 1.0, ALU.mult, ALU.add, ["s4"], ["s4"])
                    TT("dve", Pi[:], s2[:], s4[:], ALU.mult, ["s2", "s4"], ["Pi"])
                    STT(fl(Pi), fl(Pi), 2.0, fl(mag), ALU.mult, ALU.mult, ["Pi", "mag"], ["Pi"])
                    TT("dve", s2[:], s2[:], s2[:], ALU.mult, ["s2"], ["s2"])
                    TS("dve", s2[:], s2[:], -2.0, 1.0, ALU.mult, ALU.add, ["s2"], ["s2"])
                    TT("dve", Pr[:], s2[:], mag[:], ALU.mult, ["s2", "mag"], ["Pr"])
                    TSS("dve", t1[:], Pr[:, :, 6], -1.0, ALU.add, ["Pr"], ["t1"])
                    TT("dve", den[:], lamr[:], lamr[:], ALU.mult, ["lamr"], ["den"])
                    TT("dve", t2[:], lami[:], lami[:], ALU.mult, ["lami"], ["t2"])
                    TT("dve", den[:], den[:], t2[:], ALU.add, ["den", "t2"], ["den"])
                    S.op("dve", lambda e: e.reciprocal(out=den[:], in_=den[:]), ["den"], ["den"])
                    TT("dve", kr[:], t1[:], lamr[:], ALU.mult, ["t1", "lamr"], ["kr"])
                    TT("dve", t2[:], Pi[:, :, 6], lami[:], ALU.mult, ["Pi", "lami"], ["t2"])
                    TT("dve", kr[:], kr[:], t2[:], ALU.add, ["kr", "t2"], ["kr"])
                    TT("dve", kr[:], kr[:], den[:], ALU.mult, ["kr", "den"], ["kr"])
                    TT("dve", ki[:], Pi[:, :, 6], lamr[:], ALU.mult, ["Pi", "lamr"], ["ki"])
                    TT("dve", t2[:], t1[:], lami[:], ALU.mult, ["t1", "lami"], ["t2"])
                    TT("dve", ki[:], ki[:], t2[:], ALU.subtract, ["ki", "t2"], ["ki"])
                    TT("dve", ki[:], ki[:], den[:], ALU.mult, ["ki", "den"], ["ki"])
                    k8 = lambda tl: tl[:, :].unsqueeze(2).to_broadcast([64, 64, 8])
                    TT("dve", PKr[:], Pr[:, :, 0:8], k8(kr), ALU.mult, ["Pr", "kr"], ["PKr"])
                    TT("dve", u1[:], Pi[:, :, 0:8], k8(ki), ALU.mult, ["Pi", "ki"], ["u1"])
                    TT("dve", PKr[:], PKr[:], u1[:], ALU.subtract, ["PKr", "u1"], ["PKr"])
                    TT("dve", PKi[:], Pr[:, :, 0:8], k8(ki), ALU.mult, ["Pr", "ki"], ["PKi"])
                    TT("dve", u1[:], Pi[:, :, 0:8], k8(kr), ALU.mult, ["Pi", "kr"], ["u1"])
                    TT("dve", PKi[:], PKi[:], u1[:], ALU.add, ["PKi", "u1"], ["PKi"])
                    for (ti, src, sc) in ((0, Pr, 1.0), (1, Pi, 1.0), (2, Pi, -1.0)):
                        for gh in range(2):
                            ACT(L8[gh * 64:(gh + 1) * 64, ti, :].rearrange("p (g j) -> p g j", j=8),
                                src[:, :, 24].rearrange("p (j g) -> p g j", g=8)[:, gh * 4:(gh + 1) * 4, :], AF.Copy, ["Pr", "Pi"], ["L8"], scale=sc)

                    def cmul(outr, outi, PA_r, PA_i, Xr, Xi, jb, key, rd, neg):
                        pbc = lambda P_: P_.unsqueeze(3).to_broadcast([64, 8, 8, 16])
                        xbc = lambda X_: X_[:, jb * 8:(jb + 1) * 8, :].unsqueeze(2).to_broadcast([64, 8, 8, 16])
                        TT("dve", outr[:], pbc(PA_r), xbc(Xr), ALU.mult, rd, [key + "r"])
                        TT("dve", v1[:], pbc(PA_i), xbc(Xi), ALU.mult, rd, ["v1"])
                        TT("dve", outr[:], outr[:], v1[:], ALU.subtract, [key + "r", "v1"], [key + "r"])
                        TT("dve", outi[:], pbc(PA_r), xbc(Xi), ALU.mult, rd, [key + "i"])
                        TT("dve", v2[:], pbc(PA_i), xbc(Xr), ALU.mult, rd, ["v2"])
                        TT("dve", outi[:], outi[:], v2[:], ALU.add, [key + "i", "v2"], [key + "i"])
                        if neg:
                            TSS("dve", outi[:], outi[:], -1.0, ALU.mult, [key + "i"], [key + "i"])

                    base_rd = ["Pr", "Pi", "PKr", "PKi", "Btr", "Bti", "Ctr", "Cti"]
                    sc_ = lambda ap: ap.rearrange("p s c -> p (s c)")
                    for jb in range(8):
                        gs = slice(jb * 8, (jb + 1) * 8)
                        cmul(Wtr, Wti, PKr[:, gs, :], PKi[:, gs, :], Btr, Bti, jb, "Wt", base_rd, False)
                        for (ri, src, nm) in ((0, Wtr, "Wtr"), (1, Wti, "Wti")):
                            for g8 in range(8):
                                bank, bk = (pa, "pa") if g8 < 4 else (pb, "pb")
                                TR(bank[:, (g8 % 4) * 64:(g8 % 4 + 1) * 64], sc_(src[:, g8, :, :]), ident_f[:64, :64], [nm], [bk])
                            for hb, (bank, bk) in enumerate(((pa, "pa"), (pb, "pb"))):
                                ACT(W_sb[:, :, ri * 64:(ri + 1) * 64].rearrange("p (g j) x -> p g j x", j=8)[:, hb * 4:(hb + 1) * 4, jb, :],
                                    bank[:, 0:256].rearrange("p (g x) -> p g x", x=64), AF.Copy, [bk], ["W_sb"])
                        cmul(Cxr, Cxn, Pr[:, gs, 16:24], Pi[:, gs, 16:24], Ctr, Cti, jb, "Cx", base_rd, True)
                        for g8 in range(8):
                            bank, bk = (pc, "pc") if g8 < 4 else (pd, "pd")
                            o = bank[:, (g8 % 4) * 128:(g8 % 4 + 1) * 128]
                            MM(o, sc_(Wtr[:, g8, :, :]), sc_(Cxr[:, g8, :, :]), True, False, ["Wtr", "Cxr"], [bk])
                            MM(o, sc_(Wti[:, g8, :, :]), sc_(Cxn[:, g8, :, :]), False, True, ["Wti", "Cxi"], [bk])
                        for hb, (bank, bk) in enumerate(((pc, "pc"), (pd, "pd"))):
                            TT("dve", T0_sb[:].rearrange("p (g j) x -> p g j x", j=8)[:, hb * 4:(hb + 1) * 4, jb, :],
                               bank[:].rearrange("p (g x) -> p g x", x=128), bmask[:].unsqueeze(1).to_broadcast([128, 4, 128]), ALU.mult,
                               [bk, "bmask"], ["T0_sb"])
                        cmul(Cxr, Cxn, Pr[:, gs, 8:16], Pi[:, gs, 8:16], Ctr, Cti, jb, "Cx", base_rd, True)
                        for (ri, src, nm) in ((0, Cxr, "Cxr"), (1, Cxn, "Cxi")):
                            for gh in range(2):
                                ACT(Cm_sb[gh * 64:(gh + 1) * 64, :, ri, :].rearrange("p (g j) x -> p g j x", j=8)[:, :, jb, :],
                                    src[:, gh * 4:(gh + 1) * 4, :, :].rearrange("p g s c -> p g (s c)"), AF.Copy, [nm], ["Cm_sb"])
                S.barrier()
                with ExitStack() as pm:
                    Hb = sb(pm, "p3_Hb", [128, 2, 32, 65], F32)
                    U_sb = sb(pm, "p3_U", [128, 64, 64], BF16)
                    Hbf = sb(pm, "p3_Hbf", [128, 2, 32, 64], BF16)
                    Yst = sb(pm, "p3_Yst", [128, 64, 64], F32)
                    tA = sb(pm, "p3_tA", [128, 2, 32], F32)
                    tB = sb(pm, "p3_tB", [128, 2, 32], F32)
                    lbanks = [pa, pb]
                    ybanks = [pc, pd]
                    for (t0, W, seq, first, last) in TILES:
                        nC = W // 8
                        n0 = t0 // 8
                        if first:
                            if seq < 2:
                                MSET("pool", Hb[:, :, :, 0:1], 0.0, ["Hb"])
                            else:
                                for (ri, src) in ((0, s5r0), (1, s5i0)):
                                    for g8 in range(8):
                                        gh, g4 = g8 // 4, g8 % 4
                                        DMA(Hcar[gh * 64:(gh + 1) * 64, ri, g4 * 8:(g4 + 1) * 8],
                                            src[l].rearrange("(j g) p -> p g j", g=8)[:, g8, :], [], ["Hcar"])
                                CP("dve", Hb[:, :, :, 0], Hcar[:], ["Hcar"], ["Hb"])
                        DMA(U_sb[:, :, :nC], UD[:, :, :, :, n0:n0 + nC].rearrange("s c g j n -> (s c) (g j) n"), [("UD", t0)], ["U_sb"])
                        for gl4 in range(8):
                            bank = lbanks[gl4 % 2]
                            bk = ("lb", gl4 % 2)
                            bv = bank[:].rearrange("p (r g n) -> p r g n", r=2, g=4)
                            for gh in range(2):
                                for gi in range(4):
                                    gidx = gh * 32 + gl4 * 4 + gi
                                    for ri in range(2):
                                        MM(bv[gh * 64:(gh + 1) * 64, ri, gi, :nC], W_sb[:, gidx, ri * 64:(ri + 1) * 64], U_sb[:, gidx, :nC], True, True,
                                           ["U_sb"], [bk])
                            CP("act", Hb[:, :, gl4 * 4:(gl4 + 1) * 4, 1:1 + nC], bv[:, :, :, :nC], [bk], ["Hb"])
                        for n in range(1, nC + 1):
                            TT("dve", tA[:], Hb[:, :, :, n - 1], L8[:, 0, :].unsqueeze(1).to_broadcast([128, 2, 32]), ALU.mult, ["Hb"], ["tA"])
                            TT("dve", tB[:, 0, :], Hb[:, 1, :, n - 1], L8[:, 2, :], ALU.mult, ["Hb"], ["tB0"])
                            TT("dve", tB[:, 1, :], Hb[:, 0, :, n - 1], L8[:, 1, :], ALU.mult, ["Hb"], ["tB1"])
                            TT("dve", tA[:], tA[:], tB[:], ALU.add, ["tA", "tB0", "tB1"], ["tA"])
                            TT("dve", Hb[:, :, :, n], Hb[:, :, :, n], tA[:], ALU.add, ["tA", "Hb"], ["Hb"])
                        CP("act", Hbf[:, :, :, :nC], Hb[:, :, :, 0:nC], ["Hb"], ["Hbf"])
                        CP("dve", Hcar[:], Hb[:, :, :, nC], ["Hb"], ["Hcar"])
                        CP("dve", Hb[:, :, :, 0], Hcar[:], ["Hcar", "Hbf"], ["Hb"])
                        for g8b in range(8):
                            bank = ybanks[g8b % 2]
                            bk = ("yb", g8b % 2)
                            bv = bank[:].rearrange("p (g n) -> p g n", g=8)
                            for gi in range(8):
                                gidx = g8b * 8 + gi
                                gh, gl = gidx // 32, gidx % 32
                                MM(bv[:, gi, :nC], T0_sb[:, gidx, :], U_sb[:, gidx, :nC], True, False, ["U_sb"], [bk])
                                for ri in range(2):
                                    MM(bv[:, gi, :nC], Cm_sb[gh * 64:(gh + 1) * 64, gl, ri, :], Hbf[gh * 64:(gh + 1) * 64, ri, gl, :nC], False, ri == 1,
                                       ["Hbf"], [bk])
                            CP("act" if g8b % 2 else "dve", Yst[:, g8b * 8:(g8b + 1) * 8, :nC], bv[:, :, :nC], [bk], ["Yst"])
                        for s_ in range(8):
                            DMA(YD[:, :, :, s_, n0:n0 + nC].rearrange("c g j n -> c (g j) n"), Yst[s_ * 16:(s_ + 1) * 16, :, :nC], ["Yst"], [("YD", t0)])
                        if last:
                            for (ri, dst) in ((0, s5r_o), (1, s5i_o)):
                                for g8 in range(8):
                                    gh, g4 = g8 // 4, g8 % 4
                                    DMA(dst[l, seq].rearrange("(j g) p -> p g j", g=8)[:, g8, :],
                                        Hcar[gh * 64:(gh + 1) * 64, ri, g4 * 8:(g4 + 1) * 8], ["Hcar"], [])
            S.barrier()

        def phase4(l):
            with ExitStack() as ph:
                wg = sb(ph, "p4_w", [128, 8, 1024], BF16)
                ds5 = sb(ph, "p4_ds5", [128, 8], F32)
                bgl = sb(ph, "p4_bgl", [128, 8], F32)
                yp4 = [sb(ph, "p4_yp%d" % i, [128, 8, 8, 64], F32) for i in range(2)]
                ut4 = [sb(ph, "p4_ut%d" % i, [128, 8, 512], BF16) for i in range(2)]
                szb4 = [sb(ph, "p4_szb%d" % i, [128, 8, 512], BF16) for i in range(2)]
                yb = sb(ph, "p4_yb", [128, 8, 512], F32)
                gf = sb(ph, "p4_gf", [128, 8, 512], F32)
                gb = sb(ph, "p4_gb", [128, 8, 512], BF16)
                sg = [sb(ph, "p4_sg%d" % i, [128, 512], F32) for i in range(2)]
                ybst = sb(ph, "p4_ybst", [128, 8, 512], BF16)
                mm = [ps(ph, "p4_mm%d" % i, [128, 512]) for i in range(2)]
                for kt in range(8):
                    DMA(wg[:, kt, :], Wd["w_glu"][l, kt * 128:(kt + 1) * 128, :], [], ["wg"], q="pool")
                DMA(ds5[:], Wd["d_s5"][l].rearrange("(k p) -> p k", p=128), [], ["ds5"])
                DMA(bgl[:], Wd["b_glu"][l].rearrange("(k p) -> p k", p=128), [], ["bgl"])
                def p4_load(ti):
                    (t0, W, seq, first, last) = TILES[ti]
                    nC = W // 8
                    n0 = t0 // 8
                    pr = ti % 2
                    for g8 in range(8):
                        DMA(yp4[pr][g8 * 16:(g8 + 1) * 16, :, :, :nC].rearrange("c j s n -> c (j s) n"),
                            YD[:, g8, :, :, n0:n0 + nC].rearrange("c j s n -> c (j s) n"), [("YD", t0)], [("yp4", pr)])
                        DMA(ut4[pr][g8 * 16:(g8 + 1) * 16, :, :W], UT[:, :, g8, t0:t0 + W].rearrange("j c t -> c j t"), [("UT", t0)], [("ut4", pr)])
                    DMA(szb4[pr][:, :, :W], SZB[:, :, t0:t0 + W].rearrange("m p t -> p m t"), [("SZB", t0)], [("szb4", pr)])

                p4_load(0)
                for ti, (t0, W, seq, first, last) in enumerate(TILES):
                    nC = W // 8
                    n0 = t0 // 8
                    pr = ti % 2
                    if ti + 1 < len(TILES):
                        p4_load(ti + 1)
                    for j in range(8):
                        STT(yb[:, j, :W].rearrange("p (n s) -> p n s", s=8), ut4[pr][:, j, :W].rearrange("p (n s) -> p n s", s=8), ds5[:, j:j + 1],
                            yp4[pr][:, j, :, :nC].rearrange("p s n -> p n s"), ALU.mult, ALU.add, [("ut4", pr), ("yp4", pr), "ds5"], [("yb", j)])
                        ACT(gf[:, j, :W], yb[:, j, :W], AF.Gelu_apprx_tanh, [("yb", j)], [("gf", j)])
                        CP("act" if j % 2 else "dve", gb[:, j, :W], gf[:, j, :W], [("gf", j)], ["gb"])
                    for jo in range(8):
                        bank = mm[jo % 2]
                        bk = ("mm", jo % 2)
                        for ji in range(8):
                            MM(bank[:, :W], wg[:, ji, jo * 128:(jo + 1) * 128], gb[:, ji, :W], ji == 0, ji == 7, ["wg", "gb"], [bk])
                        sgt = sg[jo % 2]
                        sk = ("sg", jo % 2)
                        ACT(sgt[:, :W], bank[:, :W], AF.Sigmoid, [bk, "bgl"], [sk], bias=bgl[:, jo:jo + 1], scale=1.0)
                        TT("dve", sgt[:, :W], sgt[:, :W], gf[:, jo, :W], ALU.mult, [sk, ("gf", jo)], [sk])
                        TT("dve", ybst[:, jo, :W], sgt[:, :W], szb4[pr][:, jo, :W], ALU.mult, [sk, ("szb4", pr)], ["ybst"])
                    DMA(YB[:, :, t0:t0 + W].rearrange("m p t -> p m t"), ybst[:, :, :W], ["ybst"], [("YB", t0)])
            S.barrier()

        def phase5(l):
            with ExitStack() as ph:
                wo = sb(ph, "p5_w", [128, 16, 1024], BF16)
                y5 = sb(ph, "p5_y", [128, 16, 512], BF16)
                xT5 = sb(ph, "p5_x", [128, 8, 512], F32)
                xo = sb(ph, "p5_xo", [128, 8, 512], F32)
                mm = [ps(ph, "p5_mm%d" % i, [128, 512]) for i in range(2)]
                for kt in range(16):
                    DMA(wo[:, kt, :], Wd["w_out"][l, kt * 128:(kt + 1) * 128, :], [], ["wo"], q="pool")
                for (t0, W, seq, first, last) in TILES:
                    DMA(y5[:, 0:8, :W], YA[:, :, t0:t0 + W].rearrange("m p t -> p m t"), [("YA", t0)], ["y5a"])
                    DMA(y5[:, 8:16, :W], YB[:, :, t0:t0 + W].rearrange("m p t -> p m t"), [("YB", t0)], ["y5b"])
                    DMA(xT5[:, :, :W], xt_ap(t0, W), [("XT", t0)], ["xT5"])
                    for dm in range(8):
                        bank = mm[dm % 2]
                        bk = ("mm", dm % 2)
                        for kt in range(16):
                            MM(bank[:, :W], wo[:, kt, dm * 128:(dm + 1) * 128], y5[:, kt, :W], kt == 0, kt == 15, ["wo", "y5a", "y5b"], [bk])
                        TT("dve", xo[:, dm, :W], bank[:, :W], xT5[:, dm, :W], ALU.add, [bk, "xT5"], ["xo"])
                    DMA(xt_ap(t0, W), xo[:, :, :W], ["xo"], [("XT", t0)])
            S.barrier()

        def phase_final():
            with ExitStack() as ph:
                fg = sb(ph, "pf_g", [128, 8], F32)
                xT = sb(ph, "pf_xT", [128, 8, 512], F32)
                sq = sb(ph, "pf_sq", [128, 8, 512], BF16)
                hf = sb(ph, "pf_hf", [128, 8, 512], F32)
                rs = sb(ph, "pf_rs", [128, 512], F32)
                rs2 = sb(ph, "pf_rs2", [128, 512], F32)
                yst = sb(ph, "pf_yst", [128, 4, 1024], F32)
                msp = ps(ph, "pf_ms", [128, 512])
                tp = [ps(ph, "pf_tp%d" % i, [128, 512]) for i in range(4)]
                DMA(fg[:], Wd["final_norm_g"].rearrange("(k p) -> p k", p=128), [], ["fg"])
                for (t0, W, seq, first, last) in TILES:
                    Pb = min(128, W)
                    nb = W // Pb
                    DMA(xT[:, :, :W], xt_ap(t0, W), [("XT", t0)], ["xT"])
                    ACT(sq[:, :, :W], xT[:, :, :W], AF.Square, ["xT"], ["nf_sq"])
                    rms_rstd(msp, [sq[:, kt, :W] for kt in range(8)], W, 1.0 / 1024, rs, rs2, "nf")
                    for kt in range(8):
                        STT(hf[:, kt, :W], xT[:, kt, :W], fg[:, kt:kt + 1], rs2[:, :W], ALU.mult, ALU.mult, ["xT", "fg", "nf_rs2"], ["hf"])
                    for b in range(nb):
                        for half in range(2):
                            bank = tp[(b * 2 + half) % 4]
                            bk = ("tp", (b * 2 + half) % 4)
                            for k4 in range(4):
                                kt = half * 4 + k4
                                TR(bank[:Pb, k4 * 128:(k4 + 1) * 128], hf[:, kt, b * Pb:(b + 1) * Pb], ident_f[:], ["hf"], [bk])
                            CP("act" if half else "dve", yst[:Pb, b, half * 512:(half + 1) * 512], bank[:Pb, :], [bk], ["yst"])
                    DMA(y_out[t0:t0 + W, :].rearrange("(b p) d -> p b d", p=Pb), yst[:Pb, :nb, :], ["yst"], [])
            S.barrier()

        wst = ExitStack()
        w_cur = sb(wst, "w_in_sb", [128, 8, IN_COLS], BF16)
        load_w_in(0, w_cur)
        phase0()
        for l in range(n_layers):
            if 1 in phases:
                phase1(l, w_cur)
            wst.close()
            if 2 in phases:
                phase2(l)
            if 3 in phases:
                phase3(l)
            if 4 in phases:
                phase4(l)
            if l + 1 < n_layers:
                wst = ExitStack()
                w_cur = sb(wst, "w_in_sb", [128, 8, IN_COLS], BF16)
                load_w_in(l + 1, w_cur)
            if 5 in phases:
                phase5(l)
        if do_final:
            phase_final()
        S.finish(block)
        nc._n_instr = S.n_instr
    return nc


def make_in_maps(inputs):
    xp = np.ascontiguousarray(inputs["x_prompt"], dtype=np.float32)
    xs = np.ascontiguousarray(inputs["x_sample"], dtype=np.float32)
    maps = []
    for c in range(NCORES):
        m = {}
        m["x_core"] = np.ascontiguousarray(np.concatenate([xp[2 * c], xp[2 * c + 1], xs[c]], axis=0))
        m["conv0"] = np.ascontiguousarray(inputs["state_ssd_conv"][:, c])
        m["ssd0"] = np.ascontiguousarray(inputs["state_ssd"][:, c])
        m["s5r0"] = np.ascontiguousarray(inputs["state_s5_re"][:, c])
        m["s5i0"] = np.ascontiguousarray(inputs["state_s5_im"][:, c])
        for n in W_NAMES:
            m[n] = np.ascontiguousarray(inputs[n], dtype=np.float32)
        maps.append(m)
    return maps


_NC_CACHE = {}


def kernel(**inputs):
    inputs = {k: np.asarray(v) for k, v in inputs.items()}
    if "nc" not in _NC_CACHE:
        _NC_CACHE["nc"] = build_nc()
    nc = _NC_CACHE["nc"]
    maps = make_in_maps(inputs)
    res = run_bass_kernel_spmd(nc, maps, core_ids=list(range(NCORES)))
    R = res.results
    y_prompt = np.zeros((16, 2048, 1024), np.float32)
    y_sample = np.zeros((8, 64, 1024), np.float32)
    conv_p = np.zeros((4, 16, 3, 2048), np.float32)
    ssd_p = np.zeros((4, 16, 16, 64, 128), np.float32)
    s5r_p = np.zeros((4, 16, 64, 64), np.float32)
    s5i_p = np.zeros((4, 16, 64, 64), np.float32)
    conv_s = np.zeros((4, 8, 3, 2048), np.float32)
    ssd_s = np.zeros((4, 8, 16, 64, 128), np.float32)
    s5r_s = np.zeros((4, 8, 64, 64), np.float32)
    s5i_s = np.zeros((4, 8, 64, 64), np.float32)
    for c in range(NCORES):
        r = R[c]
        y = r["y_out"]
        y_prompt[2 * c] = y[0:2048]
        y_prompt[2 * c + 1] = y[2048:4096]
        y_sample[c] = y[4096:4160]
        for i in range(2):
            conv_p[:, 2 * c + i] = r["conv_o"][:, i]
            ssd_p[:, 2 * c + i] = r["ssd_o"][:, i]
            s5r_p[:, 2 * c + i] = r["s5r_o"][:, i]
            s5i_p[:, 2 * c + i] = r["s5i_o"][:, i]
        conv_s[:, c] = r["conv_o"][:, 2]
        ssd_s[:, c] = r["ssd_o"][:, 2]
        s5r_s[:, c] = r["s5r_o"][:, 2]
        s5i_s[:, c] = r["s5i_o"][:, 2]
    return (y_prompt, y_sample, conv_p, ssd_p, s5r_p, s5i_p, conv_s, ssd_s, s5r_s, s5i_s)
```

```python
import math
from contextlib import ExitStack

import numpy as np
import concourse.bass as bass
import concourse.mybir as mybir
from concourse.bass_utils import run_bass_kernel_spmd

F32 = mybir.dt.float32
BF16 = mybir.dt.bfloat16
I32 = mybir.dt.int32
AF = mybir.ActivationFunctionType
ALU = mybir.AluOpType

NCORES = 8
DEPTH = 4
T = 4160
NCH = T // 8
EPS = 1e-6
IN_COLS = 5136
C_ZA, C_XBC, C_DT, C_U, C_ZB = 0, 1024, 3072, 3088, 4112
TILES = [(s * 2048 + i * 512, 512, s, i == 0, i == 3) for s in range(2) for i in range(4)]
TILES.append((4096, 64, 2, True, True))
import os as _os
if _os.environ.get("K_TILES"):
    TILES = [TILES[int(i)] for i in _os.environ["K_TILES"].split(",")]
K_SKIP = set(_os.environ.get("K_SKIP", "").split(","))

ENGS = ("pe", "dve", "act", "pool", "sp")


class Sync:
    def __init__(self, nc, stack, n_dma_sems=40, same_engine_sync=True):
        self.nc = nc
        self.esem = {e: stack.enter_context(nc.semaphore("s_" + e)) for e in ENGS}
        self.cnt = {e: 0 for e in ENGS}
        self.prog = {e: [] for e in ENGS}
        self.waited = {e: {} for e in ENGS}
        self.res = {}
        self.same = same_engine_sync
        self.dsems = [stack.enter_context(nc.semaphore("s_dma%d" % i)) for i in range(n_dma_sems)]
        self.dval = [0] * n_dma_sems
        self.dnext = 0
        self.sems = {}
        for e in ENGS:
            self.sems[("e", e)] = self.esem[e]
        for i, s in enumerate(self.dsems):
            self.sems[("d", i)] = s
        self.n_instr = 0

    def _need(self, e, reads, writes):
        need = {}

        def add(ev):
            if ev is None:
                return
            k, v = ev
            if need.get(k, 0) < v:
                need[k] = v

        for r in reads:
            st = self.res.get(r)
            if st is not None:
                add(st[0])
        for w in writes:
            st = self.res.get(w)
            if st is not None:
                add(st[0])
                for ev in st[1]:
                    add(ev)
        out = []
        for k, v in need.items():
            if k == ("e", e) and (e == "pe" or not self.same):
                continue
            if self.waited[e].get(k, 0) >= v:
                continue
            self.waited[e][k] = v
            out.append((k, v))
        return out

    def _commit(self, ev, reads, writes):
        for r in reads:
            st = self.res.setdefault(r, [None, []])
            st[1].append(ev)
            if len(st[1]) > 16:
                mx = {}
                for k, v in st[1]:
                    if mx.get(k, 0) < v:
                        mx[k] = v
                st[1] = list(mx.items())
        for w in writes:
            self.res[w] = [ev, []]

    def op(self, e, fn, reads=(), writes=()):
        waits = self._need(e, reads, writes)
        self.cnt[e] += 1
        ev = (("e", e), self.cnt[e])
        sem = self.esem[e]
        sems = self.sems

        def emit(eng, waits=waits, fn=fn, sem=sem):
            for k, v in waits:
                eng.wait_ge(sems[k], v)
            fn(eng).then_inc(sem, 1)

        self.prog[e].append(emit)
        self._commit(ev, reads, writes)
        self.n_instr += 1
        return ev

    def dma(self, e, fn, reads=(), writes=()):
        i = self.dnext
        self.dnext = (self.dnext + 1) % len(self.dsems)
        k = ("d", i)
        waits = self._need(e, reads, writes)
        if self.dval[i] > 0 and self.waited[e].get(k, 0) < self.dval[i]:
            self.waited[e][k] = self.dval[i]
            waits.append((k, self.dval[i]))
        self.dval[i] += 16
        ev = (k, self.dval[i])
        sem = self.dsems[i]
        sems = self.sems

        def emit(eng, waits=waits, fn=fn, sem=sem):
            for kk, v in waits:
                eng.wait_ge(sems[kk], v)
            fn(eng).then_inc(sem, 16)

        self.prog[e].append(emit)
        self._commit(ev, reads, writes)
        self.n_instr += 1
        return ev

    def barrier(self):
        evs = [(("e", e), self.cnt[e]) for e in ENGS if self.cnt[e] > 0]
        evs += [(("d", i), v) for i, v in enumerate(self.dval) if v > 0]
        sems = self.sems
        for e in ENGS:
            waits = []
            for k, v in evs:
                if k == ("e", e):
                    continue
                if self.waited[e].get(k, 0) >= v:
                    continue
                self.waited[e][k] = v
                waits.append((k, v))

            def emit(eng, waits=waits):
                for kk, v in waits:
                    eng.wait_ge(sems[kk], v)

            self.prog[e].append(emit)
        self.res = {}

    def finish(self, block):
        self.barrier()
        prog = self.prog

        @block.tensor
        def _(eng):
            for f in prog["pe"]:
                f(eng)

        @block.vector
        def _(eng):
            for f in prog["dve"]:
                f(eng)

        @block.scalar
        def _(eng):
            for f in prog["act"]:
                f(eng)

        @block.gpsimd
        def _(eng):
            for f in prog["pool"]:
                f(eng)

        @block.sync
        def _(eng):
            for f in prog["sp"]:
                f(eng)


W_NAMES = ["norm_g", "w_in", "conv_w", "conv_b", "dt_bias", "a_log", "d_ssd", "ssd_norm_g",
           "lam_re", "lam_im", "log_dt", "b_re", "b_im", "c_re", "c_im", "d_s5", "w_glu", "b_glu",
           "w_out", "final_norm_g"]
W_SHAPES = {
    "norm_g": [4, 1024], "w_in": [4, 1024, 5136], "conv_w": [4, 4, 2048], "conv_b": [4, 2048],
    "dt_bias": [4, 16], "a_log": [4, 16], "d_ssd": [4, 16], "ssd_norm_g": [4, 1024],
    "lam_re": [4, 64, 64], "lam_im": [4, 64, 64], "log_dt": [4, 64], "b_re": [4, 64, 64, 16],
    "b_im": [4, 64, 64, 16], "c_re": [4, 64, 16, 64], "c_im": [4, 64, 16, 64], "d_s5": [4, 1024],
    "w_glu": [4, 1024, 1024], "b_glu": [4, 1024], "w_out": [4, 2048, 1024], "final_norm_g": [1024],
}


def build_nc(n_layers=DEPTH, dbg=False, phases=(1, 2, 3, 4, 5), do_final=True):
    nc = bass.Bass("TRN2", target_bir_lowering=False)
    skind = "ExternalOutput" if dbg else "Internal"

    def din(name, shape, dt=F32):
        return nc.dram_tensor(name, shape, dt, kind="ExternalInput").ap()

    def dout(name, shape, dt=F32):
        return nc.dram_tensor(name, shape, dt, kind="ExternalOutput").ap()

    def dscr(name, shape, dt=F32):
        return nc.dram_tensor(name, shape, dt, kind=skind).ap()

    x_in = din("x_core", [T, 1024])
    conv0 = din("conv0", [4, 3, 2048])
    ssd0 = din("ssd0", [4, 16, 64, 128])
    s5r0 = din("s5r0", [4, 64, 64])
    s5i0 = din("s5i0", [4, 64, 64])
    Wd = {n: din(n, W_SHAPES[n]) for n in W_NAMES}

    y_out = dout("y_out", [T, 1024])
    conv_o = dout("conv_o", [4, 3, 3, 2048])
    ssd_o = dout("ssd_o", [4, 3, 16, 64, 128])
    s5r_o = dout("s5r_o", [4, 3, 64, 64])
    s5i_o = dout("s5i_o", [4, 3, 64, 64])

    XT = dscr("XT", [8, 128, T])
    SZA = dscr("SZA", [8, 128, T], BF16)
    XBC = dscr("XBC", [16, 128, T])
    DTR = dscr("DTR", [T, 16])
    UD = dscr("UD", [8, 16, 8, 8, NCH], BF16)
    UT = dscr("UT", [8, 16, 8, T], BF16)
    SZB = dscr("SZB", [8, 128, T], BF16)
    YA = dscr("YA", [8, 128, T], BF16)
    YD = dscr("YD", [16, 8, 8 * 8, NCH]).rearrange("c g (j s) n -> c g j s n", s=8)
    YB = dscr("YB", [8, 128, T], BF16)

    with ExitStack() as st:
        S = Sync(nc, st)
        st.enter_context(nc.allow_non_contiguous_dma(reason="small strided parameter/state loads"))

        uniq = [0]

        def sb(stack, name, shape, dt):
            uniq[0] += 1
            return stack.enter_context(nc.sbuf_tensor("%s_%d" % (name, uniq[0]), shape, dt))

        def ps(stack, name, shape, dt=F32):
            uniq[0] += 1
            return stack.enter_context(nc.psum_tensor("%s_%d" % (name, uniq[0]), shape, dt))


        def TT(eng, out, in0, in1, op, reads, writes):
            return S.op(eng, lambda e: e.tensor_tensor(out=out, in0=in0, in1=in1, op=op), reads, writes)

        def TS(eng, out, in0, s1, s2, op0, op1, reads, writes):
            return S.op(eng, lambda e: e.tensor_scalar(out=out, in0=in0, scalar1=s1, scalar2=s2, op0=op0, op1=op1), reads, writes)

        def TSS(eng, out, in_, scalar, op, reads, writes):
            return S.op(eng, lambda e: e.tensor_single_scalar(out=out, in_=in_, scalar=scalar, op=op), reads, writes)

        def STT(out, in0, scalar, in1, op0, op1, reads, writes):
            return S.op("dve", lambda e: e.scalar_tensor_tensor(out=out, in0=in0, scalar=scalar, in1=in1, op0=op0, op1=op1), reads, writes)

        def ACT(out, in_, func, reads, writes, bias=None, scale=None):
            kw = {}
            if bias is not None:
                kw["bias"] = bias
            if scale is not None:
                kw["scale"] = scale
            return S.op("act", lambda e: e.activation(out=out, in_=in_, func=func, **kw), reads, writes)

        def CP(eng, out, in_, reads, writes):
            if eng == "act":
                return ACT(out, in_, AF.Copy, reads, writes)
            return S.op(eng, lambda e: e.tensor_copy(out=out, in_=in_), reads, writes)

        def MM(out, lhsT, rhs, start, stop, reads, writes):
            return S.op("pe", lambda e: e.matmul(out, lhsT=lhsT, rhs=rhs, start=start, stop=stop), reads, writes)

        def TR(out, in_, ident, reads, writes):
            return S.op("pe", lambda e: e.transpose(out=out, in_=in_, identity=ident), reads, writes)

        def DMA(out, in_, reads=(), writes=(), q="sp"):
            return S.dma(q, lambda e: e.dma_start(out=out, in_=in_), reads, writes)

        def MSET(eng, ap, val, writes):
            return S.op(eng, lambda e: e.memset(ap, val), (), writes)

        ones_f = sb(st, "ones_f", [128, 128], F32)
        ident_f = sb(st, "ident_f", [128, 128], F32)
        ident_b = sb(st, "ident_b", [128, 128], BF16)
        ones_b = sb(st, "ones_b", [128, 128], BF16)
        tri_f = sb(st, "tri_f", [128, 128], F32)
        tri_b = sb(st, "tri_b", [128, 128], BF16)
        bmask = sb(st, "bmask", [128, 128], F32)
        block = st.enter_context(nc.Block())

        S.op("pool", lambda e: e.memset(ones_f[:], 1.0), writes=["ones_f"])
        S.op("pool", lambda e: e.affine_select(out=ident_f[:], in_=ones_f[:], pattern=[[1, 128]],
                                               compare_op=ALU.is_equal, fill=0.0, base=0, channel_multiplier=-1),
             reads=["ones_f"], writes=["ident_f"])
        S.op("pool", lambda e: e.affine_select(out=tri_f[:], in_=ones_f[:], pattern=[[1, 128]],
                                               compare_op=ALU.is_ge, fill=0.0, base=0, channel_multiplier=-1),
             reads=["ones_f"], writes=["tri_f"])
        S.op("pool", lambda e: e.affine_select(out=bmask[:], in_=ones_f[:], pattern=[[16, 8], [0, 16]],
                                               compare_op=ALU.is_ge, fill=0.0, base=15, channel_multiplier=-1),
             reads=["ones_f"], writes=["bmask"])
        S.op("dve", lambda e: e.tensor_copy(out=ident_b[:], in_=ident_f[:]), reads=["ident_f"], writes=["ident_b"])
        S.op("dve", lambda e: e.tensor_copy(out=ones_b[:], in_=ones_f[:]), reads=["ones_f"], writes=["ones_b"])
        S.op("dve", lambda e: e.tensor_copy(out=tri_b[:], in_=tri_f[:]), reads=["tri_f"], writes=["tri_b"])
        S.barrier()

        def xt_ap(t0, W):
            return XT[:, :, t0:t0 + W].rearrange("k p t -> p k t")

        def rms_rstd(ph_ps, src_sq, W, scale, rs, rs2, key):
            nk = len(src_sq)
            for i, a in enumerate(src_sq):
                MM(ph_ps[:, :W], ones_b[:], a, i == 0, i == nk - 1, [key + "_sq"], [key + "_ms"])
            ACT(rs[:, :W], ph_ps[:, :W], AF.Ln, [key + "_ms"], [key + "_rs"], bias=EPS, scale=scale)
            ACT(rs2[:, :W], rs[:, :W], AF.Exp, [key + "_rs"], [key + "_rs2"], scale=-0.5)

        def phase0():
            with ExitStack() as ph:
                xs = sb(ph, "p0_xs", [128, 4, 1024], F32)
                xo = sb(ph, "p0_xo", [128, 8, 512], F32)
                tp = [ps(ph, "p0_tp%d" % i, [128, 512]) for i in range(2)]
                for (t0, W, seq, first, last) in TILES:
                    Pb = min(128, W)
                    nb = W // Pb
                    DMA(xs[:Pb, :nb, :], x_in[t0:t0 + W, :].rearrange("(b p) d -> p b d", p=Pb), [], ["xs"])
                    for kt in range(8):
                        tpk = tp[kt % 2]
                        for b in range(nb):
                            TR(tpk[:, b * Pb:(b + 1) * Pb], xs[:Pb, b, kt * 128:(kt + 1) * 128], ident_f[:Pb, :Pb], ["xs"], [("tp", kt % 2)])
                        CP("act" if kt % 2 else "dve", xo[:, kt, :W], tpk[:, :W], [("tp", kt % 2)], ["xo"])
                    DMA(xt_ap(t0, W), xo[:, :, :W], ["xo"], [("XT", t0)])
            S.barrier()

        def load_w_in(l, w_sb):
            for kt in range(8):
                for (c0, cw_) in ((0, 2048), (2048, 2048), (4096, 1040)):
                    DMA(w_sb[:, kt, c0:c0 + cw_], Wd["w_in"][l, kt * 128:(kt + 1) * 128, c0:c0 + cw_], [], ["w"], q="pool")

        def phase1(l, w_sb):
            with ExitStack() as ph:
                g1 = sb(ph, "p1_g", [128, 8], F32)
                xT = sb(ph, "p1_xT", [128, 8, 512], F32)
                sq = sb(ph, "p1_sq", [128, 8, 512], BF16)
                hT = sb(ph, "p1_hT", [128, 8, 512], BF16)
                rs = sb(ph, "p1_rs", [128, 512], F32)
                rs2 = sb(ph, "p1_rs2", [128, 512], F32)
                st_za = sb(ph, "p1_za", [128, 8, 512], BF16)
                st_xbc = [sb(ph, "p1_xbc%d" % i, [128, 2, 512], F32) for i in range(2)]
                st_us = sb(ph, "p1_us", [128, 8, 8, 64], BF16)
                st_ut = sb(ph, "p1_ut", [128, 8, 512], BF16)
                us32 = [sb(ph, "p1_us32_%d" % i, [128, 8, 64], F32) for i in range(2)]
                st_zb = sb(ph, "p1_zb", [128, 8, 512], BF16)
                st_dt = sb(ph, "p1_dt", [128, 4, 16], F32)
                mm = [ps(ph, "p1_mm%d" % i, [128, 512]) for i in range(4)]
                msp = ps(ph, "p1_ms", [128, 512])
                dtp = ps(ph, "p1_dtp", [128, 4, 16])
                DMA(g1[:], Wd["norm_g"][l].rearrange("(k p) -> p k", p=128), [], ["g1"])
                for kt in range(0 if "perm" in K_SKIP else 8):
                    tmpu = st_ut[:, (kt % 2) * 2:(kt % 2) * 2 + 2, :].rearrange("p a t -> p (a t)")
                    CP("dve", tmpu, w_sb[:, kt, C_U:C_U + 1024], ["w"], [("tmpu", kt % 2), "st_ut"])
                    CP("act", w_sb[:, kt, C_U:C_U + 1024].rearrange("p (j c g) -> p j c g", c=16, g=8),
                       tmpu.rearrange("p (j g c) -> p j c g", g=8, c=16), [("tmpu", kt % 2), "st_ut"], ["w"])
                mcount = [0]
                def pro_load(ti):
                    (t0_, W_, _, _, _) = TILES[ti]
                    DMA(xT[:, :, :W_], xt_ap(t0_, W_), [("XT", t0_)], ["xT"])

                def pro_sq(ti):
                    W_ = TILES[ti][1]
                    ACT(sq[:, :, :W_], xT[:, :, :W_], AF.Square, ["xT"], ["n1_sq"])

                def pro_ms(ti):
                    W_ = TILES[ti][1]
                    rms_rstd(msp, [sq[:, kt, :W_] for kt in range(8)], W_, 1.0 / 1024, rs, rs2, "n1")

                pro_load(0)
                pro_sq(0)
                pro_ms(0)
                for ti, (t0, W, seq, first, last) in enumerate(TILES):
                    Pb = min(128, W)
                    nb = W // Pb
                    nC = W // 8
                    n0 = t0 // 8
                    nxt = ti + 1 if ti + 1 < len(TILES) else None
                    for kt in range(8):
                        STT(hT[:, kt, :W], xT[:, kt, :W], g1[:, kt:kt + 1], rs2[:, :W], ALU.mult, ALU.mult, ["xT", "g1", "n1_rs2"], ["hT"])
                    if nxt is not None:
                        pro_load(nxt)

                    def mtile(lhs_fn, W=W):
                        bank = mm[mcount[0] % 4]
                        key = ("mm", mcount[0] % 4)
                        mcount[0] += 1
                        for kt in range(8):
                            MM(bank[:, :W], lhs_fn(kt), hT[:, kt, :W], kt == 0, kt == 7, ["w", "hT"], [key])
                        return bank, key

                    for m in range(0 if "za" in K_SKIP else 8):
                        bank, key = mtile(lambda kt, m=m: w_sb[:, kt, C_ZA + m * 128:C_ZA + (m + 1) * 128])
                        ACT(st_za[:, m, :W], bank[:, :W], AF.Silu, [key], ["st_za"])
                    if "za" not in K_SKIP:
                        DMA(SZA[:, :, t0:t0 + W].rearrange("m p t -> p m t"), st_za[:, :, :W], ["st_za"], [("SZA", t0)])
                    if nxt is not None:
                        pro_sq(nxt)
                    for m in range(0 if "xbc" in K_SKIP else 16):
                        bank, key = mtile(lambda kt, m=m: w_sb[:, kt, C_XBC + m * 128:C_XBC + (m + 1) * 128])
                        pr = m // 2
                        stx = st_xbc[pr % 2]
                        skey = ("st_xbc", pr % 2)
                        CP("act" if m % 2 == 0 else "dve", stx[:, m % 2, :W], bank[:, :W], [key], [skey])
                        if m % 2 == 1:
                            DMA(XBC[pr * 2:(pr + 1) * 2, :, t0:t0 + W].rearrange("m p t -> p m t"), stx[:, :, :W], [skey], [("XBC", t0, pr)])
                    if nxt is not None:
                        pro_ms(nxt)
                    for j in range(0 if "u" in K_SKIP else 8):
                        bank, key = mtile(lambda kt, j=j: w_sb[:, kt, C_U + j * 128:C_U + (j + 1) * 128])
                        u32 = us32[j % 2]
                        ACT(u32[:, :, :nC].rearrange("p s n -> p n s"), bank[:, :W].rearrange("p (n s) -> p n s", s=8), AF.Copy, [key], [("us32", j % 2), ("brd", key)])
                        CP("pool", st_us[:, j, :, :nC], u32[:, :, :nC], [("us32", j % 2)], ["st_us"])
                        CP("dve", st_ut[:, j, :W], bank[:, :W], [key, ("brd", key)], ["st_ut"])
                    for j in range(0 if "ud" in K_SKIP else 8):
                        DMA(UD[:, :, :, j, n0:n0 + nC].rearrange("s c g n -> (c g) s n"), st_us[:, j, :, :nC], ["st_us"], [("UD", t0)])
                    if "u" not in K_SKIP and "utdma" not in K_SKIP:
                        DMA(UT[:, :, :, t0:t0 + W].rearrange("j c g t -> (c g) j t"), st_ut[:, :, :W], ["st_ut"], [("UT", t0)])
                    for m in range(0 if "zb" in K_SKIP else 8):
                        bank, key = mtile(lambda kt, m=m: w_sb[:, kt, C_ZB + m * 128:C_ZB + (m + 1) * 128])
                        ACT(st_zb[:, m, :W], bank[:, :W], AF.Silu, [key], ["st_zb"])
                    if "zb" not in K_SKIP:
                        DMA(SZB[:, :, t0:t0 + W].rearrange("m p t -> p m t"), st_zb[:, :, :W], ["st_zb"], [("SZB", t0)])
                    for b in range(0 if "dt" in K_SKIP else nb):
                        for kt in range(8):
                            MM(dtp[:Pb, b, :], hT[:, kt, b * Pb:(b + 1) * Pb], w_sb[:, kt, C_DT:C_DT + 16], kt == 0, kt == 7, ["w", "hT"], ["dtp"])
                    if "dt" not in K_SKIP:
                        ACT(st_dt[:Pb, :nb, :], dtp[:Pb, :nb, :], AF.Copy, ["dtp"], ["st_dt"])
                        DMA(DTR[t0:t0 + W, :].rearrange("(b p) h -> p b h", p=Pb), st_dt[:Pb, :nb, :], ["st_dt"], [("DTR", t0)])
            S.barrier()

        def phase2(l):
            with ExitStack() as ph:
                cw = sb(ph, "p2_cw", [128, 16, 4], F32)
                cbias = sb(ph, "p2_cb", [128, 16], F32)
                dtb = sb(ph, "p2_dtb", [128, 16], F32)
                arow = sb(ph, "p2_arow", [128, 16], F32)
                dvec = sb(ph, "p2_dvec", [128, 8], F32)
                ng = sb(ph, "p2_ng", [128, 8], F32)
                tails = sb(ph, "p2_tails", [128, 16, 3], F32)
                hstate = sb(ph, "p2_hst", [128, 1024], F32)
                hbf = sb(ph, "p2_hbf", [128, 1024], BF16)
                hio = sb(ph, "p2_hio", [128, 8, 128], F32)
                raw = sb(ph, "p2_raw", [128, 4, 515], F32)
                acc = [sb(ph, "p2_acc%d" % i, [128, 512], F32) for i in range(2)]
                xc = sb(ph, "p2_xc", [128, 16, 512], BF16)
                sza = sb(ph, "p2_sza", [128, 8, 512], BF16)
                dtr = sb(ph, "p2_dtr", [128, 4, 16], F32)
                x1 = sb(ph, "p2_x1", [128, 4, 16], F32)
                tA = sb(ph, "p2_tA", [128, 4, 16], F32)
                tB = sb(ph, "p2_tB", [128, 4, 16], F32)
                dtv = sb(ph, "p2_dtv", [128, 4, 16], F32)
                av = sb(ph, "p2_av", [128, 4, 16], F32)
                a_hi = sb(ph, "p2_ahi", [128, 4, 16], BF16)
                a_lo = sb(ph, "p2_alo", [128, 4, 16], BF16)
                xdt = sb(ph, "p2_xdt", [128, 1024], BF16)
                xe = sb(ph, "p2_xe", [128, 1024], BF16)
                btok = sb(ph, "p2_btok", [128, 512], BF16)
                acs = sb(ph, "p2_acs", [128, 16], F32)
                tmd = sb(ph, "p2_tmd", [128, 16], F32)
                te = sb(ph, "p2_te", [128, 16], F32)
                etot = sb(ph, "p2_etot", [128, 16], F32)
                cbm = sb(ph, "p2_cbm", [128, 4, 128], BF16)
                E = [sb(ph, "p2_E%d" % i, [128, 512], BF16) for i in range(2)]
                arg = [sb(ph, "p2_arg%d" % i, [128, 512], F32) for i in range(2)]
                dec = [sb(ph, "p2_dec%d" % i, [128, 512], BF16) for i in range(2)]
                Mh = [sb(ph, "p2_M%d" % i, [128, 512], BF16) for i in range(2)]
                Cs = [sb(ph, "p2_Cs%d" % i, [128, 512], BF16) for i in range(2)]
                yg = sb(ph, "p2_yg", [128, 8, 512], F32)
                sq = sb(ph, "p2_sq", [128, 8, 512], BF16)
                rs = sb(ph, "p2_rs", [128, 512], F32)
                rs2 = sb(ph, "p2_rs2", [128, 512], F32)
                htmp = sb(ph, "p2_htmp", [128, 512], F32)
                ya = sb(ph, "p2_ya", [128, 8, 512], BF16)
                smallp = ps(ph, "p2_small", [128, 512])
                tpp = ps(ph, "p2_tp", [128, 1024], BF16)
                cbp = ps(ph, "p2_cbp", [128, 512])
                acsb = [ps(ph, "p2_acsb%d" % i, [128, 512]) for i in range(2)]
                yps = [ps(ph, "p2_yps%d" % i, [128, 512]) for i in range(2)]
                hps = ps(ph, "p2_hps", [128, 512])

                for k in range(4):
                    DMA(cw[:, :, k], Wd["conv_w"][l, k].rearrange("(m p) -> p m", p=128), [], ["cw"])
                DMA(cbias[:], Wd["conv_b"][l].rearrange("(m p) -> p m", p=128), [], ["cbias"])
                DMA(dtb[:], Wd["dt_bias"][l].partition_broadcast(128), [], ["dtb"])
                DMA(arow[:], Wd["a_log"][l].partition_broadcast(128), [], ["arow"])
                for h2 in range(2):
                    DMA(dvec[h2 * 64:(h2 + 1) * 64, :], Wd["d_ssd"][l].rearrange("(m h) -> h m", h=2)[h2].partition_broadcast(64), [], ["dvec"])
                DMA(ng[:], Wd["ssd_norm_g"][l].rearrange("(k p) -> p k", p=128), [], ["ng"])
                ACT(arow[:], arow[:], AF.Exp, ["arow"], ["arow"])
                TSS("dve", arow[:], arow[:], -1.0, ALU.mult, ["arow"], ["arow"])

                for (t0, W, seq, first, last) in TILES:
                    Qc = min(128, W)
                    nck = W // Qc
                    if first:
                        if seq < 2:
                            MSET("pool", tails[:], 0.0, ["tails"])
                            MSET("pool", hstate[:], 0.0, ["hstate"])
                            MSET("pool", hbf[:], 0.0, ["hbf"])
                        else:
                            for k in range(3):
                                DMA(tails[:, :, k], conv0[l, k].rearrange("(m p) -> p m", p=128), [], ["tails"])
                            DMA(hio[:], ssd0[l].rearrange("(m h) p n -> (h p) m n", h=2), [], ["hio"])
                            for m in range(8):
                                TR(hps[:, (m % 4) * 128:(m % 4 + 1) * 128], hio[:, m, :], ident_f[:], ["hio"], ["hps"])
                                if m % 4 == 3:
                                    hh = m // 4
                                    CP("dve", hstate[:, hh * 512:(hh + 1) * 512], hps[:], ["hps"], ["hstate"])
                            CP("act", hbf[:], hstate[:], ["hstate"], ["hbf"])
                    for gq in range(4):
                        DMA(raw[:, :, 3:3 + W], XBC[gq * 4:(gq + 1) * 4, :, t0:t0 + W].rearrange("m p t -> p m t"),
                            [("XBC", t0, 2 * gq), ("XBC", t0, 2 * gq + 1)], ["raw"])
                        CP("dve", raw[:, :, 0:3], tails[:, gq * 4:(gq + 1) * 4, :], ["tails"], ["raw"])
                        for mi in range(4):
                            m = gq * 4 + mi
                            ac = acc[m % 2]
                            akey = ("acc", m % 2)
                            ACT(ac[:, :W], raw[:, mi, 0:W], AF.Identity, ["raw", "cw", "cbias"], [akey], bias=cbias[:, m:m + 1], scale=cw[:, m, 0:1])
                            for k in range(1, 4):
                                STT(ac[:, :W], raw[:, mi, k:k + W], cw[:, m, k:k + 1], ac[:, :W], ALU.mult, ALU.add, ["raw", "cw", akey], [akey])
                            ACT(xc[:, m, :W], ac[:, :W], AF.Silu, [akey], [("xc", m)])
                        CP("dve", tails[:, gq * 4:(gq + 1) * 4, :], raw[:, :, W:W + 3], ["raw"], ["tails"])
                    DMA(sza[:, :, :W], SZA[:, :, t0:t0 + W].rearrange("m p t -> p m t"), [("SZA", t0)], ["sza"])
                    DMA(dtr[:Qc, :nck, :], DTR[t0:t0 + W, :].rearrange("(b p) h -> p b h", p=Qc), [("DTR", t0)], ["dtr"])
                    sl = lambda tl: tl[:Qc, :nck, :]
                    bc = lambda tl: tl[:Qc, :].unsqueeze(1).to_broadcast([Qc, nck, 16])
                    TT("dve", sl(x1), sl(dtr), bc(dtb), ALU.add, ["dtr", "dtb"], ["x1"])
                    STT(sl(tA), sl(x1), -1.0, sl(x1), ALU.mult, ALU.max, ["x1"], ["tA"])
                    ACT(sl(tA), sl(tA), AF.Exp, ["tA"], ["tA"], scale=-1.0)
                    ACT(sl(tA), sl(tA), AF.Ln, ["tA"], ["tA"], bias=1.0, scale=1.0)
                    TSS("dve", sl(tB), sl(x1), 0.0, ALU.max, ["x1"], ["tB"])
                    TT("dve", sl(dtv), sl(tA), sl(tB), ALU.add, ["tA", "tB"], ["dtv"])
                    TT("dve", sl(av), sl(dtv), bc(arow), ALU.mult, ["dtv", "arow"], ["av"])
                    CP("dve", sl(a_hi), sl(av), ["av"], ["a_hi"])
                    TT("dve", sl(a_lo), sl(av), sl(a_hi), ALU.subtract, ["av", "a_hi"], ["a_lo"])
                    for c in range(nck):
                        cs = c * Qc
                        hd = lambda ap: ap.rearrange("p (h d) -> p h d", d=64)
                        for mq in range(2):
                            for mi in range(4):
                                TR(tpp[:Qc, mq * 512 + mi * 128:mq * 512 + (mi + 1) * 128], xc[:, mq * 4 + mi, cs:cs + Qc], ident_b[:],
                                   [("xc", mq * 4 + mi)], ["tp"])
                            TT("dve", hd(xdt[:Qc, mq * 512:(mq + 1) * 512]), hd(tpp[:Qc, mq * 512:(mq + 1) * 512]),
                               dtv[:Qc, c, mq * 8:(mq + 1) * 8].unsqueeze(2).to_broadcast([Qc, 8, 64]), ALU.mult, ["tp", "dtv"], [("xdt", mq)])
                        for g in range(4):
                            TR(tpp[:Qc, g * 128:(g + 1) * 128], xc[:, 8 + g, cs:cs + Qc], ident_b[:], [("xc", 8 + g)], ["tp"])
                        CP("dve", btok[:Qc, :], tpp[:Qc, 0:512], ["tp"], ["btok"])
                        for i, aa in enumerate((a_hi, a_lo)):
                            MM(smallp[:Qc, 0:16], tri_b[:Qc, :Qc], aa[:Qc, c, :], i == 0, i == 1, ["a_hi", "a_lo", "tri_b"], ["small"])
                        for i, aa in enumerate((a_hi, a_lo)):
                            MM(smallp[:, 16:32], ones_b[:Qc, :], aa[:Qc, c, :], i == 0, i == 1, ["a_hi", "a_lo"], ["small"])
                        CP("act", acs[:Qc, :], smallp[:Qc, 0:16], ["small"], ["acs"])
                        ACT(etot[:], smallp[:, 16:32], AF.Exp, ["small"], ["etot"])
                        TT("dve", tmd[:Qc, :], smallp[:Qc, 16:32], acs[:Qc, :], ALU.subtract, ["small", "acs", "etot"], ["tmd"])
                        ACT(te[:Qc, :], tmd[:Qc, :], AF.Exp, ["tmd"], ["te"])
                        TT("pool", hd(xe[:Qc, :]), hd(xdt[:Qc, :]), te[:Qc, :].unsqueeze(2).to_broadcast([Qc, 16, 64]), ALU.mult,
                           [("xdt", 0), ("xdt", 1), "te"], ["xe"])
                        for g in range(4):
                            MM(cbp[:Qc, g * 128:g * 128 + Qc], xc[:, 8 + g, cs:cs + Qc], xc[:, 12 + g, cs:cs + Qc], True, True,
                               [("xc", 8 + g), ("xc", 12 + g)], ["cbp"])
                        TT("dve", cbm[:Qc, :, :Qc], cbp[:Qc, :].rearrange("p (g l) -> p g l", g=4)[:, :, :Qc],
                           tri_b[:Qc, :Qc].unsqueeze(1).to_broadcast([Qc, 4, Qc]), ALU.mult, ["cbp", "tri_b"], ["cbm"])
                        for g in range(4):
                            par = g % 2
                            ab = acsb[par]
                            abv = ab[:].rearrange("p (h l) -> p h l", h=4)
                            for hh in range(4):
                                h = g * 4 + hh
                                for i, aa in enumerate((a_hi, a_lo)):
                                    MM(abv[:, hh, :Qc], aa[:Qc, c, h:h + 1].to_broadcast([Qc, 128]), tri_b[:Qc, :Qc], i == 0, i == 1,
                                       ["a_hi", "a_lo"], [("acsb", par)])
                            Ev = E[par][:].rearrange("p (h l) -> p h l", h=4)
                            ACT(Ev[:, :, :Qc], abv[:, :, :Qc], AF.Exp, [("acsb", par)], [("E", par)])
                            xv = arg[par][:].rearrange("p (h l) -> p h l", h=4)
                            TT("dve", xv[:Qc, :, :Qc], abv[:Qc, :, :Qc], acs[:Qc, g * 4:(g + 1) * 4].unsqueeze(2).to_broadcast([Qc, 4, Qc]), ALU.subtract,
                               [("acsb", par), "acs", ("E", par)], [("arg", par)])
                            dv = dec[par][:].rearrange("p (h l) -> p h l", h=4)
                            ACT(dv[:Qc, :, :Qc], xv[:Qc, :, :Qc], AF.Exp, [("arg", par)], [("dec", par)])
                            Mv = Mh[par][:].rearrange("p (h l) -> p h l", h=4)
                            for hh in range(4):
                                STT(Mv[:Qc, hh, :Qc], dv[:Qc, hh, :Qc], 1.0, cbm[:Qc, g, :Qc], ALU.min, ALU.mult, [("dec", par), "cbm"], [("Mh", par)])
                            Cv = Cs[par][:].rearrange("p (h l) -> p h l", h=4)
                            TT("pool", Cv[:, :, :Qc], xc[:, 12 + g, cs:cs + Qc].unsqueeze(1).to_broadcast([128, 4, Qc]), Ev[:, :, :Qc], ALU.mult,
                               [("xc", 12 + g), ("E", par)], [("Cs", par)])
                            yp = yps[par]
                            ykey = ("yps", par)
                            for hh in range(4):
                                h = g * 4 + hh
                                o = yp[(hh % 2) * 64:(hh % 2 + 1) * 64, (hh // 2) * 128:(hh // 2) * 128 + Qc]
                                MM(o, xdt[:Qc, h * 64:(h + 1) * 64], Mv[:Qc, hh, :Qc], True, False, [("xdt", h // 8), ("Mh", par)], [ykey])
                                MM(o, hbf[:, h * 64:(h + 1) * 64], Cv[:, hh, :Qc], False, True, ["hbf", ("Cs", par)], [ykey])
                            for mm_ in range(2):
                                m = 2 * g + mm_
                                STT(yg[:, m, cs:cs + Qc], xc[:, m, cs:cs + Qc], dvec[:, m:m + 1], yp[:, mm_ * 128:mm_ * 128 + Qc], ALU.mult, ALU.add,
                                    [("xc", m), "dvec", ykey], ["yg"])
                        for half in range(2):
                            for gi in range(2):
                                g = half * 2 + gi
                                MM(hps[:, gi * 256:(gi + 1) * 256], btok[:Qc, g * 128:(g + 1) * 128], xe[:Qc, g * 256:(g + 1) * 256], True, True,
                                   ["btok", "xe"], ["hps"])
                            TT("dve", hd(htmp[:]), hd(hstate[:, half * 512:(half + 1) * 512]),
                               etot[:, half * 8:(half + 1) * 8].unsqueeze(2).to_broadcast([128, 8, 64]), ALU.mult, ["hstate", "etot"], ["htmp"])
                            TT("dve", hstate[:, half * 512:(half + 1) * 512], htmp[:], hps[:], ALU.add, ["htmp", "hps"], ["hstate"])
                        CP("act", hbf[:], hstate[:], ["hstate"], ["hbf"])
                    TT("dve", yg[:, :, :W], yg[:, :, :W], sza[:, :, :W], ALU.mult, ["yg", "sza"], ["yg"])
                    ACT(sq[:, :, :W], yg[:, :, :W], AF.Square, ["yg"], ["n2_sq"])
                    for gg in range(4):
                        rms_rstd(cbp, [sq[:, 2 * gg, :W], sq[:, 2 * gg + 1, :W]], W, 1.0 / 256, rs, rs2, "n2")
                        for m in (2 * gg, 2 * gg + 1):
                            STT(ya[:, m, :W], yg[:, m, :W], ng[:, m:m + 1], rs2[:, :W], ALU.mult, ALU.mult, ["yg", "ng", "n2_rs2"], ["ya"])
                    DMA(YA[:, :, t0:t0 + W].rearrange("m p t -> p m t"), ya[:, :, :W], ["ya"], [("YA", t0)])
                    if last:
                        for k in range(3):
                            DMA(conv_o[l, seq, k].rearrange("(m p) -> p m", p=128), tails[:, :, k], ["tails"], [])
                        for m in range(8):
                            TR(hps[:, (m % 4) * 128:(m % 4 + 1) * 128], hstate[:, m * 128:(m + 1) * 128], ident_f[:], ["hstate"], ["hps"])
                            if m % 4 == 3:
                                hh = m // 4
                                CP("dve", hio[:, hh * 4:(hh + 1) * 4, :], hps[:].rearrange("p (m n) -> p m n", n=128), ["hps"], ["hio"])
                        DMA(ssd_o[l, seq].rearrange("(m h) p n -> (h p) m n", h=2), hio[:], ["hio"], [])
            S.barrier()

        def phase3(l):
            with ExitStack() as ph:
                W_sb = sb(ph, "p3_W", [128, 64, 128], BF16)
                T0_sb = sb(ph, "p3_T0", [128, 64, 128], BF16)
                Cm_sb = sb(ph, "p3_Cm", [128, 32, 2, 128], BF16)
                L8 = sb(ph, "p3_L8", [128, 4, 32], F32)
                Hcar = sb(ph, "p3_Hcar", [128, 2, 32], F32)
                pa = ps(ph, "p3_pa", [128, 512])
                pb = ps(ph, "p3_pb", [128, 512])
                pc = ps(ph, "p3_pc", [128, 512])
                pd = ps(ph, "p3_pd", [128, 512])
                with ExitStack() as pp:
                    f64 = lambda nm, shp, dt=F32: sb(pp, nm, shp, dt)
                    lamr, lami, dtg, zr, zi = [f64("q_" + n, [64, 64]) for n in ("lamr", "lami", "dtg", "zr", "zi")]
                    kr, ki, t1, t2, den = [f64("q_" + n, [64, 64]) for n in ("kr", "ki", "t1", "t2", "den")]
                    erow_i = f64("q_erowi", [64, 25], I32)
                    erow = f64("q_erow", [64, 25])
                    ang, mag, s4, s2, Pr, Pi = [f64("q_" + n, [64, 64, 25]) for n in ("ang", "mag", "s4", "s2", "Pr", "Pi")]
                    kqi = f64("q_kqi", [64, 64, 25], I32)
                    PKr, PKi, u1 = [f64("q_" + n, [64, 64, 8]) for n in ("PKr", "PKi", "u1")]
                    Btr, Bti, Ctr, Cti = [f64("q_" + n, [64, 64, 16]) for n in ("Btr", "Bti", "Ctr", "Cti")]
                    Cn = sb(pp, "q_Cn", [128, 8, 64], F32)
                    Wtr, Wti, v1, v2, Cxr, Cxn = [f64("q_" + n, [64, 8, 8, 16]) for n in ("Wtr", "Wti", "v1", "v2", "Cxr", "Cxn")]

                    DMA(lamr[:], Wd["lam_re"][l].rearrange("g p -> p g"), [], ["lamr"])
                    DMA(lami[:], Wd["lam_im"][l].rearrange("g p -> p g"), [], ["lami"])
                    DMA(dtg[:], Wd["log_dt"][l].partition_broadcast(64), [], ["dtg"])
                    DMA(Btr[:], Wd["b_re"][l].rearrange("g p c -> p g c"), [], ["Btr"])
                    DMA(Bti[:], Wd["b_im"][l].rearrange("g p c -> p g c"), [], ["Bti"])
                    for (src, dst, nm) in ((Wd["c_re"], Ctr, "Ctr"), (Wd["c_im"], Cti, "Cti")):
                        DMA(Cn[:], src[l].rearrange("(j g) c p -> (g c) j p", g=8), [], ["Cn"])
                        for j in range(8):
                            TR(pa[:64, (j % 4) * 128:(j % 4 + 1) * 128], Cn[:, j, :], ident_f[:], ["Cn"], ["pa"])
                            if j % 4 == 3:
                                jj = j // 4
                                CP("dve", dst[:, jj * 32:(jj + 1) * 32, :], pa[:64, :].rearrange("p (g c) -> p g c", c=16), ["pa"], [nm])
                    ACT(dtg[:], dtg[:], AF.Exp, ["dtg"], ["dtg"])
                    TT("dve", zr[:], lamr[:], dtg[:], ALU.mult, ["lamr", "dtg"], ["zr"])
                    TT("dve", zi[:], lami[:], dtg[:], ALU.mult, ["lami", "dtg"], ["zi"])
                    for (a, b_, pat, base) in ((0, 8, [[-1, 8]], 7), (8, 16, [[1, 8]], 1), (16, 24, [[1, 8]], -7), (24, 25, [[1, 1]], 8)):
                        oap = erow_i[:, a:b_]
                        S.op("pool", lambda e, oap=oap, pat=pat, base=base: e.iota(out=oap, pattern=pat, base=base, channel_multiplier=0), (), ["erow_i"])
                    CP("dve", erow[:], erow_i[:], ["erow_i"], ["erow"])
                    ebc = erow[:, :].unsqueeze(1).to_broadcast([64, 64, 25])
                    gbc = lambda tl: tl[:, :].unsqueeze(2).to_broadcast([64, 64, 25])
                    fl = lambda tl: tl[:].rearrange("p g m -> p (g m)")
                    TT("dve", ang[:], gbc(zi), ebc, ALU.mult, ["zi", "erow"], ["ang"])
                    TT("dve", mag[:], gbc(zr), ebc, ALU.mult, ["zr", "erow"], ["mag"])
                    ACT(mag[:], mag[:], AF.Exp, ["mag"], ["mag"])
                    TSS("dve", s4[:], ang[:], 1.0 / (2 * math.pi), ALU.mult, ["ang"], ["s4"])
                    CP("dve", kqi[:], s4[:], ["s4"], ["kqi"])
                    CP("dve", s4[:], kqi[:], ["kqi"], ["s4"])
                    STT(fl(ang), fl(s4), -2 * math.pi, fl(ang), ALU.mult, ALU.add, ["s4", "ang"], ["ang"])
                    ACT(s4[:], ang[:], AF.Sin, ["ang"], ["s4"], scale=0.25)
                    ACT(s2[:], ang[:], AF.Sin, ["ang"], ["s2"], scale=0.5)
                    TT("dve", s4[:], s4[:], s4[:], ALU.mult, ["s4"], ["s4"])
                    TS("dve", s4[:], s4[:], -2.0, 1.0, ALU.mult, ALU.add, ["s4"], ["s4"])
                    TT("dve", Pi[:], s2[:], s4[:], ALU.mult, ["s2", "s4"], ["Pi"])
                    STT(fl(Pi), fl(Pi), 2.0, fl(mag), ALU.mult, ALU.mult, ["Pi", "mag"], ["Pi"])
                    TT("dve", s2[:], s2[:], s2[:], ALU.mult, ["s2"], ["s2"])
                    TS("dve", s2[:], s2[:], -2.0, 1.0, ALU.mult, ALU.add, ["s2"], ["s2"])
                    TT("dve", Pr[:], s2[:], mag[:], ALU.mult, ["s2", "mag"], ["Pr"])
                    TSS("dve", t1[:], Pr[:, :, 6], -1.0, ALU.add, ["Pr"], ["t1"])
                    TT("dve", den[:], lamr[:], lamr[:], ALU.mult, ["lamr"], ["den"])
                    TT("dve", t2[:], lami[:], lami[:], ALU.mult, ["lami"], ["t2"])
                    TT("dve", den[:], den[:], t2[:], ALU.add, ["den", "t2"], ["den"])
                    S.op("dve", lambda e: e.reciprocal(out=den[:], in_=den[:]), ["den"], ["den"])
                    TT("dve", kr[:], t1[:], lamr[:], ALU.mult, ["t1", "lamr"], ["kr"])
                    TT("dve", t2[:], Pi[:, :, 6], lami[:], ALU.mult, ["Pi", "lami"], ["t2"])
                    TT("dve", kr[:], kr[:], t2[:], ALU.add, ["kr", "t2"], ["kr"])
                    TT("dve", kr[:], kr[:], den[:], ALU.mult, ["kr", "den"], ["kr"])
                    TT("dve", ki[:], Pi[:, :, 6], lamr[:], ALU.mult, ["Pi", "lamr"], ["ki"])
                    TT("dve", t2[:], t1[:], lami[:], ALU.mult, ["t1", "lami"], ["t2"])
                    TT("dve", ki[:], ki[:], t2[:], ALU.subtract, ["ki", "t2"], ["ki"])
                    TT("dve", ki[:], ki[:], den[:], ALU.mult, ["ki", "den"], ["ki"])
                    k8 = lambda tl: tl[:, :].unsqueeze(2).to_broadcast([64, 64, 8])
                    TT("dve", PKr[:], Pr[:, :, 0:8], k8(kr), ALU.mult, ["Pr", "kr"], ["PKr"])
                    TT("dve", u1[:], Pi[:, :, 0:8], k8(ki), ALU.mult, ["Pi", "ki"], ["u1"])
                    TT("dve", PKr[:], PKr[:], u1[:], ALU.subtract, ["PKr", "u1"], ["PKr"])
                    TT("dve", PKi[:], Pr[:, :, 0:8], k8(ki), ALU.mult, ["Pr", "ki"], ["PKi"])
                    TT("dve", u1[:], Pi[:, :, 0:8], k8(kr), ALU.mult, ["Pi", "kr"], ["u1"])
                    TT("dve", PKi[:], PKi[:], u1[:], ALU.add, ["PKi", "u1"], ["PKi"])
                    for (ti, src, sc) in ((0, Pr, 1.0), (1, Pi, 1.0), (2, Pi, -1.0), (3, mag, 1.0)):
                        for gh in range(2):
                            ACT(L8[gh * 64:(gh + 1) * 64, ti, :].rearrange("p (g j) -> p g j", j=8),
                                src[:, :, 24].rearrange("p (j g) -> p g j", g=8)[:, gh * 4:(gh + 1) * 4, :], AF.Copy, ["Pr", "Pi", "mag"], ["L8"], scale=sc)

                    def cmul(outr, outi, PA_r, PA_i, Xr, Xi, jb, key, rd, neg):
                        pbc = lambda P_: P_.unsqueeze(3).to_broadcast([64, 8, 8, 16])
                        xbc = lambda X_: X_[:, jb * 8:(jb + 1) * 8, :].unsqueeze(2).to_broadcast([64, 8, 8, 16])
                        TT("dve", outr[:], pbc(PA_r), xbc(Xr), ALU.mult, rd, [key + "r"])
                        TT("dve", v1[:], pbc(PA_i), xbc(Xi), ALU.mult, rd, ["v1"])
                        TT("dve", outr[:], outr[:], v1[:], ALU.subtract, [key + "r", "v1"], [key + "r"])
                        TT("dve", outi[:], pbc(PA_r), xbc(Xi), ALU.mult, rd, [key + "i"])
                        TT("dve", v2[:], pbc(PA_i), xbc(Xr), ALU.mult, rd, ["v2"])
                        TT("dve", outi[:], outi[:], v2[:], ALU.add, [key + "i", "v2"], [key + "i"])
                        if neg:
                            TSS("dve", outi[:], outi[:], -1.0, ALU.mult, [key + "i"], [key + "i"])

                    base_rd = ["Pr", "Pi", "PKr", "PKi", "Btr", "Bti", "Ctr", "Cti"]
                    sc_ = lambda ap: ap.rearrange("p s c -> p (s c)")
                    for jb in range(8):
                        gs = slice(jb * 8, (jb + 1) * 8)
                        cmul(Wtr, Wti, PKr[:, gs, :], PKi[:, gs, :], Btr, Bti, jb, "Wt", base_rd, False)
                        for (ri, src, nm) in ((0, Wtr, "Wtr"), (1, Wti, "Wti")):
                            for g8 in range(8):
                                bank, bk = (pa, "pa") if g8 < 4 else (pb, "pb")
                                TR(bank[:, (g8 % 4) * 64:(g8 % 4 + 1) * 64], sc_(src[:, g8, :, :]), ident_f[:64, :64], [nm], [bk])
                            for hb, (bank, bk) in enumerate(((pa, "pa"), (pb, "pb"))):
                                ACT(W_sb[:, :, ri * 64:(ri + 1) * 64].rearrange("p (g j) x -> p g j x", j=8)[:, hb * 4:(hb + 1) * 4, jb, :],
                                    bank[:, 0:256].rearrange("p (g x) -> p g x", x=64), AF.Copy, [bk], ["W_sb"])
                        cmul(Cxr, Cxn, Pr[:, gs, 16:24], Pi[:, gs, 16:24], Ctr, Cti, jb, "Cx", base_rd, True)
                        for g8 in range(8):
                            bank, bk = (pc, "pc") if g8 < 4 else (pd, "pd")
                            o = bank[:, (g8 % 4) * 128:(g8 % 4 + 1) * 128]
                            MM(o, sc_(Wtr[:, g8, :, :]), sc_(Cxr[:, g8, :, :]), True, False, ["Wtr", "Cxr"], [bk])
                            MM(o, sc_(Wti[:, g8, :, :]), sc_(Cxn[:, g8, :, :]), False, True, ["Wti", "Cxi"], [bk])
                        for hb, (bank, bk) in enumerate(((pc, "pc"), (pd, "pd"))):
                            TT("dve", T0_sb[:].rearrange("p (g j) x -> p g j x", j=8)[:, hb * 4:(hb + 1) * 4, jb, :],
                               bank[:].rearrange("p (g x) -> p g x", x=128), bmask[:].unsqueeze(1).to_broadcast([128, 4, 128]), ALU.mult,
                               [bk, "bmask"], ["T0_sb"])
                        cmul(Cxr, Cxn, Pr[:, gs, 8:16], Pi[:, gs, 8:16], Ctr, Cti, jb, "Cx", base_rd, True)
                        for (ri, src, nm) in ((0, Cxr, "Cxr"), (1, Cxn, "Cxi")):
                            for gh in range(2):
                                ACT(Cm_sb[gh * 64:(gh + 1) * 64, :, ri, :].rearrange("p (g j) x -> p g j x", j=8)[:, :, jb, :],
                                    src[:, gh * 4:(gh + 1) * 4, :, :].rearrange("p g s c -> p g (s c)"), AF.Copy, [nm], ["Cm_sb"])
                S.barrier()
                with ExitStack() as pm:
                    Hb = sb(pm, "p3_Hb", [128, 2, 32, 65], F32)
                    U_sb = sb(pm, "p3_U", [128, 64, 64], BF16)
                    Hbf = sb(pm, "p3_Hbf", [128, 2, 32, 64], BF16)
                    Yst = sb(pm, "p3_Yst", [128, 64, 64], F32)
                    Fr = sb(pm, "p3_Fr", [128, 32, 65], F32)
                    Fi = sb(pm, "p3_Fi", [128, 32, 65], F32)
                    Rt = sb(pm, "p3_R", [128, 32, 65], F32)
                    Gir = sb(pm, "p3_Gir", [128, 32, 65], F32)
                    Gii = sb(pm, "p3_Gii", [128, 32, 65], F32)
                    Gsr = sb(pm, "p3_Gsr", [128, 32, 65], F32)
                    Gsi = sb(pm, "p3_Gsi", [128, 32, 65], F32)
                    c1 = sb(pm, "p3_c1", [128, 3, 32], F32)
                    S.op("dve", lambda e: e.reciprocal(out=c1[:, 2, :], in_=L8[:, 3, :]), ["L8"], ["c1"])
                    TT("dve", c1[:, 0, :], L8[:, 0, :], c1[:, 2, :], ALU.mult, ["L8", "c1"], ["c1"])
                    TT("dve", c1[:, 1, :], L8[:, 1, :], c1[:, 2, :], ALU.mult, ["L8", "c1"], ["c1"])
                    MSET("pool", Fr[:, :, 0:1], 1.0, ["Fr"])
                    MSET("pool", Fi[:, :, 0:1], 0.0, ["Fi"])
                    MSET("pool", Rt[:, :, 0:1], 0.0, ["Rt"])
                    MSET("pool", Gir[:], 0.0, ["Gir"])
                    MSET("pool", Gii[:], 0.0, ["Gii"])
                    CP("dve", Fr[:, :, 1], c1[:, 0, :], ["c1", "Fr"], ["Fr"])
                    CP("dve", Fi[:, :, 1], c1[:, 1, :], ["c1", "Fi"], ["Fi"])
                    CP("dve", Rt[:, :, 1:65], L8[:, 3, :].unsqueeze(2).to_broadcast([128, 32, 64]), ["L8", "Rt"], ["Rt"])
                    for k in range(6):
                        m_ = 1 << k
                        a_bc = Fr[:, :, m_:m_ + 1].to_broadcast([128, 32, m_])
                        b_bc = Fi[:, :, m_:m_ + 1].to_broadcast([128, 32, m_])
                        sr, si = Fr[:, :, 1:1 + m_], Fi[:, :, 1:1 + m_]
                        dr, di = Fr[:, :, m_ + 1:2 * m_ + 1], Fi[:, :, m_ + 1:2 * m_ + 1]
                        u_, v_ = Gsr[:, :, 0:m_], Gsi[:, :, 0:m_]
                        TT("dve", u_, sr, a_bc, ALU.mult, ["Fr"], ["Gsr"])
                        TT("dve", v_, si, b_bc, ALU.mult, ["Fi"], ["Gsi"])
                        TT("dve", dr, u_, v_, ALU.subtract, ["Gsr", "Gsi", "Fr"], ["Fr"])
                        TT("dve", u_, sr, b_bc, ALU.mult, ["Fr", "Fi"], ["Gsr"])
                        TT("dve", v_, si, a_bc, ALU.mult, ["Fr", "Fi"], ["Gsi"])
                        TT("dve", di, u_, v_, ALU.add, ["Gsr", "Gsi", "Fi"], ["Fi"])
                    lbanks = [pa, pb]
                    ybanks = [pc, pd]
                    for (t0, W, seq, first, last) in TILES:
                        nC = W // 8
                        n0 = t0 // 8
                        if first:
                            if seq < 2:
                                MSET("pool", Hb[:, :, :, 0:1], 0.0, ["Hb"])
                            else:
                                for (ri, src) in ((0, s5r0), (1, s5i0)):
                                    for g8 in range(8):
                                        gh, g4 = g8 // 4, g8 % 4
                                        DMA(Hcar[gh * 64:(gh + 1) * 64, ri, g4 * 8:(g4 + 1) * 8],
                                            src[l].rearrange("(j g) p -> p g j", g=8)[:, g8, :], [], ["Hcar"])
                                CP("dve", Hb[:, :, :, 0], Hcar[:], ["Hcar"], ["Hb"])
                        DMA(U_sb[:, :, :nC], UD[:, :, :, :, n0:n0 + nC].rearrange("s c g j n -> (s c) (g j) n"), [("UD", t0)], ["U_sb"])
                        for gl4 in range(8):
                            bank = lbanks[gl4 % 2]
                            bk = ("lb", gl4 % 2)
                            bv = bank[:].rearrange("p (r g n) -> p r g n", r=2, g=4)
                            for gh in range(2):
                                for gi in range(4):
                                    gidx = gh * 32 + gl4 * 4 + gi
                                    for ri in range(2):
                                        MM(bv[gh * 64:(gh + 1) * 64, ri, gi, :nC], W_sb[:, gidx, ri * 64:(ri + 1) * 64], U_sb[:, gidx, :nC], True, True,
                                           ["U_sb"], [bk])
                            CP("act", Hb[:, :, gl4 * 4:(gl4 + 1) * 4, 1:1 + nC], bv[:, :, :, :nC], [bk], ["Hb"])
                        n1 = nC + 1
                        Lr_, Li_ = Hb[:, 0, :, 1:n1], Hb[:, 1, :, 1:n1]
                        Frs, Fis = Fr[:, :, 1:n1], Fi[:, :, 1:n1]
                        CP("dve", Gir[:, :, 0], Hb[:, 0, :, 0], ["Hb"], ["Gir"])
                        CP("dve", Gii[:, :, 0], Hb[:, 1, :, 0], ["Hb"], ["Gii"])
                        TT("dve", Gsr[:, :, 1:n1], Lr_, Frs, ALU.mult, ["Hb", "Fr"], ["Gsr"])
                        TT("dve", Gsi[:, :, 1:n1], Li_, Fis, ALU.mult, ["Hb", "Fi"], ["Gsi"])
                        TT("dve", Gir[:, :, 1:n1], Gsr[:, :, 1:n1], Gsi[:, :, 1:n1], ALU.add, ["Gsr", "Gsi"], ["Gir"])
                        TT("dve", Gsr[:, :, 1:n1], Li_, Frs, ALU.mult, ["Hb", "Fr", "Gir"], ["Gsr"])
                        TT("dve", Gsi[:, :, 1:n1], Lr_, Fis, ALU.mult, ["Hb", "Fi", "Gir"], ["Gsi"])
                        TT("dve", Gii[:, :, 1:n1], Gsr[:, :, 1:n1], Gsi[:, :, 1:n1], ALU.subtract, ["Gsr", "Gsi"], ["Gii"])
                        fl_ = lambda tl: tl[:].rearrange("p g n -> p (g n)")
                        S.op("dve", lambda e, o=fl_(Gsr), d0=fl_(Rt), d1=fl_(Gir): e.tensor_tensor_scan(out=o, data0=d0, data1=d1, initial=0.0, op0=ALU.mult, op1=ALU.add),
                             ["Rt", "Gir", "Gii"], ["Gsr"])
                        S.op("dve", lambda e, o=fl_(Gsi), d0=fl_(Rt), d1=fl_(Gii): e.tensor_tensor_scan(out=o, data0=d0, data1=d1, initial=0.0, op0=ALU.mult, op1=ALU.add),
                             ["Rt", "Gii", "Gsr"], ["Gsi"])
                        Frn, Fin = Fr[:, :, 0:n1], Fi[:, :, 0:n1]
                        TT("dve", Gir[:, :, 0:n1], Gsr[:, :, 0:n1], Frn, ALU.mult, ["Gsr", "Gsi"], ["Gir"])
                        TT("dve", Gii[:, :, 0:n1], Gsi[:, :, 0:n1], Fin, ALU.mult, ["Gsr", "Gsi"], ["Gii"])
                        TT("dve", Hb[:, 0, :, 0:n1], Gir[:, :, 0:n1], Gii[:, :, 0:n1], ALU.subtract, ["Gir", "Gii", "Hb"], ["Hb"])
                        TT("dve", Gir[:, :, 0:n1], Gsr[:, :, 0:n1], Fin, ALU.mult, ["Gsr", "Gsi", "Hb"], ["Gir"])
                        TT("dve", Gii[:, :, 0:n1], Gsi[:, :, 0:n1], Frn, ALU.mult, ["Gsr", "Gsi", "Hb"], ["Gii"])
                        TT("dve", Hb[:, 1, :, 0:n1], Gir[:, :, 0:n1], Gii[:, :, 0:n1], ALU.add, ["Gir", "Gii", "Hb"], ["Hb"])
                        CP("act", Hbf[:, :, :, :nC], Hb[:, :, :, 0:nC], ["Hb"], ["Hbf"])
                        CP("dve", Hcar[:], Hb[:, :, :, nC], ["Hb"], ["Hcar"])
                        CP("dve", Hb[:, :, :, 0], Hcar[:], ["Hcar", "Hbf"], ["Hb"])
                        for g8b in range(8):
                            bank = ybanks[g8b % 2]
                            bk = ("yb", g8b % 2)
                            bv = bank[:].rearrange("p (g n) -> p g n", g=8)
                            for gi in range(8):
                                gidx = g8b * 8 + gi
                                gh, gl = gidx // 32, gidx % 32
                                MM(bv[:, gi, :nC], T0_sb[:, gidx, :], U_sb[:, gidx, :nC], True, False, ["U_sb"], [bk])
                                for ri in range(2):
                                    MM(bv[:, gi, :nC], Cm_sb[gh * 64:(gh + 1) * 64, gl, ri, :], Hbf[gh * 64:(gh + 1) * 64, ri, gl, :nC], False, ri == 1,
                                       ["Hbf"], [bk])
                            CP("act" if g8b % 2 else "dve", Yst[:, g8b * 8:(g8b + 1) * 8, :nC], bv[:, :, :nC], [bk], ["Yst"])
                        for s_ in range(8):
                            DMA(YD[:, :, :, s_, n0:n0 + nC].rearrange("c g j n -> c (g j) n"), Yst[s_ * 16:(s_ + 1) * 16, :, :nC], ["Yst"], [("YD", t0)])
                        if last:
                            for (ri, dst) in ((0, s5r_o), (1, s5i_o)):
                                for g8 in range(8):
                                    gh, g4 = g8 // 4, g8 % 4
                                    DMA(dst[l, seq].rearrange("(j g) p -> p g j", g=8)[:, g8, :],
                                        Hcar[gh * 64:(gh + 1) * 64, ri, g4 * 8:(g4 + 1) * 8], ["Hcar"], [])
            S.barrier()

        def phase4(l):
            with ExitStack() as ph:
                wg = sb(ph, "p4_w", [128, 8, 1024], BF16)
                ds5 = sb(ph, "p4_ds5", [128, 8], F32)
                bgl = sb(ph, "p4_bgl", [128, 8], F32)
                yp4 = [sb(ph, "p4_yp%d" % i, [128, 8, 8, 64], F32) for i in range(2)]
                ut4 = [sb(ph, "p4_ut%d" % i, [128, 8, 512], BF16) for i in range(2)]
                szb4 = [sb(ph, "p4_szb%d" % i, [128, 8, 512], BF16) for i in range(2)]
                yb = sb(ph, "p4_yb", [128, 8, 512], F32)
                gf = sb(ph, "p4_gf", [128, 8, 512], F32)
                gb = sb(ph, "p4_gb", [128, 8, 512], BF16)
                sg = [sb(ph, "p4_sg%d" % i, [128, 512], F32) for i in range(2)]
                ybst = sb(ph, "p4_ybst", [128, 8, 512], BF16)
                mm = [ps(ph, "p4_mm%d" % i, [128, 512]) for i in range(2)]
                for kt in range(8):
                    DMA(wg[:, kt, :], Wd["w_glu"][l, kt * 128:(kt + 1) * 128, :], [], ["wg"], q="pool")
                DMA(ds5[:], Wd["d_s5"][l].rearrange("(k p) -> p k", p=128), [], ["ds5"])
                DMA(bgl[:], Wd["b_glu"][l].rearrange("(k p) -> p k", p=128), [], ["bgl"])
                def p4_load(ti):
                    (t0, W, seq, first, last) = TILES[ti]
                    nC = W // 8
                    n0 = t0 // 8
                    pr = ti % 2
                    for g8 in range(8):
                        DMA(yp4[pr][g8 * 16:(g8 + 1) * 16, :, :, :nC].rearrange("c j s n -> c (j s) n"),
                            YD[:, g8, :, :, n0:n0 + nC].rearrange("c j s n -> c (j s) n"), [("YD", t0)], [("yp4", pr)])
                        DMA(ut4[pr][g8 * 16:(g8 + 1) * 16, :, :W], UT[:, :, g8, t0:t0 + W].rearrange("j c t -> c j t"), [("UT", t0)], [("ut4", pr)])
                    DMA(szb4[pr][:, :, :W], SZB[:, :, t0:t0 + W].rearrange("m p t -> p m t"), [("SZB", t0)], [("szb4", pr)])

                p4_load(0)
                for ti, (t0, W, seq, first, last) in enumerate(TILES):
                    nC = W // 8
                    n0 = t0 // 8
                    pr = ti % 2
                    if ti + 1 < len(TILES):
                        p4_load(ti + 1)
                    for j in range(8):
                        STT(yb[:, j, :W].rearrange("p (n s) -> p n s", s=8), ut4[pr][:, j, :W].rearrange("p (n s) -> p n s", s=8), ds5[:, j:j + 1],
                            yp4[pr][:, j, :, :nC].rearrange("p s n -> p n s"), ALU.mult, ALU.add, [("ut4", pr), ("yp4", pr), "ds5"], [("yb", j)])
                        ACT(gf[:, j, :W], yb[:, j, :W], AF.Gelu_apprx_tanh, [("yb", j)], [("gf", j)])
                        CP("act" if j % 2 else "dve", gb[:, j, :W], gf[:, j, :W], [("gf", j)], ["gb"])
                    for jo in range(8):
                        bank = mm[jo % 2]
                        bk = ("mm", jo % 2)
                        for ji in range(8):
                            MM(bank[:, :W], wg[:, ji, jo * 128:(jo + 1) * 128], gb[:, ji, :W], ji == 0, ji == 7, ["wg", "gb"], [bk])
                        sgt = sg[jo % 2]
                        sk = ("sg", jo % 2)
                        ACT(sgt[:, :W], bank[:, :W], AF.Sigmoid, [bk, "bgl"], [sk], bias=bgl[:, jo:jo + 1], scale=1.0)
                        TT("dve", sgt[:, :W], sgt[:, :W], gf[:, jo, :W], ALU.mult, [sk, ("gf", jo)], [sk])
                        TT("dve", ybst[:, jo, :W], sgt[:, :W], szb4[pr][:, jo, :W], ALU.mult, [sk, ("szb4", pr)], ["ybst"])
                    DMA(YB[:, :, t0:t0 + W].rearrange("m p t -> p m t"), ybst[:, :, :W], ["ybst"], [("YB", t0)])
            S.barrier()

        def phase5(l):
            with ExitStack() as ph:
                wo = sb(ph, "p5_w", [128, 16, 1024], BF16)
                y5 = sb(ph, "p5_y", [128, 16, 512], BF16)
                xT5 = sb(ph, "p5_x", [128, 8, 512], F32)
                xo = sb(ph, "p5_xo", [128, 8, 512], F32)
                mm = [ps(ph, "p5_mm%d" % i, [128, 512]) for i in range(2)]
                for kt in range(16):
                    DMA(wo[:, kt, :], Wd["w_out"][l, kt * 128:(kt + 1) * 128, :], [], ["wo"], q="pool")
                for (t0, W, seq, first, last) in TILES:
                    DMA(y5[:, 0:8, :W], YA[:, :, t0:t0 + W].rearrange("m p t -> p m t"), [("YA", t0)], ["y5a"])
                    DMA(y5[:, 8:16, :W], YB[:, :, t0:t0 + W].rearrange("m p t -> p m t"), [("YB", t0)], ["y5b"])
                    DMA(xT5[:, :, :W], xt_ap(t0, W), [("XT", t0)], ["xT5"])
                    for dm in range(8):
                        bank = mm[dm % 2]
                        bk = ("mm", dm % 2)
                        for kt in range(16):
                            MM(bank[:, :W], wo[:, kt, dm * 128:(dm + 1) * 128], y5[:, kt, :W], kt == 0, kt == 15, ["wo", "y5a", "y5b"], [bk])
                        TT("dve", xo[:, dm, :W], bank[:, :W], xT5[:, dm, :W], ALU.add, [bk, "xT5"], ["xo"])
                    DMA(xt_ap(t0, W), xo[:, :, :W], ["xo"], [("XT", t0)])
            S.barrier()

        def phase_final():
            with ExitStack() as ph:
                fg = sb(ph, "pf_g", [128, 8], F32)
                xT = sb(ph, "pf_xT", [128, 8, 512], F32)
                sq = sb(ph, "pf_sq", [128, 8, 512], BF16)
                hf = sb(ph, "pf_hf", [128, 8, 512], F32)
                rs = sb(ph, "pf_rs", [128, 512], F32)
                rs2 = sb(ph, "pf_rs2", [128, 512], F32)
                yst = sb(ph, "pf_yst", [128, 4, 1024], F32)
                msp = ps(ph, "pf_ms", [128, 512])
                tp = [ps(ph, "pf_tp%d" % i, [128, 512]) for i in range(4)]
                DMA(fg[:], Wd["final_norm_g"].rearrange("(k p) -> p k", p=128), [], ["fg"])
                for (t0, W, seq, first, last) in TILES:
                    Pb = min(128, W)
                    nb = W // Pb
                    DMA(xT[:, :, :W], xt_ap(t0, W), [("XT", t0)], ["xT"])
                    ACT(sq[:, :, :W], xT[:, :, :W], AF.Square, ["xT"], ["nf_sq"])
                    rms_rstd(msp, [sq[:, kt, :W] for kt in range(8)], W, 1.0 / 1024, rs, rs2, "nf")
                    for kt in range(8):
                        STT(hf[:, kt, :W], xT[:, kt, :W], fg[:, kt:kt + 1], rs2[:, :W], ALU.mult, ALU.mult, ["xT", "fg", "nf_rs2"], ["hf"])
                    for b in range(nb):
                        for half in range(2):
                            bank = tp[(b * 2 + half) % 4]
                            bk = ("tp", (b * 2 + half) % 4)
                            for k4 in range(4):
                                kt = half * 4 + k4
                                TR(bank[:Pb, k4 * 128:(k4 + 1) * 128], hf[:, kt, b * Pb:(b + 1) * Pb], ident_f[:], ["hf"], [bk])
                            CP("act" if half else "dve", yst[:Pb, b, half * 512:(half + 1) * 512], bank[:Pb, :], [bk], ["yst"])
                    DMA(y_out[t0:t0 + W, :].rearrange("(b p) d -> p b d", p=Pb), yst[:Pb, :nb, :], ["yst"], [])
            S.barrier()

        wst = ExitStack()
        w_cur = sb(wst, "w_in_sb", [128, 8, IN_COLS], BF16)
        load_w_in(0, w_cur)
        phase0()
        for l in range(n_layers):
            if 1 in phases:
                phase1(l, w_cur)
            wst.close()
            if 2 in phases:
                phase2(l)
            if 3 in phases:
                phase3(l)
            if 4 in phases:
                phase4(l)
            if l + 1 < n_layers:
                wst = ExitStack()
                w_cur = sb(wst, "w_in_sb", [128, 8, IN_COLS], BF16)
                load_w_in(l + 1, w_cur)
            if 5 in phases:
                phase5(l)
        if do_final:
            phase_final()
        S.finish(block)
        nc._n_instr = S.n_instr
    return nc


def make_in_maps(inputs):
    xp = np.ascontiguousarray(inputs["x_prompt"], dtype=np.float32)
    xs = np.ascontiguousarray(inputs["x_sample"], dtype=np.float32)
    maps = []
    for c in range(NCORES):
        m = {}
        m["x_core"] = np.ascontiguousarray(np.concatenate([xp[2 * c], xp[2 * c + 1], xs[c]], axis=0))
        m["conv0"] = np.ascontiguousarray(inputs["state_ssd_conv"][:, c])
        m["ssd0"] = np.ascontiguousarray(inputs["state_ssd"][:, c])
        m["s5r0"] = np.ascontiguousarray(inputs["state_s5_re"][:, c])
        m["s5i0"] = np.ascontiguousarray(inputs["state_s5_im"][:, c])
        for n in W_NAMES:
            m[n] = np.ascontiguousarray(inputs[n], dtype=np.float32)
        maps.append(m)
    return maps


_NC_CACHE = {}


def kernel(**inputs):
    inputs = {k: np.asarray(v) for k, v in inputs.items()}
    if "nc" not in _NC_CACHE:
        _NC_CACHE["nc"] = build_nc()
    nc = _NC_CACHE["nc"]
    maps = make_in_maps(inputs)
    res = run_bass_kernel_spmd(nc, maps, core_ids=list(range(NCORES)))
    R = res.results
    y_prompt = np.zeros((16, 2048, 1024), np.float32)
    y_sample = np.zeros((8, 64, 1024), np.float32)
    conv_p = np.zeros((4, 16, 3, 2048), np.float32)
    ssd_p = np.zeros((4, 16, 16, 64, 128), np.float32)
    s5r_p = np.zeros((4, 16, 64, 64), np.float32)
    s5i_p = np.zeros((4, 16, 64, 64), np.float32)
    conv_s = np.zeros((4, 8, 3, 2048), np.float32)
    ssd_s = np.zeros((4, 8, 16, 64, 128), np.float32)
    s5r_s = np.zeros((4, 8, 64, 64), np.float32)
    s5i_s = np.zeros((4, 8, 64, 64), np.float32)
    for c in range(NCORES):
        r = R[c]
        y = r["y_out"]
        y_prompt[2 * c] = y[0:2048]
        y_prompt[2 * c + 1] = y[2048:4096]
        y_sample[c] = y[4096:4160]
        for i in range(2):
            conv_p[:, 2 * c + i] = r["conv_o"][:, i]
            ssd_p[:, 2 * c + i] = r["ssd_o"][:, i]
            s5r_p[:, 2 * c + i] = r["s5r_o"][:, i]
            s5i_p[:, 2 * c + i] = r["s5i_o"][:, i]
        conv_s[:, c] = r["conv_o"][:, 2]
        ssd_s[:, c] = r["ssd_o"][:, 2]
        s5r_s[:, c] = r["s5r_o"][:, 2]
        s5i_s[:, c] = r["s5i_o"][:, 2]
    return (y_prompt, y_sample, conv_p, ssd_p, s5r_p, s5i_p, conv_s, ssd_s, s5r_s, s5i_s)
```

```python
import math
from contextlib import ExitStack

import numpy as np
import concourse.bass as bass
import concourse.mybir as mybir
from concourse.bass_utils import run_bass_kernel_spmd

F32 = mybir.dt.float32
BF16 = mybir.dt.bfloat16
I32 = mybir.dt.int32
AF = mybir.ActivationFunctionType
ALU = mybir.AluOpType

NCORES = 8
DEPTH = 4
T = 4160
NCH = T // 8
EPS = 1e-6
IN_COLS = 5136
C_ZA, C_XBC, C_DT, C_U, C_ZB = 0, 1024, 3072, 3088, 4112
TILES = [(s * 2048 + i * 512, 512, s, i == 0, i == 3) for s in range(2) for i in range(4)]
TILES.append((4096, 64, 2, True, True))
import os as _os
if _os.environ.get("K_TILES"):
    TILES = [TILES[int(i)] for i in _os.environ["K_TILES"].split(",")]
K_SKIP = set(_os.environ.get("K_SKIP", "").split(","))

ENGS = ("pe", "dve", "act", "pool", "sp")


class Sync:
    def __init__(self, nc, stack, n_dma_sems=40, same_engine_sync=True):
        self.nc = nc
        self.esem = {e: stack.enter_context(nc.semaphore("s_" + e)) for e in ENGS}
        self.cnt = {e: 0 for e in ENGS}
        self.prog = {e: [] for e in ENGS}
        self.waited = {e: {} for e in ENGS}
        self.res = {}
        self.same = same_engine_sync
        self.dsems = [stack.enter_context(nc.semaphore("s_dma%d" % i)) for i in range(n_dma_sems)]
        self.dval = [0] * n_dma_sems
        self.dnext = 0
        self.sems = {}
        for e in ENGS:
            self.sems[("e", e)] = self.esem[e]
        for i, s in enumerate(self.dsems):
            self.sems[("d", i)] = s
        self.n_instr = 0

    def _need(self, e, reads, writes):
        need = {}

        def add(ev):
            if ev is None:
                return
            k, v = ev
            if need.get(k, 0) < v:
                need[k] = v

        for r in reads:
            st = self.res.get(r)
            if st is not None:
                add(st[0])
        for w in writes:
            st = self.res.get(w)
            if st is not None:
                add(st[0])
                for ev in st[1]:
                    add(ev)
        out = []
        for k, v in need.items():
            if k == ("e", e) and (e == "pe" or not self.same):
                continue
            if self.waited[e].get(k, 0) >= v:
                continue
            self.waited[e][k] = v
            out.append((k, v))
        return out

    def _commit(self, ev, reads, writes):
        for r in reads:
            st = self.res.setdefault(r, [None, []])
            st[1].append(ev)
            if len(st[1]) > 16:
                mx = {}
                for k, v in st[1]:
                    if mx.get(k, 0) < v:
                        mx[k] = v
                st[1] = list(mx.items())
        for w in writes:
            self.res[w] = [ev, []]

    def op(self, e, fn, reads=(), writes=()):
        waits = self._need(e, reads, writes)
        self.cnt[e] += 1
        ev = (("e", e), self.cnt[e])
        sem = self.esem[e]
        sems = self.sems

        def emit(eng, waits=waits, fn=fn, sem=sem):
            for k, v in waits:
                eng.wait_ge(sems[k], v)
            fn(eng).then_inc(sem, 1)

        self.prog[e].append(emit)
        self._commit(ev, reads, writes)
        self.n_instr += 1
        return ev

    def dma(self, e, fn, reads=(), writes=()):
        i = self.dnext
        self.dnext = (self.dnext + 1) % len(self.dsems)
        k = ("d", i)
        waits = self._need(e, reads, writes)
        if self.dval[i] > 0 and self.waited[e].get(k, 0) < self.dval[i]:
            self.waited[e][k] = self.dval[i]
            waits.append((k, self.dval[i]))
        self.dval[i] += 16
        ev = (k, self.dval[i])
        sem = self.dsems[i]
        sems = self.sems

        def emit(eng, waits=waits, fn=fn, sem=sem):
            for kk, v in waits:
                eng.wait_ge(sems[kk], v)
            fn(eng).then_inc(sem, 16)

        self.prog[e].append(emit)
        self._commit(ev, reads, writes)
        self.n_instr += 1
        return ev

    def barrier(self):
        evs = [(("e", e), self.cnt[e]) for e in ENGS if self.cnt[e] > 0]
        evs += [(("d", i), v) for i, v in enumerate(self.dval) if v > 0]
        sems = self.sems
        for e in ENGS:
            waits = []
            for k, v in evs:
                if k == ("e", e):
                    continue
                if self.waited[e].get(k, 0) >= v:
                    continue
                self.waited[e][k] = v
                waits.append((k, v))

            def emit(eng, waits=waits):
                for kk, v in waits:
                    eng.wait_ge(sems[kk], v)

            self.prog[e].append(emit)
        self.res = {}

    def finish(self, block):
        self.barrier()
        prog = self.prog

        @block.tensor
        def _(eng):
            for f in prog["pe"]:
                f(eng)

        @block.vector
        def _(eng):
            for f in prog["dve"]:
                f(eng)

        @block.scalar
        def _(eng):
            for f in prog["act"]:
                f(eng)

        @block.gpsimd
        def _(eng):
            for f in prog["pool"]:
                f(eng)

        @block.sync
        def _(eng):
            for f in prog["sp"]:
                f(eng)


W_NAMES = ["norm_g", "w_in", "conv_w", "conv_b", "dt_bias", "a_log", "d_ssd", "ssd_norm_g",
           "lam_re", "lam_im", "log_dt", "b_re", "b_im", "c_re", "c_im", "d_s5", "w_glu", "b_glu",
           "w_out", "final_norm_g"]
W_SHAPES = {
    "norm_g": [4, 1024], "w_in": [4, 1024, 5136], "conv_w": [4, 4, 2048], "conv_b": [4, 2048],
    "dt_bias": [4, 16], "a_log": [4, 16], "d_ssd": [4, 16], "ssd_norm_g": [4, 1024],
    "lam_re": [4, 64, 64], "lam_im": [4, 64, 64], "log_dt": [4, 64], "b_re": [4, 64, 64, 16],
    "b_im": [4, 64, 64, 16], "c_re": [4, 64, 16, 64], "c_im": [4, 64, 16, 64], "d_s5": [4, 1024],
    "w_glu": [4, 1024, 1024], "b_glu": [4, 1024], "w_out": [4, 2048, 1024], "final_norm_g": [1024],
}


def build_nc(n_layers=DEPTH, dbg=False, phases=(1, 2, 3, 4, 5), do_final=True):
    nc = bass.Bass("TRN2", target_bir_lowering=False)
    skind = "ExternalOutput" if dbg else "Internal"

    def din(name, shape, dt=F32):
        return nc.dram_tensor(name, shape, dt, kind="ExternalInput").ap()

    def dout(name, shape, dt=F32):
        return nc.dram_tensor(name, shape, dt, kind="ExternalOutput").ap()

    def dscr(name, shape, dt=F32):
        return nc.dram_tensor(name, shape, dt, kind=skind).ap()

    x_in = din("x_core", [T, 1024])
    conv0 = din("conv0", [4, 3, 2048])
    ssd0 = din("ssd0", [4, 16, 64, 128])
    s5r0 = din("s5r0", [4, 64, 64])
    s5i0 = din("s5i0", [4, 64, 64])
    Wd = {n: din(n, W_SHAPES[n]) for n in W_NAMES}

    y_out = dout("y_out", [T, 1024])
    conv_o = dout("conv_o", [4, 3, 3, 2048])
    ssd_o = dout("ssd_o", [4, 3, 16, 64, 128])
    s5r_o = dout("s5r_o", [4, 3, 64, 64])
    s5i_o = dout("s5i_o", [4, 3, 64, 64])

    XT = dscr("XT", [8, 128, T])
    SZA = dscr("SZA", [8, 128, T], BF16)
    XBC = dscr("XBC", [16, 128, T])
    DTR = dscr("DTR", [T, 16])
    UD = dscr("UD", [8, 16, 8, 8, NCH], BF16)
    UT = dscr("UT", [8, 16, 8, T], BF16)
    SZB = dscr("SZB", [8, 128, T], BF16)
    YA = dscr("YA", [8, 128, T], BF16)
    YD = dscr("YD", [16, 8, 8 * 8, NCH]).rearrange("c g (j s) n -> c g j s n", s=8)
    YB = dscr("YB", [8, 128, T], BF16)

    with ExitStack() as st:
        S = Sync(nc, st)
        st.enter_context(nc.allow_non_contiguous_dma(reason="small strided parameter/state loads"))

        uniq = [0]

        def sb(stack, name, shape, dt):
            uniq[0] += 1
            return stack.enter_context(nc.sbuf_tensor("%s_%d" % (name, uniq[0]), shape, dt))

        def ps(stack, name, shape, dt=F32):
            uniq[0] += 1
            return stack.enter_context(nc.psum_tensor("%s_%d" % (name, uniq[0]), shape, dt))


        def TT(eng, out, in0, in1, op, reads, writes):
            return S.op(eng, lambda e: e.tensor_tensor(out=out, in0=in0, in1=in1, op=op), reads, writes)

        def TS(eng, out, in0, s1, s2, op0, op1, reads, writes):
            return S.op(eng, lambda e: e.tensor_scalar(out=out, in0=in0, scalar1=s1, scalar2=s2, op0=op0, op1=op1), reads, writes)

        def TSS(eng, out, in_, scalar, op, reads, writes):
            return S.op(eng, lambda e: e.tensor_single_scalar(out=out, in_=in_, scalar=scalar, op=op), reads, writes)

        def STT(out, in0, scalar, in1, op0, op1, reads, writes):
            return S.op("dve", lambda e: e.scalar_tensor_tensor(out=out, in0=in0, scalar=scalar, in1=in1, op0=op0, op1=op1), reads, writes)

        def ACT(out, in_, func, reads, writes, bias=None, scale=None):
            kw = {}
            if bias is not None:
                kw["bias"] = bias
            if scale is not None:
                kw["scale"] = scale
            return S.op("act", lambda e: e.activation(out=out, in_=in_, func=func, **kw), reads, writes)

        def CP(eng, out, in_, reads, writes):
            if eng == "act":
                return ACT(out, in_, AF.Copy, reads, writes)
            return S.op(eng, lambda e: e.tensor_copy(out=out, in_=in_), reads, writes)

        def MM(out, lhsT, rhs, start, stop, reads, writes):
            return S.op("pe", lambda e: e.matmul(out, lhsT=lhsT, rhs=rhs, start=start, stop=stop), reads, writes)

        def TR(out, in_, ident, reads, writes):
            return S.op("pe", lambda e: e.transpose(out=out, in_=in_, identity=ident), reads, writes)

        def DMA(out, in_, reads=(), writes=(), q="sp"):
            return S.dma(q, lambda e: e.dma_start(out=out, in_=in_), reads, writes)

        def MSET(eng, ap, val, writes):
            return S.op(eng, lambda e: e.memset(ap, val), (), writes)

        ones_f = sb(st, "ones_f", [128, 128], F32)
        ident_f = sb(st, "ident_f", [128, 128], F32)
        ident_b = sb(st, "ident_b", [128, 128], BF16)
        ones_b = sb(st, "ones_b", [128, 128], BF16)
        tri_f = sb(st, "tri_f", [128, 128], F32)
        tri_b = sb(st, "tri_b", [128, 128], BF16)
        bmask = sb(st, "bmask", [128, 128], F32)
        block = st.enter_context(nc.Block())

        S.op("pool", lambda e: e.memset(ones_f[:], 1.0), writes=["ones_f"])
        S.op("pool", lambda e: e.affine_select(out=ident_f[:], in_=ones_f[:], pattern=[[1, 128]],
                                               compare_op=ALU.is_equal, fill=0.0, base=0, channel_multiplier=-1),
             reads=["ones_f"], writes=["ident_f"])
        S.op("pool", lambda e: e.affine_select(out=tri_f[:], in_=ones_f[:], pattern=[[1, 128]],
                                               compare_op=ALU.is_ge, fill=0.0, base=0, channel_multiplier=-1),
             reads=["ones_f"], writes=["tri_f"])
        S.op("pool", lambda e: e.affine_select(out=bmask[:], in_=ones_f[:], pattern=[[16, 8], [0, 16]],
                                               compare_op=ALU.is_ge, fill=0.0, base=15, channel_multiplier=-1),
             reads=["ones_f"], writes=["bmask"])
        S.op("dve", lambda e: e.tensor_copy(out=ident_b[:], in_=ident_f[:]), reads=["ident_f"], writes=["ident_b"])
        S.op("dve", lambda e: e.tensor_copy(out=ones_b[:], in_=ones_f[:]), reads=["ones_f"], writes=["ones_b"])
        S.op("dve", lambda e: e.tensor_copy(out=tri_b[:], in_=tri_f[:]), reads=["tri_f"], writes=["tri_b"])
        S.barrier()

        def xt_ap(t0, W):
            return XT[:, :, t0:t0 + W].rearrange("k p t -> p k t")

        def rms_rstd(ph_ps, src_sq, W, scale, rs, rs2, key):
            nk = len(src_sq)
            for i, a in enumerate(src_sq):
                MM(ph_ps[:, :W], ones_b[:], a, i == 0, i == nk - 1, [key + "_sq"], [key + "_ms"])
            ACT(rs[:, :W], ph_ps[:, :W], AF.Ln, [key + "_ms"], [key + "_rs"], bias=EPS, scale=scale)
            ACT(rs2[:, :W], rs[:, :W], AF.Exp, [key + "_rs"], [key + "_rs2"], scale=-0.5)

        def phase0():
            with ExitStack() as ph:
                xs = sb(ph, "p0_xs", [128, 4, 1024], F32)
                xo = sb(ph, "p0_xo", [128, 8, 512], F32)
                tp = [ps(ph, "p0_tp%d" % i, [128, 512]) for i in range(2)]
                for (t0, W, seq, first, last) in TILES:
                    Pb = min(128, W)
                    nb = W // Pb
                    DMA(xs[:Pb, :nb, :], x_in[t0:t0 + W, :].rearrange("(b p) d -> p b d", p=Pb), [], ["xs"])
                    for kt in range(8):
                        tpk = tp[kt % 2]
                        for b in range(nb):
                            TR(tpk[:, b * Pb:(b + 1) * Pb], xs[:Pb, b, kt * 128:(kt + 1) * 128], ident_f[:Pb, :Pb], ["xs"], [("tp", kt % 2)])
                        CP("act" if kt % 2 else "dve", xo[:, kt, :W], tpk[:, :W], [("tp", kt % 2)], ["xo"])
                    DMA(xt_ap(t0, W), xo[:, :, :W], ["xo"], [("XT", t0)])
            S.barrier()

        def load_w_in(l, w_sb):
            for kt in range(8):
                for (c0, cw_) in ((0, 2048), (2048, 2048), (4096, 1040)):
                    DMA(w_sb[:, kt, c0:c0 + cw_], Wd["w_in"][l, kt * 128:(kt + 1) * 128, c0:c0 + cw_], [], ["w"], q="pool")

        def phase1(l, w_sb):
            with ExitStack() as ph:
                g1 = sb(ph, "p1_g", [128, 8], F32)
                xT = sb(ph, "p1_xT", [128, 8, 512], F32)
                sq = sb(ph, "p1_sq", [128, 8, 512], BF16)
                hT = sb(ph, "p1_hT", [128, 8, 512], BF16)
                rs = sb(ph, "p1_rs", [128, 512], F32)
                rs2 = sb(ph, "p1_rs2", [128, 512], F32)
                st_za = sb(ph, "p1_za", [128, 8, 512], BF16)
                st_xbc = [sb(ph, "p1_xbc%d" % i, [128, 2, 512], F32) for i in range(2)]
                st_us = sb(ph, "p1_us", [128, 8, 8, 64], BF16)
                st_ut = sb(ph, "p1_ut", [128, 8, 512], BF16)
                us32 = [sb(ph, "p1_us32_%d" % i, [128, 8, 64], F32) for i in range(2)]
                st_zb = sb(ph, "p1_zb", [128, 8, 512], BF16)
                st_dt = sb(ph, "p1_dt", [128, 4, 16], F32)
                mm = [ps(ph, "p1_mm%d" % i, [128, 512]) for i in range(4)]
                msp = ps(ph, "p1_ms", [128, 512])
                dtp = ps(ph, "p1_dtp", [128, 4, 16])
                DMA(g1[:], Wd["norm_g"][l].rearrange("(k p) -> p k", p=128), [], ["g1"])
                for kt in range(0 if "perm" in K_SKIP else 8):
                    tmpu = st_ut[:, (kt % 2) * 2:(kt % 2) * 2 + 2, :].rearrange("p a t -> p (a t)")
                    CP("dve", tmpu, w_sb[:, kt, C_U:C_U + 1024], ["w"], [("tmpu", kt % 2), "st_ut"])
                    CP("act", w_sb[:, kt, C_U:C_U + 1024].rearrange("p (j c g) -> p j c g", c=16, g=8),
                       tmpu.rearrange("p (j g c) -> p j c g", g=8, c=16), [("tmpu", kt % 2), "st_ut"], ["w"])
                mcount = [0]
                def pro_load(ti):
                    (t0_, W_, _, _, _) = TILES[ti]
                    DMA(xT[:, :, :W_], xt_ap(t0_, W_), [("XT", t0_)], ["xT"])

                def pro_sq(ti):
                    W_ = TILES[ti][1]
                    ACT(sq[:, :, :W_], xT[:, :, :W_], AF.Square, ["xT"], ["n1_sq"])

                def pro_ms(ti):
                    W_ = TILES[ti][1]
                    rms_rstd(msp, [sq[:, kt, :W_] for kt in range(8)], W_, 1.0 / 1024, rs, rs2, "n1")

                pro_load(0)
                pro_sq(0)
                pro_ms(0)
                for ti, (t0, W, seq, first, last) in enumerate(TILES):
                    Pb = min(128, W)
                    nb = W // Pb
                    nC = W // 8
                    n0 = t0 // 8
                    nxt = ti + 1 if ti + 1 < len(TILES) else None
                    for kt in range(8):
                        STT(hT[:, kt, :W], xT[:, kt, :W], g1[:, kt:kt + 1], rs2[:, :W], ALU.mult, ALU.mult, ["xT", "g1", "n1_rs2"], ["hT"])
                    if nxt is not None:
                        pro_load(nxt)

                    def mtile(lhs_fn, W=W):
                        bank = mm[mcount[0] % 4]
                        key = ("mm", mcount[0] % 4)
                        mcount[0] += 1
                        for kt in range(8):
                            MM(bank[:, :W], lhs_fn(kt), hT[:, kt, :W], kt == 0, kt == 7, ["w", "hT"], [key])
                        return bank, key

                    for m in range(0 if "za" in K_SKIP else 8):
                        bank, key = mtile(lambda kt, m=m: w_sb[:, kt, C_ZA + m * 128:C_ZA + (m + 1) * 128])
                        ACT(st_za[:, m, :W], bank[:, :W], AF.Silu, [key], ["st_za"])
                    if "za" not in K_SKIP:
                        DMA(SZA[:, :, t0:t0 + W].rearrange("m p t -> p m t"), st_za[:, :, :W], ["st_za"], [("SZA", t0)])
                    if nxt is not None:
                        pro_sq(nxt)
                    for m in range(0 if "xbc" in K_SKIP else 16):
                        bank, key = mtile(lambda kt, m=m: w_sb[:, kt, C_XBC + m * 128:C_XBC + (m + 1) * 128])
                        pr = m // 2
                        stx = st_xbc[pr % 2]
                        skey = ("st_xbc", pr % 2)
                        CP("act" if m % 2 == 0 else "dve", stx[:, m % 2, :W], bank[:, :W], [key], [skey])
                        if m % 2 == 1:
                            DMA(XBC[pr * 2:(pr + 1) * 2, :, t0:t0 + W].rearrange("m p t -> p m t"), stx[:, :, :W], [skey], [("XBC", t0, pr)])
                    if nxt is not None:
                        pro_ms(nxt)
                    for j in range(0 if "u" in K_SKIP else 8):
                        bank, key = mtile(lambda kt, j=j: w_sb[:, kt, C_U + j * 128:C_U + (j + 1) * 128])
                        u32 = us32[j % 2]
                        ACT(u32[:, :, :nC].rearrange("p s n -> p n s"), bank[:, :W].rearrange("p (n s) -> p n s", s=8), AF.Copy, [key], [("us32", j % 2), ("brd", key)])
                        CP("pool", st_us[:, j, :, :nC], u32[:, :, :nC], [("us32", j % 2)], ["st_us"])
                        CP("dve", st_ut[:, j, :W], bank[:, :W], [key, ("brd", key)], ["st_ut"])
                    for j in range(0 if "ud" in K_SKIP else 8):
                        DMA(UD[:, :, :, j, n0:n0 + nC].rearrange("s c g n -> (c g) s n"), st_us[:, j, :, :nC], ["st_us"], [("UD", t0)])
                    if "u" not in K_SKIP and "utdma" not in K_SKIP:
                        DMA(UT[:, :, :, t0:t0 + W].rearrange("j c g t -> (c g) j t"), st_ut[:, :, :W], ["st_ut"], [("UT", t0)])
                    for m in range(0 if "zb" in K_SKIP else 8):
                        bank, key = mtile(lambda kt, m=m: w_sb[:, kt, C_ZB + m * 128:C_ZB + (m + 1) * 128])
                        ACT(st_zb[:, m, :W], bank[:, :W], AF.Silu, [key], ["st_zb"])
                    if "zb" not in K_SKIP:
                        DMA(SZB[:, :, t0:t0 + W].rearrange("m p t -> p m t"), st_zb[:, :, :W], ["st_zb"], [("SZB", t0)])
                    for b in range(0 if "dt" in K_SKIP else nb):
                        for kt in range(8):
                            MM(dtp[:Pb, b, :], hT[:, kt, b * Pb:(b + 1) * Pb], w_sb[:, kt, C_DT:C_DT + 16], kt == 0, kt == 7, ["w", "hT"], ["dtp"])
                    if "dt" not in K_SKIP:
                        ACT(st_dt[:Pb, :nb, :], dtp[:Pb, :nb, :], AF.Copy, ["dtp"], ["st_dt"])
                        DMA(DTR[t0:t0 + W, :].rearrange("(b p) h -> p b h", p=Pb), st_dt[:Pb, :nb, :], ["st_dt"], [("DTR", t0)])
            S.barrier()

        def phase2(l):
            with ExitStack() as ph:
                cw = sb(ph, "p2_cw", [128, 16, 4], F32)
                cbias = sb(ph, "p2_cb", [128, 16], F32)
                dtb = sb(ph, "p2_dtb", [128, 16], F32)
                arow = sb(ph, "p2_arow", [128, 16], F32)
                dvec = sb(ph, "p2_dvec", [128, 8], F32)
                ng = sb(ph, "p2_ng", [128, 8], F32)
                tails = sb(ph, "p2_tails", [128, 16, 3], F32)
                hstate = sb(ph, "p2_hst", [128, 1024], F32)
                hbf = sb(ph, "p2_hbf", [128, 1024], BF16)
                hio = sb(ph, "p2_hio", [128, 8, 128], F32)
                raw = sb(ph, "p2_raw", [128, 4, 515], F32)
                acc = [sb(ph, "p2_acc%d" % i, [128, 512], F32) for i in range(2)]
                xc = sb(ph, "p2_xc", [128, 16, 512], BF16)
                sza = sb(ph, "p2_sza", [128, 8, 512], BF16)
                dtr = sb(ph, "p2_dtr", [128, 4, 16], F32)
                x1 = sb(ph, "p2_x1", [128, 4, 16], F32)
                tA = sb(ph, "p2_tA", [128, 4, 16], F32)
                tB = sb(ph, "p2_tB", [128, 4, 16], F32)
                dtv = sb(ph, "p2_dtv", [128, 4, 16], F32)
                av = sb(ph, "p2_av", [128, 4, 16], F32)
                a_hi = sb(ph, "p2_ahi", [128, 4, 16], BF16)
                a_lo = sb(ph, "p2_alo", [128, 4, 16], BF16)
                xdt2 = [sb(ph, "p2_xdt%d" % i, [128, 1024], BF16) for i in range(2)]
                xe2 = [sb(ph, "p2_xe%d" % i, [128, 1024], BF16) for i in range(2)]
                btok2 = [sb(ph, "p2_btok%d" % i, [128, 512], BF16) for i in range(2)]
                acs2 = [sb(ph, "p2_acs%d" % i, [128, 16], F32) for i in range(2)]
                tmd2 = [sb(ph, "p2_tmd%d" % i, [128, 16], F32) for i in range(2)]
                te2 = [sb(ph, "p2_te%d" % i, [128, 16], F32) for i in range(2)]
                etot2 = [sb(ph, "p2_etot%d" % i, [128, 16], F32) for i in range(2)]
                cbm2 = [sb(ph, "p2_cbm%d" % i, [128, 4, 128], BF16) for i in range(2)]
                Mh4 = [[sb(ph, "p2_Mh%d_%d" % (i, g), [128, 512], BF16) for g in range(4)] for i in range(2)]
                Cs4 = [[sb(ph, "p2_Cs%d_%d" % (i, g), [128, 512], BF16) for g in range(4)] for i in range(2)]
                E = [sb(ph, "p2_E%d" % i, [128, 512], BF16) for i in range(2)]
                arg = [sb(ph, "p2_arg%d" % i, [128, 512], F32) for i in range(2)]
                dec = [sb(ph, "p2_dec%d" % i, [128, 512], BF16) for i in range(2)]
                yg = sb(ph, "p2_yg", [128, 8, 512], F32)
                sq = sb(ph, "p2_sq", [128, 8, 512], BF16)
                rs4 = [sb(ph, "p2_rs4_%d" % i, [128, 512], F32) for i in range(4)]
                htmp = sb(ph, "p2_htmp", [128, 512], F32)
                ya = sb(ph, "p2_ya", [128, 8, 512], BF16)
                smallp = ps(ph, "p2_small", [128, 512])
                tpp = ps(ph, "p2_tp", [128, 1024], BF16)
                cbp = ps(ph, "p2_cbp", [128, 512])
                acsb = [ps(ph, "p2_acsb%d" % i, [128, 512]) for i in range(2)]
                yps = [ps(ph, "p2_yps%d" % i, [128, 512]) for i in range(2)]
                hps = ps(ph, "p2_hps", [128, 512])

                for k in range(4):
                    DMA(cw[:, :, k], Wd["conv_w"][l, k].rearrange("(m p) -> p m", p=128), [], ["cw"])
                DMA(cbias[:], Wd["conv_b"][l].rearrange("(m p) -> p m", p=128), [], ["cbias"])
                DMA(dtb[:], Wd["dt_bias"][l].partition_broadcast(128), [], ["dtb"])
                DMA(arow[:], Wd["a_log"][l].partition_broadcast(128), [], ["arow"])
                for h2 in range(2):
                    DMA(dvec[h2 * 64:(h2 + 1) * 64, :], Wd["d_ssd"][l].rearrange("(m h) -> h m", h=2)[h2].partition_broadcast(64), [], ["dvec"])
                DMA(ng[:], Wd["ssd_norm_g"][l].rearrange("(k p) -> p k", p=128), [], ["ng"])
                ACT(arow[:], arow[:], AF.Exp, ["arow"], ["arow"])
                TSS("dve", arow[:], arow[:], -1.0, ALU.mult, ["arow"], ["arow"])

                for (t0, W, seq, first, last) in TILES:
                    Qc = min(128, W)
                    nck = W // Qc
                    if first:
                        if seq < 2:
                            MSET("pool", tails[:], 0.0, ["tails"])
                            MSET("pool", hstate[:], 0.0, ["hstate"])
                            MSET("pool", hbf[:], 0.0, ["hbf"])
                        else:
                            for k in range(3):
                                DMA(tails[:, :, k], conv0[l, k].rearrange("(m p) -> p m", p=128), [], ["tails"])
                            DMA(hio[:], ssd0[l].rearrange("(m h) p n -> (h p) m n", h=2), [], ["hio"])
                            for m in range(8):
                                TR(hps[:, (m % 4) * 128:(m % 4 + 1) * 128], hio[:, m, :], ident_f[:], ["hio"], ["hps"])
                                if m % 4 == 3:
                                    hh = m // 4
                                    CP("dve", hstate[:, hh * 512:(hh + 1) * 512], hps[:], ["hps"], ["hstate"])
                            CP("act", hbf[:], hstate[:], ["hstate"], ["hbf"])
                    for gq in range(4):
                        DMA(raw[:, :, 3:3 + W], XBC[gq * 4:(gq + 1) * 4, :, t0:t0 + W].rearrange("m p t -> p m t"),
                            [("XBC", t0, 2 * gq), ("XBC", t0, 2 * gq + 1)], ["raw"])
                        CP("dve", raw[:, :, 0:3], tails[:, gq * 4:(gq + 1) * 4, :], ["tails"], ["raw"])
                        def tap0(mi):
                            m = gq * 4 + mi
                            ACT(acc[m % 2][:, :W], raw[:, mi, 0:W], AF.Identity, ["raw", "cw", "cbias"], [("acc", m % 2)],
                                bias=cbias[:, m:m + 1], scale=cw[:, m, 0:1])

                        tap0(0)
                        for mi in range(4):
                            m = gq * 4 + mi
                            ac = acc[m % 2]
                            akey = ("acc", m % 2)
                            if mi + 1 < 4:
                                tap0(mi + 1)
                            for k in range(1, 4):
                                STT(ac[:, :W], raw[:, mi, k:k + W], cw[:, m, k:k + 1], ac[:, :W], ALU.mult, ALU.add, ["raw", "cw", akey], [akey])
                            ACT(xc[:, m, :W], ac[:, :W], AF.Silu, [akey], [("xc", m)])
                        CP("dve", tails[:, gq * 4:(gq + 1) * 4, :], raw[:, :, W:W + 3], ["raw"], ["tails"])
                    DMA(sza[:, :, :W], SZA[:, :, t0:t0 + W].rearrange("m p t -> p m t"), [("SZA", t0)], ["sza"])
                    DMA(dtr[:Qc, :nck, :], DTR[t0:t0 + W, :].rearrange("(b p) h -> p b h", p=Qc), [("DTR", t0)], ["dtr"])
                    sl = lambda tl: tl[:Qc, :nck, :]
                    bc = lambda tl: tl[:Qc, :].unsqueeze(1).to_broadcast([Qc, nck, 16])
                    TT("dve", sl(x1), sl(dtr), bc(dtb), ALU.add, ["dtr", "dtb"], ["x1"])
                    STT(sl(tA), sl(x1), -1.0, sl(x1), ALU.mult, ALU.max, ["x1"], ["tA"])
                    ACT(sl(tA), sl(tA), AF.Exp, ["tA"], ["tA"], scale=-1.0)
                    ACT(sl(tA), sl(tA), AF.Ln, ["tA"], ["tA"], bias=1.0, scale=1.0)
                    TSS("dve", sl(tB), sl(x1), 0.0, ALU.max, ["x1"], ["tB"])
                    TT("dve", sl(dtv), sl(tA), sl(tB), ALU.add, ["tA", "tB"], ["dtv"])
                    TT("dve", sl(av), sl(dtv), bc(arow), ALU.mult, ["dtv", "arow"], ["av"])
                    CP("dve", sl(a_hi), sl(av), ["av"], ["a_hi"])
                    TT("dve", sl(a_lo), sl(av), sl(a_hi), ALU.subtract, ["av", "a_hi"], ["a_lo"])
                    hd = lambda ap: ap.rearrange("p (h d) -> p h d", d=64)

                    def front_pre(c):
                        cp = c % 2
                        cs = c * Qc
                        xdt, xe, btok, acs, tmd, te, etot, cbm = xdt2[cp], xe2[cp], btok2[cp], acs2[cp], tmd2[cp], te2[cp], etot2[cp], cbm2[cp]
                        k = lambda n: (n, cp)
                        for mq in range(2):
                            for mi in range(4):
                                TR(tpp[:Qc, mq * 512 + mi * 128:mq * 512 + (mi + 1) * 128], xc[:, mq * 4 + mi, cs:cs + Qc], ident_b[:],
                                   [("xc", mq * 4 + mi)], ["tp"])
                            TT("dve", hd(xdt[:Qc, mq * 512:(mq + 1) * 512]), hd(tpp[:Qc, mq * 512:(mq + 1) * 512]),
                               dtv[:Qc, c, mq * 8:(mq + 1) * 8].unsqueeze(2).to_broadcast([Qc, 8, 64]), ALU.mult, ["tp", "dtv"], [k("xdt%d" % mq)])
                        for g in range(4):
                            TR(tpp[:Qc, g * 128:(g + 1) * 128], xc[:, 8 + g, cs:cs + Qc], ident_b[:], [("xc", 8 + g)], ["tp"])
                        CP("dve", btok[:Qc, :], tpp[:Qc, 0:512], ["tp"], [k("btok")])
                        for i, aa in enumerate((a_hi, a_lo)):
                            MM(smallp[:Qc, 0:16], tri_b[:Qc, :Qc], aa[:Qc, c, :], i == 0, i == 1, ["a_hi", "a_lo", "tri_b"], ["small"])
                        for i, aa in enumerate((a_hi, a_lo)):
                            MM(smallp[:, 16:32], ones_b[:Qc, :], aa[:Qc, c, :], i == 0, i == 1, ["a_hi", "a_lo"], ["small"])
                        CP("act", acs[:Qc, :], smallp[:Qc, 0:16], ["small"], [k("acs")])
                        ACT(etot[:], smallp[:, 16:32], AF.Exp, ["small"], [k("etot")])
                        TT("dve", tmd[:Qc, :], smallp[:Qc, 16:32], acs[:Qc, :], ALU.subtract, ["small", k("acs"), k("etot")], [k("tmd")])
                        ACT(te[:Qc, :], tmd[:Qc, :], AF.Exp, [k("tmd")], [k("te")])
                        TT("pool", hd(xe[:Qc, :]), hd(xdt[:Qc, :]), te[:Qc, :].unsqueeze(2).to_broadcast([Qc, 16, 64]), ALU.mult,
                           [k("xdt0"), k("xdt1"), k("te")], [k("xe")])
                        for g in range(4):
                            MM(cbp[:Qc, g * 128:g * 128 + Qc], xc[:, 8 + g, cs:cs + Qc], xc[:, 12 + g, cs:cs + Qc], True, True,
                               [("xc", 8 + g), ("xc", 12 + g)], ["cbp"])
                        TT("dve", cbm[:Qc, :, :Qc], cbp[:Qc, :].rearrange("p (g l) -> p g l", g=4)[:, :, :Qc],
                           tri_b[:Qc, :Qc].unsqueeze(1).to_broadcast([Qc, 4, Qc]), ALU.mult, ["cbp", "tri_b"], [k("cbm")])

                    def front_grp(c, g):
                        cp = c % 2
                        cs = c * Qc
                        par = g % 2
                        acs, cbm = acs2[cp], cbm2[cp]
                        k = lambda n: (n, cp)
                        ab = acsb[par]
                        abv = ab[:].rearrange("p (h l) -> p h l", h=4)
                        for hh in range(4):
                            h = g * 4 + hh
                            for i, aa in enumerate((a_hi, a_lo)):
                                MM(abv[:, hh, :Qc], aa[:Qc, c, h:h + 1].to_broadcast([Qc, 128]), tri_b[:Qc, :Qc], i == 0, i == 1,
                                   ["a_hi", "a_lo"], [("acsb", par)])
                        Ev = E[par][:].rearrange("p (h l) -> p h l", h=4)
                        ACT(Ev[:, :, :Qc], abv[:, :, :Qc], AF.Exp, [("acsb", par)], [("E", par)])
                        xv = arg[par][:].rearrange("p (h l) -> p h l", h=4)
                        TT("dve", xv[:Qc, :, :Qc], abv[:Qc, :, :Qc], acs[:Qc, g * 4:(g + 1) * 4].unsqueeze(2).to_broadcast([Qc, 4, Qc]), ALU.subtract,
                           [("acsb", par), k("acs"), ("E", par)], [("arg", par)])
                        dv = dec[par][:].rearrange("p (h l) -> p h l", h=4)
                        ACT(dv[:Qc, :, :Qc], xv[:Qc, :, :Qc], AF.Exp, [("arg", par)], [("dec", par)])
                        Mv = Mh4[cp][g][:].rearrange("p (h l) -> p h l", h=4)
                        STT(Mv[:Qc, :, :Qc], dv[:Qc, :, :Qc], 1.0, cbm[:Qc, g, :Qc].unsqueeze(1).to_broadcast([Qc, 4, Qc]), ALU.min, ALU.mult,
                            [("dec", par), k("cbm")], [("Mh", cp, g)])
                        Cv = Cs4[cp][g][:].rearrange("p (h l) -> p h l", h=4)
                        TT("pool", Cv[:, :, :Qc], xc[:, 12 + g, cs:cs + Qc].unsqueeze(1).to_broadcast([128, 4, Qc]), Ev[:, :, :Qc], ALU.mult,
                           [("xc", 12 + g), ("E", par)], [("Cs", cp, g)])

                    def back_grp(c, g):
                        cp = c % 2
                        cs = c * Qc
                        par = g % 2
                        xdt = xdt2[cp]
                        k = lambda n: (n, cp)
                        Mv = Mh4[cp][g][:].rearrange("p (h l) -> p h l", h=4)
                        Cv = Cs4[cp][g][:].rearrange("p (h l) -> p h l", h=4)
                        yp = yps[par]
                        ykey = ("yps", par)
                        for hh in range(4):
                            h = g * 4 + hh
                            o = yp[(hh % 2) * 64:(hh % 2 + 1) * 64, (hh // 2) * 128:(hh // 2) * 128 + Qc]
                            MM(o, xdt[:Qc, h * 64:(h + 1) * 64], Mv[:Qc, hh, :Qc], True, False, [k("xdt%d" % (h // 8)), ("Mh", cp, g)], [ykey])
                            MM(o, hbf[:, h * 64:(h + 1) * 64], Cv[:, hh, :Qc], False, True, ["hbf", ("Cs", cp, g)], [ykey])
                        for mm_ in range(2):
                            m = 2 * g + mm_
                            STT(yg[:, m, cs:cs + Qc], xc[:, m, cs:cs + Qc], dvec[:, m:m + 1], yp[:, mm_ * 128:mm_ * 128 + Qc], ALU.mult, ALU.add,
                                [("xc", m), "dvec", ykey], ["yg"])

                    def back_post(c):
                        cp = c % 2
                        xe, btok, etot = xe2[cp], btok2[cp], etot2[cp]
                        k = lambda n: (n, cp)
                        for half in range(2):
                            for gi in range(2):
                                g = half * 2 + gi
                                MM(hps[:, gi * 256:(gi + 1) * 256], btok[:Qc, g * 128:(g + 1) * 128], xe[:Qc, g * 256:(g + 1) * 256], True, True,
                                   [k("btok"), k("xe")], ["hps"])
                            TT("dve", hd(htmp[:]), hd(hstate[:, half * 512:(half + 1) * 512]),
                               etot[:, half * 8:(half + 1) * 8].unsqueeze(2).to_broadcast([128, 8, 64]), ALU.mult, ["hstate", k("etot")], ["htmp"])
                            TT("dve", hstate[:, half * 512:(half + 1) * 512], htmp[:], hps[:], ALU.add, ["htmp", "hps"], ["hstate"])
                        CP("act", hbf[:], hstate[:], ["hstate"], ["hbf"])

                    front_pre(0)
                    for g in range(4):
                        front_grp(0, g)
                    for c in range(nck):
                        more = c + 1 < nck
                        if more:
                            front_pre(c + 1)
                        for g in range(4):
                            if more:
                                front_grp(c + 1, g)
                            back_grp(c, g)
                        back_post(c)
                    TT("dve", yg[:, :, :W], yg[:, :, :W], sza[:, :, :W], ALU.mult, ["yg", "sza"], ["yg"])
                    ACT(sq[:, :, :W], yg[:, :, :W], AF.Square, ["yg"], ["n2_sq"])
                    nbanks = [(cbp, "cbp"), (acsb[0], ("acsb", 0)), (acsb[1], ("acsb", 1)), (hps, "hps")]
                    for gg in range(4):
                        bnk, bkey = nbanks[gg]
                        for i in range(2):
                            MM(bnk[:, :W], ones_b[:], sq[:, 2 * gg + i, :W], i == 0, i == 1, ["n2_sq"], [bkey])
                    for gg in range(4):
                        bnk, bkey = nbanks[gg]
                        ACT(rs4[gg][:, :W], bnk[:, :W], AF.Ln, [bkey], [("rs4", gg)], bias=EPS, scale=1.0 / 256)
                    for gg in range(4):
                        ACT(rs4[gg][:, :W], rs4[gg][:, :W], AF.Exp, [("rs4", gg)], [("rs4", gg)], scale=-0.5)
                    for gg in range(4):
                        for m in (2 * gg, 2 * gg + 1):
                            STT(ya[:, m, :W], yg[:, m, :W], ng[:, m:m + 1], rs4[gg][:, :W], ALU.mult, ALU.mult, ["yg", "ng", ("rs4", gg)], ["ya"])
                    DMA(YA[:, :, t0:t0 + W].rearrange("m p t -> p m t"), ya[:, :, :W], ["ya"], [("YA", t0)])
                    if last:
                        for k in range(3):
                            DMA(conv_o[l, seq, k].rearrange("(m p) -> p m", p=128), tails[:, :, k], ["tails"], [])
                        for m in range(8):
                            TR(hps[:, (m % 4) * 128:(m % 4 + 1) * 128], hstate[:, m * 128:(m + 1) * 128], ident_f[:], ["hstate"], ["hps"])
                            if m % 4 == 3:
                                hh = m // 4
                                CP("dve", hio[:, hh * 4:(hh + 1) * 4, :], hps[:].rearrange("p (m n) -> p m n", n=128), ["hps"], ["hio"])
                        DMA(ssd_o[l, seq].rearrange("(m h) p n -> (h p) m n", h=2), hio[:], ["hio"], [])
            S.barrier()

        def phase3(l):
            with ExitStack() as ph:
                W_sb = sb(ph, "p3_W", [128, 64, 128], BF16)
                T0_sb = sb(ph, "p3_T0", [128, 64, 128], BF16)
                Cm_sb = sb(ph, "p3_Cm", [128, 32, 2, 128], BF16)
                L8 = sb(ph, "p3_L8", [128, 4, 32], F32)
                Hcar = sb(ph, "p3_Hcar", [128, 2, 32], F32)
                pa = ps(ph, "p3_pa", [128, 512])
                pb = ps(ph, "p3_pb", [128, 512])
                pc = ps(ph, "p3_pc", [128, 512])
                pd = ps(ph, "p3_pd", [128, 512])
                with ExitStack() as pp:
                    f64 = lambda nm, shp, dt=F32: sb(pp, nm, shp, dt)
                    lamr, lami, dtg, zr, zi = [f64("q_" + n, [64, 64]) for n in ("lamr", "lami", "dtg", "zr", "zi")]
                    kr, ki, t1, t2, den = [f64("q_" + n, [64, 64]) for n in ("kr", "ki", "t1", "t2", "den")]
                    erow_i = f64("q_erowi", [64, 25], I32)
                    erow = f64("q_erow", [64, 25])
                    ang, mag, s4, s2, Pr, Pi = [f64("q_" + n, [64, 64, 25]) for n in ("ang", "mag", "s4", "s2", "Pr", "Pi")]
                    kqi = f64("q_kqi", [64, 64, 25], I32)
                    PKr, PKi, u1 = [f64("q_" + n, [64, 64, 8]) for n in ("PKr", "PKi", "u1")]
                    Btr, Bti, Ctr, Cti = [f64("q_" + n, [64, 64, 16]) for n in ("Btr", "Bti", "Ctr", "Cti")]
                    Cn = sb(pp, "q_Cn", [128, 8, 64], F32)
                    Wtr, Wti, v1, v2, Cxr, Cxn = [f64("q_" + n, [64, 8, 8, 16]) for n in ("Wtr", "Wti", "v1", "v2", "Cxr", "Cxn")]

                    DMA(lamr[:], Wd["lam_re"][l].rearrange("g p -> p g"), [], ["lamr"])
                    DMA(lami[:], Wd["lam_im"][l].rearrange("g p -> p g"), [], ["lami"])
                    DMA(dtg[:], Wd["log_dt"][l].partition_broadcast(64), [], ["dtg"])
                    DMA(Btr[:], Wd["b_re"][l].rearrange("g p c -> p g c"), [], ["Btr"])
                    DMA(Bti[:], Wd["b_im"][l].rearrange("g p c -> p g c"), [], ["Bti"])
                    for (src, dst, nm) in ((Wd["c_re"], Ctr, "Ctr"), (Wd["c_im"], Cti, "Cti")):
                        DMA(Cn[:], src[l].rearrange("(j g) c p -> (g c) j p", g=8), [], ["Cn"])
                        for j in range(8):
                            TR(pa[:64, (j % 4) * 128:(j % 4 + 1) * 128], Cn[:, j, :], ident_f[:], ["Cn"], ["pa"])
                            if j % 4 == 3:
                                jj = j // 4
                                CP("dve", dst[:, jj * 32:(jj + 1) * 32, :], pa[:64, :].rearrange("p (g c) -> p g c", c=16), ["pa"], [nm])
                    ACT(dtg[:], dtg[:], AF.Exp, ["dtg"], ["dtg"])
                    TT("dve", zr[:], lamr[:], dtg[:], ALU.mult, ["lamr", "dtg"], ["zr"])
                    TT("dve", zi[:], lami[:], dtg[:], ALU.mult, ["lami", "dtg"], ["zi"])
                    for (a, b_, pat, base) in ((0, 8, [[-1, 8]], 7), (8, 16, [[1, 8]], 1), (16, 24, [[1, 8]], -7), (24, 25, [[1, 1]], 8)):
                        oap = erow_i[:, a:b_]
                        S.op("pool", lambda e, oap=oap, pat=pat, base=base: e.iota(out=oap, pattern=pat, base=base, channel_multiplier=0), (), ["erow_i"])
                    CP("dve", erow[:], erow_i[:], ["erow_i"], ["erow"])
                    ebc = erow[:, :].unsqueeze(1).to_broadcast([64, 64, 25])
                    gbc = lambda tl: tl[:, :].unsqueeze(2).to_broadcast([64, 64, 25])
                    fl = lambda tl: tl[:].rearrange("p g m -> p (g m)")
                    TT("dve", ang[:], gbc(zi), ebc, ALU.mult, ["zi", "erow"], ["ang"])
                    TT("dve", mag[:], gbc(zr), ebc, ALU.mult, ["zr", "erow"], ["mag"])
                    ACT(mag[:], mag[:], AF.Exp, ["mag"], ["mag"])
                    TSS("dve", s4[:], ang[:], 1.0 / (2 * math.pi), ALU.mult, ["ang"], ["s4"])
                    CP("dve", kqi[:], s4[:], ["s4"], ["kqi"])
                    CP("dve", s4[:], kqi[:], ["kqi"], ["s4"])
                    STT(fl(ang), fl(s4), -2 * math.pi, fl(ang), ALU.mult, ALU.add, ["s4", "ang"], ["ang"])
                    ACT(s4[:], ang[:], AF.Sin, ["ang"], ["s4"], scale=0.25)
                    ACT(s2[:], ang[:], AF.Sin, ["ang"], ["s2"], scale=0.5)
                    TT("dve", s4[:], s4[:], s4[:], ALU.mult, ["s4"], ["s4"])
                    TS("dve", s4[:], s4[:], -2.0, 1.0, ALU.mult, ALU.add, ["s4"], ["s4"])
                    TT("dve", Pi[:], s2[:], s4[:], ALU.mult, ["s2", "s4"], ["Pi"])
                    STT(fl(Pi), fl(Pi), 2.0, fl(mag), ALU.mult, ALU.mult, ["Pi", "mag"], ["Pi"])
                    TT("dve", s2[:], s2[:], s2[:], ALU.mult, ["s2"], ["s2"])
                    TS("dve", s2[:], s2[:], -2.0, 1.0, ALU.mult, ALU.add, ["s2"], ["s2"])
                    TT("dve", Pr[:], s2[:], mag[:], ALU.mult, ["s2", "mag"], ["Pr"])
                    TSS("dve", t1[:], Pr[:, :, 6], -1.0, ALU.add, ["Pr"], ["t1"])
                    TT("dve", den[:], lamr[:], lamr[:], ALU.mult, ["lamr"], ["den"])
                    TT("dve", t2[:], lami[:], lami[:], ALU.mult, ["lami"], ["t2"])
                    TT("dve", den[:], den[:], t2[:], ALU.add, ["den", "t2"], ["den"])
                    S.op("dve", lambda e: e.reciprocal(out=den[:], in_=den[:]), ["den"], ["den"])
                    TT("dve", kr[:], t1[:], lamr[:], ALU.mult, ["t1", "lamr"], ["kr"])
                    TT("dve", t2[:], Pi[:, :, 6], lami[:], ALU.mult, ["Pi", "lami"], ["t2"])
                    TT("dve", kr[:], kr[:], t2[:], ALU.add, ["kr", "t2"], ["kr"])
                    TT("dve", kr[:], kr[:], den[:], ALU.mult, ["kr", "den"], ["kr"])
                    TT("dve", ki[:], Pi[:, :, 6], lamr[:], ALU.mult, ["Pi", "lamr"], ["ki"])
                    TT("dve", t2[:], t1[:], lami[:], ALU.mult, ["t1", "lami"], ["t2"])
                    TT("dve", ki[:], ki[:], t2[:], ALU.subtract, ["ki", "t2"], ["ki"])
                    TT("dve", ki[:], ki[:], den[:], ALU.mult, ["ki", "den"], ["ki"])
                    k8 = lambda tl: tl[:, :].unsqueeze(2).to_broadcast([64, 64, 8])
                    TT("dve", PKr[:], Pr[:, :, 0:8], k8(kr), ALU.mult, ["Pr", "kr"], ["PKr"])
                    TT("dve", u1[:], Pi[:, :, 0:8], k8(ki), ALU.mult, ["Pi", "ki"], ["u1"])
                    TT("dve", PKr[:], PKr[:], u1[:], ALU.subtract, ["PKr", "u1"], ["PKr"])
                    TT("dve", PKi[:], Pr[:, :, 0:8], k8(ki), ALU.mult, ["Pr", "ki"], ["PKi"])
                    TT("dve", u1[:], Pi[:, :, 0:8], k8(kr), ALU.mult, ["Pi", "kr"], ["u1"])
                    TT("dve", PKi[:], PKi[:], u1[:], ALU.add, ["PKi", "u1"], ["PKi"])
                    for (ti, src, sc) in ((0, Pr, 1.0), (1, Pi, 1.0), (2, Pi, -1.0), (3, mag, 1.0)):
                        for gh in range(2):
                            ACT(L8[gh * 64:(gh + 1) * 64, ti, :].rearrange("p (g j) -> p g j", j=8),
                                src[:, :, 24].rearrange("p (j g) -> p g j", g=8)[:, gh * 4:(gh + 1) * 4, :], AF.Copy, ["Pr", "Pi", "mag"], ["L8"], scale=sc)

                    def cmul(outr, outi, PA_r, PA_i, Xr, Xi, jb, key, rd, neg):
                        pbc = lambda P_: P_.unsqueeze(3).to_broadcast([64, 8, 8, 16])
                        xbc = lambda X_: X_[:, jb * 8:(jb + 1) * 8, :].unsqueeze(2).to_broadcast([64, 8, 8, 16])
                        TT("dve", outr[:], pbc(PA_r), xbc(Xr), ALU.mult, rd, [key + "r"])
                        TT("pool", v1[:], pbc(PA_i), xbc(Xi), ALU.mult, rd, ["v1"])
                        TT("dve", outr[:], outr[:], v1[:], ALU.subtract, [key + "r", "v1"], [key + "r"])
                        TT("dve", outi[:], pbc(PA_r), xbc(Xi), ALU.mult, rd, [key + "i"])
                        TT("pool", v2[:], pbc(PA_i), xbc(Xr), ALU.mult, rd, ["v2"])
                        TT("dve", outi[:], outi[:], v2[:], ALU.add, [key + "i", "v2"], [key + "i"])
                        if neg:
                            TSS("dve", outi[:], outi[:], -1.0, ALU.mult, [key + "i"], [key + "i"])

                    base_rd = ["Pr", "Pi", "PKr", "PKi", "Btr", "Bti", "Ctr", "Cti"]
                    sc_ = lambda ap: ap.rearrange("p s c -> p (s c)")
                    for jb in range(8):
                        gs = slice(jb * 8, (jb + 1) * 8)
                        cmul(Wtr, Wti, PKr[:, gs, :], PKi[:, gs, :], Btr, Bti, jb, "Wt", base_rd, False)
                        for (ri, src, nm) in ((0, Wtr, "Wtr"), (1, Wti, "Wti")):
                            for g8 in range(8):
                                bank, bk = (pa, "pa") if g8 < 4 else (pb, "pb")
                                TR(bank[:, (g8 % 4) * 64:(g8 % 4 + 1) * 64], sc_(src[:, g8, :, :]), ident_f[:64, :64], [nm], [bk])
                            for hb, (bank, bk) in enumerate(((pa, "pa"), (pb, "pb"))):
                                ACT(W_sb[:, :, ri * 64:(ri + 1) * 64].rearrange("p (g j) x -> p g j x", j=8)[:, hb * 4:(hb + 1) * 4, jb, :],
                                    bank[:, 0:256].rearrange("p (g x) -> p g x", x=64), AF.Copy, [bk], ["W_sb"])
                        cmul(Cxr, Cxn, Pr[:, gs, 16:24], Pi[:, gs, 16:24], Ctr, Cti, jb, "Cx", base_rd, True)
                        for g8 in range(8):
                            bank, bk = (pc, "pc") if g8 < 4 else (pd, "pd")
                            o = bank[:, (g8 % 4) * 128:(g8 % 4 + 1) * 128]
                            MM(o, sc_(Wtr[:, g8, :, :]), sc_(Cxr[:, g8, :, :]), True, False, ["Wtr", "Cxr"], [bk])
                            MM(o, sc_(Wti[:, g8, :, :]), sc_(Cxn[:, g8, :, :]), False, True, ["Wti", "Cxi"], [bk])
                        for hb, (bank, bk) in enumerate(((pc, "pc"), (pd, "pd"))):
                            TT("dve", T0_sb[:].rearrange("p (g j) x -> p g j x", j=8)[:, hb * 4:(hb + 1) * 4, jb, :],
                               bank[:].rearrange("p (g x) -> p g x", x=128), bmask[:].unsqueeze(1).to_broadcast([128, 4, 128]), ALU.mult,
                               [bk, "bmask"], ["T0_sb"])
                        cmul(Cxr, Cxn, Pr[:, gs, 8:16], Pi[:, gs, 8:16], Ctr, Cti, jb, "Cx", base_rd, True)
                        for (ri, src, nm) in ((0, Cxr, "Cxr"), (1, Cxn, "Cxi")):
                            for gh in range(2):
                                ACT(Cm_sb[gh * 64:(gh + 1) * 64, :, ri, :].rearrange("p (g j) x -> p g j x", j=8)[:, :, jb, :],
                                    src[:, gh * 4:(gh + 1) * 4, :, :].rearrange("p g s c -> p g (s c)"), AF.Copy, [nm], ["Cm_sb"])
                S.barrier()
                with ExitStack() as pm:
                    Hb = sb(pm, "p3_Hb", [128, 2, 32, 65], F32)
                    U2 = [sb(pm, "p3_U%d" % i, [128, 64, 64], BF16) for i in range(2)]
                    Hbf = sb(pm, "p3_Hbf", [128, 2, 32, 64], BF16)
                    Ysth = [sb(pm, "p3_Yst%d" % i, [128, 32, 64], F32) for i in range(2)]
                    Fr = sb(pm, "p3_Fr", [128, 32, 65], F32)
                    Fi = sb(pm, "p3_Fi", [128, 32, 65], F32)
                    Rt = sb(pm, "p3_R", [128, 32, 65], F32)
                    Gir = sb(pm, "p3_Gir", [128, 32, 65], F32)
                    Gii = sb(pm, "p3_Gii", [128, 32, 65], F32)
                    Gsr = sb(pm, "p3_Gsr", [128, 32, 65], F32)
                    Gsi = sb(pm, "p3_Gsi", [128, 32, 65], F32)
                    c1 = sb(pm, "p3_c1", [128, 3, 32], F32)
                    S.op("dve", lambda e: e.reciprocal(out=c1[:, 2, :], in_=L8[:, 3, :]), ["L8"], ["c1"])
                    TT("dve", c1[:, 0, :], L8[:, 0, :], c1[:, 2, :], ALU.mult, ["L8", "c1"], ["c1"])
                    TT("dve", c1[:, 1, :], L8[:, 1, :], c1[:, 2, :], ALU.mult, ["L8", "c1"], ["c1"])
                    MSET("pool", Fr[:, :, 0:1], 1.0, ["Fr"])
                    MSET("pool", Fi[:, :, 0:1], 0.0, ["Fi"])
                    MSET("pool", Rt[:, :, 0:1], 0.0, ["Rt"])
                    MSET("pool", Gir[:], 0.0, ["Gir"])
                    MSET("pool", Gii[:], 0.0, ["Gii"])
                    CP("dve", Fr[:, :, 1], c1[:, 0, :], ["c1", "Fr"], ["Fr"])
                    CP("dve", Fi[:, :, 1], c1[:, 1, :], ["c1", "Fi"], ["Fi"])
                    CP("dve", Rt[:, :, 1:65], L8[:, 3, :].unsqueeze(2).to_broadcast([128, 32, 64]), ["L8", "Rt"], ["Rt"])
                    for k in range(6):
                        m_ = 1 << k
                        a_bc = Fr[:, :, m_:m_ + 1].to_broadcast([128, 32, m_])
                        b_bc = Fi[:, :, m_:m_ + 1].to_broadcast([128, 32, m_])
                        sr, si = Fr[:, :, 1:1 + m_], Fi[:, :, 1:1 + m_]
                        dr, di = Fr[:, :, m_ + 1:2 * m_ + 1], Fi[:, :, m_ + 1:2 * m_ + 1]
                        u_, v_ = Gsr[:, :, 0:m_], Gsi[:, :, 0:m_]
                        TT("dve", u_, sr, a_bc, ALU.mult, ["Fr"], ["Gsr"])
                        TT("dve", v_, si, b_bc, ALU.mult, ["Fi"], ["Gsi"])
                        TT("dve", dr, u_, v_, ALU.subtract, ["Gsr", "Gsi", "Fr"], ["Fr"])
                        TT("dve", u_, sr, b_bc, ALU.mult, ["Fr", "Fi"], ["Gsr"])
                        TT("dve", v_, si, a_bc, ALU.mult, ["Fr", "Fi"], ["Gsi"])
                        TT("dve", di, u_, v_, ALU.add, ["Gsr", "Gsi", "Fi"], ["Fi"])
                    lbanks = [pa, pb]
                    ybanks = [pc, pd]
                    def u_load(ti):
                        (t0_, W_, _, _, _) = TILES[ti]
                        DMA(U2[ti % 2][:, :, :W_ // 8], UD[:, :, :, :, t0_ // 8:(t0_ + W_) // 8].rearrange("s c g j n -> (s c) (g j) n"),
                            [("UD", t0_)], [("U_sb", ti % 2)])

                    u_load(0)
                    for ti, (t0, W, seq, first, last) in enumerate(TILES):
                        nC = W // 8
                        n0 = t0 // 8
                        U_sb = U2[ti % 2]
                        ukey = ("U_sb", ti % 2)
                        if ti + 1 < len(TILES):
                            u_load(ti + 1)
                        if first:
                            if seq < 2:
                                MSET("pool", Hb[:, :, :, 0:1], 0.0, ["Hb"])
                            else:
                                for (ri, src) in ((0, s5r0), (1, s5i0)):
                                    for g8 in range(8):
                                        gh, g4 = g8 // 4, g8 % 4
                                        DMA(Hcar[gh * 64:(gh + 1) * 64, ri, g4 * 8:(g4 + 1) * 8],
                                            src[l].rearrange("(j g) p -> p g j", g=8)[:, g8, :], [], ["Hcar"])
                                CP("dve", Hb[:, :, :, 0], Hcar[:], ["Hcar"], ["Hb"])
                        for gl4 in range(8):
                            bank = lbanks[gl4 % 2]
                            bk = ("lb", gl4 % 2)
                            bv = bank[:].rearrange("p (r g n) -> p r g n", r=2, g=4)
                            for gh in range(2):
                                for gi in range(4):
                                    gidx = gh * 32 + gl4 * 4 + gi
                                    for ri in range(2):
                                        MM(bv[gh * 64:(gh + 1) * 64, ri, gi, :nC], W_sb[:, gidx, ri * 64:(ri + 1) * 64], U_sb[:, gidx, :nC], True, True,
                                           [ukey], [bk])
                            CP("act", Hb[:, :, gl4 * 4:(gl4 + 1) * 4, 1:1 + nC], bv[:, :, :, :nC], [bk], ["Hb"])
                        n1 = nC + 1
                        Lr_, Li_ = Hb[:, 0, :, 1:n1], Hb[:, 1, :, 1:n1]
                        Frs, Fis = Fr[:, :, 1:n1], Fi[:, :, 1:n1]
                        CP("dve", Gir[:, :, 0], Hb[:, 0, :, 0], ["Hb"], ["Gir"])
                        CP("dve", Gii[:, :, 0], Hb[:, 1, :, 0], ["Hb"], ["Gii"])
                        TT("dve", Gsr[:, :, 1:n1], Lr_, Frs, ALU.mult, ["Hb", "Fr"], ["Gsr"])
                        TT("dve", Gsi[:, :, 1:n1], Li_, Fis, ALU.mult, ["Hb", "Fi"], ["Gsi"])
                        TT("dve", Gir[:, :, 1:n1], Gsr[:, :, 1:n1], Gsi[:, :, 1:n1], ALU.add, ["Gsr", "Gsi"], ["Gir"])
                        TT("dve", Gsr[:, :, 1:n1], Li_, Frs, ALU.mult, ["Hb", "Fr", "Gir"], ["Gsr"])
                        TT("dve", Gsi[:, :, 1:n1], Lr_, Fis, ALU.mult, ["Hb", "Fi", "Gir"], ["Gsi"])
                        TT("dve", Gii[:, :, 1:n1], Gsr[:, :, 1:n1], Gsi[:, :, 1:n1], ALU.subtract, ["Gsr", "Gsi"], ["Gii"])
                        fl_ = lambda tl: tl[:].rearrange("p g n -> p (g n)")
                        S.op("dve", lambda e, o=fl_(Gsr), d0=fl_(Rt), d1=fl_(Gir): e.tensor_tensor_scan(out=o, data0=d0, data1=d1, initial=0.0, op0=ALU.mult, op1=ALU.add),
                             ["Rt", "Gir", "Gii"], ["Gsr"])
                        S.op("dve", lambda e, o=fl_(Gsi), d0=fl_(Rt), d1=fl_(Gii): e.tensor_tensor_scan(out=o, data0=d0, data1=d1, initial=0.0, op0=ALU.mult, op1=ALU.add),
                             ["Rt", "Gii", "Gsr"], ["Gsi"])
                        Frn, Fin = Fr[:, :, 0:n1], Fi[:, :, 0:n1]
                        TT("dve", Gir[:, :, 0:n1], Gsr[:, :, 0:n1], Frn, ALU.mult, ["Gsr", "Gsi"], ["Gir"])
                        TT("dve", Gii[:, :, 0:n1], Gsi[:, :, 0:n1], Fin, ALU.mult, ["Gsr", "Gsi"], ["Gii"])
                        TT("dve", Hb[:, 0, :, 0:n1], Gir[:, :, 0:n1], Gii[:, :, 0:n1], ALU.subtract, ["Gir", "Gii", "Hb"], ["Hb"])
                        TT("dve", Gir[:, :, 0:n1], Gsr[:, :, 0:n1], Fin, ALU.mult, ["Gsr", "Gsi", "Hb"], ["Gir"])
                        TT("dve", Gii[:, :, 0:n1], Gsi[:, :, 0:n1], Frn, ALU.mult, ["Gsr", "Gsi", "Hb"], ["Gii"])
                        TT("dve", Hb[:, 1, :, 0:n1], Gir[:, :, 0:n1], Gii[:, :, 0:n1], ALU.add, ["Gir", "Gii", "Hb"], ["Hb"])
                        CP("act", Hbf[:, :, :, :nC], Hb[:, :, :, 0:nC], ["Hb"], ["Hbf"])
                        CP("dve", Hcar[:], Hb[:, :, :, nC], ["Hb"], ["Hcar"])
                        CP("dve", Hb[:, :, :, 0], Hcar[:], ["Hcar", "Hbf"], ["Hb"])
                        for g8b in range(8):
                            bank = ybanks[g8b % 2]
                            bk = ("yb", g8b % 2)
                            bv = bank[:].rearrange("p (g n) -> p g n", g=8)
                            for gi in range(8):
                                gidx = g8b * 8 + gi
                                gh, gl = gidx // 32, gidx % 32
                                MM(bv[:, gi, :nC], T0_sb[:, gidx, :], U_sb[:, gidx, :nC], True, False, [ukey], [bk])
                                for ri in range(2):
                                    MM(bv[:, gi, :nC], Cm_sb[gh * 64:(gh + 1) * 64, gl, ri, :], Hbf[gh * 64:(gh + 1) * 64, ri, gl, :nC], False, ri == 1,
                                       ["Hbf"], [bk])
                            hb_ = g8b // 4
                            CP("act" if g8b % 2 else "dve", Ysth[hb_][:, (g8b % 4) * 8:(g8b % 4 + 1) * 8, :nC], bv[:, :, :nC], [bk], [("Yst", hb_)])
                            if g8b % 4 == 3:
                                for s_ in range(8):
                                    DMA(YD[:, hb_ * 4:(hb_ + 1) * 4, :, s_, n0:n0 + nC].rearrange("c g j n -> c (g j) n"),
                                        Ysth[hb_][s_ * 16:(s_ + 1) * 16, :, :nC], [("Yst", hb_)], [("YD", t0)])
                        if last:
                            for (ri, dst) in ((0, s5r_o), (1, s5i_o)):
                                for g8 in range(8):
                                    gh, g4 = g8 // 4, g8 % 4
                                    DMA(dst[l, seq].rearrange("(j g) p -> p g j", g=8)[:, g8, :],
                                        Hcar[gh * 64:(gh + 1) * 64, ri, g4 * 8:(g4 + 1) * 8], ["Hcar"], [])
            S.barrier()

        def phase4(l):
            with ExitStack() as ph:
                wg = sb(ph, "p4_w", [128, 8, 1024], BF16)
                ds5 = sb(ph, "p4_ds5", [128, 8], F32)
                bgl = sb(ph, "p4_bgl", [128, 8], F32)
                yp4 = [sb(ph, "p4_yp%d" % i, [128, 8, 8, 64], F32) for i in range(2)]
                ut4 = [sb(ph, "p4_ut%d" % i, [128, 8, 512], BF16) for i in range(2)]
                szb4 = [sb(ph, "p4_szb%d" % i, [128, 8, 512], BF16) for i in range(2)]
                yb = sb(ph, "p4_yb", [128, 8, 512], F32)
                gf = sb(ph, "p4_gf", [128, 8, 512], F32)
                gb = sb(ph, "p4_gb", [128, 8, 512], BF16)
                sg = [sb(ph, "p4_sg%d" % i, [128, 512], F32) for i in range(2)]
                ybst = sb(ph, "p4_ybst", [128, 8, 512], BF16)
                mm = [ps(ph, "p4_mm%d" % i, [128, 512]) for i in range(2)]
                for kt in range(8):
                    DMA(wg[:, kt, :], Wd["w_glu"][l, kt * 128:(kt + 1) * 128, :], [], ["wg"], q="pool")
                DMA(ds5[:], Wd["d_s5"][l].rearrange("(k p) -> p k", p=128), [], ["ds5"])
                DMA(bgl[:], Wd["b_glu"][l].rearrange("(k p) -> p k", p=128), [], ["bgl"])
                def p4_load(ti):
                    (t0, W, seq, first, last) = TILES[ti]
                    nC = W // 8
                    n0 = t0 // 8
                    pr = ti % 2
                    for g8 in range(8):
                        DMA(yp4[pr][g8 * 16:(g8 + 1) * 16, :, :, :nC].rearrange("c j s n -> c (j s) n"),
                            YD[:, g8, :, :, n0:n0 + nC].rearrange("c j s n -> c (j s) n"), [("YD", t0)], [("yp4", pr)])
                        DMA(ut4[pr][g8 * 16:(g8 + 1) * 16, :, :W], UT[:, :, g8, t0:t0 + W].rearrange("j c t -> c j t"), [("UT", t0)], [("ut4", pr)])
                    DMA(szb4[pr][:, :, :W], SZB[:, :, t0:t0 + W].rearrange("m p t -> p m t"), [("SZB", t0)], [("szb4", pr)])

                p4_load(0)
                for ti, (t0, W, seq, first, last) in enumerate(TILES):
                    nC = W // 8
                    n0 = t0 // 8
                    pr = ti % 2
                    if ti + 1 < len(TILES):
                        p4_load(ti + 1)
                    for j in range(8):
                        STT(yb[:, j, :W].rearrange("p (n s) -> p n s", s=8), ut4[pr][:, j, :W].rearrange("p (n s) -> p n s", s=8), ds5[:, j:j + 1],
                            yp4[pr][:, j, :, :nC].rearrange("p s n -> p n s"), ALU.mult, ALU.add, [("ut4", pr), ("yp4", pr), "ds5"], [("yb", j)])
                        ACT(gf[:, j, :W], yb[:, j, :W], AF.Gelu_apprx_tanh, [("yb", j)], [("gf", j)])
                        CP("act" if j % 2 else "dve", gb[:, j, :W], gf[:, j, :W], [("gf", j)], ["gb"])
                    for jo in range(8):
                        bank = mm[jo % 2]
                        bk = ("mm", jo % 2)
                        for ji in range(8):
                            MM(bank[:, :W], wg[:, ji, jo * 128:(jo + 1) * 128], gb[:, ji, :W], ji == 0, ji == 7, ["wg", "gb"], [bk])
                        sgt = sg[jo % 2]
                        sk = ("sg", jo % 2)
                        ACT(sgt[:, :W], bank[:, :W], AF.Sigmoid, [bk, "bgl"], [sk], bias=bgl[:, jo:jo + 1], scale=1.0)
                        TT("dve", sgt[:, :W], sgt[:, :W], gf[:, jo, :W], ALU.mult, [sk, ("gf", jo)], [sk])
                        TT("dve", ybst[:, jo, :W], sgt[:, :W], szb4[pr][:, jo, :W], ALU.mult, [sk, ("szb4", pr)], ["ybst"])
                    DMA(YB[:, :, t0:t0 + W].rearrange("m p t -> p m t"), ybst[:, :, :W], ["ybst"], [("YB", t0)])
            S.barrier()

        def phase5(l):
            with ExitStack() as ph:
                wo = sb(ph, "p5_w", [128, 16, 1024], BF16)
                y5 = sb(ph, "p5_y", [128, 16, 512], BF16)
                xT5 = sb(ph, "p5_x", [128, 8, 512], F32)
                xo = sb(ph, "p5_xo", [128, 8, 512], F32)
                mm = [ps(ph, "p5_mm%d" % i, [128, 512]) for i in range(2)]
                for kt in range(16):
                    DMA(wo[:, kt, :], Wd["w_out"][l, kt * 128:(kt + 1) * 128, :], [], ["wo"], q="pool")
                for (t0, W, seq, first, last) in TILES:
                    DMA(y5[:, 0:8, :W], YA[:, :, t0:t0 + W].rearrange("m p t -> p m t"), [("YA", t0)], ["y5a"])
                    DMA(y5[:, 8:16, :W], YB[:, :, t0:t0 + W].rearrange("m p t -> p m t"), [("YB", t0)], ["y5b"])
                    DMA(xT5[:, :, :W], xt_ap(t0, W), [("XT", t0)], ["xT5"])
                    for dm in range(8):
                        bank = mm[dm % 2]
                        bk = ("mm", dm % 2)
                        for kt in range(16):
                            MM(bank[:, :W], wo[:, kt, dm * 128:(dm + 1) * 128], y5[:, kt, :W], kt == 0, kt == 15, ["wo", "y5a", "y5b"], [bk])
                        TT("dve", xo[:, dm, :W], bank[:, :W], xT5[:, dm, :W], ALU.add, [bk, "xT5"], ["xo"])
                    DMA(xt_ap(t0, W), xo[:, :, :W], ["xo"], [("XT", t0)])
            S.barrier()

        def phase_final():
            with ExitStack() as ph:
                fg = sb(ph, "pf_g", [128, 8], F32)
                xT = sb(ph, "pf_xT", [128, 8, 512], F32)
                sq = sb(ph, "pf_sq", [128, 8, 512], BF16)
                hf = sb(ph, "pf_hf", [128, 8, 512], F32)
                rs = sb(ph, "pf_rs", [128, 512], F32)
                rs2 = sb(ph, "pf_rs2", [128, 512], F32)
                yst = sb(ph, "pf_yst", [128, 4, 1024], F32)
                msp = ps(ph, "pf_ms", [128, 512])
                tp = [ps(ph, "pf_tp%d" % i, [128, 512]) for i in range(4)]
                DMA(fg[:], Wd["final_norm_g"].rearrange("(k p) -> p k", p=128), [], ["fg"])
                for (t0, W, seq, first, last) in TILES:
                    Pb = min(128, W)
                    nb = W // Pb
                    DMA(xT[:, :, :W], xt_ap(t0, W), [("XT", t0)], ["xT"])
                    ACT(sq[:, :, :W], xT[:, :, :W], AF.Square, ["xT"], ["nf_sq"])
                    rms_rstd(msp, [sq[:, kt, :W] for kt in range(8)], W, 1.0 / 1024, rs, rs2, "nf")
                    for kt in range(8):
                        STT(hf[:, kt, :W], xT[:, kt, :W], fg[:, kt:kt + 1], rs2[:, :W], ALU.mult, ALU.mult, ["xT", "fg", "nf_rs2"], ["hf"])
                    for b in range(nb):
                        for half in range(2):
                            bank = tp[(b * 2 + half) % 4]
                            bk = ("tp", (b * 2 + half) % 4)
                            for k4 in range(4):
                                kt = half * 4 + k4
                                TR(bank[:Pb, k4 * 128:(k4 + 1) * 128], hf[:, kt, b * Pb:(b + 1) * Pb], ident_f[:], ["hf"], [bk])
                            CP("act" if half else "dve", yst[:Pb, b, half * 512:(half + 1) * 512], bank[:Pb, :], [bk], ["yst"])
                    DMA(y_out[t0:t0 + W, :].rearrange("(b p) d -> p b d", p=Pb), yst[:Pb, :nb, :], ["yst"], [])
            S.barrier()

        wst = ExitStack()
        w_cur = sb(wst, "w_in_sb", [128, 8, IN_COLS], BF16)
        load_w_in(0, w_cur)
        phase0()
        for l in range(n_layers):
            if 1 in phases:
                phase1(l, w_cur)
            wst.close()
            if 2 in phases:
                phase2(l)
            if 3 in phases:
                phase3(l)
            if 4 in phases:
                phase4(l)
            if l + 1 < n_layers:
                wst = ExitStack()
                w_cur = sb(wst, "w_in_sb", [128, 8, IN_COLS], BF16)
                load_w_in(l + 1, w_cur)
            if 5 in phases:
                phase5(l)
        if do_final:
            phase_final()
        S.finish(block)
        nc._n_instr = S.n_instr
    return nc


def make_in_maps(inputs):
    xp = np.ascontiguousarray(inputs["x_prompt"], dtype=np.float32)
    xs = np.ascontiguousarray(inputs["x_sample"], dtype=np.float32)
    maps = []
    for c in range(NCORES):
        m = {}
        m["x_core"] = np.ascontiguousarray(np.concatenate([xp[2 * c], xp[2 * c + 1], xs[c]], axis=0))
        m["conv0"] = np.ascontiguousarray(inputs["state_ssd_conv"][:, c])
        m["ssd0"] = np.ascontiguousarray(inputs["state_ssd"][:, c])
        m["s5r0"] = np.ascontiguousarray(inputs["state_s5_re"][:, c])
        m["s5i0"] = np.ascontiguousarray(inputs["state_s5_im"][:, c])
        for n in W_NAMES:
            m[n] = np.ascontiguousarray(inputs[n], dtype=np.float32)
        maps.append(m)
    return maps


_NC_CACHE = {}


def kernel(**inputs):
    inputs = {k: np.asarray(v) for k, v in inputs.items()}
    if "nc" not in _NC_CACHE:
        _NC_CACHE["nc"] = build_nc()
    nc = _NC_CACHE["nc"]
    maps = make_in_maps(inputs)
    res = run_bass_kernel_spmd(nc, maps, core_ids=list(range(NCORES)))
    R = res.results
    y_prompt = np.zeros((16, 2048, 1024), np.float32)
    y_sample = np.zeros((8, 64, 1024), np.float32)
    conv_p = np.zeros((4, 16, 3, 2048), np.float32)
    ssd_p = np.zeros((4, 16, 16, 64, 128), np.float32)
    s5r_p = np.zeros((4, 16, 64, 64), np.float32)
    s5i_p = np.zeros((4, 16, 64, 64), np.float32)
    conv_s = np.zeros((4, 8, 3, 2048), np.float32)
    ssd_s = np.zeros((4, 8, 16, 64, 128), np.float32)
    s5r_s = np.zeros((4, 8, 64, 64), np.float32)
    s5i_s = np.zeros((4, 8, 64, 64), np.float32)
    for c in range(NCORES):
        r = R[c]
        y = r["y_out"]
        y_prompt[2 * c] = y[0:2048]
        y_prompt[2 * c + 1] = y[2048:4096]
        y_sample[c] = y[4096:4160]
        for i in range(2):
            conv_p[:, 2 * c + i] = r["conv_o"][:, i]
            ssd_p[:, 2 * c + i] = r["ssd_o"][:, i]
            s5r_p[:, 2 * c + i] = r["s5r_o"][:, i]
            s5i_p[:, 2 * c + i] = r["s5i_o"][:, i]
        conv_s[:, c] = r["conv_o"][:, 2]
        ssd_s[:, c] = r["ssd_o"][:, 2]
        s5r_s[:, c] = r["s5r_o"][:, 2]
        s5i_s[:, c] = r["s5i_o"][:, 2]
    return (y_prompt, y_sample, conv_p, ssd_p, s5r_p, s5i_p, conv_s, ssd_s, s5r_s, s5i_s)
```

```python
import math
from contextlib import ExitStack

import numpy as np
import concourse.bass as bass
import concourse.mybir as mybir
from concourse.bass_utils import run_bass_kernel_spmd

F32 = mybir.dt.float32
BF16 = mybir.dt.bfloat16
I32 = mybir.dt.int32
AF = mybir.ActivationFunctionType
ALU = mybir.AluOpType

NCORES = 8
DEPTH = 4
T = 4160
NCH = T // 8
EPS = 1e-6
IN_COLS = 5136
C_ZA, C_XBC, C_DT, C_U, C_ZB = 0, 1024, 3072, 3088, 4112
TILES = [(s * 2048 + i * 512, 512, s, i == 0, i == 3) for s in range(2) for i in range(4)]
TILES.append((4096, 64, 2, True, True))
import os as _os
if _os.environ.get("K_TILES"):
    TILES = [TILES[int(i)] for i in _os.environ["K_TILES"].split(",")]
K_SKIP = set(_os.environ.get("K_SKIP", "").split(","))

ENGS = ("pe", "dve", "act", "pool", "sp")


class Sync:
    def __init__(self, nc, stack, n_dma_sems=40, same_engine_sync=True):
        self.nc = nc
        self.esem = {e: stack.enter_context(nc.semaphore("s_" + e)) for e in ENGS}
        self.cnt = {e: 0 for e in ENGS}
        self.prog = {e: [] for e in ENGS}
        self.waited = {e: {} for e in ENGS}
        self.res = {}
        self.same = same_engine_sync
        self.dsems = [stack.enter_context(nc.semaphore("s_dma%d" % i)) for i in range(n_dma_sems)]
        self.dval = [0] * n_dma_sems
        self.dnext = 0
        self.sems = {}
        for e in ENGS:
            self.sems[("e", e)] = self.esem[e]
        for i, s in enumerate(self.dsems):
            self.sems[("d", i)] = s
        self.n_instr = 0

    def _need(self, e, reads, writes):
        need = {}

        def add(ev):
            if ev is None:
                return
            k, v = ev
            if need.get(k, 0) < v:
                need[k] = v

        for r in reads:
            st = self.res.get(r)
            if st is not None:
                add(st[0])
        for w in writes:
            st = self.res.get(w)
            if st is not None:
                add(st[0])
                for ev in st[1]:
                    add(ev)
        out = []
        for k, v in need.items():
            if k == ("e", e) and (e == "pe" or not self.same):
                continue
            if self.waited[e].get(k, 0) >= v:
                continue
            self.waited[e][k] = v
            out.append((k, v))
        return out

    def _commit(self, ev, reads, writes):
        for r in reads:
            st = self.res.setdefault(r, [None, []])
            st[1].append(ev)
            if len(st[1]) > 16:
                mx = {}
                for k, v in st[1]:
                    if mx.get(k, 0) < v:
                        mx[k] = v
                st[1] = list(mx.items())
        for w in writes:
            self.res[w] = [ev, []]

    def op(self, e, fn, reads=(), writes=()):
        waits = self._need(e, reads, writes)
        self.cnt[e] += 1
        ev = (("e", e), self.cnt[e])
        sem = self.esem[e]
        sems = self.sems

        def emit(eng, waits=waits, fn=fn, sem=sem):
            for k, v in waits:
                eng.wait_ge(sems[k], v)
            fn(eng).then_inc(sem, 1)

        self.prog[e].append(emit)
        self._commit(ev, reads, writes)
        self.n_instr += 1
        return ev

    def dma(self, e, fn, reads=(), writes=()):
        i = self.dnext
        self.dnext = (self.dnext + 1) % len(self.dsems)
        k = ("d", i)
        waits = self._need(e, reads, writes)
        if self.dval[i] > 0 and self.waited[e].get(k, 0) < self.dval[i]:
            self.waited[e][k] = self.dval[i]
            waits.append((k, self.dval[i]))
        self.dval[i] += 16
        ev = (k, self.dval[i])
        sem = self.dsems[i]
        sems = self.sems

        def emit(eng, waits=waits, fn=fn, sem=sem):
            for kk, v in waits:
                eng.wait_ge(sems[kk], v)
            fn(eng).then_inc(sem, 16)

        self.prog[e].append(emit)
        self._commit(ev, reads, writes)
        self.n_instr += 1
        return ev

    def barrier(self):
        evs = [(("e", e), self.cnt[e]) for e in ENGS if self.cnt[e] > 0]
        evs += [(("d", i), v) for i, v in enumerate(self.dval) if v > 0]
        sems = self.sems
        for e in ENGS:
            waits = []
            for k, v in evs:
                if k == ("e", e):
                    continue
                if self.waited[e].get(k, 0) >= v:
                    continue
                self.waited[e][k] = v
                waits.append((k, v))

            def emit(eng, waits=waits):
                for kk, v in waits:
                    eng.wait_ge(sems[kk], v)

            self.prog[e].append(emit)
        self.res = {}

    def finish(self, block):
        self.barrier()
        prog = self.prog

        @block.tensor
        def _(eng):
            for f in prog["pe"]:
                f(eng)

        @block.vector
        def _(eng):
            for f in prog["dve"]:
                f(eng)

        @block.scalar
        def _(eng):
            for f in prog["act"]:
                f(eng)

        @block.gpsimd
        def _(eng):
            for f in prog["pool"]:
                f(eng)

        @block.sync
        def _(eng):
            for f in prog["sp"]:
                f(eng)


W_NAMES = ["norm_g", "w_in", "conv_w", "conv_b", "dt_bias", "a_log", "d_ssd", "ssd_norm_g",
           "lam_re", "lam_im", "log_dt", "b_re", "b_im", "c_re", "c_im", "d_s5", "w_glu", "b_glu",
           "w_out", "final_norm_g"]
W_SHAPES = {
    "norm_g": [4, 1024], "w_in": [4, 1024, 5136], "conv_w": [4, 4, 2048], "conv_b": [4, 2048],
    "dt_bias": [4, 16], "a_log": [4, 16], "d_ssd": [4, 16], "ssd_norm_g": [4, 1024],
    "lam_re": [4, 64, 64], "lam_im": [4, 64, 64], "log_dt": [4, 64], "b_re": [4, 64, 64, 16],
    "b_im": [4, 64, 64, 16], "c_re": [4, 64, 16, 64], "c_im": [4, 64, 16, 64], "d_s5": [4, 1024],
    "w_glu": [4, 1024, 1024], "b_glu": [4, 1024], "w_out": [4, 2048, 1024], "final_norm_g": [1024],
}


def build_nc(n_layers=DEPTH, dbg=False, phases=(1, 2, 3, 4, 5), do_final=True):
    nc = bass.Bass("TRN2", target_bir_lowering=False)
    skind = "ExternalOutput" if dbg else "Internal"

    def din(name, shape, dt=F32):
        return nc.dram_tensor(name, shape, dt, kind="ExternalInput").ap()

    def dout(name, shape, dt=F32):
        return nc.dram_tensor(name, shape, dt, kind="ExternalOutput").ap()

    def dscr(name, shape, dt=F32):
        return nc.dram_tensor(name, shape, dt, kind=skind).ap()

    x_in = din("x_core", [T, 1024])
    conv0 = din("conv0", [4, 3, 2048])
    ssd0 = din("ssd0", [4, 16, 64, 128])
    s5r0 = din("s5r0", [4, 64, 64])
    s5i0 = din("s5i0", [4, 64, 64])
    Wd = {n: din(n, W_SHAPES[n]) for n in W_NAMES}

    y_out = dout("y_out", [T, 1024])
    conv_o = dout("conv_o", [4, 3, 3, 2048])
    ssd_o = dout("ssd_o", [4, 3, 16, 64, 128])
    s5r_o = dout("s5r_o", [4, 3, 64, 64])
    s5i_o = dout("s5i_o", [4, 3, 64, 64])

    XT = dscr("XT", [8, 128, T])
    SZA = dscr("SZA", [8, 128, T], BF16)
    XBC = dscr("XBC", [16, 128, T])
    DTR = dscr("DTR", [T, 16])
    UD = dscr("UD", [8, 16, 8, 8, NCH], BF16)
    UT = dscr("UT", [8, 16, 8, T], BF16)
    SZB = dscr("SZB", [8, 128, T], BF16)
    YA = dscr("YA", [8, 128, T], BF16)
    YD = dscr("YD", [16, 8, 8 * 8, NCH]).rearrange("c g (j s) n -> c g j s n", s=8)
    YB = dscr("YB", [8, 128, T], BF16)

    with ExitStack() as st:
        S = Sync(nc, st)
        st.enter_context(nc.allow_non_contiguous_dma(reason="small strided parameter/state loads"))

        uniq = [0]

        def sb(stack, name, shape, dt):
            uniq[0] += 1
            return stack.enter_context(nc.sbuf_tensor("%s_%d" % (name, uniq[0]), shape, dt))

        def ps(stack, name, shape, dt=F32):
            uniq[0] += 1
            return stack.enter_context(nc.psum_tensor("%s_%d" % (name, uniq[0]), shape, dt))


        def TT(eng, out, in0, in1, op, reads, writes):
            return S.op(eng, lambda e: e.tensor_tensor(out=out, in0=in0, in1=in1, op=op), reads, writes)

        def TS(eng, out, in0, s1, s2, op0, op1, reads, writes):
            return S.op(eng, lambda e: e.tensor_scalar(out=out, in0=in0, scalar1=s1, scalar2=s2, op0=op0, op1=op1), reads, writes)

        def TSS(eng, out, in_, scalar, op, reads, writes):
            return S.op(eng, lambda e: e.tensor_single_scalar(out=out, in_=in_, scalar=scalar, op=op), reads, writes)

        def STT(out, in0, scalar, in1, op0, op1, reads, writes):
            return S.op("dve", lambda e: e.scalar_tensor_tensor(out=out, in0=in0, scalar=scalar, in1=in1, op0=op0, op1=op1), reads, writes)

        def ACT(out, in_, func, reads, writes, bias=None, scale=None):
            kw = {}
            if bias is not None:
                kw["bias"] = bias
            if scale is not None:
                kw["scale"] = scale
            return S.op("act", lambda e: e.activation(out=out, in_=in_, func=func, **kw), reads, writes)

        def CP(eng, out, in_, reads, writes):
            if eng == "act":
                return ACT(out, in_, AF.Copy, reads, writes)
            return S.op(eng, lambda e: e.tensor_copy(out=out, in_=in_), reads, writes)

        def MM(out, lhsT, rhs, start, stop, reads, writes):
            return S.op("pe", lambda e: e.matmul(out, lhsT=lhsT, rhs=rhs, start=start, stop=stop), reads, writes)

        def TR(out, in_, ident, reads, writes):
            return S.op("pe", lambda e: e.transpose(out=out, in_=in_, identity=ident), reads, writes)

        def DMA(out, in_, reads=(), writes=(), q="sp"):
            return S.dma(q, lambda e: e.dma_start(out=out, in_=in_), reads, writes)

        def MSET(eng, ap, val, writes):
            return S.op(eng, lambda e: e.memset(ap, val), (), writes)

        ones_f = sb(st, "ones_f", [128, 128], F32)
        ident_f = sb(st, "ident_f", [128, 128], F32)
        ident_b = sb(st, "ident_b", [128, 128], BF16)
        ones_b = sb(st, "ones_b", [128, 128], BF16)
        tri_f = sb(st, "tri_f", [128, 128], F32)
        tri_b = sb(st, "tri_b", [128, 128], BF16)
        bmask = sb(st, "bmask", [128, 128], F32)
        block = st.enter_context(nc.Block())

        S.op("pool", lambda e: e.memset(ones_f[:], 1.0), writes=["ones_f"])
        S.op("pool", lambda e: e.affine_select(out=ident_f[:], in_=ones_f[:], pattern=[[1, 128]],
                                               compare_op=ALU.is_equal, fill=0.0, base=0, channel_multiplier=-1),
             reads=["ones_f"], writes=["ident_f"])
        S.op("pool", lambda e: e.affine_select(out=tri_f[:], in_=ones_f[:], pattern=[[1, 128]],
                                               compare_op=ALU.is_ge, fill=0.0, base=0, channel_multiplier=-1),
             reads=["ones_f"], writes=["tri_f"])
        S.op("pool", lambda e: e.affine_select(out=bmask[:], in_=ones_f[:], pattern=[[16, 8], [0, 16]],
                                               compare_op=ALU.is_ge, fill=0.0, base=15, channel_multiplier=-1),
             reads=["ones_f"], writes=["bmask"])
        S.op("dve", lambda e: e.tensor_copy(out=ident_b[:], in_=ident_f[:]), reads=["ident_f"], writes=["ident_b"])
        S.op("dve", lambda e: e.tensor_copy(out=ones_b[:], in_=ones_f[:]), reads=["ones_f"], writes=["ones_b"])
        S.op("dve", lambda e: e.tensor_copy(out=tri_b[:], in_=tri_f[:]), reads=["tri_f"], writes=["tri_b"])
        S.barrier()

        def xt_ap(t0, W):
            return XT[:, :, t0:t0 + W].rearrange("k p t -> p k t")

        def rms_rstd(ph_ps, src_sq, W, scale, rs, rs2, key):
            nk = len(src_sq)
            for i, a in enumerate(src_sq):
                MM(ph_ps[:, :W], ones_b[:], a, i == 0, i == nk - 1, [key + "_sq"], [key + "_ms"])
            ACT(rs[:, :W], ph_ps[:, :W], AF.Ln, [key + "_ms"], [key + "_rs"], bias=EPS, scale=scale)
            ACT(rs2[:, :W], rs[:, :W], AF.Exp, [key + "_rs"], [key + "_rs2"], scale=-0.5)

        def phase0():
            with ExitStack() as ph:
                xs = sb(ph, "p0_xs", [128, 4, 1024], F32)
                xo = sb(ph, "p0_xo", [128, 8, 512], F32)
                tp = [ps(ph, "p0_tp%d" % i, [128, 512]) for i in range(2)]
                for (t0, W, seq, first, last) in TILES:
                    Pb = min(128, W)
                    nb = W // Pb
                    DMA(xs[:Pb, :nb, :], x_in[t0:t0 + W, :].rearrange("(b p) d -> p b d", p=Pb), [], ["xs"])
                    for kt in range(8):
                        tpk = tp[kt % 2]
                        for b in range(nb):
                            TR(tpk[:, b * Pb:(b + 1) * Pb], xs[:Pb, b, kt * 128:(kt + 1) * 128], ident_f[:Pb, :Pb], ["xs"], [("tp", kt % 2)])
                        CP("act" if kt % 2 else "dve", xo[:, kt, :W], tpk[:, :W], [("tp", kt % 2)], ["xo"])
                    DMA(xt_ap(t0, W), xo[:, :, :W], ["xo"], [("XT", t0)])
            S.barrier()

        def load_w_in(l, w_sb):
            for kt in range(8):
                for (c0, cw_) in ((0, 2048), (2048, 2048), (4096, 1040)):
                    DMA(w_sb[:, kt, c0:c0 + cw_], Wd["w_in"][l, kt * 128:(kt + 1) * 128, c0:c0 + cw_], [], ["w"], q="pool")

        def phase1(l, w_sb):
            with ExitStack() as ph:
                g1 = sb(ph, "p1_g", [128, 8], F32)
                xT = sb(ph, "p1_xT", [128, 8, 512], F32)
                sq = sb(ph, "p1_sq", [128, 8, 512], BF16)
                hT = sb(ph, "p1_hT", [128, 8, 512], BF16)
                rs = sb(ph, "p1_rs", [128, 512], F32)
                rs2 = sb(ph, "p1_rs2", [128, 512], F32)
                st_za = sb(ph, "p1_za", [128, 8, 512], BF16)
                st_xbc = [sb(ph, "p1_xbc%d" % i, [128, 2, 512], F32) for i in range(2)]
                st_us = sb(ph, "p1_us", [128, 8, 8, 64], BF16)
                st_ut = sb(ph, "p1_ut", [128, 8, 512], BF16)
                us32 = [sb(ph, "p1_us32_%d" % i, [128, 8, 64], F32) for i in range(2)]
                st_zb = sb(ph, "p1_zb", [128, 8, 512], BF16)
                st_dt = sb(ph, "p1_dt", [128, 4, 16], F32)
                mm = [ps(ph, "p1_mm%d" % i, [128, 512]) for i in range(4)]
                msp = ps(ph, "p1_ms", [128, 512])
                dtp = ps(ph, "p1_dtp", [128, 4, 16])
                DMA(g1[:], Wd["norm_g"][l].rearrange("(k p) -> p k", p=128), [], ["g1"])
                for kt in range(0 if "perm" in K_SKIP else 8):
                    tmpu = st_ut[:, (kt % 2) * 2:(kt % 2) * 2 + 2, :].rearrange("p a t -> p (a t)")
                    CP("dve", tmpu, w_sb[:, kt, C_U:C_U + 1024], ["w"], [("tmpu", kt % 2), "st_ut"])
                    CP("act", w_sb[:, kt, C_U:C_U + 1024].rearrange("p (j c g) -> p j c g", c=16, g=8),
                       tmpu.rearrange("p (j g c) -> p j c g", g=8, c=16), [("tmpu", kt % 2), "st_ut"], ["w"])
                mcount = [0]
                def pro_load(ti):
                    (t0_, W_, _, _, _) = TILES[ti]
                    DMA(xT[:, :, :W_], xt_ap(t0_, W_), [("XT", t0_)], ["xT"])

                def pro_sq(ti):
                    W_ = TILES[ti][1]
                    ACT(sq[:, :, :W_], xT[:, :, :W_], AF.Square, ["xT"], ["n1_sq"])

                def pro_ms(ti):
                    W_ = TILES[ti][1]
                    rms_rstd(msp, [sq[:, kt, :W_] for kt in range(8)], W_, 1.0 / 1024, rs, rs2, "n1")

                pro_load(0)
                pro_sq(0)
                pro_ms(0)
                for ti, (t0, W, seq, first, last) in enumerate(TILES):
                    Pb = min(128, W)
                    nb = W // Pb
                    nC = W // 8
                    n0 = t0 // 8
                    nxt = ti + 1 if ti + 1 < len(TILES) else None
                    for kt in range(8):
                        STT(hT[:, kt, :W], xT[:, kt, :W], g1[:, kt:kt + 1], rs2[:, :W], ALU.mult, ALU.mult, ["xT", "g1", "n1_rs2"], ["hT"])
                    if nxt is not None:
                        pro_load(nxt)

                    def mtile(lhs_fn, W=W):
                        bank = mm[mcount[0] % 4]
                        key = ("mm", mcount[0] % 4)
                        mcount[0] += 1
                        for kt in range(8):
                            MM(bank[:, :W], lhs_fn(kt), hT[:, kt, :W], kt == 0, kt == 7, ["w", "hT"], [key])
                        return bank, key

                    for m in range(0 if "za" in K_SKIP else 8):
                        bank, key = mtile(lambda kt, m=m: w_sb[:, kt, C_ZA + m * 128:C_ZA + (m + 1) * 128])
                        ACT(st_za[:, m, :W], bank[:, :W], AF.Silu, [key], ["st_za"])
                    if "za" not in K_SKIP:
                        DMA(SZA[:, :, t0:t0 + W].rearrange("m p t -> p m t"), st_za[:, :, :W], ["st_za"], [("SZA", t0)])
                    if nxt is not None:
                        pro_sq(nxt)
                    for m in range(0 if "xbc" in K_SKIP else 16):
                        bank, key = mtile(lambda kt, m=m: w_sb[:, kt, C_XBC + m * 128:C_XBC + (m + 1) * 128])
                        pr = m // 2
                        stx = st_xbc[pr % 2]
                        skey = ("st_xbc", pr % 2)
                        CP("act" if m % 2 == 0 else "dve", stx[:, m % 2, :W], bank[:, :W], [key], [skey])
                        if m % 2 == 1:
                            DMA(XBC[pr * 2:(pr + 1) * 2, :, t0:t0 + W].rearrange("m p t -> p m t"), stx[:, :, :W], [skey], [("XBC", t0, pr)])
                    if nxt is not None:
                        pro_ms(nxt)
                    for j in range(0 if "u" in K_SKIP else 8):
                        bank, key = mtile(lambda kt, j=j: w_sb[:, kt, C_U + j * 128:C_U + (j + 1) * 128])
                        u32 = us32[j % 2]
                        ACT(u32[:, :, :nC].rearrange("p s n -> p n s"), bank[:, :W].rearrange("p (n s) -> p n s", s=8), AF.Copy, [key], [("us32", j % 2), ("brd", key)])
                        CP("pool", st_us[:, j, :, :nC], u32[:, :, :nC], [("us32", j % 2)], ["st_us"])
                        CP("dve", st_ut[:, j, :W], bank[:, :W], [key, ("brd", key)], ["st_ut"])
                    for j in range(0 if "ud" in K_SKIP else 8):
                        DMA(UD[:, :, :, j, n0:n0 + nC].rearrange("s c g n -> (c g) s n"), st_us[:, j, :, :nC], ["st_us"], [("UD", t0)])
                    if "u" not in K_SKIP and "utdma" not in K_SKIP:
                        DMA(UT[:, :, :, t0:t0 + W].rearrange("j c g t -> (c g) j t"), st_ut[:, :, :W], ["st_ut"], [("UT", t0)])
                    for m in range(0 if "zb" in K_SKIP else 8):
                        bank, key = mtile(lambda kt, m=m: w_sb[:, kt, C_ZB + m * 128:C_ZB + (m + 1) * 128])
                        ACT(st_zb[:, m, :W], bank[:, :W], AF.Silu, [key], ["st_zb"])
                    if "zb" not in K_SKIP:
                        DMA(SZB[:, :, t0:t0 + W].rearrange("m p t -> p m t"), st_zb[:, :, :W], ["st_zb"], [("SZB", t0)])
                    for b in range(0 if "dt" in K_SKIP else nb):
                        for kt in range(8):
                            MM(dtp[:Pb, b, :], hT[:, kt, b * Pb:(b + 1) * Pb], w_sb[:, kt, C_DT:C_DT + 16], kt == 0, kt == 7, ["w", "hT"], ["dtp"])
                    if "dt" not in K_SKIP:
                        ACT(st_dt[:Pb, :nb, :], dtp[:Pb, :nb, :], AF.Copy, ["dtp"], ["st_dt"])
                        DMA(DTR[t0:t0 + W, :].rearrange("(b p) h -> p b h", p=Pb), st_dt[:Pb, :nb, :], ["st_dt"], [("DTR", t0)])
            S.barrier()

        def phase2(l):
            with ExitStack() as ph:
                cw = sb(ph, "p2_cw", [128, 16, 4], F32)
                cbias = sb(ph, "p2_cb", [128, 16], F32)
                dtb = sb(ph, "p2_dtb", [128, 16], F32)
                arow = sb(ph, "p2_arow", [128, 16], F32)
                dvec = sb(ph, "p2_dvec", [128, 8], F32)
                ng = sb(ph, "p2_ng", [128, 8], F32)
                tails = sb(ph, "p2_tails", [128, 16, 3], F32)
                hstate = sb(ph, "p2_hst", [128, 1024], F32)
                hbf = sb(ph, "p2_hbf", [128, 1024], BF16)
                hio = sb(ph, "p2_hio", [128, 8, 128], F32)
                raw = sb(ph, "p2_raw", [128, 4, 515], F32)
                rawb = sb(ph, "p2_rawb", [128, 4, 515], BF16)
                dw = sb(ph, "p2_dw", [128, 16, 4, 128], BF16)
                xc2 = [sb(ph, "p2_xc%d" % i, [128, 16, 512], BF16) for i in range(2)]
                sza2 = [sb(ph, "p2_sza%d" % i, [128, 8, 512], BF16) for i in range(2)]
                dtr = sb(ph, "p2_dtr", [128, 4, 16], F32)
                x1 = sb(ph, "p2_x1", [128, 4, 16], F32)
                tA = sb(ph, "p2_tA", [128, 4, 16], F32)
                tB = sb(ph, "p2_tB", [128, 4, 16], F32)
                dtv2 = [sb(ph, "p2_dtv%d" % i, [128, 4, 16], F32) for i in range(2)]
                av = sb(ph, "p2_av", [128, 4, 16], F32)
                a_hi2 = [sb(ph, "p2_ahi%d" % i, [128, 4, 16], BF16) for i in range(2)]
                a_lo2 = [sb(ph, "p2_alo%d" % i, [128, 4, 16], BF16) for i in range(2)]
                xdt2 = [sb(ph, "p2_xdt%d" % i, [128, 1024], BF16) for i in range(2)]
                xe2 = [sb(ph, "p2_xe%d" % i, [128, 1024], BF16) for i in range(2)]
                btok2 = [sb(ph, "p2_btok%d" % i, [128, 512], BF16) for i in range(2)]
                acs2 = [sb(ph, "p2_acs%d" % i, [128, 16], F32) for i in range(2)]
                nacs2 = [sb(ph, "p2_nacs%d" % i, [128, 16], F32) for i in range(2)]
                tmd2 = [sb(ph, "p2_tmd%d" % i, [128, 16], F32) for i in range(2)]
                te2 = [sb(ph, "p2_te%d" % i, [128, 16], F32) for i in range(2)]
                etot2 = [sb(ph, "p2_etot%d" % i, [128, 16], F32) for i in range(2)]
                cbm2 = [sb(ph, "p2_cbm%d" % i, [128, 4, 128], BF16) for i in range(2)]
                Mh4 = [[sb(ph, "p2_Mh%d_%d" % (i, g), [128, 512], BF16) for g in range(4)] for i in range(2)]
                Cs4 = [[sb(ph, "p2_Cs%d_%d" % (i, g), [128, 512], BF16) for g in range(4)] for i in range(2)]
                E = [sb(ph, "p2_E%d" % i, [128, 512], BF16) for i in range(2)]
                dec = [sb(ph, "p2_dec%d" % i, [128, 512], BF16) for i in range(2)]
                yg = sb(ph, "p2_yg", [128, 8, 512], F32)
                sq = sb(ph, "p2_sq", [128, 8, 512], BF16)
                rs4 = [sb(ph, "p2_rs4_%d" % i, [128, 512], F32) for i in range(4)]
                htmp = sb(ph, "p2_htmp", [128, 512], F32)
                ya = sb(ph, "p2_ya", [128, 8, 512], BF16)
                smallp = ps(ph, "p2_small", [128, 512])
                tpp = ps(ph, "p2_tp", [128, 1024], BF16)
                cbp = ps(ph, "p2_cbp", [128, 512])
                acsb = [ps(ph, "p2_acsb%d" % i, [128, 512]) for i in range(2)]
                yps = [ps(ph, "p2_yps%d" % i, [128, 512]) for i in range(2)]
                hps = ps(ph, "p2_hps", [128, 512])

                for k in range(4):
                    DMA(cw[:, :, k], Wd["conv_w"][l, k].rearrange("(m p) -> p m", p=128), [], ["cw"])
                DMA(cbias[:], Wd["conv_b"][l].rearrange("(m p) -> p m", p=128), [], ["cbias"])
                DMA(dtb[:], Wd["dt_bias"][l].partition_broadcast(128), [], ["dtb"])
                DMA(arow[:], Wd["a_log"][l].partition_broadcast(128), [], ["arow"])
                for h2 in range(2):
                    DMA(dvec[h2 * 64:(h2 + 1) * 64, :], Wd["d_ssd"][l].rearrange("(m h) -> h m", h=2)[h2].partition_broadcast(64), [], ["dvec"])
                DMA(ng[:], Wd["ssd_norm_g"][l].rearrange("(k p) -> p k", p=128), [], ["ng"])
                for m in range(16):
                    for k in range(4):
                        TS("pool" if (m * 4 + k) % 3 == 0 else "dve", dw[:, m, k, :], ident_b[:], cw[:, m, k:k + 1], None, ALU.mult, ALU.bypass, ["cw", "ident_b"], ["dw"])
                ACT(arow[:], arow[:], AF.Exp, ["arow"], ["arow"])
                TSS("dve", arow[:], arow[:], -1.0, ALU.mult, ["arow"], ["arow"])

                def seq_state_init(ti):
                    (t0, W, seq, first, last) = TILES[ti]
                    if not first:
                        return
                    if seq < 2:
                        MSET("pool", hstate[:], 0.0, ["hstate"])
                        MSET("pool", hbf[:], 0.0, ["hbf"])
                    else:
                        DMA(hio[:], ssd0[l].rearrange("(m h) p n -> (h p) m n", h=2), [], ["hio"])
                        for m in range(8):
                            TR(hps[:, (m % 4) * 128:(m % 4 + 1) * 128], hio[:, m, :], ident_f[:], ["hio"], ["hps"])
                            if m % 4 == 3:
                                hh = m // 4
                                CP("dve", hstate[:, hh * 512:(hh + 1) * 512], hps[:], ["hps"], ["hstate"])
                        CP("act", hbf[:], hstate[:], ["hstate"], ["hbf"])

                def conv_blk(ti, gq):
                    (t0, W, seq, first, last) = TILES[ti]
                    Qc = min(128, W)
                    nck = W // Qc
                    tp_ = ti % 2
                    xc, sza, dtv, a_hi, a_lo = xc2[tp_], sza2[tp_], dtv2[tp_], a_hi2[tp_], a_lo2[tp_]
                    if gq == 0 and first:
                        if seq < 2:
                            MSET("pool", tails[:], 0.0, ["tails"])
                        else:
                            for k in range(3):
                                DMA(tails[:, :, k], conv0[l, k].rearrange("(m p) -> p m", p=128), [], ["tails"])
                    DMA(raw[:, :, 3:3 + W], XBC[gq * 4:(gq + 1) * 4, :, t0:t0 + W].rearrange("m p t -> p m t"),
                        [("XBC", t0, 2 * gq), ("XBC", t0, 2 * gq + 1)], ["raw"])
                    CP("dve", raw[:, :, 0:3], tails[:, gq * 4:(gq + 1) * 4, :], ["tails"], ["raw"])
                    ACT(rawb[:, :, 0:3 + W], raw[:, :, 0:3 + W], AF.Copy, ["raw"], ["rawb"])
                    for mi in range(4):
                        m = gq * 4 + mi
                        bnk, bkey = ((cbp, "cbp"), (hps, "hps"))[mi % 2]
                        for k in range(4):
                            MM(bnk[:, :W], dw[:, m, k, :], rawb[:, mi, k:k + W], k == 0, k == 3, ["rawb", "dw"], [bkey])
                        ACT(xc[:, m, :W], bnk[:, :W], AF.Silu, [bkey, "cbias"], [("xc", tp_, m)], bias=cbias[:, m:m + 1], scale=1.0)
                    CP("dve", tails[:, gq * 4:(gq + 1) * 4, :], raw[:, :, W:W + 3], ["raw"], ["tails"])
                    if gq == 3 and last:
                        for k in range(3):
                            DMA(conv_o[l, seq, k].rearrange("(m p) -> p m", p=128), tails[:, :, k], ["tails"], [])

                def pre(ti):
                    (t0, W, seq, first, last) = TILES[ti]
                    Qc = min(128, W)
                    nck = W // Qc
                    tp_ = ti % 2
                    xc, sza, dtv, a_hi, a_lo = xc2[tp_], sza2[tp_], dtv2[tp_], a_hi2[tp_], a_lo2[tp_]
                    DMA(sza[:, :, :W], SZA[:, :, t0:t0 + W].rearrange("m p t -> p m t"), [("SZA", t0)], [("sza", tp_)])
                    DMA(dtr[:Qc, :nck, :], DTR[t0:t0 + W, :].rearrange("(b p) h -> p b h", p=Qc), [("DTR", t0)], ["dtr"])
                    sl = lambda tl: tl[:Qc, :nck, :]
                    bc = lambda tl: tl[:Qc, :].unsqueeze(1).to_broadcast([Qc, nck, 16])
                    TT("dve", sl(x1), sl(dtr), bc(dtb), ALU.add, ["dtr", "dtb"], ["x1"])
                    STT(sl(tA), sl(x1), -1.0, sl(x1), ALU.mult, ALU.max, ["x1"], ["tA"])
                    ACT(sl(tA), sl(tA), AF.Exp, ["tA"], ["tA"], scale=-1.0)
                    ACT(sl(tA), sl(tA), AF.Ln, ["tA"], ["tA"], bias=1.0, scale=1.0)
                    TSS("dve", sl(tB), sl(x1), 0.0, ALU.max, ["x1"], ["tB"])
                    TT("dve", sl(dtv), sl(tA), sl(tB), ALU.add, ["tA", "tB"], [("dtv", tp_)])
                    TT("dve", sl(av), sl(dtv), bc(arow), ALU.mult, [("dtv", tp_), "arow"], ["av"])
                    CP("dve", sl(a_hi), sl(av), ["av"], [("a_hi", tp_)])
                    TT("dve", sl(a_lo), sl(av), sl(a_hi), ALU.subtract, ["av", ("a_hi", tp_)], [("a_lo", tp_)])

                def chunks(ti, hook):
                    (t0, W, seq, first, last) = TILES[ti]
                    Qc = min(128, W)
                    nck = W // Qc
                    tp_ = ti % 2
                    xc, sza, dtv, a_hi, a_lo = xc2[tp_], sza2[tp_], dtv2[tp_], a_hi2[tp_], a_lo2[tp_]
                    hd = lambda ap: ap.rearrange("p (h d) -> p h d", d=64)

                    def front_pre(c):
                        cp = c % 2
                        cs = c * Qc
                        xdt, xe, btok, acs, tmd, te, etot, cbm = xdt2[cp], xe2[cp], btok2[cp], acs2[cp], tmd2[cp], te2[cp], etot2[cp], cbm2[cp]
                        k = lambda n: (n, cp)
                        for mq in range(2):
                            for mi in range(4):
                                TR(tpp[:Qc, mq * 512 + mi * 128:mq * 512 + (mi + 1) * 128], xc[:, mq * 4 + mi, cs:cs + Qc], ident_b[:],
                                   [("xc", tp_, mq * 4 + mi)], ["tp"])
                            TT("dve", hd(xdt[:Qc, mq * 512:(mq + 1) * 512]), hd(tpp[:Qc, mq * 512:(mq + 1) * 512]),
                               dtv[:Qc, c, mq * 8:(mq + 1) * 8].unsqueeze(2).to_broadcast([Qc, 8, 64]), ALU.mult, ["tp", ("dtv", tp_)], [k("xdt%d" % mq)])
                        for g in range(4):
                            TR(tpp[:Qc, g * 128:(g + 1) * 128], xc[:, 8 + g, cs:cs + Qc], ident_b[:], [("xc", tp_, 8 + g)], ["tp"])
                        CP("act", btok[:Qc, :], tpp[:Qc, 0:512], ["tp"], [k("btok")])
                        for i, aa in enumerate((a_hi, a_lo)):
                            MM(smallp[:Qc, 0:16], tri_b[:Qc, :Qc], aa[:Qc, c, :], i == 0, i == 1, [("a_hi", tp_), ("a_lo", tp_), "tri_b"], ["small"])
                        for i, aa in enumerate((a_hi, a_lo)):
                            MM(smallp[:, 16:32], ones_b[:Qc, :], aa[:Qc, c, :], i == 0, i == 1, [("a_hi", tp_), ("a_lo", tp_)], ["small"])
                        CP("act", acs[:Qc, :], smallp[:Qc, 0:16], ["small"], [k("acs")])
                        ACT(nacs2[cp][:Qc, :], smallp[:Qc, 0:16], AF.Copy, ["small"], [k("nacs")], scale=-1.0)
                        ACT(etot[:], smallp[:, 16:32], AF.Exp, ["small"], [k("etot")])
                        TT("dve", tmd[:Qc, :], smallp[:Qc, 16:32], acs[:Qc, :], ALU.subtract, ["small", k("acs"), k("etot")], [k("tmd")])
                        ACT(te[:Qc, :], tmd[:Qc, :], AF.Exp, [k("tmd")], [k("te")])
                        TT("pool", hd(xe[:Qc, :]), hd(xdt[:Qc, :]), te[:Qc, :].unsqueeze(2).to_broadcast([Qc, 16, 64]), ALU.mult,
                           [k("xdt0"), k("xdt1"), k("te")], [k("xe")])
                        for g in range(4):
                            MM(cbp[:Qc, g * 128:g * 128 + Qc], xc[:, 8 + g, cs:cs + Qc], xc[:, 12 + g, cs:cs + Qc], True, True,
                               [("xc", tp_, 8 + g), ("xc", tp_, 12 + g)], ["cbp"])
                        TT("dve", cbm[:Qc, :, :Qc], cbp[:Qc, :].rearrange("p (g l) -> p g l", g=4)[:, :, :Qc],
                           tri_b[:Qc, :Qc].unsqueeze(1).to_broadcast([Qc, 4, Qc]), ALU.mult, ["cbp", "tri_b"], [k("cbm")])

                    def front_grp(c, g):
                        cp = c % 2
                        cs = c * Qc
                        par = g % 2
                        acs, cbm = acs2[cp], cbm2[cp]
                        k = lambda n: (n, cp)
                        ab = acsb[par]
                        abv = ab[:].rearrange("p (h l) -> p h l", h=4)
                        for hh in range(4):
                            h = g * 4 + hh
                            for i, aa in enumerate((a_hi, a_lo)):
                                MM(abv[:, hh, :Qc], aa[:Qc, c, h:h + 1].to_broadcast([Qc, 128]), tri_b[:Qc, :Qc], i == 0, i == 1,
                                   [("a_hi", tp_), ("a_lo", tp_)], [("acsb", par)])
                        Ev = E[par][:].rearrange("p (h l) -> p h l", h=4)
                        ACT(Ev[:, :, :Qc], abv[:, :, :Qc], AF.Exp, [("acsb", par)], [("E", par)])
                        dv = dec[par][:].rearrange("p (h l) -> p h l", h=4)
                        for hh in range(4):
                            h = g * 4 + hh
                            ACT(dv[:Qc, hh, :Qc], abv[:Qc, hh, :Qc], AF.Exp, [("acsb", par), k("nacs")], [("dec", par)], bias=nacs2[cp][:Qc, h:h + 1], scale=1.0)
                        Mv = Mh4[cp][g][:].rearrange("p (h l) -> p h l", h=4)
                        STT(Mv[:Qc, :, :Qc], dv[:Qc, :, :Qc], 1.0, cbm[:Qc, g, :Qc].unsqueeze(1).to_broadcast([Qc, 4, Qc]), ALU.min, ALU.mult,
                            [("dec", par), k("cbm")], [("Mh", cp, g)])
                        Cv = Cs4[cp][g][:].rearrange("p (h l) -> p h l", h=4)
                        TT("pool", Cv[:, :, :Qc], xc[:, 12 + g, cs:cs + Qc].unsqueeze(1).to_broadcast([128, 4, Qc]), Ev[:, :, :Qc], ALU.mult,
                           [("xc", tp_, 12 + g), ("E", par)], [("Cs", cp, g)])

                    def back_grp(c, g):
                        cp = c % 2
                        cs = c * Qc
                        par = g % 2
                        xdt = xdt2[cp]
                        k = lambda n: (n, cp)
                        Mv = Mh4[cp][g][:].rearrange("p (h l) -> p h l", h=4)
                        Cv = Cs4[cp][g][:].rearrange("p (h l) -> p h l", h=4)
                        yp = yps[par]
                        ykey = ("yps", par)
                        for hh in range(4):
                            h = g * 4 + hh
                            o = yp[(hh % 2) * 64:(hh % 2 + 1) * 64, (hh // 2) * 128:(hh // 2) * 128 + Qc]
                            MM(o, xdt[:Qc, h * 64:(h + 1) * 64], Mv[:Qc, hh, :Qc], True, False, [k("xdt%d" % (h // 8)), ("Mh", cp, g)], [ykey])
                            MM(o, hbf[:, h * 64:(h + 1) * 64], Cv[:, hh, :Qc], False, True, ["hbf", ("Cs", cp, g)], [ykey])
                        for mm_ in range(2):
                            m = 2 * g + mm_
                            STT(yg[:, m, cs:cs + Qc], xc[:, m, cs:cs + Qc], dvec[:, m:m + 1], yp[:, mm_ * 128:mm_ * 128 + Qc], ALU.mult, ALU.add,
                                [("xc", tp_, m), "dvec", ykey], [("yg", m)])
                            TT("pool", yg[:, m, cs:cs + Qc], yg[:, m, cs:cs + Qc], sza[:, m, cs:cs + Qc], ALU.mult, [("yg", m), ("sza", tp_)], [("yg", m)])
                            ACT(sq[:, m, cs:cs + Qc], yg[:, m, cs:cs + Qc], AF.Square, [("yg", m)], [("sq", m)])

                    def back_post(c):
                        cp = c % 2
                        xe, btok, etot = xe2[cp], btok2[cp], etot2[cp]
                        k = lambda n: (n, cp)
                        for half in range(2):
                            for gi in range(2):
                                g = half * 2 + gi
                                MM(hps[:, gi * 256:(gi + 1) * 256], btok[:Qc, g * 128:(g + 1) * 128], xe[:Qc, g * 256:(g + 1) * 256], True, True,
                                   [k("btok"), k("xe")], ["hps"])
                            TT("pool", hd(htmp[:]), hd(hstate[:, half * 512:(half + 1) * 512]),
                               etot[:, half * 8:(half + 1) * 8].unsqueeze(2).to_broadcast([128, 8, 64]), ALU.mult, ["hstate", k("etot")], ["htmp"])
                            TT("dve", hstate[:, half * 512:(half + 1) * 512], htmp[:], hps[:], ALU.add, ["htmp", "hps"], ["hstate"])
                        CP("act", hbf[:], hstate[:], ["hstate"], ["hbf"])
                        cs = c * Qc
                        for gg in range(4):
                            for i in range(2):
                                MM(smallp[:, gg * 128:gg * 128 + Qc], ones_b[:], sq[:, 2 * gg + i, cs:cs + Qc], i == 0, i == 1, [("sq", 2 * gg + i)], ["small"])
                        rsv = rs4[0][:].rearrange("p (g l) -> p g l", g=4)
                        smv = smallp[:].rearrange("p (g l) -> p g l", g=4)
                        ACT(rsv[:, :, :Qc], smv[:, :, :Qc], AF.Ln, ["small"], ["rs4"], bias=EPS, scale=1.0 / 256)
                        ACT(rsv[:, :, :Qc], rsv[:, :, :Qc], AF.Exp, ["rs4"], ["rs4"], scale=-0.5)
                        for m in range(8):
                            STT(ya[:, m, cs:cs + Qc], yg[:, m, cs:cs + Qc], ng[:, m:m + 1], rsv[:, m // 2, :Qc], ALU.mult, ALU.mult, [("yg", m), "ng", "rs4"], ["ya"])

                    front_pre(0)
                    for g in range(4):
                        front_grp(0, g)
                    for c in range(nck):
                        more = c + 1 < nck
                        if more:
                            front_pre(c + 1)
                        for g in range(4):
                            if more:
                                front_grp(c + 1, g)
                            back_grp(c, g)
                        back_post(c)
                        hook(c, nck)

                def post(ti):
                    (t0, W, seq, first, last) = TILES[ti]
                    Qc = min(128, W)
                    nck = W // Qc
                    tp_ = ti % 2
                    xc, sza, dtv, a_hi, a_lo = xc2[tp_], sza2[tp_], dtv2[tp_], a_hi2[tp_], a_lo2[tp_]
                    DMA(YA[:, :, t0:t0 + W].rearrange("m p t -> p m t"), ya[:, :, :W], ["ya"], [("YA", t0)])
                    if last:
                        for m in range(8):
                            TR(hps[:, (m % 4) * 128:(m % 4 + 1) * 128], hstate[:, m * 128:(m + 1) * 128], ident_f[:], ["hstate"], ["hps"])
                            if m % 4 == 3:
                                hh = m // 4
                                CP("dve", hio[:, hh * 4:(hh + 1) * 4, :], hps[:].rearrange("p (m n) -> p m n", n=128), ["hps"], ["hio"])
                        DMA(ssd_o[l, seq].rearrange("(m h) p n -> (h p) m n", h=2), hio[:], ["hio"], [])

                for gq in range(4):
                    conv_blk(0, gq)
                pre(0)
                for ti in range(len(TILES)):
                    nxt = ti + 1 if ti + 1 < len(TILES) else None

                    def hook(c, nck, nxt=nxt):
                        if nxt is None:
                            return
                        per = 4 // nck if nck <= 4 else 1
                        for gq in range(c * per, (c + 1) * per):
                            conv_blk(nxt, gq)
                        if c == nck - 1:
                            pre(nxt)

                    seq_state_init(ti)
                    chunks(ti, hook)
                    post(ti)
            S.barrier()

        def phase3(l):
            with ExitStack() as ph:
                W_sb = sb(ph, "p3_W", [128, 64, 128], BF16)
                T0_sb = sb(ph, "p3_T0", [128, 64, 128], BF16)
                Cm_sb = sb(ph, "p3_Cm", [128, 32, 2, 128], BF16)
                L8 = sb(ph, "p3_L8", [128, 4, 32], F32)
                Hcar = sb(ph, "p3_Hcar", [128, 2, 32], F32)
                pa = ps(ph, "p3_pa", [128, 512])
                pb = ps(ph, "p3_pb", [128, 512])
                pc = ps(ph, "p3_pc", [128, 512])
                pd = ps(ph, "p3_pd", [128, 512])
                with ExitStack() as pp:
                    f64 = lambda nm, shp, dt=F32: sb(pp, nm, shp, dt)
                    lamr, lami, dtg, zr, zi = [f64("q_" + n, [64, 64]) for n in ("lamr", "lami", "dtg", "zr", "zi")]
                    kr, ki, t1, t2, den = [f64("q_" + n, [64, 64]) for n in ("kr", "ki", "t1", "t2", "den")]
                    erow_i = f64("q_erowi", [64, 25], I32)
                    erow = f64("q_erow", [64, 25])
                    ang, mag, s4, s2, Pr, Pi = [f64("q_" + n, [64, 64, 25]) for n in ("ang", "mag", "s4", "s2", "Pr", "Pi")]
                    kqi = f64("q_kqi", [64, 64, 25], I32)
                    PKr, PKi, u1 = [f64("q_" + n, [64, 64, 8]) for n in ("PKr", "PKi", "u1")]
                    Btr, Bti, Ctr, Cti = [f64("q_" + n, [64, 64, 16]) for n in ("Btr", "Bti", "Ctr", "Cti")]
                    Cn = sb(pp, "q_Cn", [128, 8, 64], F32)
                    Wtr, Wti, v1, v2, Cxr, Cxn = [f64("q_" + n, [64, 8, 8, 16]) for n in ("Wtr", "Wti", "v1", "v2", "Cxr", "Cxn")]

                    DMA(lamr[:], Wd["lam_re"][l].rearrange("g p -> p g"), [], ["lamr"])
                    DMA(lami[:], Wd["lam_im"][l].rearrange("g p -> p g"), [], ["lami"])
                    DMA(dtg[:], Wd["log_dt"][l].partition_broadcast(64), [], ["dtg"])
                    DMA(Btr[:], Wd["b_re"][l].rearrange("g p c -> p g c"), [], ["Btr"])
                    DMA(Bti[:], Wd["b_im"][l].rearrange("g p c -> p g c"), [], ["Bti"])
                    for (src, dst, nm) in ((Wd["c_re"], Ctr, "Ctr"), (Wd["c_im"], Cti, "Cti")):
                        DMA(Cn[:], src[l].rearrange("(j g) c p -> (g c) j p", g=8), [], ["Cn"])
                        for j in range(8):
                            TR(pa[:64, (j % 4) * 128:(j % 4 + 1) * 128], Cn[:, j, :], ident_f[:], ["Cn"], ["pa"])
                            if j % 4 == 3:
                                jj = j // 4
                                CP("dve", dst[:, jj * 32:(jj + 1) * 32, :], pa[:64, :].rearrange("p (g c) -> p g c", c=16), ["pa"], [nm])
                    ACT(dtg[:], dtg[:], AF.Exp, ["dtg"], ["dtg"])
                    TT("dve", zr[:], lamr[:], dtg[:], ALU.mult, ["lamr", "dtg"], ["zr"])
                    TT("dve", zi[:], lami[:], dtg[:], ALU.mult, ["lami", "dtg"], ["zi"])
                    for (a, b_, pat, base) in ((0, 8, [[-1, 8]], 7), (8, 16, [[1, 8]], 1), (16, 24, [[1, 8]], -7), (24, 25, [[1, 1]], 8)):
                        oap = erow_i[:, a:b_]
                        S.op("pool", lambda e, oap=oap, pat=pat, base=base: e.iota(out=oap, pattern=pat, base=base, channel_multiplier=0), (), ["erow_i"])
                    CP("dve", erow[:], erow_i[:], ["erow_i"], ["erow"])
                    ebc = erow[:, :].unsqueeze(1).to_broadcast([64, 64, 25])
                    gbc = lambda tl: tl[:, :].unsqueeze(2).to_broadcast([64, 64, 25])
                    fl = lambda tl: tl[:].rearrange("p g m -> p (g m)")
                    TT("dve", ang[:], gbc(zi), ebc, ALU.mult, ["zi", "erow"], ["ang"])
                    TT("dve", mag[:], gbc(zr), ebc, ALU.mult, ["zr", "erow"], ["mag"])
                    ACT(mag[:], mag[:], AF.Exp, ["mag"], ["mag"])
                    TSS("dve", s4[:], ang[:], 1.0 / (2 * math.pi), ALU.mult, ["ang"], ["s4"])
                    CP("dve", kqi[:], s4[:], ["s4"], ["kqi"])
                    CP("dve", s4[:], kqi[:], ["kqi"], ["s4"])
                    STT(fl(ang), fl(s4), -2 * math.pi, fl(ang), ALU.mult, ALU.add, ["s4", "ang"], ["ang"])
                    ACT(s4[:], ang[:], AF.Sin, ["ang"], ["s4"], scale=0.25)
                    ACT(s2[:], ang[:], AF.Sin, ["ang"], ["s2"], scale=0.5)
                    TT("dve", s4[:], s4[:], s4[:], ALU.mult, ["s4"], ["s4"])
                    TS("dve", s4[:], s4[:], -2.0, 1.0, ALU.mult, ALU.add, ["s4"], ["s4"])
                    TT("dve", Pi[:], s2[:], s4[:], ALU.mult, ["s2", "s4"], ["Pi"])
                    STT(fl(Pi), fl(Pi), 2.0, fl(mag), ALU.mult, ALU.mult, ["Pi", "mag"], ["Pi"])
                    TT("dve", s2[:], s2[:], s2[:], ALU.mult, ["s2"], ["s2"])
                    TS("dve", s2[:], s2[:], -2.0, 1.0, ALU.mult, ALU.add, ["s2"], ["s2"])
                    TT("dve", Pr[:], s2[:], mag[:], ALU.mult, ["s2", "mag"], ["Pr"])
                    TSS("dve", t1[:], Pr[:, :, 6], -1.0, ALU.add, ["Pr"], ["t1"])
                    TT("dve", den[:], lamr[:], lamr[:], ALU.mult, ["lamr"], ["den"])
                    TT("dve", t2[:], lami[:], lami[:], ALU.mult, ["lami"], ["t2"])
                    TT("dve", den[:], den[:], t2[:], ALU.add, ["den", "t2"], ["den"])
                    S.op("dve", lambda e: e.reciprocal(out=den[:], in_=den[:]), ["den"], ["den"])
                    TT("dve", kr[:], t1[:], lamr[:], ALU.mult, ["t1", "lamr"], ["kr"])
                    TT("dve", t2[:], Pi[:, :, 6], lami[:], ALU.mult, ["Pi", "lami"], ["t2"])
                    TT("dve", kr[:], kr[:], t2[:], ALU.add, ["kr", "t2"], ["kr"])
                    TT("dve", kr[:], kr[:], den[:], ALU.mult, ["kr", "den"], ["kr"])
                    TT("dve", ki[:], Pi[:, :, 6], lamr[:], ALU.mult, ["Pi", "lamr"], ["ki"])
                    TT("dve", t2[:], t1[:], lami[:], ALU.mult, ["t1", "lami"], ["t2"])
                    TT("dve", ki[:], ki[:], t2[:], ALU.subtract, ["ki", "t2"], ["ki"])
                    TT("dve", ki[:], ki[:], den[:], ALU.mult, ["ki", "den"], ["ki"])
                    k8 = lambda tl: tl[:, :].unsqueeze(2).to_broadcast([64, 64, 8])
                    TT("dve", PKr[:], Pr[:, :, 0:8], k8(kr), ALU.mult, ["Pr", "kr"], ["PKr"])
                    TT("dve", u1[:], Pi[:, :, 0:8], k8(ki), ALU.mult, ["Pi", "ki"], ["u1"])
                    TT("dve", PKr[:], PKr[:], u1[:], ALU.subtract, ["PKr", "u1"], ["PKr"])
                    TT("dve", PKi[:], Pr[:, :, 0:8], k8(ki), ALU.mult, ["Pr", "ki"], ["PKi"])
                    TT("dve", u1[:], Pi[:, :, 0:8], k8(kr), ALU.mult, ["Pi", "kr"], ["u1"])
                    TT("dve", PKi[:], PKi[:], u1[:], ALU.add, ["PKi", "u1"], ["PKi"])
                    for (ti, src, sc) in ((0, Pr, 1.0), (1, Pi, 1.0), (2, Pi, -1.0), (3, mag, 1.0)):
                        for gh in range(2):
                            ACT(L8[gh * 64:(gh + 1) * 64, ti, :].rearrange("p (g j) -> p g j", j=8),
                                src[:, :, 24].rearrange("p (j g) -> p g j", g=8)[:, gh * 4:(gh + 1) * 4, :], AF.Copy, ["Pr", "Pi", "mag"], ["L8"], scale=sc)

                    def cmul(outr, outi, PA_r, PA_i, Xr, Xi, jb, key, rd, neg):
                        pbc = lambda P_: P_.unsqueeze(3).to_broadcast([64, 8, 8, 16])
                        xbc = lambda X_: X_[:, jb * 8:(jb + 1) * 8, :].unsqueeze(2).to_broadcast([64, 8, 8, 16])
                        TT("dve", outr[:], pbc(PA_r), xbc(Xr), ALU.mult, rd, [key + "r"])
                        TT("pool", v1[:], pbc(PA_i), xbc(Xi), ALU.mult, rd, ["v1"])
                        TT("dve", outr[:], outr[:], v1[:], ALU.subtract, [key + "r", "v1"], [key + "r"])
                        TT("dve", outi[:], pbc(PA_r), xbc(Xi), ALU.mult, rd, [key + "i"])
                        TT("pool", v2[:], pbc(PA_i), xbc(Xr), ALU.mult, rd, ["v2"])
                        TT("dve", outi[:], outi[:], v2[:], ALU.add, [key + "i", "v2"], [key + "i"])
                        if neg:
                            TSS("dve", outi[:], outi[:], -1.0, ALU.mult, [key + "i"], [key + "i"])

                    base_rd = ["Pr", "Pi", "PKr", "PKi", "Btr", "Bti", "Ctr", "Cti"]
                    sc_ = lambda ap: ap.rearrange("p s c -> p (s c)")
                    for jb in range(8):
                        gs = slice(jb * 8, (jb + 1) * 8)
                        cmul(Wtr, Wti, PKr[:, gs, :], PKi[:, gs, :], Btr, Bti, jb, "Wt", base_rd, False)
                        for (ri, src, nm) in ((0, Wtr, "Wtr"), (1, Wti, "Wti")):
                            for g8 in range(8):
                                bank, bk = (pa, "pa") if g8 < 4 else (pb, "pb")
                                TR(bank[:, (g8 % 4) * 64:(g8 % 4 + 1) * 64], sc_(src[:, g8, :, :]), ident_f[:64, :64], [nm], [bk])
                            for hb, (bank, bk) in enumerate(((pa, "pa"), (pb, "pb"))):
                                ACT(W_sb[:, :, ri * 64:(ri + 1) * 64].rearrange("p (g j) x -> p g j x", j=8)[:, hb * 4:(hb + 1) * 4, jb, :],
                                    bank[:, 0:256].rearrange("p (g x) -> p g x", x=64), AF.Copy, [bk], ["W_sb"])
                        cmul(Cxr, Cxn, Pr[:, gs, 16:24], Pi[:, gs, 16:24], Ctr, Cti, jb, "Cx", base_rd, True)
                        for g8 in range(8):
                            bank, bk = (pc, "pc") if g8 < 4 else (pd, "pd")
                            o = bank[:, (g8 % 4) * 128:(g8 % 4 + 1) * 128]
                            MM(o, sc_(Wtr[:, g8, :, :]), sc_(Cxr[:, g8, :, :]), True, False, ["Wtr", "Cxr"], [bk])
                            MM(o, sc_(Wti[:, g8, :, :]), sc_(Cxn[:, g8, :, :]), False, True, ["Wti", "Cxi"], [bk])
                        for hb, (bank, bk) in enumerate(((pc, "pc"), (pd, "pd"))):
                            TT("dve", T0_sb[:].rearrange("p (g j) x -> p g j x", j=8)[:, hb * 4:(hb + 1) * 4, jb, :],
                               bank[:].rearrange("p (g x) -> p g x", x=128), bmask[:].unsqueeze(1).to_broadcast([128, 4, 128]), ALU.mult,
                               [bk, "bmask"], ["T0_sb"])
                        cmul(Cxr, Cxn, Pr[:, gs, 8:16], Pi[:, gs, 8:16], Ctr, Cti, jb, "Cx", base_rd, True)
                        for (ri, src, nm) in ((0, Cxr, "Cxr"), (1, Cxn, "Cxi")):
                            for gh in range(2):
                                ACT(Cm_sb[gh * 64:(gh + 1) * 64, :, ri, :].rearrange("p (g j) x -> p g j x", j=8)[:, :, jb, :],
                                    src[:, gh * 4:(gh + 1) * 4, :, :].rearrange("p g s c -> p g (s c)"), AF.Copy, [nm], ["Cm_sb"])
                S.barrier()
                with ExitStack() as pm:
                    Hb = sb(pm, "p3_Hb", [128, 2, 32, 65], F32)
                    U2 = [sb(pm, "p3_U%d" % i, [128, 64, 64], BF16) for i in range(2)]
                    Hbf = sb(pm, "p3_Hbf", [128, 2, 32, 64], BF16)
                    Ysth = [sb(pm, "p3_Yst%d" % i, [128, 32, 64], F32) for i in range(2)]
                    Fr = sb(pm, "p3_Fr", [128, 32, 65], F32)
                    Fi = sb(pm, "p3_Fi", [128, 32, 65], F32)
                    Rt = sb(pm, "p3_R", [128, 32, 65], F32)
                    Gir = sb(pm, "p3_Gir", [128, 32, 65], F32)
                    Gii = sb(pm, "p3_Gii", [128, 32, 65], F32)
                    Gsr = sb(pm, "p3_Gsr", [128, 32, 65], F32)
                    Gsi = sb(pm, "p3_Gsi", [128, 32, 65], F32)
                    c1 = sb(pm, "p3_c1", [128, 3, 32], F32)
                    S.op("dve", lambda e: e.reciprocal(out=c1[:, 2, :], in_=L8[:, 3, :]), ["L8"], ["c1"])
                    TT("dve", c1[:, 0, :], L8[:, 0, :], c1[:, 2, :], ALU.mult, ["L8", "c1"], ["c1"])
                    TT("dve", c1[:, 1, :], L8[:, 1, :], c1[:, 2, :], ALU.mult, ["L8", "c1"], ["c1"])
                    MSET("pool", Fr[:, :, 0:1], 1.0, ["Fr"])
                    MSET("pool", Fi[:, :, 0:1], 0.0, ["Fi"])
                    MSET("pool", Rt[:, :, 0:1], 0.0, ["Rt"])
                    MSET("pool", Gir[:], 0.0, ["Gir"])
                    MSET("pool", Gii[:], 0.0, ["Gii"])
                    CP("dve", Fr[:, :, 1], c1[:, 0, :], ["c1", "Fr"], ["Fr"])
                    CP("dve", Fi[:, :, 1], c1[:, 1, :], ["c1", "Fi"], ["Fi"])
                    CP("dve", Rt[:, :, 1:65], L8[:, 3, :].unsqueeze(2).to_broadcast([128, 32, 64]), ["L8", "Rt"], ["Rt"])
                    for k in range(6):
                        m_ = 1 << k
                        a_bc = Fr[:, :, m_:m_ + 1].to_broadcast([128, 32, m_])
                        b_bc = Fi[:, :, m_:m_ + 1].to_broadcast([128, 32, m_])
                        sr, si = Fr[:, :, 1:1 + m_], Fi[:, :, 1:1 + m_]
                        dr, di = Fr[:, :, m_ + 1:2 * m_ + 1], Fi[:, :, m_ + 1:2 * m_ + 1]
                        u_, v_ = Gsr[:, :, 0:m_], Gsi[:, :, 0:m_]
                        TT("dve", u_, sr, a_bc, ALU.mult, ["Fr"], ["Gsr"])
                        TT("dve", v_, si, b_bc, ALU.mult, ["Fi"], ["Gsi"])
                        TT("dve", dr, u_, v_, ALU.subtract, ["Gsr", "Gsi", "Fr"], ["Fr"])
                        TT("dve", u_, sr, b_bc, ALU.mult, ["Fr", "Fi"], ["Gsr"])
                        TT("dve", v_, si, a_bc, ALU.mult, ["Fr", "Fi"], ["Gsi"])
                        TT("dve", di, u_, v_, ALU.add, ["Gsr", "Gsi", "Fi"], ["Fi"])
                    lbanks = [pa, pb]
                    ybanks = [pc, pd]
                    def u_load(ti):
                        (t0_, W_, _, _, _) = TILES[ti]
                        DMA(U2[ti % 2][:, :, :W_ // 8], UD[:, :, :, :, t0_ // 8:(t0_ + W_) // 8].rearrange("s c g j n -> (s c) (g j) n"),
                            [("UD", t0_)], [("U_sb", ti % 2)])

                    u_load(0)
                    for ti, (t0, W, seq, first, last) in enumerate(TILES):
                        nC = W // 8
                        n0 = t0 // 8
                        U_sb = U2[ti % 2]
                        ukey = ("U_sb", ti % 2)
                        if ti + 1 < len(TILES):
                            u_load(ti + 1)
                        if first:
                            if seq < 2:
                                MSET("pool", Hb[:, :, :, 0:1], 0.0, ["Hb"])
                            else:
                                for (ri, src) in ((0, s5r0), (1, s5i0)):
                                    for g8 in range(8):
                                        gh, g4 = g8 // 4, g8 % 4
                                        DMA(Hcar[gh * 64:(gh + 1) * 64, ri, g4 * 8:(g4 + 1) * 8],
                                            src[l].rearrange("(j g) p -> p g j", g=8)[:, g8, :], [], ["Hcar"])
                                CP("dve", Hb[:, :, :, 0], Hcar[:], ["Hcar"], ["Hb"])
                        for gl4 in range(8):
                            bank = lbanks[gl4 % 2]
                            bk = ("lb", gl4 % 2)
                            bv = bank[:].rearrange("p (r g n) -> p r g n", r=2, g=4)
                            for gh in range(2):
                                for gi in range(4):
                                    gidx = gh * 32 + gl4 * 4 + gi
                                    for ri in range(2):
                                        MM(bv[gh * 64:(gh + 1) * 64, ri, gi, :nC], W_sb[:, gidx, ri * 64:(ri + 1) * 64], U_sb[:, gidx, :nC], True, True,
                                           [ukey], [bk])
                            CP("act", Hb[:, :, gl4 * 4:(gl4 + 1) * 4, 1:1 + nC], bv[:, :, :, :nC], [bk], ["Hb"])
                        n1 = nC + 1
                        Lr_, Li_ = Hb[:, 0, :, 1:n1], Hb[:, 1, :, 1:n1]
                        Frs, Fis = Fr[:, :, 1:n1], Fi[:, :, 1:n1]
                        CP("dve", Gir[:, :, 0], Hb[:, 0, :, 0], ["Hb"], ["Gir"])
                        CP("dve", Gii[:, :, 0], Hb[:, 1, :, 0], ["Hb"], ["Gii"])
                        TT("dve", Gsr[:, :, 1:n1], Lr_, Frs, ALU.mult, ["Hb", "Fr"], ["Gsr"])
                        TT("dve", Gsi[:, :, 1:n1], Li_, Fis, ALU.mult, ["Hb", "Fi"], ["Gsi"])
                        TT("dve", Gir[:, :, 1:n1], Gsr[:, :, 1:n1], Gsi[:, :, 1:n1], ALU.add, ["Gsr", "Gsi"], ["Gir"])
                        TT("dve", Gsr[:, :, 1:n1], Li_, Frs, ALU.mult, ["Hb", "Fr", "Gir"], ["Gsr"])
                        TT("dve", Gsi[:, :, 1:n1], Lr_, Fis, ALU.mult, ["Hb", "Fi", "Gir"], ["Gsi"])
                        TT("dve", Gii[:, :, 1:n1], Gsr[:, :, 1:n1], Gsi[:, :, 1:n1], ALU.subtract, ["Gsr", "Gsi"], ["Gii"])
                        fl_ = lambda tl: tl[:].rearrange("p g n -> p (g n)")
                        S.op("dve", lambda e, o=fl_(Gsr), d0=fl_(Rt), d1=fl_(Gir): e.tensor_tensor_scan(out=o, data0=d0, data1=d1, initial=0.0, op0=ALU.mult, op1=ALU.add),
                             ["Rt", "Gir", "Gii"], ["Gsr"])
                        S.op("dve", lambda e, o=fl_(Gsi), d0=fl_(Rt), d1=fl_(Gii): e.tensor_tensor_scan(out=o, data0=d0, data1=d1, initial=0.0, op0=ALU.mult, op1=ALU.add),
                             ["Rt", "Gii", "Gsr"], ["Gsi"])
                        Frn, Fin = Fr[:, :, 0:n1], Fi[:, :, 0:n1]
                        TT("dve", Gir[:, :, 0:n1], Gsr[:, :, 0:n1], Frn, ALU.mult, ["Gsr", "Gsi"], ["Gir"])
                        TT("dve", Gii[:, :, 0:n1], Gsi[:, :, 0:n1], Fin, ALU.mult, ["Gsr", "Gsi"], ["Gii"])
                        TT("dve", Hb[:, 0, :, 0:n1], Gir[:, :, 0:n1], Gii[:, :, 0:n1], ALU.subtract, ["Gir", "Gii", "Hb"], ["Hb"])
                        TT("dve", Gir[:, :, 0:n1], Gsr[:, :, 0:n1], Fin, ALU.mult, ["Gsr", "Gsi", "Hb"], ["Gir"])
                        TT("dve", Gii[:, :, 0:n1], Gsi[:, :, 0:n1], Frn, ALU.mult, ["Gsr", "Gsi", "Hb"], ["Gii"])
                        TT("dve", Hb[:, 1, :, 0:n1], Gir[:, :, 0:n1], Gii[:, :, 0:n1], ALU.add, ["Gir", "Gii", "Hb"], ["Hb"])
                        CP("act", Hbf[:, :, :, :nC], Hb[:, :, :, 0:nC], ["Hb"], ["Hbf"])
                        CP("dve", Hcar[:], Hb[:, :, :, nC], ["Hb"], ["Hcar"])
                        CP("dve", Hb[:, :, :, 0], Hcar[:], ["Hcar", "Hbf"], ["Hb"])
                        for g8b in range(8):
                            bank = ybanks[g8b % 2]
                            bk = ("yb", g8b % 2)
                            bv = bank[:].rearrange("p (g n) -> p g n", g=8)
                            for gi in range(8):
                                gidx = g8b * 8 + gi
                                gh, gl = gidx // 32, gidx % 32
                                MM(bv[:, gi, :nC], T0_sb[:, gidx, :], U_sb[:, gidx, :nC], True, False, [ukey], [bk])
                                for ri in range(2):
                                    MM(bv[:, gi, :nC], Cm_sb[gh * 64:(gh + 1) * 64, gl, ri, :], Hbf[gh * 64:(gh + 1) * 64, ri, gl, :nC], False, ri == 1,
                                       ["Hbf"], [bk])
                            hb_ = g8b // 4
                            CP("act" if g8b % 2 else "dve", Ysth[hb_][:, (g8b % 4) * 8:(g8b % 4 + 1) * 8, :nC], bv[:, :, :nC], [bk], [("Yst", hb_)])
                            if g8b % 4 == 3:
                                for s_ in range(8):
                                    DMA(YD[:, hb_ * 4:(hb_ + 1) * 4, :, s_, n0:n0 + nC].rearrange("c g j n -> c (g j) n"),
                                        Ysth[hb_][s_ * 16:(s_ + 1) * 16, :, :nC], [("Yst", hb_)], [("YD", t0)])
                        if last:
                            for (ri, dst) in ((0, s5r_o), (1, s5i_o)):
                                for g8 in range(8):
                                    gh, g4 = g8 // 4, g8 % 4
                                    DMA(dst[l, seq].rearrange("(j g) p -> p g j", g=8)[:, g8, :],
                                        Hcar[gh * 64:(gh + 1) * 64, ri, g4 * 8:(g4 + 1) * 8], ["Hcar"], [])
            S.barrier()

        def phase4(l):
            with ExitStack() as ph:
                wg = sb(ph, "p4_w", [128, 8, 1024], BF16)
                ds5 = sb(ph, "p4_ds5", [128, 8], F32)
                bgl = sb(ph, "p4_bgl", [128, 8], F32)
                yp4 = [sb(ph, "p4_yp%d" % i, [128, 8, 8, 64], F32) for i in range(2)]
                ut4 = [sb(ph, "p4_ut%d" % i, [128, 8, 512], BF16) for i in range(2)]
                szb4 = [sb(ph, "p4_szb%d" % i, [128, 8, 512], BF16) for i in range(2)]
                yb = sb(ph, "p4_yb", [128, 8, 512], F32)
                gf = sb(ph, "p4_gf", [128, 8, 512], F32)
                gb = sb(ph, "p4_gb", [128, 8, 512], BF16)
                sg = [sb(ph, "p4_sg%d" % i, [128, 512], F32) for i in range(2)]
                ybst = sb(ph, "p4_ybst", [128, 8, 512], BF16)
                mm = [ps(ph, "p4_mm%d" % i, [128, 512]) for i in range(2)]
                for kt in range(8):
                    DMA(wg[:, kt, :], Wd["w_glu"][l, kt * 128:(kt + 1) * 128, :], [], ["wg"], q="pool")
                DMA(ds5[:], Wd["d_s5"][l].rearrange("(k p) -> p k", p=128), [], ["ds5"])
                DMA(bgl[:], Wd["b_glu"][l].rearrange("(k p) -> p k", p=128), [], ["bgl"])
                def p4_load(ti):
                    (t0, W, seq, first, last) = TILES[ti]
                    nC = W // 8
                    n0 = t0 // 8
                    pr = ti % 2
                    for g8 in range(8):
                        DMA(yp4[pr][g8 * 16:(g8 + 1) * 16, :, :, :nC].rearrange("c j s n -> c (j s) n"),
                            YD[:, g8, :, :, n0:n0 + nC].rearrange("c j s n -> c (j s) n"), [("YD", t0)], [("yp4", pr)])
                        DMA(ut4[pr][g8 * 16:(g8 + 1) * 16, :, :W], UT[:, :, g8, t0:t0 + W].rearrange("j c t -> c j t"), [("UT", t0)], [("ut4", pr)])
                    DMA(szb4[pr][:, :, :W], SZB[:, :, t0:t0 + W].rearrange("m p t -> p m t"), [("SZB", t0)], [("szb4", pr)])

                p4_load(0)
                for ti, (t0, W, seq, first, last) in enumerate(TILES):
                    nC = W // 8
                    n0 = t0 // 8
                    pr = ti % 2
                    if ti + 1 < len(TILES):
                        p4_load(ti + 1)
                    for j in range(8):
                        STT(yb[:, j, :W].rearrange("p (n s) -> p n s", s=8), ut4[pr][:, j, :W].rearrange("p (n s) -> p n s", s=8), ds5[:, j:j + 1],
                            yp4[pr][:, j, :, :nC].rearrange("p s n -> p n s"), ALU.mult, ALU.add, [("ut4", pr), ("yp4", pr), "ds5"], [("yb", j)])
                        ACT(gf[:, j, :W], yb[:, j, :W], AF.Gelu_apprx_tanh, [("yb", j)], [("gf", j)])
                        CP("act" if j % 2 else "dve", gb[:, j, :W], gf[:, j, :W], [("gf", j)], ["gb"])
                    for jo in range(8):
                        bank = mm[jo % 2]
                        bk = ("mm", jo % 2)
                        for ji in range(8):
                            MM(bank[:, :W], wg[:, ji, jo * 128:(jo + 1) * 128], gb[:, ji, :W], ji == 0, ji == 7, ["wg", "gb"], [bk])
                        sgt = sg[jo % 2]
                        sk = ("sg", jo % 2)
                        ACT(sgt[:, :W], bank[:, :W], AF.Sigmoid, [bk, "bgl"], [sk], bias=bgl[:, jo:jo + 1], scale=1.0)
                        TT("dve", sgt[:, :W], sgt[:, :W], gf[:, jo, :W], ALU.mult, [sk, ("gf", jo)], [sk])
                        TT("dve", ybst[:, jo, :W], sgt[:, :W], szb4[pr][:, jo, :W], ALU.mult, [sk, ("szb4", pr)], ["ybst"])
                    DMA(YB[:, :, t0:t0 + W].rearrange("m p t -> p m t"), ybst[:, :, :W], ["ybst"], [("YB", t0)])
            S.barrier()

        def phase5(l):
            with ExitStack() as ph:
                wo = sb(ph, "p5_w", [128, 16, 1024], BF16)
                y5 = sb(ph, "p5_y", [128, 16, 512], BF16)
                xT5 = sb(ph, "p5_x", [128, 8, 512], F32)
                xo = sb(ph, "p5_xo", [128, 8, 512], F32)
                mm = [ps(ph, "p5_mm%d" % i, [128, 512]) for i in range(2)]
                for kt in range(16):
                    DMA(wo[:, kt, :], Wd["w_out"][l, kt * 128:(kt + 1) * 128, :], [], ["wo"], q="pool")
                for (t0, W, seq, first, last) in TILES:
                    DMA(y5[:, 0:8, :W], YA[:, :, t0:t0 + W].rearrange("m p t -> p m t"), [("YA", t0)], ["y5a"])
                    DMA(y5[:, 8:16, :W], YB[:, :, t0:t0 + W].rearrange("m p t -> p m t"), [("YB", t0)], ["y5b"])
                    DMA(xT5[:, :, :W], xt_ap(t0, W), [("XT", t0)], ["xT5"])
                    for dm in range(8):
                        bank = mm[dm % 2]
                        bk = ("mm", dm % 2)
                        for kt in range(16):
                            MM(bank[:, :W], wo[:, kt, dm * 128:(dm + 1) * 128], y5[:, kt, :W], kt == 0, kt == 15, ["wo", "y5a", "y5b"], [bk])
                        TT("dve", xo[:, dm, :W], bank[:, :W], xT5[:, dm, :W], ALU.add, [bk, "xT5"], ["xo"])
                    DMA(xt_ap(t0, W), xo[:, :, :W], ["xo"], [("XT", t0)])
            S.barrier()

        def phase_final():
            with ExitStack() as ph:
                fg = sb(ph, "pf_g", [128, 8], F32)
                xT = sb(ph, "pf_xT", [128, 8, 512], F32)
                sq = sb(ph, "pf_sq", [128, 8, 512], BF16)
                hf = sb(ph, "pf_hf", [128, 8, 512], F32)
                rs = sb(ph, "pf_rs", [128, 512], F32)
                rs2 = sb(ph, "pf_rs2", [128, 512], F32)
                yst = sb(ph, "pf_yst", [128, 4, 1024], F32)
                msp = ps(ph, "pf_ms", [128, 512])
                tp = [ps(ph, "pf_tp%d" % i, [128, 512]) for i in range(4)]
                DMA(fg[:], Wd["final_norm_g"].rearrange("(k p) -> p k", p=128), [], ["fg"])
                for (t0, W, seq, first, last) in TILES:
                    Pb = min(128, W)
                    nb = W // Pb
                    DMA(xT[:, :, :W], xt_ap(t0, W), [("XT", t0)], ["xT"])
                    ACT(sq[:, :, :W], xT[:, :, :W], AF.Square, ["xT"], ["nf_sq"])
                    rms_rstd(msp, [sq[:, kt, :W] for kt in range(8)], W, 1.0 / 1024, rs, rs2, "nf")
                    for kt in range(8):
                        STT(hf[:, kt, :W], xT[:, kt, :W], fg[:, kt:kt + 1], rs2[:, :W], ALU.mult, ALU.mult, ["xT", "fg", "nf_rs2"], ["hf"])
                    for b in range(nb):
                        for half in range(2):
                            bank = tp[(b * 2 + half) % 4]
                            bk = ("tp", (b * 2 + half) % 4)
                            for k4 in range(4):
                                kt = half * 4 + k4
                                TR(bank[:Pb, k4 * 128:(k4 + 1) * 128], hf[:, kt, b * Pb:(b + 1) * Pb], ident_f[:], ["hf"], [bk])
                            CP("act" if half else "dve", yst[:Pb, b, half * 512:(half + 1) * 512], bank[:Pb, :], [bk], ["yst"])
                    DMA(y_out[t0:t0 + W, :].rearrange("(b p) d -> p b d", p=Pb), yst[:Pb, :nb, :], ["yst"], [])
            S.barrier()

        wst = ExitStack()
        w_cur = sb(wst, "w_in_sb", [128, 8, IN_COLS], BF16)
        load_w_in(0, w_cur)
        phase0()
        for l in range(n_layers):
            if 1 in phases:
                phase1(l, w_cur)
            wst.close()
            if 2 in phases:
                phase2(l)
            if 3 in phases:
                phase3(l)
            if 4 in phases:
                phase4(l)
            if l + 1 < n_layers:
                wst = ExitStack()
                w_cur = sb(wst, "w_in_sb", [128, 8, IN_COLS], BF16)
                load_w_in(l + 1, w_cur)
            if 5 in phases:
                phase5(l)
        if do_final:
            phase_final()
        S.finish(block)
        nc._n_instr = S.n_instr
    return nc


def make_in_maps(inputs):
    xp = np.ascontiguousarray(inputs["x_prompt"], dtype=np.float32)
    xs = np.ascontiguousarray(inputs["x_sample"], dtype=np.float32)
    maps = []
    for c in range(NCORES):
        m = {}
        m["x_core"] = np.ascontiguousarray(np.concatenate([xp[2 * c], xp[2 * c + 1], xs[c]], axis=0))
        m["conv0"] = np.ascontiguousarray(inputs["state_ssd_conv"][:, c])
        m["ssd0"] = np.ascontiguousarray(inputs["state_ssd"][:, c])
        m["s5r0"] = np.ascontiguousarray(inputs["state_s5_re"][:, c])
        m["s5i0"] = np.ascontiguousarray(inputs["state_s5_im"][:, c])
        for n in W_NAMES:
            m[n] = np.ascontiguousarray(inputs[n], dtype=np.float32)
        maps.append(m)
    return maps


_NC_CACHE = {}


def kernel(**inputs):
    inputs = {k: np.asarray(v) for k, v in inputs.items()}
    if "nc" not in _NC_CACHE:
        _NC_CACHE["nc"] = build_nc()
    nc = _NC_CACHE["nc"]
    maps = make_in_maps(inputs)
    res = run_bass_kernel_spmd(nc, maps, core_ids=list(range(NCORES)))
    R = res.results
    y_prompt = np.zeros((16, 2048, 1024), np.float32)
    y_sample = np.zeros((8, 64, 1024), np.float32)
    conv_p = np.zeros((4, 16, 3, 2048), np.float32)
    ssd_p = np.zeros((4, 16, 16, 64, 128), np.float32)
    s5r_p = np.zeros((4, 16, 64, 64), np.float32)
    s5i_p = np.zeros((4, 16, 64, 64), np.float32)
    conv_s = np.zeros((4, 8, 3, 2048), np.float32)
    ssd_s = np.zeros((4, 8, 16, 64, 128), np.float32)
    s5r_s = np.zeros((4, 8, 64, 64), np.float32)
    s5i_s = np.zeros((4, 8, 64, 64), np.float32)
    for c in range(NCORES):
        r = R[c]
        y = r["y_out"]
        y_prompt[2 * c] = y[0:2048]
        y_prompt[2 * c + 1] = y[2048:4096]
        y_sample[c] = y[4096:4160]
        for i in range(2):
            conv_p[:, 2 * c + i] = r["conv_o"][:, i]
            ssd_p[:, 2 * c + i] = r["ssd_o"][:, i]
            s5r_p[:, 2 * c + i] = r["s5r_o"][:, i]
            s5i_p[:, 2 * c + i] = r["s5i_o"][:, i]
        conv_s[:, c] = r["conv_o"][:, 2]
        ssd_s[:, c] = r["ssd_o"][:, 2]
        s5r_s[:, c] = r["s5r_o"][:, 2]
        s5i_s[:, c] = r["s5i_o"][:, 2]
    return (y_prompt, y_sample, conv_p, ssd_p, s5r_p, s5i_p, conv_s, ssd_s, s5r_s, s5i_s)
```

```python
import math
from contextlib import ExitStack

import numpy as np
import concourse.bass as bass
import concourse.mybir as mybir
from concourse.bass_utils import run_bass_kernel_spmd

F32 = mybir.dt.float32
BF16 = mybir.dt.bfloat16
I32 = mybir.dt.int32
AF = mybir.ActivationFunctionType
ALU = mybir.AluOpType

NCORES = 8
DEPTH = 4
T = 4160
NCH = T // 8
EPS = 1e-6
IN_COLS = 5136
C_ZA, C_XBC, C_DT, C_U, C_ZB = 0, 1024, 3072, 3088, 4112
TILES = [(s * 2048 + i * 512, 512, s, i == 0, i == 3) for s in range(2) for i in range(4)]
TILES.append((4096, 64, 2, True, True))
import os as _os
if _os.environ.get("K_TILES"):
    TILES = [TILES[int(i)] for i in _os.environ["K_TILES"].split(",")]
K_SKIP = set(_os.environ.get("K_SKIP", "").split(","))

ENGS = ("pe", "dve", "act", "pool", "sp")


class Sync:
    def __init__(self, nc, stack, n_dma_sems=40, same_engine_sync=True):
        self.nc = nc
        self.esem = {e: stack.enter_context(nc.semaphore("s_" + e)) for e in ENGS}
        self.cnt = {e: 0 for e in ENGS}
        self.prog = {e: [] for e in ENGS}
        self.waited = {e: {} for e in ENGS}
        self.res = {}
        self.same = same_engine_sync
        self.dsems = [stack.enter_context(nc.semaphore("s_dma%d" % i)) for i in range(n_dma_sems)]
        self.dval = [0] * n_dma_sems
        self.dnext = 0
        self.sems = {}
        for e in ENGS:
            self.sems[("e", e)] = self.esem[e]
        for i, s in enumerate(self.dsems):
            self.sems[("d", i)] = s
        self.n_instr = 0

    def _need(self, e, reads, writes):
        need = {}

        def add(ev):
            if ev is None:
                return
            k, v = ev
            if need.get(k, 0) < v:
                need[k] = v

        for r in reads:
            st = self.res.get(r)
            if st is not None:
                add(st[0])
        for w in writes:
            st = self.res.get(w)
            if st is not None:
                add(st[0])
                for ev in st[1]:
                    add(ev)
        out = []
        for k, v in need.items():
            if k == ("e", e) and (e == "pe" or not self.same):
                continue
            if self.waited[e].get(k, 0) >= v:
                continue
            self.waited[e][k] = v
            out.append((k, v))
        return out

    def _commit(self, ev, reads, writes):
        for r in reads:
            st = self.res.setdefault(r, [None, []])
            st[1].append(ev)
            if len(st[1]) > 16:
                mx = {}
                for k, v in st[1]:
                    if mx.get(k, 0) < v:
                        mx[k] = v
                st[1] = list(mx.items())
        for w in writes:
            self.res[w] = [ev, []]

    def op(self, e, fn, reads=(), writes=()):
        waits = self._need(e, reads, writes)
        self.cnt[e] += 1
        ev = (("e", e), self.cnt[e])
        sem = self.esem[e]
        sems = self.sems

        def emit(eng, waits=waits, fn=fn, sem=sem):
            for k, v in waits:
                eng.wait_ge(sems[k], v)
            fn(eng).then_inc(sem, 1)

        self.prog[e].append(emit)
        self._commit(ev, reads, writes)
        self.n_instr += 1
        return ev

    def dma(self, e, fn, reads=(), writes=()):
        i = self.dnext
        self.dnext = (self.dnext + 1) % len(self.dsems)
        k = ("d", i)
        waits = self._need(e, reads, writes)
        if self.dval[i] > 0 and self.waited[e].get(k, 0) < self.dval[i]:
            self.waited[e][k] = self.dval[i]
            waits.append((k, self.dval[i]))
        self.dval[i] += 16
        ev = (k, self.dval[i])
        sem = self.dsems[i]
        sems = self.sems

        def emit(eng, waits=waits, fn=fn, sem=sem):
            for kk, v in waits:
                eng.wait_ge(sems[kk], v)
            fn(eng).then_inc(sem, 16)

        self.prog[e].append(emit)
        self._commit(ev, reads, writes)
        self.n_instr += 1
        return ev

    def barrier(self):
        evs = [(("e", e), self.cnt[e]) for e in ENGS if self.cnt[e] > 0]
        evs += [(("d", i), v) for i, v in enumerate(self.dval) if v > 0]
        sems = self.sems
        for e in ENGS:
            waits = []
            for k, v in evs:
                if k == ("e", e):
                    continue
                if self.waited[e].get(k, 0) >= v:
                    continue
                self.waited[e][k] = v
                waits.append((k, v))

            def emit(eng, waits=waits):
                for kk, v in waits:
                    eng.wait_ge(sems[kk], v)

            self.prog[e].append(emit)
        self.res = {}

    def finish(self, block):
        self.barrier()
        prog = self.prog

        @block.tensor
        def _(eng):
            for f in prog["pe"]:
                f(eng)

        @block.vector
        def _(eng):
            for f in prog["dve"]:
                f(eng)

        @block.scalar
        def _(eng):
            for f in prog["act"]:
                f(eng)

        @block.gpsimd
        def _(eng):
            for f in prog["pool"]:
                f(eng)

        @block.sync
        def _(eng):
            for f in prog["sp"]:
                f(eng)


W_NAMES = ["norm_g", "w_in", "conv_w", "conv_b", "dt_bias", "a_log", "d_ssd", "ssd_norm_g",
           "lam_re", "lam_im", "log_dt", "b_re", "b_im", "c_re", "c_im", "d_s5", "w_glu", "b_glu",
           "w_out", "final_norm_g"]
W_SHAPES = {
    "norm_g": [4, 1024], "w_in": [4, 1024, 5136], "conv_w": [4, 4, 2048], "conv_b": [4, 2048],
    "dt_bias": [4, 16], "a_log": [4, 16], "d_ssd": [4, 16], "ssd_norm_g": [4, 1024],
    "lam_re": [4, 64, 64], "lam_im": [4, 64, 64], "log_dt": [4, 64], "b_re": [4, 64, 64, 16],
    "b_im": [4, 64, 64, 16], "c_re": [4, 64, 16, 64], "c_im": [4, 64, 16, 64], "d_s5": [4, 1024],
    "w_glu": [4, 1024, 1024], "b_glu": [4, 1024], "w_out": [4, 2048, 1024], "final_norm_g": [1024],
}


def build_nc(n_layers=DEPTH, dbg=False, phases=(1, 2, 3, 4, 5), do_final=True):
    nc = bass.Bass("TRN2", target_bir_lowering=False)
    skind = "ExternalOutput" if dbg else "Internal"

    def din(name, shape, dt=F32):
        return nc.dram_tensor(name, shape, dt, kind="ExternalInput").ap()

    def dout(name, shape, dt=F32):
        return nc.dram_tensor(name, shape, dt, kind="ExternalOutput").ap()

    def dscr(name, shape, dt=F32):
        return nc.dram_tensor(name, shape, dt, kind=skind).ap()

    x_in = din("x_core", [T, 1024])
    conv0 = din("conv0", [4, 3, 2048])
    ssd0 = din("ssd0", [4, 16, 64, 128])
    s5r0 = din("s5r0", [4, 64, 64])
    s5i0 = din("s5i0", [4, 64, 64])
    Wd = {n: din(n, W_SHAPES[n]) for n in W_NAMES}

    y_out = dout("y_out", [T, 1024])
    conv_o = dout("conv_o", [4, 3, 3, 2048])
    ssd_o = dout("ssd_o", [4, 3, 16, 64, 128])
    s5r_o = dout("s5r_o", [4, 3, 64, 64])
    s5i_o = dout("s5i_o", [4, 3, 64, 64])

    XT = dscr("XT", [8, 128, T])
    SZA = dscr("SZA", [8, 128, T], BF16)
    XBC = dscr("XBC", [16, 128, T])
    DTR = dscr("DTR", [T, 16])
    UD = dscr("UD", [8, 16, 8, 8, NCH], BF16)
    UT = dscr("UT", [8, 16, 8, T], BF16)
    SZB = dscr("SZB", [8, 128, T], BF16)
    YA = dscr("YA", [8, 128, T], BF16)
    YD = dscr("YD", [16, 8, 8 * 8, NCH]).rearrange("c g (j s) n -> c g j s n", s=8)
    YB = dscr("YB", [8, 128, T], BF16)

    with ExitStack() as st:
        S = Sync(nc, st)
        st.enter_context(nc.allow_non_contiguous_dma(reason="small strided parameter/state loads"))

        uniq = [0]

        def sb(stack, name, shape, dt):
            uniq[0] += 1
            return stack.enter_context(nc.sbuf_tensor("%s_%d" % (name, uniq[0]), shape, dt))

        def ps(stack, name, shape, dt=F32):
            uniq[0] += 1
            return stack.enter_context(nc.psum_tensor("%s_%d" % (name, uniq[0]), shape, dt))


        def TT(eng, out, in0, in1, op, reads, writes):
            return S.op(eng, lambda e: e.tensor_tensor(out=out, in0=in0, in1=in1, op=op), reads, writes)

        def TS(eng, out, in0, s1, s2, op0, op1, reads, writes):
            return S.op(eng, lambda e: e.tensor_scalar(out=out, in0=in0, scalar1=s1, scalar2=s2, op0=op0, op1=op1), reads, writes)

        def TSS(eng, out, in_, scalar, op, reads, writes):
            return S.op(eng, lambda e: e.tensor_single_scalar(out=out, in_=in_, scalar=scalar, op=op), reads, writes)

        def STT(out, in0, scalar, in1, op0, op1, reads, writes):
            return S.op("dve", lambda e: e.scalar_tensor_tensor(out=out, in0=in0, scalar=scalar, in1=in1, op0=op0, op1=op1), reads, writes)

        def ACT(out, in_, func, reads, writes, bias=None, scale=None):
            kw = {}
            if bias is not None:
                kw["bias"] = bias
            if scale is not None:
                kw["scale"] = scale
            return S.op("act", lambda e: e.activation(out=out, in_=in_, func=func, **kw), reads, writes)

        def CP(eng, out, in_, reads, writes):
            if eng == "act":
                return ACT(out, in_, AF.Copy, reads, writes)
            return S.op(eng, lambda e: e.tensor_copy(out=out, in_=in_), reads, writes)

        def MM(out, lhsT, rhs, start, stop, reads, writes):
            return S.op("pe", lambda e: e.matmul(out, lhsT=lhsT, rhs=rhs, start=start, stop=stop), reads, writes)

        def TR(out, in_, ident, reads, writes):
            return S.op("pe", lambda e: e.transpose(out=out, in_=in_, identity=ident), reads, writes)

        def DMA(out, in_, reads=(), writes=(), q="sp"):
            return S.dma(q, lambda e: e.dma_start(out=out, in_=in_), reads, writes)

        def MSET(eng, ap, val, writes):
            return S.op(eng, lambda e: e.memset(ap, val), (), writes)

        ones_f = sb(st, "ones_f", [128, 128], F32)
        ident_f = sb(st, "ident_f", [128, 128], F32)
        ident_b = sb(st, "ident_b", [128, 128], BF16)
        ones_b = sb(st, "ones_b", [128, 128], BF16)
        tri_f = sb(st, "tri_f", [128, 128], F32)
        tri_b = sb(st, "tri_b", [128, 128], BF16)
        bmask = sb(st, "bmask", [128, 128], F32)
        block = st.enter_context(nc.Block())

        S.op("pool", lambda e: e.memset(ones_f[:], 1.0), writes=["ones_f"])
        S.op("pool", lambda e: e.affine_select(out=ident_f[:], in_=ones_f[:], pattern=[[1, 128]],
                                               compare_op=ALU.is_equal, fill=0.0, base=0, channel_multiplier=-1),
             reads=["ones_f"], writes=["ident_f"])
        S.op("pool", lambda e: e.affine_select(out=tri_f[:], in_=ones_f[:], pattern=[[1, 128]],
                                               compare_op=ALU.is_ge, fill=0.0, base=0, channel_multiplier=-1),
             reads=["ones_f"], writes=["tri_f"])
        S.op("pool", lambda e: e.affine_select(out=bmask[:], in_=ones_f[:], pattern=[[16, 8], [0, 16]],
                                               compare_op=ALU.is_ge, fill=0.0, base=15, channel_multiplier=-1),
             reads=["ones_f"], writes=["bmask"])
        S.op("dve", lambda e: e.tensor_copy(out=ident_b[:], in_=ident_f[:]), reads=["ident_f"], writes=["ident_b"])
        S.op("dve", lambda e: e.tensor_copy(out=ones_b[:], in_=ones_f[:]), reads=["ones_f"], writes=["ones_b"])
        S.op("dve", lambda e: e.tensor_copy(out=tri_b[:], in_=tri_f[:]), reads=["tri_f"], writes=["tri_b"])
        S.barrier()

        def xt_ap(t0, W):
            return XT[:, :, t0:t0 + W].rearrange("k p t -> p k t")

        def rms_rstd(ph_ps, src_sq, W, scale, rs, rs2, key):
            nk = len(src_sq)
            for i, a in enumerate(src_sq):
                MM(ph_ps[:, :W], ones_b[:], a, i == 0, i == nk - 1, [key + "_sq"], [key + "_ms"])
            ACT(rs[:, :W], ph_ps[:, :W], AF.Ln, [key + "_ms"], [key + "_rs"], bias=EPS, scale=scale)
            ACT(rs2[:, :W], rs[:, :W], AF.Exp, [key + "_rs"], [key + "_rs2"], scale=-0.5)

        def phase0():
            with ExitStack() as ph:
                xs = sb(ph, "p0_xs", [128, 4, 1024], F32)
                xo = sb(ph, "p0_xo", [128, 8, 512], F32)
                tp = [ps(ph, "p0_tp%d" % i, [128, 512]) for i in range(2)]
                for (t0, W, seq, first, last) in TILES:
                    Pb = min(128, W)
                    nb = W // Pb
                    DMA(xs[:Pb, :nb, :], x_in[t0:t0 + W, :].rearrange("(b p) d -> p b d", p=Pb), [], ["xs"])
                    for kt in range(8):
                        tpk = tp[kt % 2]
                        for b in range(nb):
                            TR(tpk[:, b * Pb:(b + 1) * Pb], xs[:Pb, b, kt * 128:(kt + 1) * 128], ident_f[:Pb, :Pb], ["xs"], [("tp", kt % 2)])
                        CP("act" if kt % 2 else "dve", xo[:, kt, :W], tpk[:, :W], [("tp", kt % 2)], ["xo"])
                    DMA(xt_ap(t0, W), xo[:, :, :W], ["xo"], [("XT", t0)])
            S.barrier()

        def load_w_in(l, w_sb):
            for kt in range(8):
                for (c0, cw_) in ((0, 2048), (2048, 2048), (4096, 1040)):
                    DMA(w_sb[:, kt, c0:c0 + cw_], Wd["w_in"][l, kt * 128:(kt + 1) * 128, c0:c0 + cw_], [], ["w"], q="pool")

        def phase1(l, w_sb):
            with ExitStack() as ph:
                g1 = sb(ph, "p1_g", [128, 8], F32)
                xT = sb(ph, "p1_xT", [128, 8, 512], F32)
                sq = sb(ph, "p1_sq", [128, 8, 512], BF16)
                hT = sb(ph, "p1_hT", [128, 8, 512], BF16)
                rs = sb(ph, "p1_rs", [128, 512], F32)
                rs2 = sb(ph, "p1_rs2", [128, 512], F32)
                st_za = sb(ph, "p1_za", [128, 8, 512], BF16)
                st_xbc = [sb(ph, "p1_xbc%d" % i, [128, 2, 512], F32) for i in range(2)]
                st_us = sb(ph, "p1_us", [128, 8, 8, 64], BF16)
                st_ut = sb(ph, "p1_ut", [128, 8, 512], BF16)
                us32 = [sb(ph, "p1_us32_%d" % i, [128, 8, 64], F32) for i in range(2)]
                st_zb = sb(ph, "p1_zb", [128, 8, 512], BF16)
                st_dt = sb(ph, "p1_dt", [128, 4, 16], F32)
                mm = [ps(ph, "p1_mm%d" % i, [128, 512]) for i in range(4)]
                msp = ps(ph, "p1_ms", [128, 512])
                dtp = ps(ph, "p1_dtp", [128, 4, 16])
                DMA(g1[:], Wd["norm_g"][l].rearrange("(k p) -> p k", p=128), [], ["g1"])
                for kt in range(0 if "perm" in K_SKIP else 8):
                    tmpu = st_ut[:, (kt % 2) * 2:(kt % 2) * 2 + 2, :].rearrange("p a t -> p (a t)")
                    CP("dve", tmpu, w_sb[:, kt, C_U:C_U + 1024], ["w"], [("tmpu", kt % 2), "st_ut"])
                    CP("act", w_sb[:, kt, C_U:C_U + 1024].rearrange("p (j c g) -> p j c g", c=16, g=8),
                       tmpu.rearrange("p (j g c) -> p j c g", g=8, c=16), [("tmpu", kt % 2), "st_ut"], ["w"])
                mcount = [0]
                def pro_load(ti):
                    (t0_, W_, _, _, _) = TILES[ti]
                    DMA(xT[:, :, :W_], xt_ap(t0_, W_), [("XT", t0_)], ["xT"])

                def pro_sq(ti):
                    W_ = TILES[ti][1]
                    ACT(sq[:, :, :W_], xT[:, :, :W_], AF.Square, ["xT"], ["n1_sq"])

                def pro_ms(ti):
                    W_ = TILES[ti][1]
                    rms_rstd(msp, [sq[:, kt, :W_] for kt in range(8)], W_, 1.0 / 1024, rs, rs2, "n1")

                pro_load(0)
                pro_sq(0)
                pro_ms(0)
                for ti, (t0, W, seq, first, last) in enumerate(TILES):
                    Pb = min(128, W)
                    nb = W // Pb
                    nC = W // 8
                    n0 = t0 // 8
                    nxt = ti + 1 if ti + 1 < len(TILES) else None
                    for kt in range(8):
                        STT(hT[:, kt, :W], xT[:, kt, :W], g1[:, kt:kt + 1], rs2[:, :W], ALU.mult, ALU.mult, ["xT", "g1", "n1_rs2"], ["hT"])
                    if nxt is not None:
                        pro_load(nxt)

                    def mtile(lhs_fn, W=W):
                        bank = mm[mcount[0] % 4]
                        key = ("mm", mcount[0] % 4)
                        mcount[0] += 1
                        for kt in range(8):
                            MM(bank[:, :W], lhs_fn(kt), hT[:, kt, :W], kt == 0, kt == 7, ["w", "hT"], [key])
                        return bank, key

                    for m in range(0 if "za" in K_SKIP else 8):
                        bank, key = mtile(lambda kt, m=m: w_sb[:, kt, C_ZA + m * 128:C_ZA + (m + 1) * 128])
                        ACT(st_za[:, m, :W], bank[:, :W], AF.Silu, [key], ["st_za"])
                    if "za" not in K_SKIP:
                        DMA(SZA[:, :, t0:t0 + W].rearrange("m p t -> p m t"), st_za[:, :, :W], ["st_za"], [("SZA", t0)])
                    if nxt is not None:
                        pro_sq(nxt)
                    for m in range(0 if "xbc" in K_SKIP else 16):
                        bank, key = mtile(lambda kt, m=m: w_sb[:, kt, C_XBC + m * 128:C_XBC + (m + 1) * 128])
                        pr = m // 2
                        stx = st_xbc[pr % 2]
                        skey = ("st_xbc", pr % 2)
                        CP("act" if m % 2 == 0 else "dve", stx[:, m % 2, :W], bank[:, :W], [key], [skey])
                        if m % 2 == 1:
                            DMA(XBC[pr * 2:(pr + 1) * 2, :, t0:t0 + W].rearrange("m p t -> p m t"), stx[:, :, :W], [skey], [("XBC", t0, pr)])
                    if nxt is not None:
                        pro_ms(nxt)
                    for j in range(0 if "u" in K_SKIP else 8):
                        bank, key = mtile(lambda kt, j=j: w_sb[:, kt, C_U + j * 128:C_U + (j + 1) * 128])
                        u32 = us32[j % 2]
                        ACT(u32[:, :, :nC].rearrange("p s n -> p n s"), bank[:, :W].rearrange("p (n s) -> p n s", s=8), AF.Copy, [key], [("us32", j % 2), ("brd", key)])
                        CP("pool", st_us[:, j, :, :nC], u32[:, :, :nC], [("us32", j % 2)], ["st_us"])
                        CP("dve", st_ut[:, j, :W], bank[:, :W], [key, ("brd", key)], ["st_ut"])
                    for j in range(0 if "ud" in K_SKIP else 8):
                        DMA(UD[:, :, :, j, n0:n0 + nC].rearrange("s c g n -> (c g) s n"), st_us[:, j, :, :nC], ["st_us"], [("UD", t0)])
                    if "u" not in K_SKIP and "utdma" not in K_SKIP:
                        DMA(UT[:, :, :, t0:t0 + W].rearrange("j c g t -> (c g) j t"), st_ut[:, :, :W], ["st_ut"], [("UT", t0)])
                    for m in range(0 if "zb" in K_SKIP else 8):
                        bank, key = mtile(lambda kt, m=m: w_sb[:, kt, C_ZB + m * 128:C_ZB + (m + 1) * 128])
                        ACT(st_zb[:, m, :W], bank[:, :W], AF.Silu, [key], ["st_zb"])
                    if "zb" not in K_SKIP:
                        DMA(SZB[:, :, t0:t0 + W].rearrange("m p t -> p m t"), st_zb[:, :, :W], ["st_zb"], [("SZB", t0)])
                    for b in range(0 if "dt" in K_SKIP else nb):
                        for kt in range(8):
                            MM(dtp[:Pb, b, :], hT[:, kt, b * Pb:(b + 1) * Pb], w_sb[:, kt, C_DT:C_DT + 16], kt == 0, kt == 7, ["w", "hT"], ["dtp"])
                    if "dt" not in K_SKIP:
                        ACT(st_dt[:Pb, :nb, :], dtp[:Pb, :nb, :], AF.Copy, ["dtp"], ["st_dt"])
                        DMA(DTR[t0:t0 + W, :].rearrange("(b p) h -> p b h", p=Pb), st_dt[:Pb, :nb, :], ["st_dt"], [("DTR", t0)])
            S.barrier()

        def phase2(l):
            with ExitStack() as ph:
                cw = sb(ph, "p2_cw", [128, 16, 4], F32)
                cbias = sb(ph, "p2_cb", [128, 16], F32)
                dtb = sb(ph, "p2_dtb", [128, 16], F32)
                arow = sb(ph, "p2_arow", [128, 16], F32)
                dvec = sb(ph, "p2_dvec", [128, 8], F32)
                ng = sb(ph, "p2_ng", [128, 8], F32)
                tails = sb(ph, "p2_tails", [128, 16, 3], F32)
                hstate = sb(ph, "p2_hst", [128, 1024], F32)
                hbf = sb(ph, "p2_hbf", [128, 1024], BF16)
                hio = sb(ph, "p2_hio", [128, 8, 128], F32)
                raw = sb(ph, "p2_raw", [128, 4, 515], F32)
                rawb = sb(ph, "p2_rawb", [128, 4, 515], BF16)
                dw = sb(ph, "p2_dw", [128, 16, 4, 128], BF16)
                xc2 = [sb(ph, "p2_xc%d" % i, [128, 16, 512], BF16) for i in range(2)]
                sza2 = [sb(ph, "p2_sza%d" % i, [128, 8, 512], BF16) for i in range(2)]
                dtr = sb(ph, "p2_dtr", [128, 4, 16], F32)
                x1 = sb(ph, "p2_x1", [128, 4, 16], F32)
                tA = sb(ph, "p2_tA", [128, 4, 16], F32)
                tB = sb(ph, "p2_tB", [128, 4, 16], F32)
                dtv2 = [sb(ph, "p2_dtv%d" % i, [128, 4, 16], F32) for i in range(2)]
                av = sb(ph, "p2_av", [128, 4, 16], F32)
                a_hi2 = [sb(ph, "p2_ahi%d" % i, [128, 4, 16], BF16) for i in range(2)]
                a_lo2 = [sb(ph, "p2_alo%d" % i, [128, 4, 16], BF16) for i in range(2)]
                xdt2 = [sb(ph, "p2_xdt%d" % i, [128, 1024], BF16) for i in range(2)]
                xe2 = [sb(ph, "p2_xe%d" % i, [128, 1024], BF16) for i in range(2)]
                btok2 = [sb(ph, "p2_btok%d" % i, [128, 512], BF16) for i in range(2)]
                acs2 = [sb(ph, "p2_acs%d" % i, [128, 16], F32) for i in range(2)]
                nacs2 = [sb(ph, "p2_nacs%d" % i, [128, 16], F32) for i in range(2)]
                tmd2 = [sb(ph, "p2_tmd%d" % i, [128, 16], F32) for i in range(2)]
                te2 = [sb(ph, "p2_te%d" % i, [128, 16], F32) for i in range(2)]
                etot2 = [sb(ph, "p2_etot%d" % i, [128, 16], F32) for i in range(2)]
                cbm2 = [sb(ph, "p2_cbm%d" % i, [128, 4, 128], BF16) for i in range(2)]
                Mh4 = [[sb(ph, "p2_Mh%d_%d" % (i, g), [128, 512], BF16) for g in range(4)] for i in range(2)]
                Cs4 = [[sb(ph, "p2_Cs%d_%d" % (i, g), [128, 512], BF16) for g in range(4)] for i in range(2)]
                E = [sb(ph, "p2_E%d" % i, [128, 512], BF16) for i in range(2)]
                dec = [sb(ph, "p2_dec%d" % i, [128, 512], BF16) for i in range(2)]
                yg = sb(ph, "p2_yg", [128, 8, 512], F32)
                sq = sb(ph, "p2_sq", [128, 8, 512], BF16)
                rs4 = [sb(ph, "p2_rs4_%d" % i, [128, 512], F32) for i in range(4)]
                htmp = sb(ph, "p2_htmp", [128, 512], F32)
                ya = sb(ph, "p2_ya", [128, 8, 512], BF16)
                smallp = ps(ph, "p2_small", [128, 512])
                tpp = ps(ph, "p2_tp", [128, 1024], BF16)
                cbp = ps(ph, "p2_cbp", [128, 512])
                acsb = [ps(ph, "p2_acsb%d" % i, [128, 512]) for i in range(2)]
                yps = [ps(ph, "p2_yps%d" % i, [128, 512]) for i in range(2)]
                hps = ps(ph, "p2_hps", [128, 512])

                for k in range(4):
                    DMA(cw[:, :, k], Wd["conv_w"][l, k].rearrange("(m p) -> p m", p=128), [], ["cw"])
                DMA(cbias[:], Wd["conv_b"][l].rearrange("(m p) -> p m", p=128), [], ["cbias"])
                DMA(dtb[:], Wd["dt_bias"][l].partition_broadcast(128), [], ["dtb"])
                DMA(arow[:], Wd["a_log"][l].partition_broadcast(128), [], ["arow"])
                for h2 in range(2):
                    DMA(dvec[h2 * 64:(h2 + 1) * 64, :], Wd["d_ssd"][l].rearrange("(m h) -> h m", h=2)[h2].partition_broadcast(64), [], ["dvec"])
                DMA(ng[:], Wd["ssd_norm_g"][l].rearrange("(k p) -> p k", p=128), [], ["ng"])
                for m in range(16):
                    for k in range(4):
                        TS("pool" if (m * 4 + k) % 3 == 0 else "dve", dw[:, m, k, :], ident_b[:], cw[:, m, k:k + 1], None, ALU.mult, ALU.bypass, ["cw", "ident_b"], ["dw"])
                ACT(arow[:], arow[:], AF.Exp, ["arow"], ["arow"])
                TSS("dve", arow[:], arow[:], -1.0, ALU.mult, ["arow"], ["arow"])

                def seq_state_init(ti):
                    (t0, W, seq, first, last) = TILES[ti]
                    if not first:
                        return
                    if seq < 2:
                        MSET("pool", hstate[:], 0.0, ["hstate"])
                        MSET("pool", hbf[:], 0.0, ["hbf"])
                    else:
                        DMA(hio[:], ssd0[l].rearrange("(m h) p n -> (h p) m n", h=2), [], ["hio"])
                        for m in range(8):
                            TR(hps[:, (m % 4) * 128:(m % 4 + 1) * 128], hio[:, m, :], ident_f[:], ["hio"], ["hps"])
                            if m % 4 == 3:
                                hh = m // 4
                                CP("dve", hstate[:, hh * 512:(hh + 1) * 512], hps[:], ["hps"], ["hstate"])
                        CP("act", hbf[:], hstate[:], ["hstate"], ["hbf"])

                def conv_blk(ti, gq):
                    (t0, W, seq, first, last) = TILES[ti]
                    Qc = min(128, W)
                    nck = W // Qc
                    tp_ = ti % 2
                    xc, sza, dtv, a_hi, a_lo = xc2[tp_], sza2[tp_], dtv2[tp_], a_hi2[tp_], a_lo2[tp_]
                    if gq == 0 and first:
                        if seq < 2:
                            MSET("pool", tails[:], 0.0, ["tails"])
                        else:
                            for k in range(3):
                                DMA(tails[:, :, k], conv0[l, k].rearrange("(m p) -> p m", p=128), [], ["tails"])
                    DMA(raw[:, :, 3:3 + W], XBC[gq * 4:(gq + 1) * 4, :, t0:t0 + W].rearrange("m p t -> p m t"),
                        [("XBC", t0, 2 * gq), ("XBC", t0, 2 * gq + 1)], ["raw"])
                    CP("dve", raw[:, :, 0:3], tails[:, gq * 4:(gq + 1) * 4, :], ["tails"], ["raw"])
                    ACT(rawb[:, :, 0:3 + W], raw[:, :, 0:3 + W], AF.Copy, ["raw"], ["rawb"])
                    for mi in range(4):
                        m = gq * 4 + mi
                        bnk, bkey = ((cbp, "cbp"), (hps, "hps"))[mi % 2]
                        for k in range(4):
                            MM(bnk[:, :W], dw[:, m, k, :], rawb[:, mi, k:k + W], k == 0, k == 3, ["rawb", "dw"], [bkey])
                        ACT(xc[:, m, :W], bnk[:, :W], AF.Silu, [bkey, "cbias"], [("xc", tp_, m)], bias=cbias[:, m:m + 1], scale=1.0)
                    CP("dve", tails[:, gq * 4:(gq + 1) * 4, :], raw[:, :, W:W + 3], ["raw"], ["tails"])
                    if gq == 3 and last:
                        for k in range(3):
                            DMA(conv_o[l, seq, k].rearrange("(m p) -> p m", p=128), tails[:, :, k], ["tails"], [])

                def pre(ti):
                    (t0, W, seq, first, last) = TILES[ti]
                    Qc = min(128, W)
                    nck = W // Qc
                    tp_ = ti % 2
                    xc, sza, dtv, a_hi, a_lo = xc2[tp_], sza2[tp_], dtv2[tp_], a_hi2[tp_], a_lo2[tp_]
                    DMA(sza[:, :, :W], SZA[:, :, t0:t0 + W].rearrange("m p t -> p m t"), [("SZA", t0)], [("sza", tp_)])
                    DMA(dtr[:Qc, :nck, :], DTR[t0:t0 + W, :].rearrange("(b p) h -> p b h", p=Qc), [("DTR", t0)], ["dtr"])
                    sl = lambda tl: tl[:Qc, :nck, :]
                    bc = lambda tl: tl[:Qc, :].unsqueeze(1).to_broadcast([Qc, nck, 16])
                    TT("dve", sl(x1), sl(dtr), bc(dtb), ALU.add, ["dtr", "dtb"], ["x1"])
                    STT(sl(tA), sl(x1), -1.0, sl(x1), ALU.mult, ALU.max, ["x1"], ["tA"])
                    ACT(sl(tA), sl(tA), AF.Exp, ["tA"], ["tA"], scale=-1.0)
                    ACT(sl(tA), sl(tA), AF.Ln, ["tA"], ["tA"], bias=1.0, scale=1.0)
                    TSS("dve", sl(tB), sl(x1), 0.0, ALU.max, ["x1"], ["tB"])
                    TT("dve", sl(dtv), sl(tA), sl(tB), ALU.add, ["tA", "tB"], [("dtv", tp_)])
                    TT("dve", sl(av), sl(dtv), bc(arow), ALU.mult, [("dtv", tp_), "arow"], ["av"])
                    CP("dve", sl(a_hi), sl(av), ["av"], [("a_hi", tp_)])
                    TT("dve", sl(a_lo), sl(av), sl(a_hi), ALU.subtract, ["av", ("a_hi", tp_)], [("a_lo", tp_)])

                def chunks(ti, hook):
                    (t0, W, seq, first, last) = TILES[ti]
                    Qc = min(128, W)
                    nck = W // Qc
                    tp_ = ti % 2
                    xc, sza, dtv, a_hi, a_lo = xc2[tp_], sza2[tp_], dtv2[tp_], a_hi2[tp_], a_lo2[tp_]
                    hd = lambda ap: ap.rearrange("p (h d) -> p h d", d=64)

                    def front_pre(c):
                        cp = c % 2
                        cs = c * Qc
                        xdt, xe, btok, acs, tmd, te, etot, cbm = xdt2[cp], xe2[cp], btok2[cp], acs2[cp], tmd2[cp], te2[cp], etot2[cp], cbm2[cp]
                        k = lambda n: (n, cp)
                        for mq in range(2):
                            for mi in range(4):
                                TR(tpp[:Qc, mq * 512 + mi * 128:mq * 512 + (mi + 1) * 128], xc[:, mq * 4 + mi, cs:cs + Qc], ident_b[:],
                                   [("xc", tp_, mq * 4 + mi)], ["tp"])
                            TT("dve", hd(xdt[:Qc, mq * 512:(mq + 1) * 512]), hd(tpp[:Qc, mq * 512:(mq + 1) * 512]),
                               dtv[:Qc, c, mq * 8:(mq + 1) * 8].unsqueeze(2).to_broadcast([Qc, 8, 64]), ALU.mult, ["tp", ("dtv", tp_)], [k("xdt%d" % mq)])
                        for g in range(4):
                            TR(tpp[:Qc, g * 128:(g + 1) * 128], xc[:, 8 + g, cs:cs + Qc], ident_b[:], [("xc", tp_, 8 + g)], ["tp"])
                        CP("act", btok[:Qc, :], tpp[:Qc, 0:512], ["tp"], [k("btok")])
                        for i, aa in enumerate((a_hi, a_lo)):
                            MM(smallp[:Qc, 0:16], tri_b[:Qc, :Qc], aa[:Qc, c, :], i == 0, i == 1, [("a_hi", tp_), ("a_lo", tp_), "tri_b"], ["small"])
                        for i, aa in enumerate((a_hi, a_lo)):
                            MM(smallp[:, 16:32], ones_b[:Qc, :], aa[:Qc, c, :], i == 0, i == 1, [("a_hi", tp_), ("a_lo", tp_)], ["small"])
                        CP("act", acs[:Qc, :], smallp[:Qc, 0:16], ["small"], [k("acs")])
                        ACT(nacs2[cp][:Qc, :], smallp[:Qc, 0:16], AF.Copy, ["small"], [k("nacs")], scale=-1.0)
                        ACT(etot[:], smallp[:, 16:32], AF.Exp, ["small"], [k("etot")])
                        TT("dve", tmd[:Qc, :], smallp[:Qc, 16:32], acs[:Qc, :], ALU.subtract, ["small", k("acs"), k("etot")], [k("tmd")])
                        ACT(te[:Qc, :], tmd[:Qc, :], AF.Exp, [k("tmd")], [k("te")])
                        TT("pool", hd(xe[:Qc, :]), hd(xdt[:Qc, :]), te[:Qc, :].unsqueeze(2).to_broadcast([Qc, 16, 64]), ALU.mult,
                           [k("xdt0"), k("xdt1"), k("te")], [k("xe")])
                        for g in range(4):
                            MM(cbp[:Qc, g * 128:g * 128 + Qc], xc[:, 8 + g, cs:cs + Qc], xc[:, 12 + g, cs:cs + Qc], True, True,
                               [("xc", tp_, 8 + g), ("xc", tp_, 12 + g)], ["cbp"])
                        TT("dve", cbm[:Qc, :, :Qc], cbp[:Qc, :].rearrange("p (g l) -> p g l", g=4)[:, :, :Qc],
                           tri_b[:Qc, :Qc].unsqueeze(1).to_broadcast([Qc, 4, Qc]), ALU.mult, ["cbp", "tri_b"], [k("cbm")])

                    def front_grp(c, g):
                        cp = c % 2
                        cs = c * Qc
                        par = g % 2
                        acs, cbm = acs2[cp], cbm2[cp]
                        k = lambda n: (n, cp)
                        ab = acsb[par]
                        abv = ab[:].rearrange("p (h l) -> p h l", h=4)
                        for hh in range(4):
                            h = g * 4 + hh
                            for i, aa in enumerate((a_hi, a_lo)):
                                MM(abv[:, hh, :Qc], aa[:Qc, c, h:h + 1].to_broadcast([Qc, 128]), tri_b[:Qc, :Qc], i == 0, i == 1,
                                   [("a_hi", tp_), ("a_lo", tp_)], [("acsb", par)])
                        Ev = E[par][:].rearrange("p (h l) -> p h l", h=4)
                        ACT(Ev[:, :, :Qc], abv[:, :, :Qc], AF.Exp, [("acsb", par)], [("E", par)])
                        dv = dec[par][:].rearrange("p (h l) -> p h l", h=4)
                        for hh in range(4):
                            h = g * 4 + hh
                            ACT(dv[:Qc, hh, :Qc], abv[:Qc, hh, :Qc], AF.Exp, [("acsb", par), k("nacs")], [("dec", par)], bias=nacs2[cp][:Qc, h:h + 1], scale=1.0)
                        Mv = Mh4[cp][g][:].rearrange("p (h l) -> p h l", h=4)
                        STT(Mv[:Qc, :, :Qc], dv[:Qc, :, :Qc], 1.0, cbm[:Qc, g, :Qc].unsqueeze(1).to_broadcast([Qc, 4, Qc]), ALU.min, ALU.mult,
                            [("dec", par), k("cbm")], [("Mh", cp, g)])
                        Cv = Cs4[cp][g][:].rearrange("p (h l) -> p h l", h=4)
                        TT("pool", Cv[:, :, :Qc], xc[:, 12 + g, cs:cs + Qc].unsqueeze(1).to_broadcast([128, 4, Qc]), Ev[:, :, :Qc], ALU.mult,
                           [("xc", tp_, 12 + g), ("E", par)], [("Cs", cp, g)])

                    def back_grp(c, g):
                        cp = c % 2
                        cs = c * Qc
                        par = g % 2
                        xdt = xdt2[cp]
                        k = lambda n: (n, cp)
                        Mv = Mh4[cp][g][:].rearrange("p (h l) -> p h l", h=4)
                        Cv = Cs4[cp][g][:].rearrange("p (h l) -> p h l", h=4)
                        yp = yps[par]
                        ykey = ("yps", par)
                        for hh in range(4):
                            h = g * 4 + hh
                            o = yp[(hh % 2) * 64:(hh % 2 + 1) * 64, (hh // 2) * 128:(hh // 2) * 128 + Qc]
                            MM(o, xdt[:Qc, h * 64:(h + 1) * 64], Mv[:Qc, hh, :Qc], True, False, [k("xdt%d" % (h // 8)), ("Mh", cp, g)], [ykey])
                            MM(o, hbf[:, h * 64:(h + 1) * 64], Cv[:, hh, :Qc], False, True, ["hbf", ("Cs", cp, g)], [ykey])
                        for mm_ in range(2):
                            m = 2 * g + mm_
                            STT(yg[:, m, cs:cs + Qc], xc[:, m, cs:cs + Qc], dvec[:, m:m + 1], yp[:, mm_ * 128:mm_ * 128 + Qc], ALU.mult, ALU.add,
                                [("xc", tp_, m), "dvec", ykey], ["yg"])

                    def back_post(c):
                        cp = c % 2
                        xe, btok, etot = xe2[cp], btok2[cp], etot2[cp]
                        k = lambda n: (n, cp)
                        for half in range(2):
                            for gi in range(2):
                                g = half * 2 + gi
                                MM(hps[:, gi * 256:(gi + 1) * 256], btok[:Qc, g * 128:(g + 1) * 128], xe[:Qc, g * 256:(g + 1) * 256], True, True,
                                   [k("btok"), k("xe")], ["hps"])
                            TT("pool", hd(htmp[:]), hd(hstate[:, half * 512:(half + 1) * 512]),
                               etot[:, half * 8:(half + 1) * 8].unsqueeze(2).to_broadcast([128, 8, 64]), ALU.mult, ["hstate", k("etot")], ["htmp"])
                            TT("dve", hstate[:, half * 512:(half + 1) * 512], htmp[:], hps[:], ALU.add, ["htmp", "hps"], ["hstate"])
                        CP("act", hbf[:], hstate[:], ["hstate"], ["hbf"])

                    front_pre(0)
                    for g in range(4):
                        front_grp(0, g)
                    for c in range(nck):
                        more = c + 1 < nck
                        if more:
                            front_pre(c + 1)
                        for g in range(4):
                            if more:
                                front_grp(c + 1, g)
                            back_grp(c, g)
                        back_post(c)
                        hook(c, nck)

                def post(ti):
                    (t0, W, seq, first, last) = TILES[ti]
                    Qc = min(128, W)
                    nck = W // Qc
                    tp_ = ti % 2
                    xc, sza, dtv, a_hi, a_lo = xc2[tp_], sza2[tp_], dtv2[tp_], a_hi2[tp_], a_lo2[tp_]
                    TT("pool", yg[:, 0:3, :W], yg[:, 0:3, :W], sza[:, 0:3, :W], ALU.mult, ["yg", ("sza", tp_)], ["yg"])
                    TT("dve", yg[:, 3:8, :W], yg[:, 3:8, :W], sza[:, 3:8, :W], ALU.mult, ["yg", ("sza", tp_)], ["yg"])
                    ACT(sq[:, 3:8, :W], yg[:, 3:8, :W], AF.Square, ["yg"], ["sq_b"])
                    ACT(sq[:, 0:3, :W], yg[:, 0:3, :W], AF.Square, ["yg"], ["sq_a"])
                    nbanks = [(cbp, "cbp"), (acsb[0], ("acsb", 0)), (acsb[1], ("acsb", 1)), (hps, "hps")]
                    for gg in (3, 2, 1, 0):
                        bnk, bkey = nbanks[gg]
                        for i in range(2):
                            MM(bnk[:, :W], ones_b[:], sq[:, 2 * gg + i, :W], i == 0, i == 1, ["sq_a", "sq_b"], [bkey])
                    for gg in (3, 2, 1, 0):
                        bnk, bkey = nbanks[gg]
                        ACT(rs4[gg][:, :W], bnk[:, :W], AF.Ln, [bkey], [("rs4", gg)], bias=EPS, scale=1.0 / 256)
                    for gg in (3, 2, 1, 0):
                        ACT(rs4[gg][:, :W], rs4[gg][:, :W], AF.Exp, [("rs4", gg)], [("rs4", gg)], scale=-0.5)
                    for gg in (3, 2, 1, 0):
                        for m in (2 * gg, 2 * gg + 1):
                            STT(ya[:, m, :W], yg[:, m, :W], ng[:, m:m + 1], rs4[gg][:, :W], ALU.mult, ALU.mult, ["yg", "ng", ("rs4", gg)], ["ya"])
                    DMA(YA[:, :, t0:t0 + W].rearrange("m p t -> p m t"), ya[:, :, :W], ["ya"], [("YA", t0)])
                    if last:
                        for m in range(8):
                            TR(hps[:, (m % 4) * 128:(m % 4 + 1) * 128], hstate[:, m * 128:(m + 1) * 128], ident_f[:], ["hstate"], ["hps"])
                            if m % 4 == 3:
                                hh = m // 4
                                CP("dve", hio[:, hh * 4:(hh + 1) * 4, :], hps[:].rearrange("p (m n) -> p m n", n=128), ["hps"], ["hio"])
                        DMA(ssd_o[l, seq].rearrange("(m h) p n -> (h p) m n", h=2), hio[:], ["hio"], [])

                for gq in range(4):
                    conv_blk(0, gq)
                pre(0)
                for ti in range(len(TILES)):
                    nxt = ti + 1 if ti + 1 < len(TILES) else None

                    def hook(c, nck, nxt=nxt):
                        if nxt is None:
                            return
                        per = 4 // nck if nck <= 4 else 1
                        for gq in range(c * per, (c + 1) * per):
                            conv_blk(nxt, gq)
                        if c == nck - 1:
                            pre(nxt)

                    seq_state_init(ti)
                    chunks(ti, hook)
                    post(ti)
            S.barrier()

        def phase3(l):
            with ExitStack() as ph:
                W_sb = sb(ph, "p3_W", [128, 64, 128], BF16)
                T0_sb = sb(ph, "p3_T0", [128, 64, 128], BF16)
                Cm_sb = sb(ph, "p3_Cm", [128, 32, 2, 128], BF16)
                L8 = sb(ph, "p3_L8", [128, 4, 32], F32)
                Hcar = sb(ph, "p3_Hcar", [128, 2, 32], F32)
                pa = ps(ph, "p3_pa", [128, 512])
                pb = ps(ph, "p3_pb", [128, 512])
                pc = ps(ph, "p3_pc", [128, 512])
                pd = ps(ph, "p3_pd", [128, 512])
                with ExitStack() as pp:
                    f64 = lambda nm, shp, dt=F32: sb(pp, nm, shp, dt)
                    lamr, lami, dtg, zr, zi = [f64("q_" + n, [64, 64]) for n in ("lamr", "lami", "dtg", "zr", "zi")]
                    kr, ki, t1, t2, den = [f64("q_" + n, [64, 64]) for n in ("kr", "ki", "t1", "t2", "den")]
                    erow_i = f64("q_erowi", [64, 25], I32)
                    erow = f64("q_erow", [64, 25])
                    ang, mag, s4, s2, Pr, Pi = [f64("q_" + n, [64, 64, 25]) for n in ("ang", "mag", "s4", "s2", "Pr", "Pi")]
                    kqi = f64("q_kqi", [64, 64, 25], I32)
                    PKr, PKi, u1 = [f64("q_" + n, [64, 64, 8]) for n in ("PKr", "PKi", "u1")]
                    Btr, Bti, Ctr, Cti = [f64("q_" + n, [64, 64, 16]) for n in ("Btr", "Bti", "Ctr", "Cti")]
                    Cn = sb(pp, "q_Cn", [128, 8, 64], F32)
                    Wtr, Wti, v1, v2, Cxr, Cxn = [f64("q_" + n, [64, 8, 8, 16]) for n in ("Wtr", "Wti", "v1", "v2", "Cxr", "Cxn")]

                    DMA(lamr[:], Wd["lam_re"][l].rearrange("g p -> p g"), [], ["lamr"])
                    DMA(lami[:], Wd["lam_im"][l].rearrange("g p -> p g"), [], ["lami"])
                    DMA(dtg[:], Wd["log_dt"][l].partition_broadcast(64), [], ["dtg"])
                    DMA(Btr[:], Wd["b_re"][l].rearrange("g p c -> p g c"), [], ["Btr"])
                    DMA(Bti[:], Wd["b_im"][l].rearrange("g p c -> p g c"), [], ["Bti"])
                    for (src, dst, nm) in ((Wd["c_re"], Ctr, "Ctr"), (Wd["c_im"], Cti, "Cti")):
                        DMA(Cn[:], src[l].rearrange("(j g) c p -> (g c) j p", g=8), [], ["Cn"])
                        for j in range(8):
                            TR(pa[:64, (j % 4) * 128:(j % 4 + 1) * 128], Cn[:, j, :], ident_f[:], ["Cn"], ["pa"])
                            if j % 4 == 3:
                                jj = j // 4
                                CP("dve", dst[:, jj * 32:(jj + 1) * 32, :], pa[:64, :].rearrange("p (g c) -> p g c", c=16), ["pa"], [nm])
                    ACT(dtg[:], dtg[:], AF.Exp, ["dtg"], ["dtg"])
                    TT("dve", zr[:], lamr[:], dtg[:], ALU.mult, ["lamr", "dtg"], ["zr"])
                    TT("dve", zi[:], lami[:], dtg[:], ALU.mult, ["lami", "dtg"], ["zi"])
                    for (a, b_, pat, base) in ((0, 8, [[-1, 8]], 7), (8, 16, [[1, 8]], 1), (16, 24, [[1, 8]], -7), (24, 25, [[1, 1]], 8)):
                        oap = erow_i[:, a:b_]
                        S.op("pool", lambda e, oap=oap, pat=pat, base=base: e.iota(out=oap, pattern=pat, base=base, channel_multiplier=0), (), ["erow_i"])
                    CP("dve", erow[:], erow_i[:], ["erow_i"], ["erow"])
                    ebc = erow[:, :].unsqueeze(1).to_broadcast([64, 64, 25])
                    gbc = lambda tl: tl[:, :].unsqueeze(2).to_broadcast([64, 64, 25])
                    fl = lambda tl: tl[:].rearrange("p g m -> p (g m)")
                    TT("dve", ang[:], gbc(zi), ebc, ALU.mult, ["zi", "erow"], ["ang"])
                    TT("dve", mag[:], gbc(zr), ebc, ALU.mult, ["zr", "erow"], ["mag"])
                    ACT(mag[:], mag[:], AF.Exp, ["mag"], ["mag"])
                    TSS("dve", s4[:], ang[:], 1.0 / (2 * math.pi), ALU.mult, ["ang"], ["s4"])
                    CP("dve", kqi[:], s4[:], ["s4"], ["kqi"])
                    CP("dve", s4[:], kqi[:], ["kqi"], ["s4"])
                    STT(fl(ang), fl(s4), -2 * math.pi, fl(ang), ALU.mult, ALU.add, ["s4", "ang"], ["ang"])
                    ACT(s4[:], ang[:], AF.Sin, ["ang"], ["s4"], scale=0.25)
                    ACT(s2[:], ang[:], AF.Sin, ["ang"], ["s2"], scale=0.5)
                    TT("dve", s4[:], s4[:], s4[:], ALU.mult, ["s4"], ["s4"])
                    TS("dve", s4[:], s4[:], -2.0, 1.0, ALU.mult, ALU.add, ["s4"], ["s4"])
                    TT("dve", Pi[:], s2[:], s4[:], ALU.mult, ["s2", "s4"], ["Pi"])
                    STT(fl(Pi), fl(Pi), 2.0, fl(mag), ALU.mult, ALU.mult, ["Pi", "mag"], ["Pi"])
                    TT("dve", s2[:], s2[:], s2[:], ALU.mult, ["s2"], ["s2"])
                    TS("dve", s2[:], s2[:], -2.0, 1.0, ALU.mult, ALU.add, ["s2"], ["s2"])
                    TT("dve", Pr[:], s2[:], mag[:], ALU.mult, ["s2", "mag"], ["Pr"])
                    TSS("dve", t1[:], Pr[:, :, 6], -1.0, ALU.add, ["Pr"], ["t1"])
                    TT("dve", den[:], lamr[:], lamr[:], ALU.mult, ["lamr"], ["den"])
                    TT("dve", t2[:], lami[:], lami[:], ALU.mult, ["lami"], ["t2"])
                    TT("dve", den[:], den[:], t2[:], ALU.add, ["den", "t2"], ["den"])
                    S.op("dve", lambda e: e.reciprocal(out=den[:], in_=den[:]), ["den"], ["den"])
                    TT("dve", kr[:], t1[:], lamr[:], ALU.mult, ["t1", "lamr"], ["kr"])
                    TT("dve", t2[:], Pi[:, :, 6], lami[:], ALU.mult, ["Pi", "lami"], ["t2"])
                    TT("dve", kr[:], kr[:], t2[:], ALU.add, ["kr", "t2"], ["kr"])
                    TT("dve", kr[:], kr[:], den[:], ALU.mult, ["kr", "den"], ["kr"])
                    TT("dve", ki[:], Pi[:, :, 6], lamr[:], ALU.mult, ["Pi", "lamr"], ["ki"])
                    TT("dve", t2[:], t1[:], lami[:], ALU.mult, ["t1", "lami"], ["t2"])
                    TT("dve", ki[:], ki[:], t2[:], ALU.subtract, ["ki", "t2"], ["ki"])
                    TT("dve", ki[:], ki[:], den[:], ALU.mult, ["ki", "den"], ["ki"])
                    k8 = lambda tl: tl[:, :].unsqueeze(2).to_broadcast([64, 64, 8])
                    TT("dve", PKr[:], Pr[:, :, 0:8], k8(kr), ALU.mult, ["Pr", "kr"], ["PKr"])
                    TT("dve", u1[:], Pi[:, :, 0:8], k8(ki), ALU.mult, ["Pi", "ki"], ["u1"])
                    TT("dve", PKr[:], PKr[:], u1[:], ALU.subtract, ["PKr", "u1"], ["PKr"])
                    TT("dve", PKi[:], Pr[:, :, 0:8], k8(ki), ALU.mult, ["Pr", "ki"], ["PKi"])
                    TT("dve", u1[:], Pi[:, :, 0:8], k8(kr), ALU.mult, ["Pi", "kr"], ["u1"])
                    TT("dve", PKi[:], PKi[:], u1[:], ALU.add, ["PKi", "u1"], ["PKi"])
                    for (ti, src, sc) in ((0, Pr, 1.0), (1, Pi, 1.0), (2, Pi, -1.0), (3, mag, 1.0)):
                        for gh in range(2):
                            ACT(L8[gh * 64:(gh + 1) * 64, ti, :].rearrange("p (g j) -> p g j", j=8),
                                src[:, :, 24].rearrange("p (j g) -> p g j", g=8)[:, gh * 4:(gh + 1) * 4, :], AF.Copy, ["Pr", "Pi", "mag"], ["L8"], scale=sc)

                    def cmul(outr, outi, PA_r, PA_i, Xr, Xi, jb, key, rd, neg):
                        pbc = lambda P_: P_.unsqueeze(3).to_broadcast([64, 8, 8, 16])
                        xbc = lambda X_: X_[:, jb * 8:(jb + 1) * 8, :].unsqueeze(2).to_broadcast([64, 8, 8, 16])
                        TT("dve", outr[:], pbc(PA_r), xbc(Xr), ALU.mult, rd, [key + "r"])
                        TT("pool", v1[:], pbc(PA_i), xbc(Xi), ALU.mult, rd, ["v1"])
                        TT("dve", outr[:], outr[:], v1[:], ALU.subtract, [key + "r", "v1"], [key + "r"])
                        TT("dve", outi[:], pbc(PA_r), xbc(Xi), ALU.mult, rd, [key + "i"])
                        TT("pool", v2[:], pbc(PA_i), xbc(Xr), ALU.mult, rd, ["v2"])
                        TT("dve", outi[:], outi[:], v2[:], ALU.add, [key + "i", "v2"], [key + "i"])
                        if neg:
                            TSS("dve", outi[:], outi[:], -1.0, ALU.mult, [key + "i"], [key + "i"])

                    base_rd = ["Pr", "Pi", "PKr", "PKi", "Btr", "Bti", "Ctr", "Cti"]
                    sc_ = lambda ap: ap.rearrange("p s c -> p (s c)")
                    for jb in range(8):
                        gs = slice(jb * 8, (jb + 1) * 8)
                        cmul(Wtr, Wti, PKr[:, gs, :], PKi[:, gs, :], Btr, Bti, jb, "Wt", base_rd, False)
                        for (ri, src, nm) in ((0, Wtr, "Wtr"), (1, Wti, "Wti")):
                            for g8 in range(8):
                                bank, bk = (pa, "pa") if g8 < 4 else (pb, "pb")
                                TR(bank[:, (g8 % 4) * 64:(g8 % 4 + 1) * 64], sc_(src[:, g8, :, :]), ident_f[:64, :64], [nm], [bk])
                            for hb, (bank, bk) in enumerate(((pa, "pa"), (pb, "pb"))):
                                ACT(W_sb[:, :, ri * 64:(ri + 1) * 64].rearrange("p (g j) x -> p g j x", j=8)[:, hb * 4:(hb + 1) * 4, jb, :],
                                    bank[:, 0:256].rearrange("p (g x) -> p g x", x=64), AF.Copy, [bk], ["W_sb"])
                        cmul(Cxr, Cxn, Pr[:, gs, 16:24], Pi[:, gs, 16:24], Ctr, Cti, jb, "Cx", base_rd, True)
                        for g8 in range(8):
                            bank, bk = (pc, "pc") if g8 < 4 else (pd, "pd")
                            o = bank[:, (g8 % 4) * 128:(g8 % 4 + 1) * 128]
                            MM(o, sc_(Wtr[:, g8, :, :]), sc_(Cxr[:, g8, :, :]), True, False, ["Wtr", "Cxr"], [bk])
                            MM(o, sc_(Wti[:, g8, :, :]), sc_(Cxn[:, g8, :, :]), False, True, ["Wti", "Cxi"], [bk])
                        for hb, (bank, bk) in enumerate(((pc, "pc"), (pd, "pd"))):
                            TT("dve", T0_sb[:].rearrange("p (g j) x -> p g j x", j=8)[:, hb * 4:(hb + 1) * 4, jb, :],
                               bank[:].rearrange("p (g x) -> p g x", x=128), bmask[:].unsqueeze(1).to_broadcast([128, 4, 128]), ALU.mult,
                               [bk, "bmask"], ["T0_sb"])
                        cmul(Cxr, Cxn, Pr[:, gs, 8:16], Pi[:, gs, 8:16], Ctr, Cti, jb, "Cx", base_rd, True)
                        for (ri, src, nm) in ((0, Cxr, "Cxr"), (1, Cxn, "Cxi")):
                            for gh in range(2):
                                ACT(Cm_sb[gh * 64:(gh + 1) * 64, :, ri, :].rearrange("p (g j) x -> p g j x", j=8)[:, :, jb, :],
                                    src[:, gh * 4:(gh + 1) * 4, :, :].rearrange("p g s c -> p g (s c)"), AF.Copy, [nm], ["Cm_sb"])
                S.barrier()
                with ExitStack() as pm:
                    Hb = sb(pm, "p3_Hb", [128, 2, 32, 65], F32)
                    U2 = [sb(pm, "p3_U%d" % i, [128, 64, 64], BF16) for i in range(2)]
                    Hbf = sb(pm, "p3_Hbf", [128, 2, 32, 64], BF16)
                    Ysth = [sb(pm, "p3_Yst%d" % i, [128, 32, 64], F32) for i in range(2)]
                    Fr = sb(pm, "p3_Fr", [128, 32, 65], F32)
                    Fi = sb(pm, "p3_Fi", [128, 32, 65], F32)
                    Rt = sb(pm, "p3_R", [128, 32, 65], F32)
                    Gir = sb(pm, "p3_Gir", [128, 32, 65], F32)
                    Gii = sb(pm, "p3_Gii", [128, 32, 65], F32)
                    Gsr = sb(pm, "p3_Gsr", [128, 32, 65], F32)
                    Gsi = sb(pm, "p3_Gsi", [128, 32, 65], F32)
                    c1 = sb(pm, "p3_c1", [128, 3, 32], F32)
                    S.op("dve", lambda e: e.reciprocal(out=c1[:, 2, :], in_=L8[:, 3, :]), ["L8"], ["c1"])
                    TT("dve", c1[:, 0, :], L8[:, 0, :], c1[:, 2, :], ALU.mult, ["L8", "c1"], ["c1"])
                    TT("dve", c1[:, 1, :], L8[:, 1, :], c1[:, 2, :], ALU.mult, ["L8", "c1"], ["c1"])
                    MSET("pool", Fr[:, :, 0:1], 1.0, ["Fr"])
                    MSET("pool", Fi[:, :, 0:1], 0.0, ["Fi"])
                    MSET("pool", Rt[:, :, 0:1], 0.0, ["Rt"])
                    MSET("pool", Gir[:], 0.0, ["Gir"])
                    MSET("pool", Gii[:], 0.0, ["Gii"])
                    CP("dve", Fr[:, :, 1], c1[:, 0, :], ["c1", "Fr"], ["Fr"])
                    CP("dve", Fi[:, :, 1], c1[:, 1, :], ["c1", "Fi"], ["Fi"])
                    CP("dve", Rt[:, :, 1:65], L8[:, 3, :].unsqueeze(2).to_broadcast([128, 32, 64]), ["L8", "Rt"], ["Rt"])
                    for k in range(6):
                        m_ = 1 << k
                        a_bc = Fr[:, :, m_:m_ + 1].to_broadcast([128, 32, m_])
                        b_bc = Fi[:, :, m_:m_ + 1].to_broadcast([128, 32, m_])
                        sr, si = Fr[:, :, 1:1 + m_], Fi[:, :, 1:1 + m_]
                        dr, di = Fr[:, :, m_ + 1:2 * m_ + 1], Fi[:, :, m_ + 1:2 * m_ + 1]
                        u_, v_ = Gsr[:, :, 0:m_], Gsi[:, :, 0:m_]
                        TT("dve", u_, sr, a_bc, ALU.mult, ["Fr"], ["Gsr"])
                        TT("dve", v_, si, b_bc, ALU.mult, ["Fi"], ["Gsi"])
                        TT("dve", dr, u_, v_, ALU.subtract, ["Gsr", "Gsi", "Fr"], ["Fr"])
                        TT("dve", u_, sr, b_bc, ALU.mult, ["Fr", "Fi"], ["Gsr"])
                        TT("dve", v_, si, a_bc, ALU.mult, ["Fr", "Fi"], ["Gsi"])
                        TT("dve", di, u_, v_, ALU.add, ["Gsr", "Gsi", "Fi"], ["Fi"])
                    lbanks = [pa, pb]
                    ybanks = [pc, pd]
                    def u_load(ti):
                        (t0_, W_, _, _, _) = TILES[ti]
                        DMA(U2[ti % 2][:, :, :W_ // 8], UD[:, :, :, :, t0_ // 8:(t0_ + W_) // 8].rearrange("s c g j n -> (s c) (g j) n"),
                            [("UD", t0_)], [("U_sb", ti % 2)])

                    u_load(0)
                    for ti, (t0, W, seq, first, last) in enumerate(TILES):
                        nC = W // 8
                        n0 = t0 // 8
                        U_sb = U2[ti % 2]
                        ukey = ("U_sb", ti % 2)
                        if ti + 1 < len(TILES):
                            u_load(ti + 1)
                        if first:
                            if seq < 2:
                                MSET("pool", Hb[:, :, :, 0:1], 0.0, ["Hb"])
                            else:
                                for (ri, src) in ((0, s5r0), (1, s5i0)):
                                    for g8 in range(8):
                                        gh, g4 = g8 // 4, g8 % 4
                                        DMA(Hcar[gh * 64:(gh + 1) * 64, ri, g4 * 8:(g4 + 1) * 8],
                                            src[l].rearrange("(j g) p -> p g j", g=8)[:, g8, :], [], ["Hcar"])
                                CP("dve", Hb[:, :, :, 0], Hcar[:], ["Hcar"], ["Hb"])
                        for gl4 in range(8):
                            bank = lbanks[gl4 % 2]
                            bk = ("lb", gl4 % 2)
                            bv = bank[:].rearrange("p (r g n) -> p r g n", r=2, g=4)
                            for gh in range(2):
                                for gi in range(4):
                                    gidx = gh * 32 + gl4 * 4 + gi
                                    for ri in range(2):
                                        MM(bv[gh * 64:(gh + 1) * 64, ri, gi, :nC], W_sb[:, gidx, ri * 64:(ri + 1) * 64], U_sb[:, gidx, :nC], True, True,
                                           [ukey], [bk])
                            CP("act", Hb[:, :, gl4 * 4:(gl4 + 1) * 4, 1:1 + nC], bv[:, :, :, :nC], [bk], ["Hb"])
                        n1 = nC + 1
                        Lr_, Li_ = Hb[:, 0, :, 1:n1], Hb[:, 1, :, 1:n1]
                        Frs, Fis = Fr[:, :, 1:n1], Fi[:, :, 1:n1]
                        CP("dve", Gir[:, :, 0], Hb[:, 0, :, 0], ["Hb"], ["Gir"])
                        CP("dve", Gii[:, :, 0], Hb[:, 1, :, 0], ["Hb"], ["Gii"])
                        TT("dve", Gsr[:, :, 1:n1], Lr_, Frs, ALU.mult, ["Hb", "Fr"], ["Gsr"])
                        TT("dve", Gsi[:, :, 1:n1], Li_, Fis, ALU.mult, ["Hb", "Fi"], ["Gsi"])
                        TT("dve", Gir[:, :, 1:n1], Gsr[:, :, 1:n1], Gsi[:, :, 1:n1], ALU.add, ["Gsr", "Gsi"], ["Gir"])
                        TT("dve", Gsr[:, :, 1:n1], Li_, Frs, ALU.mult, ["Hb", "Fr", "Gir"], ["Gsr"])
                        TT("dve", Gsi[:, :, 1:n1], Lr_, Fis, ALU.mult, ["Hb", "Fi", "Gir"], ["Gsi"])
                        TT("dve", Gii[:, :, 1:n1], Gsr[:, :, 1:n1], Gsi[:, :, 1:n1], ALU.subtract, ["Gsr", "Gsi"], ["Gii"])
                        fl_ = lambda tl: tl[:].rearrange("p g n -> p (g n)")
                        S.op("dve", lambda e, o=fl_(Gsr), d0=fl_(Rt), d1=fl_(Gir): e.tensor_tensor_scan(out=o, data0=d0, data1=d1, initial=0.0, op0=ALU.mult, op1=ALU.add),
                             ["Rt", "Gir", "Gii"], ["Gsr"])
                        S.op("dve", lambda e, o=fl_(Gsi), d0=fl_(Rt), d1=fl_(Gii): e.tensor_tensor_scan(out=o, data0=d0, data1=d1, initial=0.0, op0=ALU.mult, op1=ALU.add),
                             ["Rt", "Gii", "Gsr"], ["Gsi"])
                        Frn, Fin = Fr[:, :, 0:n1], Fi[:, :, 0:n1]
                        TT("dve", Gir[:, :, 0:n1], Gsr[:, :, 0:n1], Frn, ALU.mult, ["Gsr", "Gsi"], ["Gir"])
                        TT("dve", Gii[:, :, 0:n1], Gsi[:, :, 0:n1], Fin, ALU.mult, ["Gsr", "Gsi"], ["Gii"])
                        TT("dve", Hb[:, 0, :, 0:n1], Gir[:, :, 0:n1], Gii[:, :, 0:n1], ALU.subtract, ["Gir", "Gii", "Hb"], ["Hb"])
                        TT("dve", Gir[:, :, 0:n1], Gsr[:, :, 0:n1], Fin, ALU.mult, ["Gsr", "Gsi", "Hb"], ["Gir"])
                        TT("dve", Gii[:, :, 0:n1], Gsi[:, :, 0:n1], Frn, ALU.mult, ["Gsr", "Gsi", "Hb"], ["Gii"])
                        TT("dve", Hb[:, 1, :, 0:n1], Gir[:, :, 0:n1], Gii[:, :, 0:n1], ALU.add, ["Gir", "Gii", "Hb"], ["Hb"])
                        CP("act", Hbf[:, :, :, :nC], Hb[:, :, :, 0:nC], ["Hb"], ["Hbf"])
                        CP("dve", Hcar[:], Hb[:, :, :, nC], ["Hb"], ["Hcar"])
                        CP("dve", Hb[:, :, :, 0], Hcar[:], ["Hcar", "Hbf"], ["Hb"])
                        for g8b in range(8):
                            bank = ybanks[g8b % 2]
                            bk = ("yb", g8b % 2)
                            bv = bank[:].rearrange("p (g n) -> p g n", g=8)
                            for gi in range(8):
                                gidx = g8b * 8 + gi
                                gh, gl = gidx // 32, gidx % 32
                                MM(bv[:, gi, :nC], T0_sb[:, gidx, :], U_sb[:, gidx, :nC], True, False, [ukey], [bk])
                                for ri in range(2):
                                    MM(bv[:, gi, :nC], Cm_sb[gh * 64:(gh + 1) * 64, gl, ri, :], Hbf[gh * 64:(gh + 1) * 64, ri, gl, :nC], False, ri == 1,
                                       ["Hbf"], [bk])
                            hb_ = g8b // 4
                            CP("act" if g8b % 2 else "dve", Ysth[hb_][:, (g8b % 4) * 8:(g8b % 4 + 1) * 8, :nC], bv[:, :, :nC], [bk], [("Yst", hb_)])
                            if g8b % 4 == 3:
                                for s_ in range(8):
                                    DMA(YD[:, hb_ * 4:(hb_ + 1) * 4, :, s_, n0:n0 + nC].rearrange("c g j n -> c (g j) n"),
                                        Ysth[hb_][s_ * 16:(s_ + 1) * 16, :, :nC], [("Yst", hb_)], [("YD", t0)])
                        if last:
                            for (ri, dst) in ((0, s5r_o), (1, s5i_o)):
                                for g8 in range(8):
                                    gh, g4 = g8 // 4, g8 % 4
                                    DMA(dst[l, seq].rearrange("(j g) p -> p g j", g=8)[:, g8, :],
                                        Hcar[gh * 64:(gh + 1) * 64, ri, g4 * 8:(g4 + 1) * 8], ["Hcar"], [])
            S.barrier()

        def phase4(l):
            with ExitStack() as ph:
                wg = sb(ph, "p4_w", [128, 8, 1024], BF16)
                ds5 = sb(ph, "p4_ds5", [128, 8], F32)
                bgl = sb(ph, "p4_bgl", [128, 8], F32)
                yp4 = [sb(ph, "p4_yp%d" % i, [128, 8, 8, 64], F32) for i in range(2)]
                ut4 = [sb(ph, "p4_ut%d" % i, [128, 8, 512], BF16) for i in range(2)]
                szb4 = [sb(ph, "p4_szb%d" % i, [128, 8, 512], BF16) for i in range(2)]
                yb = sb(ph, "p4_yb", [128, 8, 512], F32)
                gf = sb(ph, "p4_gf", [128, 8, 512], F32)
                gb = sb(ph, "p4_gb", [128, 8, 512], BF16)
                sg = [sb(ph, "p4_sg%d" % i, [128, 512], F32) for i in range(2)]
                ybst = sb(ph, "p4_ybst", [128, 8, 512], BF16)
                mm = [ps(ph, "p4_mm%d" % i, [128, 512]) for i in range(2)]
                for kt in range(8):
                    DMA(wg[:, kt, :], Wd["w_glu"][l, kt * 128:(kt + 1) * 128, :], [], ["wg"], q="pool")
                DMA(ds5[:], Wd["d_s5"][l].rearrange("(k p) -> p k", p=128), [], ["ds5"])
                DMA(bgl[:], Wd["b_glu"][l].rearrange("(k p) -> p k", p=128), [], ["bgl"])
                def p4_load(ti):
                    (t0, W, seq, first, last) = TILES[ti]
                    nC = W // 8
                    n0 = t0 // 8
                    pr = ti % 2
                    for g8 in range(8):
                        DMA(yp4[pr][g8 * 16:(g8 + 1) * 16, :, :, :nC].rearrange("c j s n -> c (j s) n"),
                            YD[:, g8, :, :, n0:n0 + nC].rearrange("c j s n -> c (j s) n"), [("YD", t0)], [("yp4", pr)])
                        DMA(ut4[pr][g8 * 16:(g8 + 1) * 16, :, :W], UT[:, :, g8, t0:t0 + W].rearrange("j c t -> c j t"), [("UT", t0)], [("ut4", pr)])
                    DMA(szb4[pr][:, :, :W], SZB[:, :, t0:t0 + W].rearrange("m p t -> p m t"), [("SZB", t0)], [("szb4", pr)])

                p4_load(0)
                for ti, (t0, W, seq, first, last) in enumerate(TILES):
                    nC = W // 8
                    n0 = t0 // 8
                    pr = ti % 2
                    if ti + 1 < len(TILES):
                        p4_load(ti + 1)
                    for j in range(8):
                        STT(yb[:, j, :W].rearrange("p (n s) -> p n s", s=8), ut4[pr][:, j, :W].rearrange("p (n s) -> p n s", s=8), ds5[:, j:j + 1],
                            yp4[pr][:, j, :, :nC].rearrange("p s n -> p n s"), ALU.mult, ALU.add, [("ut4", pr), ("yp4", pr), "ds5"], [("yb", j)])
                        ACT(gf[:, j, :W], yb[:, j, :W], AF.Gelu_apprx_tanh, [("yb", j)], [("gf", j)])
                        CP("act" if j % 2 else "dve", gb[:, j, :W], gf[:, j, :W], [("gf", j)], ["gb"])
                    for jo in range(8):
                        bank = mm[jo % 2]
                        bk = ("mm", jo % 2)
                        for ji in range(8):
                            MM(bank[:, :W], wg[:, ji, jo * 128:(jo + 1) * 128], gb[:, ji, :W], ji == 0, ji == 7, ["wg", "gb"], [bk])
                        sgt = sg[jo % 2]
                        sk = ("sg", jo % 2)
                        ACT(sgt[:, :W], bank[:, :W], AF.Sigmoid, [bk, "bgl"], [sk], bias=bgl[:, jo:jo + 1], scale=1.0)
                        TT("dve", sgt[:, :W], sgt[:, :W], gf[:, jo, :W], ALU.mult, [sk, ("gf", jo)], [sk])
                        TT("dve", ybst[:, jo, :W], sgt[:, :W], szb4[pr][:, jo, :W], ALU.mult, [sk, ("szb4", pr)], ["ybst"])
                    DMA(YB[:, :, t0:t0 + W].rearrange("m p t -> p m t"), ybst[:, :, :W], ["ybst"], [("YB", t0)])
            S.barrier()

        def phase5(l):
            with ExitStack() as ph:
                wo = sb(ph, "p5_w", [128, 16, 1024], BF16)
                y5 = sb(ph, "p5_y", [128, 16, 512], BF16)
                xT5 = sb(ph, "p5_x", [128, 8, 512], F32)
                xo = sb(ph, "p5_xo", [128, 8, 512], F32)
                mm = [ps(ph, "p5_mm%d" % i, [128, 512]) for i in range(2)]
                for kt in range(16):
                    DMA(wo[:, kt, :], Wd["w_out"][l, kt * 128:(kt + 1) * 128, :], [], ["wo"], q="pool")
                for (t0, W, seq, first, last) in TILES:
                    DMA(y5[:, 0:8, :W], YA[:, :, t0:t0 + W].rearrange("m p t -> p m t"), [("YA", t0)], ["y5a"])
                    DMA(y5[:, 8:16, :W], YB[:, :, t0:t0 + W].rearrange("m p t -> p m t"), [("YB", t0)], ["y5b"])
                    DMA(xT5[:, :, :W], xt_ap(t0, W), [("XT", t0)], ["xT5"])
                    for dm in range(8):
                        bank = mm[dm % 2]
                        bk = ("mm", dm % 2)
                        for kt in range(16):
                            MM(bank[:, :W], wo[:, kt, dm * 128:(dm + 1) * 128], y5[:, kt, :W], kt == 0, kt == 15, ["wo", "y5a", "y5b"], [bk])
                        TT("dve", xo[:, dm, :W], bank[:, :W], xT5[:, dm, :W], ALU.add, [bk, "xT5"], ["xo"])
                    DMA(xt_ap(t0, W), xo[:, :, :W], ["xo"], [("XT", t0)])
            S.barrier()

        def phase_final():
            with ExitStack() as ph:
                fg = sb(ph, "pf_g", [128, 8], F32)
                xT = sb(ph, "pf_xT", [128, 8, 512], F32)
                sq = sb(ph, "pf_sq", [128, 8, 512], BF16)
                hf = sb(ph, "pf_hf", [128, 8, 512], F32)
                rs = sb(ph, "pf_rs", [128, 512], F32)
                rs2 = sb(ph, "pf_rs2", [128, 512], F32)
                yst = sb(ph, "pf_yst", [128, 4, 1024], F32)
                msp = ps(ph, "pf_ms", [128, 512])
                tp = [ps(ph, "pf_tp%d" % i, [128, 512]) for i in range(4)]
                DMA(fg[:], Wd["final_norm_g"].rearrange("(k p) -> p k", p=128), [], ["fg"])
                for (t0, W, seq, first, last) in TILES:
                    Pb = min(128, W)
                    nb = W // Pb
                    DMA(xT[:, :, :W], xt_ap(t0, W), [("XT", t0)], ["xT"])
                    ACT(sq[:, :, :W], xT[:, :, :W], AF.Square, ["xT"], ["nf_sq"])
                    rms_rstd(msp, [sq[:, kt, :W] for kt in range(8)], W, 1.0 / 1024, rs, rs2, "nf")
                    for kt in range(8):
                        STT(hf[:, kt, :W], xT[:, kt, :W], fg[:, kt:kt + 1], rs2[:, :W], ALU.mult, ALU.mult, ["xT", "fg", "nf_rs2"], ["hf"])
                    for b in range(nb):
                        for half in range(2):
                            bank = tp[(b * 2 + half) % 4]
                            bk = ("tp", (b * 2 + half) % 4)
                            for k4 in range(4):
                                kt = half * 4 + k4
                                TR(bank[:Pb, k4 * 128:(k4 + 1) * 128], hf[:, kt, b * Pb:(b + 1) * Pb], ident_f[:], ["hf"], [bk])
                            CP("act" if half else "dve", yst[:Pb, b, half * 512:(half + 1) * 512], bank[:Pb, :], [bk], ["yst"])
                    DMA(y_out[t0:t0 + W, :].rearrange("(b p) d -> p b d", p=Pb), yst[:Pb, :nb, :], ["yst"], [])
            S.barrier()

        wst = ExitStack()
        w_cur = sb(wst, "w_in_sb", [128, 8, IN_COLS], BF16)
        load_w_in(0, w_cur)
        phase0()
        for l in range(n_layers):
            if 1 in phases:
                phase1(l, w_cur)
            wst.close()
            if 2 in phases:
                phase2(l)
            if 3 in phases:
                phase3(l)
            if 4 in phases:
                phase4(l)
            if l + 1 < n_layers:
                wst = ExitStack()
                w_cur = sb(wst, "w_in_sb", [128, 8, IN_COLS], BF16)
                load_w_in(l + 1, w_cur)
            if 5 in phases:
                phase5(l)
        if do_final:
            phase_final()
        S.finish(block)
        nc._n_instr = S.n_instr
    return nc


def make_in_maps(inputs):
    xp = np.ascontiguousarray(inputs["x_prompt"], dtype=np.float32)
    xs = np.ascontiguousarray(inputs["x_sample"], dtype=np.float32)
    maps = []
    for c in range(NCORES):
        m = {}
        m["x_core"] = np.ascontiguousarray(np.concatenate([xp[2 * c], xp[2 * c + 1], xs[c]], axis=0))
        m["conv0"] = np.ascontiguousarray(inputs["state_ssd_conv"][:, c])
        m["ssd0"] = np.ascontiguousarray(inputs["state_ssd"][:, c])
        m["s5r0"] = np.ascontiguousarray(inputs["state_s5_re"][:, c])
        m["s5i0"] = np.ascontiguousarray(inputs["state_s5_im"][:, c])
        for n in W_NAMES:
            m[n] = np.ascontiguousarray(inputs[n], dtype=np.float32)
        maps.append(m)
    return maps


_NC_CACHE = {}


def kernel(**inputs):
    inputs = {k: np.asarray(v) for k, v in inputs.items()}
    if "nc" not in _NC_CACHE:
        _NC_CACHE["nc"] = build_nc()
    nc = _NC_CACHE["nc"]
    maps = make_in_maps(inputs)
    res = run_bass_kernel_spmd(nc, maps, core_ids=list(range(NCORES)))
    R = res.results
    y_prompt = np.zeros((16, 2048, 1024), np.float32)
    y_sample = np.zeros((8, 64, 1024), np.float32)
    conv_p = np.zeros((4, 16, 3, 2048), np.float32)
    ssd_p = np.zeros((4, 16, 16, 64, 128), np.float32)
    s5r_p = np.zeros((4, 16, 64, 64), np.float32)
    s5i_p = np.zeros((4, 16, 64, 64), np.float32)
    conv_s = np.zeros((4, 8, 3, 2048), np.float32)
    ssd_s = np.zeros((4, 8, 16, 64, 128), np.float32)
    s5r_s = np.zeros((4, 8, 64, 64), np.float32)
    s5i_s = np.zeros((4, 8, 64, 64), np.float32)
    for c in range(NCORES):
        r = R[c]
        y = r["y_out"]
        y_prompt[2 * c] = y[0:2048]
        y_prompt[2 * c + 1] = y[2048:4096]
        y_sample[c] = y[4096:4160]
        for i in range(2):
            conv_p[:, 2 * c + i] = r["conv_o"][:, i]
            ssd_p[:, 2 * c + i] = r["ssd_o"][:, i]
            s5r_p[:, 2 * c + i] = r["s5r_o"][:, i]
            s5i_p[:, 2 * c + i] = r["s5i_o"][:, i]
        conv_s[:, c] = r["conv_o"][:, 2]
        ssd_s[:, c] = r["ssd_o"][:, 2]
        s5r_s[:, c] = r["s5r_o"][:, 2]
        s5i_s[:, c] = r["s5i_o"][:, 2]
    return (y_prompt, y_sample, conv_p, ssd_p, s5r_p, s5i_p, conv_s, ssd_s, s5r_s, s5i_s)
```

```python
import math
from contextlib import ExitStack

import numpy as np
import concourse.bass as bass
import concourse.mybir as mybir
from concourse.bass_utils import run_bass_kernel_spmd

F32 = mybir.dt.float32
BF16 = mybir.dt.bfloat16
I32 = mybir.dt.int32
AF = mybir.ActivationFunctionType
ALU = mybir.AluOpType

NCORES = 8
DEPTH = 4
T = 4160
NCH = T // 8
EPS = 1e-6
IN_COLS = 5136
C_ZA, C_XBC, C_DT, C_U, C_ZB = 0, 1024, 3072, 3088, 4112
TILES = [(s * 2048 + i * 512, 512, s, i == 0, i == 3) for s in range(2) for i in range(4)]
TILES.append((4096, 64, 2, True, True))
import os as _os
if _os.environ.get("K_TILES"):
    TILES = [TILES[int(i)] for i in _os.environ["K_TILES"].split(",")]
K_SKIP = set(_os.environ.get("K_SKIP", "").split(","))

ENGS = ("pe", "dve", "act", "pool", "sp")


class Sync:
    def __init__(self, nc, stack, n_dma_sems=40, same_engine_sync=True):
        self.nc = nc
        self.esem = {e: stack.enter_context(nc.semaphore("s_" + e)) for e in ENGS}
        self.cnt = {e: 0 for e in ENGS}
        self.prog = {e: [] for e in ENGS}
        self.waited = {e: {} for e in ENGS}
        self.res = {}
        self.same = same_engine_sync
        self.dsems = [stack.enter_context(nc.semaphore("s_dma%d" % i)) for i in range(n_dma_sems)]
        self.dval = [0] * n_dma_sems
        self.dnext = 0
        self.sems = {}
        for e in ENGS:
            self.sems[("e", e)] = self.esem[e]
        for i, s in enumerate(self.dsems):
            self.sems[("d", i)] = s
        self.n_instr = 0

    def _need(self, e, reads, writes):
        need = {}

        def add(ev):
            if ev is None:
                return
            k, v = ev
            if need.get(k, 0) < v:
                need[k] = v

        for r in reads:
            st = self.res.get(r)
            if st is not None:
                add(st[0])
        for w in writes:
            st = self.res.get(w)
            if st is not None:
                add(st[0])
                for ev in st[1]:
                    add(ev)
        out = []
        for k, v in need.items():
            if k == ("e", e) and (e == "pe" or not self.same):
                continue
            if self.waited[e].get(k, 0) >= v:
                continue
            self.waited[e][k] = v
            out.append((k, v))
        return out

    def _commit(self, ev, reads, writes):
        for r in reads:
            st = self.res.setdefault(r, [None, []])
            st[1].append(ev)
            if len(st[1]) > 16:
                mx = {}
                for k, v in st[1]:
                    if mx.get(k, 0) < v:
                        mx[k] = v
                st[1] = list(mx.items())
        for w in writes:
            self.res[w] = [ev, []]

    def op(self, e, fn, reads=(), writes=()):
        waits = self._need(e, reads, writes)
        self.cnt[e] += 1
        ev = (("e", e), self.cnt[e])
        sem = self.esem[e]
        sems = self.sems

        def emit(eng, waits=waits, fn=fn, sem=sem):
            for k, v in waits:
                eng.wait_ge(sems[k], v)
            fn(eng).then_inc(sem, 1)

        self.prog[e].append(emit)
        self._commit(ev, reads, writes)
        self.n_instr += 1
        return ev

    def dma(self, e, fn, reads=(), writes=()):
        i = self.dnext
        self.dnext = (self.dnext + 1) % len(self.dsems)
        k = ("d", i)
        waits = self._need(e, reads, writes)
        if self.dval[i] > 0 and self.waited[e].get(k, 0) < self.dval[i]:
            self.waited[e][k] = self.dval[i]
            waits.append((k, self.dval[i]))
        self.dval[i] += 16
        ev = (k, self.dval[i])
        sem = self.dsems[i]
        sems = self.sems

        def emit(eng, waits=waits, fn=fn, sem=sem):
            for kk, v in waits:
                eng.wait_ge(sems[kk], v)
            fn(eng).then_inc(sem, 16)

        self.prog[e].append(emit)
        self._commit(ev, reads, writes)
        self.n_instr += 1
        return ev

    def barrier(self):
        evs = [(("e", e), self.cnt[e]) for e in ENGS if self.cnt[e] > 0]
        evs += [(("d", i), v) for i, v in enumerate(self.dval) if v > 0]
        sems = self.sems
        for e in ENGS:
            waits = []
            for k, v in evs:
                if k == ("e", e):
                    continue
                if self.waited[e].get(k, 0) >= v:
                    continue
                self.waited[e][k] = v
                waits.append((k, v))

            def emit(eng, waits=waits):
                for kk, v in waits:
                    eng.wait_ge(sems[kk], v)

            self.prog[e].append(emit)
        self.res = {}

    def finish(self, block):
        self.barrier()
        prog = self.prog

        @block.tensor
        def _(eng):
            for f in prog["pe"]:
                f(eng)

        @block.vector
        def _(eng):
            for f in prog["dve"]:
                f(eng)

        @block.scalar
        def _(eng):
            for f in prog["act"]:
                f(eng)

        @block.gpsimd
        def _(eng):
            for f in prog["pool"]:
                f(eng)

        @block.sync
        def _(eng):
            for f in prog["sp"]:
                f(eng)


W_NAMES = ["norm_g", "w_in", "conv_w", "conv_b", "dt_bias", "a_log", "d_ssd", "ssd_norm_g",
           "lam_re", "lam_im", "log_dt", "b_re", "b_im", "c_re", "c_im", "d_s5", "w_glu", "b_glu",
           "w_out", "final_norm_g"]
W_SHAPES = {
    "norm_g": [4, 1024], "w_in": [4, 1024, 5136], "conv_w": [4, 4, 2048], "conv_b": [4, 2048],
    "dt_bias": [4, 16], "a_log": [4, 16], "d_ssd": [4, 16], "ssd_norm_g": [4, 1024],
    "lam_re": [4, 64, 64], "lam_im": [4, 64, 64], "log_dt": [4, 64], "b_re": [4, 64, 64, 16],
    "b_im": [4, 64, 64, 16], "c_re": [4, 64, 16, 64], "c_im": [4, 64, 16, 64], "d_s5": [4, 1024],
    "w_glu": [4, 1024, 1024], "b_glu": [4, 1024], "w_out": [4, 2048, 1024], "final_norm_g": [1024],
}


def build_nc(n_layers=DEPTH, dbg=False, phases=(1, 2, 3, 4, 5), do_final=True):
    nc = bass.Bass("TRN2", target_bir_lowering=False)
    skind = "ExternalOutput" if dbg else "Internal"

    def din(name, shape, dt=F32):
        return nc.dram_tensor(name, shape, dt, kind="ExternalInput").ap()

    def dout(name, shape, dt=F32):
        return nc.dram_tensor(name, shape, dt, kind="ExternalOutput").ap()

    def dscr(name, shape, dt=F32):
        return nc.dram_tensor(name, shape, dt, kind=skind).ap()

    x_in = din("x_core", [T, 1024])
    conv0 = din("conv0", [4, 3, 2048])
    ssd0 = din("ssd0", [4, 16, 64, 128])
    s5r0 = din("s5r0", [4, 64, 64])
    s5i0 = din("s5i0", [4, 64, 64])
    Wd = {n: din(n, W_SHAPES[n]) for n in W_NAMES}

    y_out = dout("y_out", [T, 1024])
    conv_o = dout("conv_o", [4, 3, 3, 2048])
    ssd_o = dout("ssd_o", [4, 3, 16, 64, 128])
    s5r_o = dout("s5r_o", [4, 3, 64, 64])
    s5i_o = dout("s5i_o", [4, 3, 64, 64])

    XT = dscr("XT", [8, 128, T])
    SZA = dscr("SZA", [8, 128, T], BF16)
    XBC = dscr("XBC", [16, 128, T])
    DTR = dscr("DTR", [T, 16])
    UD = dscr("UD", [len(TILES), 8, 16, 8, 8, 64], BF16)
    UT = dscr("UT", [8, 16, 8, T], BF16)
    SZB = dscr("SZB", [8, 128, T], BF16)
    YA = dscr("YA", [8, 128, T], BF16)
    YD = dscr("YD", [len(TILES), 8, 16, 8, 8, 64])
    YB = dscr("YB", [8, 128, T], BF16)

    with ExitStack() as st:
        S = Sync(nc, st)
        st.enter_context(nc.allow_non_contiguous_dma(reason="small strided parameter/state loads"))

        uniq = [0]

        def sb(stack, name, shape, dt):
            uniq[0] += 1
            return stack.enter_context(nc.sbuf_tensor("%s_%d" % (name, uniq[0]), shape, dt))

        def ps(stack, name, shape, dt=F32):
            uniq[0] += 1
            return stack.enter_context(nc.psum_tensor("%s_%d" % (name, uniq[0]), shape, dt))


        def TT(eng, out, in0, in1, op, reads, writes):
            return S.op(eng, lambda e: e.tensor_tensor(out=out, in0=in0, in1=in1, op=op), reads, writes)

        def TS(eng, out, in0, s1, s2, op0, op1, reads, writes):
            return S.op(eng, lambda e: e.tensor_scalar(out=out, in0=in0, scalar1=s1, scalar2=s2, op0=op0, op1=op1), reads, writes)

        def TSS(eng, out, in_, scalar, op, reads, writes):
            return S.op(eng, lambda e: e.tensor_single_scalar(out=out, in_=in_, scalar=scalar, op=op), reads, writes)

        def STT(out, in0, scalar, in1, op0, op1, reads, writes):
            return S.op("dve", lambda e: e.scalar_tensor_tensor(out=out, in0=in0, scalar=scalar, in1=in1, op0=op0, op1=op1), reads, writes)

        def ACT(out, in_, func, reads, writes, bias=None, scale=None):
            kw = {}
            if bias is not None:
                kw["bias"] = bias
            if scale is not None:
                kw["scale"] = scale
            return S.op("act", lambda e: e.activation(out=out, in_=in_, func=func, **kw), reads, writes)

        def CP(eng, out, in_, reads, writes):
            if eng == "act":
                return ACT(out, in_, AF.Copy, reads, writes)
            return S.op(eng, lambda e: e.tensor_copy(out=out, in_=in_), reads, writes)

        def MM(out, lhsT, rhs, start, stop, reads, writes):
            return S.op("pe", lambda e: e.matmul(out, lhsT=lhsT, rhs=rhs, start=start, stop=stop), reads, writes)

        def TR(out, in_, ident, reads, writes):
            return S.op("pe", lambda e: e.transpose(out=out, in_=in_, identity=ident), reads, writes)

        def DMA(out, in_, reads=(), writes=(), q="sp"):
            return S.dma(q, lambda e: e.dma_start(out=out, in_=in_), reads, writes)

        def MSET(eng, ap, val, writes):
            return S.op(eng, lambda e: e.memset(ap, val), (), writes)

        ones_f = sb(st, "ones_f", [128, 128], F32)
        ident_f = sb(st, "ident_f", [128, 128], F32)
        ident_b = sb(st, "ident_b", [128, 128], BF16)
        ones_b = sb(st, "ones_b", [128, 128], BF16)
        tri_f = sb(st, "tri_f", [128, 128], F32)
        tri_b = sb(st, "tri_b", [128, 128], BF16)
        bmask = sb(st, "bmask", [128, 128], F32)
        block = st.enter_context(nc.Block())

        S.op("pool", lambda e: e.memset(ones_f[:], 1.0), writes=["ones_f"])
        S.op("pool", lambda e: e.affine_select(out=ident_f[:], in_=ones_f[:], pattern=[[1, 128]],
                                               compare_op=ALU.is_equal, fill=0.0, base=0, channel_multiplier=-1),
             reads=["ones_f"], writes=["ident_f"])
        S.op("pool", lambda e: e.affine_select(out=tri_f[:], in_=ones_f[:], pattern=[[1, 128]],
                                               compare_op=ALU.is_ge, fill=0.0, base=0, channel_multiplier=-1),
             reads=["ones_f"], writes=["tri_f"])
        S.op("pool", lambda e: e.affine_select(out=bmask[:], in_=ones_f[:], pattern=[[16, 8], [0, 16]],
                                               compare_op=ALU.is_ge, fill=0.0, base=15, channel_multiplier=-1),
             reads=["ones_f"], writes=["bmask"])
        S.op("dve", lambda e: e.tensor_copy(out=ident_b[:], in_=ident_f[:]), reads=["ident_f"], writes=["ident_b"])
        S.op("dve", lambda e: e.tensor_copy(out=ones_b[:], in_=ones_f[:]), reads=["ones_f"], writes=["ones_b"])
        S.op("dve", lambda e: e.tensor_copy(out=tri_b[:], in_=tri_f[:]), reads=["tri_f"], writes=["tri_b"])
        S.barrier()

        def xt_ap(t0, W):
            return XT[:, :, t0:t0 + W].rearrange("k p t -> p k t")

        def rms_rstd(ph_ps, src_sq, W, scale, rs, rs2, key):
            nk = len(src_sq)
            for i, a in enumerate(src_sq):
                MM(ph_ps[:, :W], ones_b[:], a, i == 0, i == nk - 1, [key + "_sq"], [key + "_ms"])
            ACT(rs[:, :W], ph_ps[:, :W], AF.Ln, [key + "_ms"], [key + "_rs"], bias=EPS, scale=scale)
            ACT(rs2[:, :W], rs[:, :W], AF.Exp, [key + "_rs"], [key + "_rs2"], scale=-0.5)

        def phase0():
            with ExitStack() as ph:
                xs = sb(ph, "p0_xs", [128, 4, 1024], F32)
                xo = sb(ph, "p0_xo", [128, 8, 512], F32)
                tp = [ps(ph, "p0_tp%d" % i, [128, 512]) for i in range(2)]
                for (t0, W, seq, first, last) in TILES:
                    Pb = min(128, W)
                    nb = W // Pb
                    DMA(xs[:Pb, :nb, :], x_in[t0:t0 + W, :].rearrange("(b p) d -> p b d", p=Pb), [], ["xs"])
                    for kt in range(8):
                        tpk = tp[kt % 2]
                        for b in range(nb):
                            TR(tpk[:, b * Pb:(b + 1) * Pb], xs[:Pb, b, kt * 128:(kt + 1) * 128], ident_f[:Pb, :Pb], ["xs"], [("tp", kt % 2)])
                        CP("act" if kt % 2 else "dve", xo[:, kt, :W], tpk[:, :W], [("tp", kt % 2)], ["xo"])
                    DMA(xt_ap(t0, W), xo[:, :, :W], ["xo"], [("XT", t0)])
            S.barrier()

        def load_w_in(l, w_sb):
            for kt in range(8):
                for (c0, cw_) in ((0, 2048), (2048, 2048), (4096, 1040)):
                    DMA(w_sb[:, kt, c0:c0 + cw_], Wd["w_in"][l, kt * 128:(kt + 1) * 128, c0:c0 + cw_], [], ["w"], q="pool")

        def phase1(l, w_sb):
            with ExitStack() as ph:
                g1 = sb(ph, "p1_g", [128, 8], F32)
                xT = sb(ph, "p1_xT", [128, 8, 512], F32)
                sq = sb(ph, "p1_sq", [128, 8, 512], BF16)
                hT = sb(ph, "p1_hT", [128, 8, 512], BF16)
                rs = sb(ph, "p1_rs", [128, 512], F32)
                rs2 = sb(ph, "p1_rs2", [128, 512], F32)
                st_za = sb(ph, "p1_za", [128, 8, 512], BF16)
                st_xbc = [sb(ph, "p1_xbc%d" % i, [128, 2, 512], F32) for i in range(2)]
                st_us = sb(ph, "p1_us", [128, 8, 8, 64], BF16)
                st_ut = sb(ph, "p1_ut", [128, 8, 512], BF16)
                us32 = [sb(ph, "p1_us32_%d" % i, [128, 8, 64], F32) for i in range(2)]
                st_zb = sb(ph, "p1_zb", [128, 8, 512], BF16)
                st_dt = sb(ph, "p1_dt", [128, 4, 16], F32)
                mm = [ps(ph, "p1_mm%d" % i, [128, 512]) for i in range(4)]
                msp = ps(ph, "p1_ms", [128, 512])
                dtp = ps(ph, "p1_dtp", [128, 4, 16])
                DMA(g1[:], Wd["norm_g"][l].rearrange("(k p) -> p k", p=128), [], ["g1"])
                for kt in range(0 if "perm" in K_SKIP else 8):
                    tmpu = st_ut[:, (kt % 2) * 2:(kt % 2) * 2 + 2, :].rearrange("p a t -> p (a t)")
                    CP("dve", tmpu, w_sb[:, kt, C_U:C_U + 1024], ["w"], [("tmpu", kt % 2), "st_ut"])
                    CP("act", w_sb[:, kt, C_U:C_U + 1024].rearrange("p (j c g) -> p j c g", c=16, g=8),
                       tmpu.rearrange("p (j g c) -> p j c g", g=8, c=16), [("tmpu", kt % 2), "st_ut"], ["w"])
                mcount = [0]
                def pro_load(ti):
                    (t0_, W_, _, _, _) = TILES[ti]
                    DMA(xT[:, :, :W_], xt_ap(t0_, W_), [("XT", t0_)], ["xT"])

                def pro_sq(ti):
                    W_ = TILES[ti][1]
                    ACT(sq[:, :, :W_], xT[:, :, :W_], AF.Square, ["xT"], ["n1_sq"])

                def pro_ms(ti):
                    W_ = TILES[ti][1]
                    rms_rstd(msp, [sq[:, kt, :W_] for kt in range(8)], W_, 1.0 / 1024, rs, rs2, "n1")

                pro_load(0)
                pro_sq(0)
                pro_ms(0)
                for ti, (t0, W, seq, first, last) in enumerate(TILES):
                    Pb = min(128, W)
                    nb = W // Pb
                    nC = W // 8
                    n0 = t0 // 8
                    nxt = ti + 1 if ti + 1 < len(TILES) else None
                    for kt in range(8):
                        STT(hT[:, kt, :W], xT[:, kt, :W], g1[:, kt:kt + 1], rs2[:, :W], ALU.mult, ALU.mult, ["xT", "g1", "n1_rs2"], ["hT"])
                    if nxt is not None:
                        pro_load(nxt)

                    def mtile(lhs_fn, W=W):
                        bank = mm[mcount[0] % 4]
                        key = ("mm", mcount[0] % 4)
                        mcount[0] += 1
                        for kt in range(8):
                            MM(bank[:, :W], lhs_fn(kt), hT[:, kt, :W], kt == 0, kt == 7, ["w", "hT"], [key])
                        return bank, key

                    for m in range(0 if "za" in K_SKIP else 8):
                        bank, key = mtile(lambda kt, m=m: w_sb[:, kt, C_ZA + m * 128:C_ZA + (m + 1) * 128])
                        ACT(st_za[:, m, :W], bank[:, :W], AF.Silu, [key], ["st_za"])
                    if "za" not in K_SKIP:
                        DMA(SZA[:, :, t0:t0 + W].rearrange("m p t -> p m t"), st_za[:, :, :W], ["st_za"], [("SZA", t0)])
                    if nxt is not None:
                        pro_sq(nxt)
                    for m in range(0 if "xbc" in K_SKIP else 16):
                        bank, key = mtile(lambda kt, m=m: w_sb[:, kt, C_XBC + m * 128:C_XBC + (m + 1) * 128])
                        pr = m // 2
                        stx = st_xbc[pr % 2]
                        skey = ("st_xbc", pr % 2)
                        CP("act" if m % 2 == 0 else "dve", stx[:, m % 2, :W], bank[:, :W], [key], [skey])
                        if m % 2 == 1:
                            DMA(XBC[pr * 2:(pr + 1) * 2, :, t0:t0 + W].rearrange("m p t -> p m t"), stx[:, :, :W], [skey], [("XBC", t0, pr)])
                    if nxt is not None:
                        pro_ms(nxt)
                    for j in range(0 if "u" in K_SKIP else 8):
                        bank, key = mtile(lambda kt, j=j: w_sb[:, kt, C_U + j * 128:C_U + (j + 1) * 128])
                        u32 = us32[j % 2]
                        ACT(u32[:, :, :nC].rearrange("p s n -> p n s"), bank[:, :W].rearrange("p (n s) -> p n s", s=8), AF.Copy, [key], [("us32", j % 2), ("brd", key)])
                        CP("pool", st_us[:, j, :, :nC], u32[:, :, :nC], [("us32", j % 2)], ["st_us"])
                        CP("dve", st_ut[:, j, :W], bank[:, :W], [key, ("brd", key)], ["st_ut"])
                    for j in range(0 if "ud" in K_SKIP else 8):
                        DMA(UD[ti, :, :, :, j, 0:nC].rearrange("s c g n -> (c g) s n"), st_us[:, j, :, :nC], ["st_us"], [("UD", t0)])
                    if "u" not in K_SKIP and "utdma" not in K_SKIP:
                        DMA(UT[:, :, :, t0:t0 + W].rearrange("j c g t -> (c g) j t"), st_ut[:, :, :W], ["st_ut"], [("UT", t0)])
                    for m in range(0 if "zb" in K_SKIP else 8):
                        bank, key = mtile(lambda kt, m=m: w_sb[:, kt, C_ZB + m * 128:C_ZB + (m + 1) * 128])
                        ACT(st_zb[:, m, :W], bank[:, :W], AF.Silu, [key], ["st_zb"])
                    if "zb" not in K_SKIP:
                        DMA(SZB[:, :, t0:t0 + W].rearrange("m p t -> p m t"), st_zb[:, :, :W], ["st_zb"], [("SZB", t0)])
                    for b in range(0 if "dt" in K_SKIP else nb):
                        for kt in range(8):
                            MM(dtp[:Pb, b, :], hT[:, kt, b * Pb:(b + 1) * Pb], w_sb[:, kt, C_DT:C_DT + 16], kt == 0, kt == 7, ["w", "hT"], ["dtp"])
                    if "dt" not in K_SKIP:
                        ACT(st_dt[:Pb, :nb, :], dtp[:Pb, :nb, :], AF.Copy, ["dtp"], ["st_dt"])
                        DMA(DTR[t0:t0 + W, :].rearrange("(b p) h -> p b h", p=Pb), st_dt[:Pb, :nb, :], ["st_dt"], [("DTR", t0)])
            S.barrier()

        def phase2(l):
            with ExitStack() as ph:
                cw = sb(ph, "p2_cw", [128, 16, 4], F32)
                cbias = sb(ph, "p2_cb", [128, 16], F32)
                dtb = sb(ph, "p2_dtb", [128, 16], F32)
                arow = sb(ph, "p2_arow", [128, 16], F32)
                dvec = sb(ph, "p2_dvec", [128, 8], F32)
                ng = sb(ph, "p2_ng", [128, 8], F32)
                tails = sb(ph, "p2_tails", [128, 16, 3], F32)
                hstate = sb(ph, "p2_hst", [128, 1024], F32)
                hbf = sb(ph, "p2_hbf", [128, 1024], BF16)
                hio = sb(ph, "p2_hio", [128, 8, 128], F32)
                raw = sb(ph, "p2_raw", [128, 4, 515], F32)
                rawb = sb(ph, "p2_rawb", [128, 4, 515], BF16)
                dw = sb(ph, "p2_dw", [128, 16, 4, 128], BF16)
                xc2 = [sb(ph, "p2_xc%d" % i, [128, 16, 512], BF16) for i in range(2)]
                sza2 = [sb(ph, "p2_sza%d" % i, [128, 8, 512], BF16) for i in range(2)]
                dtr = sb(ph, "p2_dtr", [128, 4, 16], F32)
                x1 = sb(ph, "p2_x1", [128, 4, 16], F32)
                tA = sb(ph, "p2_tA", [128, 4, 16], F32)
                tB = sb(ph, "p2_tB", [128, 4, 16], F32)
                dtv2 = [sb(ph, "p2_dtv%d" % i, [128, 4, 16], F32) for i in range(2)]
                av = sb(ph, "p2_av", [128, 4, 16], F32)
                a_hi2 = [sb(ph, "p2_ahi%d" % i, [128, 4, 16], BF16) for i in range(2)]
                a_lo2 = [sb(ph, "p2_alo%d" % i, [128, 4, 16], BF16) for i in range(2)]
                xdt2 = [sb(ph, "p2_xdt%d" % i, [128, 1024], BF16) for i in range(2)]
                xe2 = [sb(ph, "p2_xe%d" % i, [128, 1024], BF16) for i in range(2)]
                btok2 = [sb(ph, "p2_btok%d" % i, [128, 512], BF16) for i in range(2)]
                acs2 = [sb(ph, "p2_acs%d" % i, [128, 16], F32) for i in range(2)]
                nacs2 = [sb(ph, "p2_nacs%d" % i, [128, 16], F32) for i in range(2)]
                tmd2 = [sb(ph, "p2_tmd%d" % i, [128, 16], F32) for i in range(2)]
                te2 = [sb(ph, "p2_te%d" % i, [128, 16], F32) for i in range(2)]
                etot2 = [sb(ph, "p2_etot%d" % i, [128, 16], F32) for i in range(2)]
                cbm2 = [sb(ph, "p2_cbm%d" % i, [128, 4, 128], BF16) for i in range(2)]
                Mh4 = [[sb(ph, "p2_Mh%d_%d" % (i, g), [128, 512], BF16) for g in range(4)] for i in range(2)]
                Cs4 = [[sb(ph, "p2_Cs%d_%d" % (i, g), [128, 512], BF16) for g in range(4)] for i in range(2)]
                E = [sb(ph, "p2_E%d" % i, [128, 512], BF16) for i in range(2)]
                dec = [sb(ph, "p2_dec%d" % i, [128, 512], BF16) for i in range(2)]
                yg = sb(ph, "p2_yg", [128, 8, 512], F32)
                sq = sb(ph, "p2_sq", [128, 8, 512], BF16)
                rs4 = [sb(ph, "p2_rs4_%d" % i, [128, 512], F32) for i in range(4)]
                htmp = sb(ph, "p2_htmp", [128, 512], F32)
                ya = sb(ph, "p2_ya", [128, 8, 512], BF16)
                smallp = ps(ph, "p2_small", [128, 512])
                tpp = ps(ph, "p2_tp", [128, 1024], BF16)
                cbp = ps(ph, "p2_cbp", [128, 512])
                acsb = [ps(ph, "p2_acsb%d" % i, [128, 512]) for i in range(2)]
                yps = [ps(ph, "p2_yps%d" % i, [128, 512]) for i in range(2)]
                hps = ps(ph, "p2_hps", [128, 512])

                for k in range(4):
                    DMA(cw[:, :, k], Wd["conv_w"][l, k].rearrange("(m p) -> p m", p=128), [], ["cw"])
                DMA(cbias[:], Wd["conv_b"][l].rearrange("(m p) -> p m", p=128), [], ["cbias"])
                DMA(dtb[:], Wd["dt_bias"][l].partition_broadcast(128), [], ["dtb"])
                DMA(arow[:], Wd["a_log"][l].partition_broadcast(128), [], ["arow"])
                for h2 in range(2):
                    DMA(dvec[h2 * 64:(h2 + 1) * 64, :], Wd["d_ssd"][l].rearrange("(m h) -> h m", h=2)[h2].partition_broadcast(64), [], ["dvec"])
                DMA(ng[:], Wd["ssd_norm_g"][l].rearrange("(k p) -> p k", p=128), [], ["ng"])
                for m in range(16):
                    for k in range(4):
                        TS("pool" if (m * 4 + k) % 3 == 0 else "dve", dw[:, m, k, :], ident_b[:], cw[:, m, k:k + 1], None, ALU.mult, ALU.bypass, ["cw", "ident_b"], ["dw"])
                ACT(arow[:], arow[:], AF.Exp, ["arow"], ["arow"])
                TSS("dve", arow[:], arow[:], -1.0, ALU.mult, ["arow"], ["arow"])

                def seq_state_init(ti):
                    (t0, W, seq, first, last) = TILES[ti]
                    if not first:
                        return
                    if seq < 2:
                        MSET("pool", hstate[:], 0.0, ["hstate"])
                        MSET("pool", hbf[:], 0.0, ["hbf"])
                    else:
                        DMA(hio[:], ssd0[l].rearrange("(m h) p n -> (h p) m n", h=2), [], ["hio"])
                        for m in range(8):
                            TR(hps[:, (m % 4) * 128:(m % 4 + 1) * 128], hio[:, m, :], ident_f[:], ["hio"], ["hps"])
                            if m % 4 == 3:
                                hh = m // 4
                                CP("dve", hstate[:, hh * 512:(hh + 1) * 512], hps[:], ["hps"], ["hstate"])
                        CP("act", hbf[:], hstate[:], ["hstate"], ["hbf"])

                def conv_blk(ti, gq):
                    (t0, W, seq, first, last) = TILES[ti]
                    Qc = min(128, W)
                    nck = W // Qc
                    tp_ = ti % 2
                    xc, sza, dtv, a_hi, a_lo = xc2[tp_], sza2[tp_], dtv2[tp_], a_hi2[tp_], a_lo2[tp_]
                    if gq == 0 and first:
                        if seq < 2:
                            MSET("pool", tails[:], 0.0, ["tails"])
                        else:
                            for k in range(3):
                                DMA(tails[:, :, k], conv0[l, k].rearrange("(m p) -> p m", p=128), [], ["tails"])
                    DMA(raw[:, :, 3:3 + W], XBC[gq * 4:(gq + 1) * 4, :, t0:t0 + W].rearrange("m p t -> p m t"),
                        [("XBC", t0, 2 * gq), ("XBC", t0, 2 * gq + 1)], ["raw"])
                    CP("dve", raw[:, :, 0:3], tails[:, gq * 4:(gq + 1) * 4, :], ["tails"], ["raw"])
                    ACT(rawb[:, :, 0:3 + W], raw[:, :, 0:3 + W], AF.Copy, ["raw"], ["rawb"])
                    for mi in range(4):
                        m = gq * 4 + mi
                        bnk, bkey = ((cbp, "cbp"), (hps, "hps"))[mi % 2]
                        for k in range(4):
                            MM(bnk[:, :W], dw[:, m, k, :], rawb[:, mi, k:k + W], k == 0, k == 3, ["rawb", "dw"], [bkey])
                        ACT(xc[:, m, :W], bnk[:, :W], AF.Silu, [bkey, "cbias"], [("xc", tp_, m)], bias=cbias[:, m:m + 1], scale=1.0)
                    CP("dve", tails[:, gq * 4:(gq + 1) * 4, :], raw[:, :, W:W + 3], ["raw"], ["tails"])
                    if gq == 3 and last:
                        for k in range(3):
                            DMA(conv_o[l, seq, k].rearrange("(m p) -> p m", p=128), tails[:, :, k], ["tails"], [])

                def pre(ti):
                    (t0, W, seq, first, last) = TILES[ti]
                    Qc = min(128, W)
                    nck = W // Qc
                    tp_ = ti % 2
                    xc, sza, dtv, a_hi, a_lo = xc2[tp_], sza2[tp_], dtv2[tp_], a_hi2[tp_], a_lo2[tp_]
                    DMA(sza[:, :, :W], SZA[:, :, t0:t0 + W].rearrange("m p t -> p m t"), [("SZA", t0)], [("sza", tp_)])
                    DMA(dtr[:Qc, :nck, :], DTR[t0:t0 + W, :].rearrange("(b p) h -> p b h", p=Qc), [("DTR", t0)], ["dtr"])
                    sl = lambda tl: tl[:Qc, :nck, :]
                    bc = lambda tl: tl[:Qc, :].unsqueeze(1).to_broadcast([Qc, nck, 16])
                    TT("dve", sl(x1), sl(dtr), bc(dtb), ALU.add, ["dtr", "dtb"], ["x1"])
                    STT(sl(tA), sl(x1), -1.0, sl(x1), ALU.mult, ALU.max, ["x1"], ["tA"])
                    ACT(sl(tA), sl(tA), AF.Exp, ["tA"], ["tA"], scale=-1.0)
                    ACT(sl(tA), sl(tA), AF.Ln, ["tA"], ["tA"], bias=1.0, scale=1.0)
                    TSS("dve", sl(tB), sl(x1), 0.0, ALU.max, ["x1"], ["tB"])
                    TT("dve", sl(dtv), sl(tA), sl(tB), ALU.add, ["tA", "tB"], [("dtv", tp_)])
                    TT("dve", sl(av), sl(dtv), bc(arow), ALU.mult, [("dtv", tp_), "arow"], ["av"])
                    CP("dve", sl(a_hi), sl(av), ["av"], [("a_hi", tp_)])
                    TT("dve", sl(a_lo), sl(av), sl(a_hi), ALU.subtract, ["av", ("a_hi", tp_)], [("a_lo", tp_)])

                def chunks(ti, hook):
                    (t0, W, seq, first, last) = TILES[ti]
                    Qc = min(128, W)
                    nck = W // Qc
                    tp_ = ti % 2
                    xc, sza, dtv, a_hi, a_lo = xc2[tp_], sza2[tp_], dtv2[tp_], a_hi2[tp_], a_lo2[tp_]
                    hd = lambda ap: ap.rearrange("p (h d) -> p h d", d=64)

                    def front_pre(c):
                        cp = c % 2
                        cs = c * Qc
                        xdt, xe, btok, acs, tmd, te, etot, cbm = xdt2[cp], xe2[cp], btok2[cp], acs2[cp], tmd2[cp], te2[cp], etot2[cp], cbm2[cp]
                        k = lambda n: (n, cp)
                        for mq in range(2):
                            for mi in range(4):
                                TR(tpp[:Qc, mq * 512 + mi * 128:mq * 512 + (mi + 1) * 128], xc[:, mq * 4 + mi, cs:cs + Qc], ident_b[:],
                                   [("xc", tp_, mq * 4 + mi)], ["tp"])
                            TT("dve", hd(xdt[:Qc, mq * 512:(mq + 1) * 512]), hd(tpp[:Qc, mq * 512:(mq + 1) * 512]),
                               dtv[:Qc, c, mq * 8:(mq + 1) * 8].unsqueeze(2).to_broadcast([Qc, 8, 64]), ALU.mult, ["tp", ("dtv", tp_)], [k("xdt%d" % mq)])
                        for g in range(4):
                            TR(tpp[:Qc, g * 128:(g + 1) * 128], xc[:, 8 + g, cs:cs + Qc], ident_b[:], [("xc", tp_, 8 + g)], ["tp"])
                        CP("act", btok[:Qc, :], tpp[:Qc, 0:512], ["tp"], [k("btok")])
                        for i, aa in enumerate((a_hi, a_lo)):
                            MM(smallp[:Qc, 0:16], tri_b[:Qc, :Qc], aa[:Qc, c, :], i == 0, i == 1, [("a_hi", tp_), ("a_lo", tp_), "tri_b"], ["small"])
                        for i, aa in enumerate((a_hi, a_lo)):
                            MM(smallp[:, 16:32], ones_b[:Qc, :], aa[:Qc, c, :], i == 0, i == 1, [("a_hi", tp_), ("a_lo", tp_)], ["small"])
                        CP("act", acs[:Qc, :], smallp[:Qc, 0:16], ["small"], [k("acs")])
                        ACT(nacs2[cp][:Qc, :], smallp[:Qc, 0:16], AF.Copy, ["small"], [k("nacs")], scale=-1.0)
                        ACT(etot[:], smallp[:, 16:32], AF.Exp, ["small"], [k("etot")])
                        TT("dve", tmd[:Qc, :], smallp[:Qc, 16:32], acs[:Qc, :], ALU.subtract, ["small", k("acs"), k("etot")], [k("tmd")])
                        ACT(te[:Qc, :], tmd[:Qc, :], AF.Exp, [k("tmd")], [k("te")])
                        TT("pool", hd(xe[:Qc, :]), hd(xdt[:Qc, :]), te[:Qc, :].unsqueeze(2).to_broadcast([Qc, 16, 64]), ALU.mult,
                           [k("xdt0"), k("xdt1"), k("te")], [k("xe")])
                        for g in range(4):
                            MM(cbp[:Qc, g * 128:g * 128 + Qc], xc[:, 8 + g, cs:cs + Qc], xc[:, 12 + g, cs:cs + Qc], True, True,
                               [("xc", tp_, 8 + g), ("xc", tp_, 12 + g)], ["cbp"])
                        TT("dve", cbm[:Qc, :, :Qc], cbp[:Qc, :].rearrange("p (g l) -> p g l", g=4)[:, :, :Qc],
                           tri_b[:Qc, :Qc].unsqueeze(1).to_broadcast([Qc, 4, Qc]), ALU.mult, ["cbp", "tri_b"], [k("cbm")])

                    def front_grp(c, g):
                        cp = c % 2
                        cs = c * Qc
                        par = g % 2
                        acs, cbm = acs2[cp], cbm2[cp]
                        k = lambda n: (n, cp)
                        ab = acsb[par]
                        abv = ab[:].rearrange("p (h l) -> p h l", h=4)
                        for hh in range(4):
                            h = g * 4 + hh
                            for i, aa in enumerate((a_hi, a_lo)):
                                MM(abv[:, hh, :Qc], aa[:Qc, c, h:h + 1].to_broadcast([Qc, 128]), tri_b[:Qc, :Qc], i == 0, i == 1,
                                   [("a_hi", tp_), ("a_lo", tp_)], [("acsb", par)])
                        Ev = E[par][:].rearrange("p (h l) -> p h l", h=4)
                        ACT(Ev[:, :, :Qc], abv[:, :, :Qc], AF.Exp, [("acsb", par)], [("E", par)])
                        dv = dec[par][:].rearrange("p (h l) -> p h l", h=4)
                        for hh in range(4):
                            h = g * 4 + hh
                            ACT(dv[:Qc, hh, :Qc], abv[:Qc, hh, :Qc], AF.Exp, [("acsb", par), k("nacs")], [("dec", par)], bias=nacs2[cp][:Qc, h:h + 1], scale=1.0)
                        Mv = Mh4[cp][g][:].rearrange("p (h l) -> p h l", h=4)
                        STT(Mv[:Qc, :, :Qc], dv[:Qc, :, :Qc], 1.0, cbm[:Qc, g, :Qc].unsqueeze(1).to_broadcast([Qc, 4, Qc]), ALU.min, ALU.mult,
                            [("dec", par), k("cbm")], [("Mh", cp, g)])
                        Cv = Cs4[cp][g][:].rearrange("p (h l) -> p h l", h=4)
                        TT("pool", Cv[:, :, :Qc], xc[:, 12 + g, cs:cs + Qc].unsqueeze(1).to_broadcast([128, 4, Qc]), Ev[:, :, :Qc], ALU.mult,
                           [("xc", tp_, 12 + g), ("E", par)], [("Cs", cp, g)])

                    def back_grp(c, g):
                        cp = c % 2
                        cs = c * Qc
                        par = g % 2
                        xdt = xdt2[cp]
                        k = lambda n: (n, cp)
                        Mv = Mh4[cp][g][:].rearrange("p (h l) -> p h l", h=4)
                        Cv = Cs4[cp][g][:].rearrange("p (h l) -> p h l", h=4)
                        yp = yps[par]
                        ykey = ("yps", par)
                        for hh in range(4):
                            h = g * 4 + hh
                            o = yp[(hh % 2) * 64:(hh % 2 + 1) * 64, (hh // 2) * 128:(hh // 2) * 128 + Qc]
                            MM(o, xdt[:Qc, h * 64:(h + 1) * 64], Mv[:Qc, hh, :Qc], True, False, [k("xdt%d" % (h // 8)), ("Mh", cp, g)], [ykey])
                            MM(o, hbf[:, h * 64:(h + 1) * 64], Cv[:, hh, :Qc], False, True, ["hbf", ("Cs", cp, g)], [ykey])
                        for mm_ in range(2):
                            m = 2 * g + mm_
                            STT(yg[:, m, cs:cs + Qc], xc[:, m, cs:cs + Qc], dvec[:, m:m + 1], yp[:, mm_ * 128:mm_ * 128 + Qc], ALU.mult, ALU.add,
                                [("xc", tp_, m), "dvec", ykey], ["yg"])

                    def back_post(c):
                        cp = c % 2
                        xe, btok, etot = xe2[cp], btok2[cp], etot2[cp]
                        k = lambda n: (n, cp)
                        for half in range(2):
                            for gi in range(2):
                                g = half * 2 + gi
                                MM(hps[:, gi * 256:(gi + 1) * 256], btok[:Qc, g * 128:(g + 1) * 128], xe[:Qc, g * 256:(g + 1) * 256], True, True,
                                   [k("btok"), k("xe")], ["hps"])
                            TT("pool", hd(htmp[:]), hd(hstate[:, half * 512:(half + 1) * 512]),
                               etot[:, half * 8:(half + 1) * 8].unsqueeze(2).to_broadcast([128, 8, 64]), ALU.mult, ["hstate", k("etot")], ["htmp"])
                            TT("dve", hstate[:, half * 512:(half + 1) * 512], htmp[:], hps[:], ALU.add, ["htmp", "hps"], ["hstate"])
                        CP("act", hbf[:], hstate[:], ["hstate"], ["hbf"])

                    front_pre(0)
                    for g in range(4):
                        front_grp(0, g)
                    for c in range(nck):
                        more = c + 1 < nck
                        if more:
                            front_pre(c + 1)
                        for g in range(4):
                            if more:
                                front_grp(c + 1, g)
                            back_grp(c, g)
                        back_post(c)
                        hook(c, nck)

                def post(ti):
                    (t0, W, seq, first, last) = TILES[ti]
                    Qc = min(128, W)
                    nck = W // Qc
                    tp_ = ti % 2
                    xc, sza, dtv, a_hi, a_lo = xc2[tp_], sza2[tp_], dtv2[tp_], a_hi2[tp_], a_lo2[tp_]
                    TT("pool", yg[:, 0:3, :W], yg[:, 0:3, :W], sza[:, 0:3, :W], ALU.mult, ["yg", ("sza", tp_)], ["yg"])
                    TT("dve", yg[:, 3:8, :W], yg[:, 3:8, :W], sza[:, 3:8, :W], ALU.mult, ["yg", ("sza", tp_)], ["yg"])
                    ACT(sq[:, 3:8, :W], yg[:, 3:8, :W], AF.Square, ["yg"], ["sq_b"])
                    ACT(sq[:, 0:3, :W], yg[:, 0:3, :W], AF.Square, ["yg"], ["sq_a"])
                    nbanks = [(cbp, "cbp"), (acsb[0], ("acsb", 0)), (acsb[1], ("acsb", 1)), (hps, "hps")]
                    for gg in (3, 2, 1, 0):
                        bnk, bkey = nbanks[gg]
                        for i in range(2):
                            MM(bnk[:, :W], ones_b[:], sq[:, 2 * gg + i, :W], i == 0, i == 1, ["sq_a", "sq_b"], [bkey])
                    for gg in (3, 2, 1, 0):
                        bnk, bkey = nbanks[gg]
                        ACT(rs4[gg][:, :W], bnk[:, :W], AF.Ln, [bkey], [("rs4", gg)], bias=EPS, scale=1.0 / 256)
                    for gg in (3, 2, 1, 0):
                        ACT(rs4[gg][:, :W], rs4[gg][:, :W], AF.Exp, [("rs4", gg)], [("rs4", gg)], scale=-0.5)
                    for gg in (3, 2, 1, 0):
                        for m in (2 * gg, 2 * gg + 1):
                            STT(ya[:, m, :W], yg[:, m, :W], ng[:, m:m + 1], rs4[gg][:, :W], ALU.mult, ALU.mult, ["yg", "ng", ("rs4", gg)], ["ya"])
                    DMA(YA[:, :, t0:t0 + W].rearrange("m p t -> p m t"), ya[:, :, :W], ["ya"], [("YA", t0)])
                    if last:
                        for m in range(8):
                            TR(hps[:, (m % 4) * 128:(m % 4 + 1) * 128], hstate[:, m * 128:(m + 1) * 128], ident_f[:], ["hstate"], ["hps"])
                            if m % 4 == 3:
                                hh = m // 4
                                CP("dve", hio[:, hh * 4:(hh + 1) * 4, :], hps[:].rearrange("p (m n) -> p m n", n=128), ["hps"], ["hio"])
                        DMA(ssd_o[l, seq].rearrange("(m h) p n -> (h p) m n", h=2), hio[:], ["hio"], [])

                for gq in range(4):
                    conv_blk(0, gq)
                pre(0)
                for ti in range(len(TILES)):
                    nxt = ti + 1 if ti + 1 < len(TILES) else None

                    def hook(c, nck, nxt=nxt):
                        if nxt is None:
                            return
                        per = 4 // nck if nck <= 4 else 1
                        for gq in range(c * per, (c + 1) * per):
                            conv_blk(nxt, gq)
                        if c == nck - 1:
                            pre(nxt)

                    seq_state_init(ti)
                    chunks(ti, hook)
                    post(ti)
            S.barrier()

        def phase3(l):
            with ExitStack() as ph:
                W_sb = sb(ph, "p3_W", [128, 64, 128], BF16)
                T0_sb = sb(ph, "p3_T0", [128, 64, 128], BF16)
                Cm_sb = sb(ph, "p3_Cm", [128, 32, 2, 128], BF16)
                L8 = sb(ph, "p3_L8", [128, 4, 32], F32)
                Hcar = sb(ph, "p3_Hcar", [128, 2, 32], F32)
                pa = ps(ph, "p3_pa", [128, 512])
                pb = ps(ph, "p3_pb", [128, 512])
                pc = ps(ph, "p3_pc", [128, 512])
                pd = ps(ph, "p3_pd", [128, 512])
                with ExitStack() as pp:
                    f64 = lambda nm, shp, dt=F32: sb(pp, nm, shp, dt)
                    lamr, lami, dtg, zr, zi = [f64("q_" + n, [64, 64]) for n in ("lamr", "lami", "dtg", "zr", "zi")]
                    kr, ki, t1, t2, den = [f64("q_" + n, [64, 64]) for n in ("kr", "ki", "t1", "t2", "den")]
                    erow_i = f64("q_erowi", [64, 25], I32)
                    erow = f64("q_erow", [64, 25])
                    ang, mag, s4, s2, Pr, Pi = [f64("q_" + n, [64, 64, 25]) for n in ("ang", "mag", "s4", "s2", "Pr", "Pi")]
                    kqi = f64("q_kqi", [64, 64, 25], I32)
                    PKr, PKi, u1 = [f64("q_" + n, [64, 64, 8]) for n in ("PKr", "PKi", "u1")]
                    Btr, Bti, Ctr, Cti = [f64("q_" + n, [64, 64, 16]) for n in ("Btr", "Bti", "Ctr", "Cti")]
                    Cn = sb(pp, "q_Cn", [128, 8, 64], F32)
                    Wtr, Wti, v1, v2, Cxr, Cxn = [f64("q_" + n, [64, 8, 8, 16]) for n in ("Wtr", "Wti", "v1", "v2", "Cxr", "Cxn")]

                    DMA(lamr[:], Wd["lam_re"][l].rearrange("g p -> p g"), [], ["lamr"])
                    DMA(lami[:], Wd["lam_im"][l].rearrange("g p -> p g"), [], ["lami"])
                    DMA(dtg[:], Wd["log_dt"][l].partition_broadcast(64), [], ["dtg"])
                    DMA(Btr[:], Wd["b_re"][l].rearrange("g p c -> p g c"), [], ["Btr"])
                    DMA(Bti[:], Wd["b_im"][l].rearrange("g p c -> p g c"), [], ["Bti"])
                    for (src, dst, nm) in ((Wd["c_re"], Ctr, "Ctr"), (Wd["c_im"], Cti, "Cti")):
                        DMA(Cn[:], src[l].rearrange("(j g) c p -> (g c) j p", g=8), [], ["Cn"])
                        for j in range(8):
                            TR(pa[:64, (j % 4) * 128:(j % 4 + 1) * 128], Cn[:, j, :], ident_f[:], ["Cn"], ["pa"])
                            if j % 4 == 3:
                                jj = j // 4
                                CP("dve", dst[:, jj * 32:(jj + 1) * 32, :], pa[:64, :].rearrange("p (g c) -> p g c", c=16), ["pa"], [nm])
                    ACT(dtg[:], dtg[:], AF.Exp, ["dtg"], ["dtg"])
                    TT("dve", zr[:], lamr[:], dtg[:], ALU.mult, ["lamr", "dtg"], ["zr"])
                    TT("dve", zi[:], lami[:], dtg[:], ALU.mult, ["lami", "dtg"], ["zi"])
                    for (a, b_, pat, base) in ((0, 8, [[-1, 8]], 7), (8, 16, [[1, 8]], 1), (16, 24, [[1, 8]], -7), (24, 25, [[1, 1]], 8)):
                        oap = erow_i[:, a:b_]
                        S.op("pool", lambda e, oap=oap, pat=pat, base=base: e.iota(out=oap, pattern=pat, base=base, channel_multiplier=0), (), ["erow_i"])
                    CP("dve", erow[:], erow_i[:], ["erow_i"], ["erow"])
                    ebc = erow[:, :].unsqueeze(1).to_broadcast([64, 64, 25])
                    gbc = lambda tl: tl[:, :].unsqueeze(2).to_broadcast([64, 64, 25])
                    fl = lambda tl: tl[:].rearrange("p g m -> p (g m)")
                    TT("dve", ang[:], gbc(zi), ebc, ALU.mult, ["zi", "erow"], ["ang"])
                    TT("dve", mag[:], gbc(zr), ebc, ALU.mult, ["zr", "erow"], ["mag"])
                    ACT(mag[:], mag[:], AF.Exp, ["mag"], ["mag"])
                    TSS("dve", s4[:], ang[:], 1.0 / (2 * math.pi), ALU.mult, ["ang"], ["s4"])
                    CP("dve", kqi[:], s4[:], ["s4"], ["kqi"])
                    CP("dve", s4[:], kqi[:], ["kqi"], ["s4"])
                    STT(fl(ang), fl(s4), -2 * math.pi, fl(ang), ALU.mult, ALU.add, ["s4", "ang"], ["ang"])
                    ACT(s4[:], ang[:], AF.Sin, ["ang"], ["s4"], scale=0.25)
                    ACT(s2[:], ang[:], AF.Sin, ["ang"], ["s2"], scale=0.5)
                    TT("dve", s4[:], s4[:], s4[:], ALU.mult, ["s4"], ["s4"])
                    TS("dve", s4[:], s4[:], -2.0, 1.0, ALU.mult, ALU.add, ["s4"], ["s4"])
                    TT("dve", Pi[:], s2[:], s4[:], ALU.mult, ["s2", "s4"], ["Pi"])
                    STT(fl(Pi), fl(Pi), 2.0, fl(mag), ALU.mult, ALU.mult, ["Pi", "mag"], ["Pi"])
                    TT("dve", s2[:], s2[:], s2[:], ALU.mult, ["s2"], ["s2"])
                    TS("dve", s2[:], s2[:], -2.0, 1.0, ALU.mult, ALU.add, ["s2"], ["s2"])
                    TT("dve", Pr[:], s2[:], mag[:], ALU.mult, ["s2", "mag"], ["Pr"])
                    TSS("dve", t1[:], Pr[:, :, 6], -1.0, ALU.add, ["Pr"], ["t1"])
                    TT("dve", den[:], lamr[:], lamr[:], ALU.mult, ["lamr"], ["den"])
                    TT("dve", t2[:], lami[:], lami[:], ALU.mult, ["lami"], ["t2"])
                    TT("dve", den[:], den[:], t2[:], ALU.add, ["den", "t2"], ["den"])
                    S.op("dve", lambda e: e.reciprocal(out=den[:], in_=den[:]), ["den"], ["den"])
                    TT("dve", kr[:], t1[:], lamr[:], ALU.mult, ["t1", "lamr"], ["kr"])
                    TT("dve", t2[:], Pi[:, :, 6], lami[:], ALU.mult, ["Pi", "lami"], ["t2"])
                    TT("dve", kr[:], kr[:], t2[:], ALU.add, ["kr", "t2"], ["kr"])
                    TT("dve", kr[:], kr[:], den[:], ALU.mult, ["kr", "den"], ["kr"])
                    TT("dve", ki[:], Pi[:, :, 6], lamr[:], ALU.mult, ["Pi", "lamr"], ["ki"])
                    TT("dve", t2[:], t1[:], lami[:], ALU.mult, ["t1", "lami"], ["t2"])
                    TT("dve", ki[:], ki[:], t2[:], ALU.subtract, ["ki", "t2"], ["ki"])
                    TT("dve", ki[:], ki[:], den[:], ALU.mult, ["ki", "den"], ["ki"])
                    k8 = lambda tl: tl[:, :].unsqueeze(2).to_broadcast([64, 64, 8])
                    TT("dve", PKr[:], Pr[:, :, 0:8], k8(kr), ALU.mult, ["Pr", "kr"], ["PKr"])
                    TT("dve", u1[:], Pi[:, :, 0:8], k8(ki), ALU.mult, ["Pi", "ki"], ["u1"])
                    TT("dve", PKr[:], PKr[:], u1[:], ALU.subtract, ["PKr", "u1"], ["PKr"])
                    TT("dve", PKi[:], Pr[:, :, 0:8], k8(ki), ALU.mult, ["Pr", "ki"], ["PKi"])
                    TT("dve", u1[:], Pi[:, :, 0:8], k8(kr), ALU.mult, ["Pi", "kr"], ["u1"])
                    TT("dve", PKi[:], PKi[:], u1[:], ALU.add, ["PKi", "u1"], ["PKi"])
                    for (ti, src, sc) in ((0, Pr, 1.0), (1, Pi, 1.0), (2, Pi, -1.0), (3, mag, 1.0)):
                        for gh in range(2):
                            ACT(L8[gh * 64:(gh + 1) * 64, ti, :].rearrange("p (g j) -> p g j", j=8),
                                src[:, :, 24].rearrange("p (j g) -> p g j", g=8)[:, gh * 4:(gh + 1) * 4, :], AF.Copy, ["Pr", "Pi", "mag"], ["L8"], scale=sc)

                    def cmul(outr, outi, PA_r, PA_i, Xr, Xi, jb, key, rd, neg):
                        pbc = lambda P_: P_.unsqueeze(3).to_broadcast([64, 8, 8, 16])
                        xbc = lambda X_: X_[:, jb * 8:(jb + 1) * 8, :].unsqueeze(2).to_broadcast([64, 8, 8, 16])
                        TT("dve", outr[:], pbc(PA_r), xbc(Xr), ALU.mult, rd, [key + "r"])
                        TT("pool", v1[:], pbc(PA_i), xbc(Xi), ALU.mult, rd, ["v1"])
                        TT("dve", outr[:], outr[:], v1[:], ALU.subtract, [key + "r", "v1"], [key + "r"])
                        TT("dve", outi[:], pbc(PA_r), xbc(Xi), ALU.mult, rd, [key + "i"])
                        TT("pool", v2[:], pbc(PA_i), xbc(Xr), ALU.mult, rd, ["v2"])
                        TT("dve", outi[:], outi[:], v2[:], ALU.add, [key + "i", "v2"], [key + "i"])
                        if neg:
                            TSS("dve", outi[:], outi[:], -1.0, ALU.mult, [key + "i"], [key + "i"])

                    base_rd = ["Pr", "Pi", "PKr", "PKi", "Btr", "Bti", "Ctr", "Cti"]
                    sc_ = lambda ap: ap.rearrange("p s c -> p (s c)")
                    for jb in range(8):
                        gs = slice(jb * 8, (jb + 1) * 8)
                        cmul(Wtr, Wti, PKr[:, gs, :], PKi[:, gs, :], Btr, Bti, jb, "Wt", base_rd, False)
                        for (ri, src, nm) in ((0, Wtr, "Wtr"), (1, Wti, "Wti")):
                            for g8 in range(8):
                                bank, bk = (pa, "pa") if g8 < 4 else (pb, "pb")
                                TR(bank[:, (g8 % 4) * 64:(g8 % 4 + 1) * 64], sc_(src[:, g8, :, :]), ident_f[:64, :64], [nm], [bk])
                            for hb, (bank, bk) in enumerate(((pa, "pa"), (pb, "pb"))):
                                ACT(W_sb[:, :, ri * 64:(ri + 1) * 64].rearrange("p (g j) x -> p g j x", j=8)[:, hb * 4:(hb + 1) * 4, jb, :],
                                    bank[:, 0:256].rearrange("p (g x) -> p g x", x=64), AF.Copy, [bk], ["W_sb"])
                        cmul(Cxr, Cxn, Pr[:, gs, 16:24], Pi[:, gs, 16:24], Ctr, Cti, jb, "Cx", base_rd, True)
                        for g8 in range(8):
                            bank, bk = (pc, "pc") if g8 < 4 else (pd, "pd")
                            o = bank[:, (g8 % 4) * 128:(g8 % 4 + 1) * 128]
                            MM(o, sc_(Wtr[:, g8, :, :]), sc_(Cxr[:, g8, :, :]), True, False, ["Wtr", "Cxr"], [bk])
                            MM(o, sc_(Wti[:, g8, :, :]), sc_(Cxn[:, g8, :, :]), False, True, ["Wti", "Cxi"], [bk])
                        for hb, (bank, bk) in enumerate(((pc, "pc"), (pd, "pd"))):
                            TT("dve", T0_sb[:].rearrange("p (g j) x -> p g j x", j=8)[:, hb * 4:(hb + 1) * 4, jb, :],
                               bank[:].rearrange("p (g x) -> p g x", x=128), bmask[:].unsqueeze(1).to_broadcast([128, 4, 128]), ALU.mult,
                               [bk, "bmask"], ["T0_sb"])
                        cmul(Cxr, Cxn, Pr[:, gs, 8:16], Pi[:, gs, 8:16], Ctr, Cti, jb, "Cx", base_rd, True)
                        for (ri, src, nm) in ((0, Cxr, "Cxr"), (1, Cxn, "Cxi")):
                            for gh in range(2):
                                ACT(Cm_sb[gh * 64:(gh + 1) * 64, :, ri, :].rearrange("p (g j) x -> p g j x", j=8)[:, :, jb, :],
                                    src[:, gh * 4:(gh + 1) * 4, :, :].rearrange("p g s c -> p g (s c)"), AF.Copy, [nm], ["Cm_sb"])
                S.barrier()
                with ExitStack() as pm:
                    Hb2 = [sb(pm, "p3_Hb%d" % i, [128, 2, 32, 64], F32) for i in range(2)]
                    U2 = [sb(pm, "p3_U%d" % i, [128, 64, 64], BF16) for i in range(3)]
                    Hbf = sb(pm, "p3_Hbf", [128, 2, 32, 64], BF16)
                    Ysth = [sb(pm, "p3_Yst%d" % i, [128, 32, 64], F32) for i in range(2)]
                    Fr = sb(pm, "p3_Fr", [128, 32, 65], F32)
                    Fi = sb(pm, "p3_Fi", [128, 32, 65], F32)
                    Rt = sb(pm, "p3_R", [128, 32, 65], F32)
                    Gir = sb(pm, "p3_Gir", [128, 32, 65], F32)
                    Gii = sb(pm, "p3_Gii", [128, 32, 65], F32)
                    Gsr = sb(pm, "p3_Gsr", [128, 32, 65], F32)
                    Gsi = sb(pm, "p3_Gsi", [128, 32, 65], F32)
                    c1 = sb(pm, "p3_c1", [128, 3, 32], F32)
                    S.op("dve", lambda e: e.reciprocal(out=c1[:, 2, :], in_=L8[:, 3, :]), ["L8"], ["c1"])
                    TT("dve", c1[:, 0, :], L8[:, 0, :], c1[:, 2, :], ALU.mult, ["L8", "c1"], ["c1"])
                    TT("dve", c1[:, 1, :], L8[:, 1, :], c1[:, 2, :], ALU.mult, ["L8", "c1"], ["c1"])
                    MSET("pool", Fr[:, :, 0:1], 1.0, ["Fr"])
                    MSET("pool", Fi[:, :, 0:1], 0.0, ["Fi"])
                    MSET("pool", Rt[:, :, 0:1], 0.0, ["Rt"])
                    MSET("pool", Gir[:], 0.0, ["Gir"])
                    MSET("pool", Gii[:], 0.0, ["Gii"])
                    CP("dve", Fr[:, :, 1], c1[:, 0, :], ["c1", "Fr"], ["Fr"])
                    CP("dve", Fi[:, :, 1], c1[:, 1, :], ["c1", "Fi"], ["Fi"])
                    CP("dve", Rt[:, :, 1:65], L8[:, 3, :].unsqueeze(2).to_broadcast([128, 32, 64]), ["L8", "Rt"], ["Rt"])
                    for k in range(6):
                        m_ = 1 << k
                        a_bc = Fr[:, :, m_:m_ + 1].to_broadcast([128, 32, m_])
                        b_bc = Fi[:, :, m_:m_ + 1].to_broadcast([128, 32, m_])
                        sr, si = Fr[:, :, 1:1 + m_], Fi[:, :, 1:1 + m_]
                        dr, di = Fr[:, :, m_ + 1:2 * m_ + 1], Fi[:, :, m_ + 1:2 * m_ + 1]
                        u_, v_ = Gsr[:, :, 0:m_], Gsi[:, :, 0:m_]
                        TT("dve", u_, sr, a_bc, ALU.mult, ["Fr"], ["Gsr"])
                        TT("dve", v_, si, b_bc, ALU.mult, ["Fi"], ["Gsi"])
                        TT("dve", dr, u_, v_, ALU.subtract, ["Gsr", "Gsi", "Fr"], ["Fr"])
                        TT("dve", u_, sr, b_bc, ALU.mult, ["Fr", "Fi"], ["Gsr"])
                        TT("dve", v_, si, a_bc, ALU.mult, ["Fr", "Fi"], ["Gsi"])
                        TT("dve", di, u_, v_, ALU.add, ["Gsr", "Gsi", "Fi"], ["Fi"])
                    lbanks = [pa, pb]
                    ybanks = [pc, pd]
                    def u_load(ti):
                        (t0_, W_, _, _, _) = TILES[ti]
                        DMA(U2[ti % 3][:, :, :W_ // 8], UD[ti].rearrange("s c g j n -> (s c) (g j) n")[:, :, 0:W_ // 8],
                            [("UD", t0_)], [("U_sb", ti % 3)])

                    def stage_L(ti):
                        (t0, W, seq, first, last) = TILES[ti]
                        nC = W // 8
                        U_sb = U2[ti % 3]
                        ukey = ("U_sb", ti % 3)
                        Lb = Hb2[ti % 2]
                        for gl4 in range(8):
                            bank = lbanks[gl4 % 2]
                            bk = ("lb", gl4 % 2)
                            bv = bank[:].rearrange("p (r g n) -> p r g n", r=2, g=4)
                            for gh in range(2):
                                for gi in range(4):
                                    gidx = gh * 32 + gl4 * 4 + gi
                                    for ri in range(2):
                                        MM(bv[gh * 64:(gh + 1) * 64, ri, gi, :nC], W_sb[:, gidx, ri * 64:(ri + 1) * 64], U_sb[:, gidx, :nC], True, True,
                                           [ukey], [bk])
                            CP("act", Lb[:, :, gl4 * 4:(gl4 + 1) * 4, :nC], bv[:, :, :, :nC], [bk], [("Lb", ti % 2)])

                    def stage_S(ti):
                        (t0, W, seq, first, last) = TILES[ti]
                        nC = W // 8
                        n1 = nC + 1
                        Lb = Hb2[ti % 2]
                        lk = ("Lb", ti % 2)
                        if first:
                            if seq < 2:
                                MSET("pool", Hcar[:], 0.0, ["Hcar"])
                            else:
                                for (ri, src) in ((0, s5r0), (1, s5i0)):
                                    for g8 in range(8):
                                        gh, g4 = g8 // 4, g8 % 4
                                        DMA(Hcar[gh * 64:(gh + 1) * 64, ri, g4 * 8:(g4 + 1) * 8],
                                            src[l].rearrange("(j g) p -> p g j", g=8)[:, g8, :], [], ["Hcar"])
                        Lr_, Li_ = Lb[:, 0, :, 0:nC], Lb[:, 1, :, 0:nC]
                        Frs, Fis = Fr[:, :, 1:n1], Fi[:, :, 1:n1]
                        CP("dve", Gir[:, :, 0], Hcar[:, 0, :], ["Hcar"], ["Gir"])
                        CP("dve", Gii[:, :, 0], Hcar[:, 1, :], ["Hcar"], ["Gii"])
                        TT("dve", Gsr[:, :, 1:n1], Lr_, Frs, ALU.mult, [lk, "Fr"], ["Gsr"])
                        TT("dve", Gsi[:, :, 1:n1], Li_, Fis, ALU.mult, [lk, "Fi"], ["Gsi"])
                        TT("dve", Gir[:, :, 1:n1], Gsr[:, :, 1:n1], Gsi[:, :, 1:n1], ALU.add, ["Gsr", "Gsi"], ["Gir"])
                        TT("dve", Gsr[:, :, 1:n1], Li_, Frs, ALU.mult, [lk, "Fr", "Gir"], ["Gsr"])
                        TT("dve", Gsi[:, :, 1:n1], Lr_, Fis, ALU.mult, [lk, "Fi", "Gir"], ["Gsi"])
                        TT("dve", Gii[:, :, 1:n1], Gsr[:, :, 1:n1], Gsi[:, :, 1:n1], ALU.subtract, ["Gsr", "Gsi"], ["Gii"])
                        fl_ = lambda tl: tl[:].rearrange("p g n -> p (g n)")
                        S.op("dve", lambda e, o=fl_(Gsr), d0=fl_(Rt), d1=fl_(Gir): e.tensor_tensor_scan(out=o, data0=d0, data1=d1, initial=0.0, op0=ALU.mult, op1=ALU.add),
                             ["Rt", "Gir", "Gii"], ["Gsr"])
                        S.op("dve", lambda e, o=fl_(Gsi), d0=fl_(Rt), d1=fl_(Gii): e.tensor_tensor_scan(out=o, data0=d0, data1=d1, initial=0.0, op0=ALU.mult, op1=ALU.add),
                             ["Rt", "Gii", "Gsr"], ["Gsi"])
                        Frn, Fin = Fr[:, :, 0:n1], Fi[:, :, 0:n1]
                        TT("dve", Gir[:, :, 0:n1], Gsr[:, :, 0:n1], Frn, ALU.mult, ["Gsr", "Gsi"], ["Gir"])
                        TT("dve", Gii[:, :, 0:n1], Gsi[:, :, 0:n1], Fin, ALU.mult, ["Gsr", "Gsi"], ["Gii"])
                        TT("dve", Gir[:, :, 0:n1], Gir[:, :, 0:n1], Gii[:, :, 0:n1], ALU.subtract, ["Gir", "Gii"], ["Gir"])
                        CP("act", Hbf[:, 0, :, :nC], Gir[:, :, 0:nC], ["Gir"], ["Hbf"])
                        CP("dve", Hcar[:, 0, :], Gir[:, :, nC], ["Gir", "Hcar"], ["Hcar"])
                        TT("dve", Gir[:, :, 0:n1], Gsr[:, :, 0:n1], Fin, ALU.mult, ["Gsr", "Gsi", "Hbf", "Hcar"], ["Gir"])
                        TT("dve", Gii[:, :, 0:n1], Gsi[:, :, 0:n1], Frn, ALU.mult, ["Gsr", "Gsi", "Gir"], ["Gii"])
                        TT("dve", Gir[:, :, 0:n1], Gir[:, :, 0:n1], Gii[:, :, 0:n1], ALU.add, ["Gir", "Gii"], ["Gir"])
                        CP("act", Hbf[:, 1, :, :nC], Gir[:, :, 0:nC], ["Gir"], ["Hbf"])
                        CP("dve", Hcar[:, 1, :], Gir[:, :, nC], ["Gir", "Hcar"], ["Hcar"])

                    def stage_Y(ti):
                        (t0, W, seq, first, last) = TILES[ti]
                        nC = W // 8
                        n0 = t0 // 8
                        U_sb = U2[ti % 3]
                        ukey = ("U_sb", ti % 3)
                        for g8b in range(8):
                            bank = ybanks[g8b % 2]
                            bk = ("yb", g8b % 2)
                            bv = bank[:].rearrange("p (g n) -> p g n", g=8)
                            for gi in range(8):
                                gidx = g8b * 8 + gi
                                gh, gl = gidx // 32, gidx % 32
                                MM(bv[:, gi, :nC], T0_sb[:, gidx, :], U_sb[:, gidx, :nC], True, False, [ukey], [bk])
                                for ri in range(2):
                                    MM(bv[:, gi, :nC], Cm_sb[gh * 64:(gh + 1) * 64, gl, ri, :], Hbf[gh * 64:(gh + 1) * 64, ri, gl, :nC], False, ri == 1,
                                       ["Hbf"], [bk])
                            hb_ = g8b // 4
                            CP("act" if g8b % 2 else "dve", Ysth[hb_][:, (g8b % 4) * 8:(g8b % 4 + 1) * 8, :nC], bv[:, :, :nC], [bk], [("Yst", hb_)])
                            if g8b % 4 == 3:
                                DMA(YD[ti, :, :, hb_ * 4:(hb_ + 1) * 4, :, 0:nC].rearrange("s c g j n -> (s c) (g j) n"),
                                    Ysth[hb_][:, :, :nC], [("Yst", hb_)], [("YD", t0)])
                        if last:
                            for (ri, dst) in ((0, s5r_o), (1, s5i_o)):
                                for g8 in range(8):
                                    gh, g4 = g8 // 4, g8 % 4
                                    DMA(dst[l, seq].rearrange("(j g) p -> p g j", g=8)[:, g8, :],
                                        Hcar[gh * 64:(gh + 1) * 64, ri, g4 * 8:(g4 + 1) * 8], ["Hcar"], [])

                    u_load(0)
                    u_load(1)
                    stage_L(0)
                    for ti in range(len(TILES)):
                        if ti + 2 < len(TILES):
                            u_load(ti + 2)
                        if ti + 1 < len(TILES):
                            stage_L(ti + 1)
                        stage_S(ti)
                        stage_Y(ti)
            S.barrier()

        def phase4(l):
            with ExitStack() as ph:
                wg = sb(ph, "p4_w", [128, 8, 1024], BF16)
                ds5 = sb(ph, "p4_ds5", [128, 8], F32)
                bgl = sb(ph, "p4_bgl", [128, 8], F32)
                yp4 = [sb(ph, "p4_yp%d" % i, [128, 8, 8, 64], F32) for i in range(2)]
                ut4 = [sb(ph, "p4_ut%d" % i, [128, 8, 512], BF16) for i in range(2)]
                szb4 = [sb(ph, "p4_szb%d" % i, [128, 8, 512], BF16) for i in range(2)]
                yb = sb(ph, "p4_yb", [128, 8, 512], F32)
                gf = sb(ph, "p4_gf", [128, 8, 512], F32)
                gb = sb(ph, "p4_gb", [128, 8, 512], BF16)
                sg = [sb(ph, "p4_sg%d" % i, [128, 512], F32) for i in range(2)]
                ybst = sb(ph, "p4_ybst", [128, 8, 512], BF16)
                mm = [ps(ph, "p4_mm%d" % i, [128, 512]) for i in range(2)]
                for kt in range(8):
                    DMA(wg[:, kt, :], Wd["w_glu"][l, kt * 128:(kt + 1) * 128, :], [], ["wg"], q="pool")
                DMA(ds5[:], Wd["d_s5"][l].rearrange("(k p) -> p k", p=128), [], ["ds5"])
                DMA(bgl[:], Wd["b_glu"][l].rearrange("(k p) -> p k", p=128), [], ["bgl"])
                def p4_load(ti):
                    (t0, W, seq, first, last) = TILES[ti]
                    nC = W // 8
                    n0 = t0 // 8
                    pr = ti % 2
                    for g8 in range(8):
                        if nC == 64:
                            DMA(yp4[pr][g8 * 16:(g8 + 1) * 16, :, :, :].rearrange("c s j n -> c s (j n)"),
                                YD[ti, :, :, g8, :, :].rearrange("s c j n -> c s (j n)"), [("YD", t0)], [("yp4", pr)])
                        else:
                            for s_ in range(8):
                                DMA(yp4[pr][g8 * 16:(g8 + 1) * 16, s_, :, :nC], YD[ti, s_, :, g8, :, 0:nC], [("YD", t0)], [("yp4", pr)])
                        DMA(ut4[pr][g8 * 16:(g8 + 1) * 16, :, :W], UT[:, :, g8, t0:t0 + W].rearrange("j c t -> c j t"), [("UT", t0)], [("ut4", pr)])
                    DMA(szb4[pr][:, :, :W], SZB[:, :, t0:t0 + W].rearrange("m p t -> p m t"), [("SZB", t0)], [("szb4", pr)])

                p4_load(0)
                for ti, (t0, W, seq, first, last) in enumerate(TILES):
                    nC = W // 8
                    n0 = t0 // 8
                    pr = ti % 2
                    if ti + 1 < len(TILES):
                        p4_load(ti + 1)
                    for j in range(8):
                        STT(yb[:, j, :W].rearrange("p (n s) -> p n s", s=8), ut4[pr][:, j, :W].rearrange("p (n s) -> p n s", s=8), ds5[:, j:j + 1],
                            yp4[pr][:, :, j, :nC].rearrange("p s n -> p n s"), ALU.mult, ALU.add, [("ut4", pr), ("yp4", pr), "ds5"], [("yb", j)])
                        ACT(gf[:, j, :W], yb[:, j, :W], AF.Gelu_apprx_tanh, [("yb", j)], [("gf", j)])
                        CP("act" if j % 2 else "dve", gb[:, j, :W], gf[:, j, :W], [("gf", j)], ["gb"])
                    for jo in range(8):
                        bank = mm[jo % 2]
                        bk = ("mm", jo % 2)
                        for ji in range(8):
                            MM(bank[:, :W], wg[:, ji, jo * 128:(jo + 1) * 128], gb[:, ji, :W], ji == 0, ji == 7, ["wg", "gb"], [bk])
                        sgt = sg[jo % 2]
                        sk = ("sg", jo % 2)
                        ACT(sgt[:, :W], bank[:, :W], AF.Sigmoid, [bk, "bgl"], [sk], bias=bgl[:, jo:jo + 1], scale=1.0)
                        TT("dve", sgt[:, :W], sgt[:, :W], gf[:, jo, :W], ALU.mult, [sk, ("gf", jo)], [sk])
                        TT("dve", ybst[:, jo, :W], sgt[:, :W], szb4[pr][:, jo, :W], ALU.mult, [sk, ("szb4", pr)], ["ybst"])
                    DMA(YB[:, :, t0:t0 + W].rearrange("m p t -> p m t"), ybst[:, :, :W], ["ybst"], [("YB", t0)])
            S.barrier()

        def phase5(l):
            with ExitStack() as ph:
                wo = sb(ph, "p5_w", [128, 16, 1024], BF16)
                y5 = sb(ph, "p5_y", [128, 16, 512], BF16)
                xT5 = sb(ph, "p5_x", [128, 8, 512], F32)
                xo = sb(ph, "p5_xo", [128, 8, 512], F32)
                mm = [ps(ph, "p5_mm%d" % i, [128, 512]) for i in range(2)]
                for kt in range(16):
                    DMA(wo[:, kt, :], Wd["w_out"][l, kt * 128:(kt + 1) * 128, :], [], ["wo"], q="pool")
                for (t0, W, seq, first, last) in TILES:
                    DMA(y5[:, 0:8, :W], YA[:, :, t0:t0 + W].rearrange("m p t -> p m t"), [("YA", t0)], ["y5a"])
                    DMA(y5[:, 8:16, :W], YB[:, :, t0:t0 + W].rearrange("m p t -> p m t"), [("YB", t0)], ["y5b"])
                    DMA(xT5[:, :, :W], xt_ap(t0, W), [("XT", t0)], ["xT5"])
                    for dm in range(8):
                        bank = mm[dm % 2]
                        bk = ("mm", dm % 2)
                        for kt in range(16):
                            MM(bank[:, :W], wo[:, kt, dm * 128:(dm + 1) * 128], y5[:, kt, :W], kt == 0, kt == 15, ["wo", "y5a", "y5b"], [bk])
                        TT("dve", xo[:, dm, :W], bank[:, :W], xT5[:, dm, :W], ALU.add, [bk, "xT5"], ["xo"])
                    DMA(xt_ap(t0, W), xo[:, :, :W], ["xo"], [("XT", t0)])
            S.barrier()

        def phase_final():
            with ExitStack() as ph:
                fg = sb(ph, "pf_g", [128, 8], F32)
                xT = sb(ph, "pf_xT", [128, 8, 512], F32)
                sq = sb(ph, "pf_sq", [128, 8, 512], BF16)
                hf = sb(ph, "pf_hf", [128, 8, 512], F32)
                rs = sb(ph, "pf_rs", [128, 512], F32)
                rs2 = sb(ph, "pf_rs2", [128, 512], F32)
                yst = sb(ph, "pf_yst", [128, 4, 1024], F32)
                msp = ps(ph, "pf_ms", [128, 512])
                tp = [ps(ph, "pf_tp%d" % i, [128, 512]) for i in range(4)]
                DMA(fg[:], Wd["final_norm_g"].rearrange("(k p) -> p k", p=128), [], ["fg"])
                for (t0, W, seq, first, last) in TILES:
                    Pb = min(128, W)
                    nb = W // Pb
                    DMA(xT[:, :, :W], xt_ap(t0, W), [("XT", t0)], ["xT"])
                    ACT(sq[:, :, :W], xT[:, :, :W], AF.Square, ["xT"], ["nf_sq"])
                    rms_rstd(msp, [sq[:, kt, :W] for kt in range(8)], W, 1.0 / 1024, rs, rs2, "nf")
                    for kt in range(8):
                        STT(hf[:, kt, :W], xT[:, kt, :W], fg[:, kt:kt + 1], rs2[:, :W], ALU.mult, ALU.mult, ["xT", "fg", "nf_rs2"], ["hf"])
                    for b in range(nb):
                        for half in range(2):
                            bank = tp[(b * 2 + half) % 4]
                            bk = ("tp", (b * 2 + half) % 4)
                            for k4 in range(4):
                                kt = half * 4 + k4
                                TR(bank[:Pb, k4 * 128:(k4 + 1) * 128], hf[:, kt, b * Pb:(b + 1) * Pb], ident_f[:], ["hf"], [bk])
                            CP("act" if half else "dve", yst[:Pb, b, half * 512:(half + 1) * 512], bank[:Pb, :], [bk], ["yst"])
                    DMA(y_out[t0:t0 + W, :].rearrange("(b p) d -> p b d", p=Pb), yst[:Pb, :nb, :], ["yst"], [])
            S.barrier()

        wst = ExitStack()
        w_cur = sb(wst, "w_in_sb", [128, 8, IN_COLS], BF16)
        load_w_in(0, w_cur)
        phase0()
        for l in range(n_layers):
            if 1 in phases:
                phase1(l, w_cur)
            wst.close()
            if 2 in phases:
                phase2(l)
            if 3 in phases:
                phase3(l)
            if 4 in phases:
                phase4(l)
            if l + 1 < n_layers:
                wst = ExitStack()
                w_cur = sb(wst, "w_in_sb", [128, 8, IN_COLS], BF16)
                load_w_in(l + 1, w_cur)
            if 5 in phases:
                phase5(l)
        if do_final:
            phase_final()
        S.finish(block)
        nc._n_instr = S.n_instr
    return nc


def make_in_maps(inputs):
    xp = np.ascontiguousarray(inputs["x_prompt"], dtype=np.float32)
    xs = np.ascontiguousarray(inputs["x_sample"], dtype=np.float32)
    maps = []
    for c in range(NCORES):
        m = {}
        m["x_core"] = np.ascontiguousarray(np.concatenate([xp[2 * c], xp[2 * c + 1], xs[c]], axis=0))
        m["conv0"] = np.ascontiguousarray(inputs["state_ssd_conv"][:, c])
        m["ssd0"] = np.ascontiguousarray(inputs["state_ssd"][:, c])
        m["s5r0"] = np.ascontiguousarray(inputs["state_s5_re"][:, c])
        m["s5i0"] = np.ascontiguousarray(inputs["state_s5_im"][:, c])
        for n in W_NAMES:
            m[n] = np.ascontiguousarray(inputs[n], dtype=np.float32)
        maps.append(m)
    return maps


_NC_CACHE = {}


def kernel(**inputs):
    inputs = {k: np.asarray(v) for k, v in inputs.items()}
    if "nc" not in _NC_CACHE:
        _NC_CACHE["nc"] = build_nc()
    nc = _NC_CACHE["nc"]
    maps = make_in_maps(inputs)
    res = run_bass_kernel_spmd(nc, maps, core_ids=list(range(NCORES)))
    R = res.results
    y_prompt = np.zeros((16, 2048, 1024), np.float32)
    y_sample = np.zeros((8, 64, 1024), np.float32)
    conv_p = np.zeros((4, 16, 3, 2048), np.float32)
    ssd_p = np.zeros((4, 16, 16, 64, 128), np.float32)
    s5r_p = np.zeros((4, 16, 64, 64), np.float32)
    s5i_p = np.zeros((4, 16, 64, 64), np.float32)
    conv_s = np.zeros((4, 8, 3, 2048), np.float32)
    ssd_s = np.zeros((4, 8, 16, 64, 128), np.float32)
    s5r_s = np.zeros((4, 8, 64, 64), np.float32)
    s5i_s = np.zeros((4, 8, 64, 64), np.float32)
    for c in range(NCORES):
        r = R[c]
        y = r["y_out"]
        y_prompt[2 * c] = y[0:2048]
        y_prompt[2 * c + 1] = y[2048:4096]
        y_sample[c] = y[4096:4160]
        for i in range(2):
            conv_p[:, 2 * c + i] = r["conv_o"][:, i]
            ssd_p[:, 2 * c + i] = r["ssd_o"][:, i]
            s5r_p[:, 2 * c + i] = r["s5r_o"][:, i]
            s5i_p[:, 2 * c + i] = r["s5i_o"][:, i]
        conv_s[:, c] = r["conv_o"][:, 2]
        ssd_s[:, c] = r["ssd_o"][:, 2]
        s5r_s[:, c] = r["s5r_o"][:, 2]
        s5i_s[:, c] = r["s5i_o"][:, 2]
    return (y_prompt, y_sample, conv_p, ssd_p, s5r_p, s5i_p, conv_s, ssd_s, s5r_s, s5i_s)
```

```python
import math
from contextlib import ExitStack

import numpy as np
import concourse.bass as bass
import concourse.mybir as mybir
from concourse.bass_utils import run_bass_kernel_spmd

F32 = mybir.dt.float32
BF16 = mybir.dt.bfloat16
I32 = mybir.dt.int32
AF = mybir.ActivationFunctionType
ALU = mybir.AluOpType

NCORES = 8
DEPTH = 4
T = 4160
NCH = T // 8
EPS = 1e-6
IN_COLS = 5136
C_ZA, C_XBC, C_DT, C_U, C_ZB = 0, 1024, 3072, 3088, 4112
TILES = [(s * 2048 + i * 512, 512, s, i == 0, i == 3) for s in range(2) for i in range(4)]
TILES.append((4096, 64, 2, True, True))
import os as _os
if _os.environ.get("K_TILES"):
    TILES = [TILES[int(i)] for i in _os.environ["K_TILES"].split(",")]
K_SKIP = set(_os.environ.get("K_SKIP", "").split(","))

ENGS = ("pe", "dve", "act", "pool", "sp")


class Sync:
    def __init__(self, nc, stack, n_dma_sems=40, same_engine_sync=True):
        self.nc = nc
        self.esem = {e: stack.enter_context(nc.semaphore("s_" + e)) for e in ENGS}
        self.cnt = {e: 0 for e in ENGS}
        self.prog = {e: [] for e in ENGS}
        self.waited = {e: {} for e in ENGS}
        self.res = {}
        self.same = same_engine_sync
        self.dsems = [stack.enter_context(nc.semaphore("s_dma%d" % i)) for i in range(n_dma_sems)]
        self.dval = [0] * n_dma_sems
        self.dnext = 0
        self.sems = {}
        for e in ENGS:
            self.sems[("e", e)] = self.esem[e]
        for i, s in enumerate(self.dsems):
            self.sems[("d", i)] = s
        self.n_instr = 0

    def _need(self, e, reads, writes):
        need = {}

        def add(ev):
            if ev is None:
                return
            k, v = ev
            if need.get(k, 0) < v:
                need[k] = v

        for r in reads:
            st = self.res.get(r)
            if st is not None:
                add(st[0])
        for w in writes:
            st = self.res.get(w)
            if st is not None:
                add(st[0])
                for ev in st[1]:
                    add(ev)
        out = []
        for k, v in need.items():
            if k == ("e", e) and (e == "pe" or not self.same):
                continue
            if self.waited[e].get(k, 0) >= v:
                continue
            self.waited[e][k] = v
            out.append((k, v))
        return out

    def _commit(self, ev, reads, writes):
        for r in reads:
            st = self.res.setdefault(r, [None, []])
            st[1].append(ev)
            if len(st[1]) > 16:
                mx = {}
                for k, v in st[1]:
                    if mx.get(k, 0) < v:
                        mx[k] = v
                st[1] = list(mx.items())
        for w in writes:
            self.res[w] = [ev, []]

    def op(self, e, fn, reads=(), writes=()):
        waits = self._need(e, reads, writes)
        self.cnt[e] += 1
        ev = (("e", e), self.cnt[e])
        sem = self.esem[e]
        sems = self.sems

        def emit(eng, waits=waits, fn=fn, sem=sem):
            for k, v in waits:
                eng.wait_ge(sems[k], v)
            fn(eng).then_inc(sem, 1)

        self.prog[e].append(emit)
        self._commit(ev, reads, writes)
        self.n_instr += 1
        return ev

    def dma(self, e, fn, reads=(), writes=()):
        i = self.dnext
        self.dnext = (self.dnext + 1) % len(self.dsems)
        k = ("d", i)
        waits = self._need(e, reads, writes)
        if self.dval[i] > 0 and self.waited[e].get(k, 0) < self.dval[i]:
            self.waited[e][k] = self.dval[i]
            waits.append((k, self.dval[i]))
        self.dval[i] += 16
        ev = (k, self.dval[i])
        sem = self.dsems[i]
        sems = self.sems

        def emit(eng, waits=waits, fn=fn, sem=sem):
            for kk, v in waits:
                eng.wait_ge(sems[kk], v)
            fn(eng).then_inc(sem, 16)

        self.prog[e].append(emit)
        self._commit(ev, reads, writes)
        self.n_instr += 1
        return ev

    def barrier(self):
        evs = [(("e", e), self.cnt[e]) for e in ENGS if self.cnt[e] > 0]
        evs += [(("d", i), v) for i, v in enumerate(self.dval) if v > 0]
        sems = self.sems
        for e in ENGS:
            waits = []
            for k, v in evs:
                if k == ("e", e):
                    continue
                if self.waited[e].get(k, 0) >= v:
                    continue
                self.waited[e][k] = v
                waits.append((k, v))

            def emit(eng, waits=waits):
                for kk, v in waits:
                    eng.wait_ge(sems[kk], v)

            self.prog[e].append(emit)
        self.res = {}

    def finish(self, block):
        self.barrier()
        prog = self.prog

        @block.tensor
        def _(eng):
            for f in prog["pe"]:
                f(eng)

        @block.vector
        def _(eng):
            for f in prog["dve"]:
                f(eng)

        @block.scalar
        def _(eng):
            for f in prog["act"]:
                f(eng)

        @block.gpsimd
        def _(eng):
            for f in prog["pool"]:
                f(eng)

        @block.sync
        def _(eng):
            for f in prog["sp"]:
                f(eng)


W_NAMES = ["norm_g", "w_in", "conv_w", "conv_b", "dt_bias", "a_log", "d_ssd", "ssd_norm_g",
           "lam_re", "lam_im", "log_dt", "b_re", "b_im", "c_re", "c_im", "d_s5", "w_glu", "b_glu",
           "w_out", "final_norm_g"]
W_SHAPES = {
    "norm_g": [4, 1024], "w_in": [4, 1024, 5136], "conv_w": [4, 4, 2048], "conv_b": [4, 2048],
    "dt_bias": [4, 16], "a_log": [4, 16], "d_ssd": [4, 16], "ssd_norm_g": [4, 1024],
    "lam_re": [4, 64, 64], "lam_im": [4, 64, 64], "log_dt": [4, 64], "b_re": [4, 64, 64, 16],
    "b_im": [4, 64, 64, 16], "c_re": [4, 64, 16, 64], "c_im": [4, 64, 16, 64], "d_s5": [4, 1024],
    "w_glu": [4, 1024, 1024], "b_glu": [4, 1024], "w_out": [4, 2048, 1024], "final_norm_g": [1024],
}


def build_nc(n_layers=DEPTH, dbg=False, phases=(1, 2, 3, 4, 5), do_final=True):
    nc = bass.Bass("TRN2", target_bir_lowering=False)
    skind = "ExternalOutput" if dbg else "Internal"

    def din(name, shape, dt=F32):
        return nc.dram_tensor(name, shape, dt, kind="ExternalInput").ap()

    def dout(name, shape, dt=F32):
        return nc.dram_tensor(name, shape, dt, kind="ExternalOutput").ap()

    def dscr(name, shape, dt=F32):
        return nc.dram_tensor(name, shape, dt, kind=skind).ap()

    x_in = din("x_core", [T, 1024])
    conv0 = din("conv0", [4, 3, 2048])
    ssd0 = din("ssd0", [4, 16, 64, 128])
    s5r0 = din("s5r0", [4, 64, 64])
    s5i0 = din("s5i0", [4, 64, 64])
    Wd = {n: din(n, W_SHAPES[n]) for n in W_NAMES}

    y_out = dout("y_out", [T, 1024])
    conv_o = dout("conv_o", [4, 3, 3, 2048])
    ssd_o = dout("ssd_o", [4, 3, 16, 64, 128])
    s5r_o = dout("s5r_o", [4, 3, 64, 64])
    s5i_o = dout("s5i_o", [4, 3, 64, 64])

    XT = dscr("XT", [8, 128, T])
    SZA = dscr("SZA", [8, 128, T], BF16)
    XBC = dscr("XBC", [16, 128, T])
    DTR = dscr("DTR", [T, 16])
    UD = dscr("UD", [len(TILES), 8, 16, 8, 8, 64], BF16)
    UT = dscr("UT", [8, 16, 8, T], BF16)
    SZB = dscr("SZB", [8, 128, T], BF16)
    YA = dscr("YA", [8, 128, T], BF16)
    YD = dscr("YD", [len(TILES), 8, 16, 8, 8, 64])
    YB = dscr("YB", [8, 128, T], BF16)

    with ExitStack() as st:
        S = Sync(nc, st)
        st.enter_context(nc.allow_non_contiguous_dma(reason="small strided parameter/state loads"))

        uniq = [0]

        def sb(stack, name, shape, dt):
            uniq[0] += 1
            return stack.enter_context(nc.sbuf_tensor("%s_%d" % (name, uniq[0]), shape, dt))

        def ps(stack, name, shape, dt=F32):
            uniq[0] += 1
            return stack.enter_context(nc.psum_tensor("%s_%d" % (name, uniq[0]), shape, dt))


        def TT(eng, out, in0, in1, op, reads, writes):
            return S.op(eng, lambda e: e.tensor_tensor(out=out, in0=in0, in1=in1, op=op), reads, writes)

        def TS(eng, out, in0, s1, s2, op0, op1, reads, writes):
            return S.op(eng, lambda e: e.tensor_scalar(out=out, in0=in0, scalar1=s1, scalar2=s2, op0=op0, op1=op1), reads, writes)

        def TSS(eng, out, in_, scalar, op, reads, writes):
            return S.op(eng, lambda e: e.tensor_single_scalar(out=out, in_=in_, scalar=scalar, op=op), reads, writes)

        def STT(out, in0, scalar, in1, op0, op1, reads, writes):
            return S.op("dve", lambda e: e.scalar_tensor_tensor(out=out, in0=in0, scalar=scalar, in1=in1, op0=op0, op1=op1), reads, writes)

        def ACT(out, in_, func, reads, writes, bias=None, scale=None):
            kw = {}
            if bias is not None:
                kw["bias"] = bias
            if scale is not None:
                kw["scale"] = scale
            return S.op("act", lambda e: e.activation(out=out, in_=in_, func=func, **kw), reads, writes)

        def CP(eng, out, in_, reads, writes):
            if eng == "act":
                return ACT(out, in_, AF.Copy, reads, writes)
            return S.op(eng, lambda e: e.tensor_copy(out=out, in_=in_), reads, writes)

        def MM(out, lhsT, rhs, start, stop, reads, writes):
            return S.op("pe", lambda e: e.matmul(out, lhsT=lhsT, rhs=rhs, start=start, stop=stop), reads, writes)

        def TR(out, in_, ident, reads, writes):
            return S.op("pe", lambda e: e.transpose(out=out, in_=in_, identity=ident), reads, writes)

        def DMA(out, in_, reads=(), writes=(), q="sp"):
            return S.dma(q, lambda e: e.dma_start(out=out, in_=in_), reads, writes)

        def MSET(eng, ap, val, writes):
            return S.op(eng, lambda e: e.memset(ap, val), (), writes)

        ones_f = sb(st, "ones_f", [128, 128], F32)
        ident_f = sb(st, "ident_f", [128, 128], F32)
        ident_b = sb(st, "ident_b", [128, 128], BF16)
        ones_b = sb(st, "ones_b", [128, 128], BF16)
        tri_f = sb(st, "tri_f", [128, 128], F32)
        tri_b = sb(st, "tri_b", [128, 128], BF16)
        bmask = sb(st, "bmask", [128, 128], F32)
        block = st.enter_context(nc.Block())

        S.op("pool", lambda e: e.memset(ones_f[:], 1.0), writes=["ones_f"])
        S.op("pool", lambda e: e.affine_select(out=ident_f[:], in_=ones_f[:], pattern=[[1, 128]],
                                               compare_op=ALU.is_equal, fill=0.0, base=0, channel_multiplier=-1),
             reads=["ones_f"], writes=["ident_f"])
        S.op("pool", lambda e: e.affine_select(out=tri_f[:], in_=ones_f[:], pattern=[[1, 128]],
                                               compare_op=ALU.is_ge, fill=0.0, base=0, channel_multiplier=-1),
             reads=["ones_f"], writes=["tri_f"])
        S.op("pool", lambda e: e.affine_select(out=bmask[:], in_=ones_f[:], pattern=[[16, 8], [0, 16]],
                                               compare_op=ALU.is_ge, fill=0.0, base=15, channel_multiplier=-1),
             reads=["ones_f"], writes=["bmask"])
        S.op("dve", lambda e: e.tensor_copy(out=ident_b[:], in_=ident_f[:]), reads=["ident_f"], writes=["ident_b"])
        S.op("dve", lambda e: e.tensor_copy(out=ones_b[:], in_=ones_f[:]), reads=["ones_f"], writes=["ones_b"])
        S.op("dve", lambda e: e.tensor_copy(out=tri_b[:], in_=tri_f[:]), reads=["tri_f"], writes=["tri_b"])
        S.barrier()

        def xt_ap(t0, W):
            return XT[:, :, t0:t0 + W].rearrange("k p t -> p k t")

        def rms_rstd(ph_ps, src_sq, W, scale, rs, rs2, key):
            nk = len(src_sq)
            for i, a in enumerate(src_sq):
                MM(ph_ps[:, :W], ones_b[:], a, i == 0, i == nk - 1, [key + "_sq"], [key + "_ms"])
            ACT(rs[:, :W], ph_ps[:, :W], AF.Ln, [key + "_ms"], [key + "_rs"], bias=EPS, scale=scale)
            ACT(rs2[:, :W], rs[:, :W], AF.Exp, [key + "_rs"], [key + "_rs2"], scale=-0.5)

        def phase0():
            with ExitStack() as ph:
                xs = sb(ph, "p0_xs", [128, 4, 1024], F32)
                xo = sb(ph, "p0_xo", [128, 8, 512], F32)
                tp = [ps(ph, "p0_tp%d" % i, [128, 512]) for i in range(2)]
                for (t0, W, seq, first, last) in TILES:
                    Pb = min(128, W)
                    nb = W // Pb
                    DMA(xs[:Pb, :nb, :], x_in[t0:t0 + W, :].rearrange("(b p) d -> p b d", p=Pb), [], ["xs"])
                    for kt in range(8):
                        tpk = tp[kt % 2]
                        for b in range(nb):
                            TR(tpk[:, b * Pb:(b + 1) * Pb], xs[:Pb, b, kt * 128:(kt + 1) * 128], ident_f[:Pb, :Pb], ["xs"], [("tp", kt % 2)])
                        CP("act" if kt % 2 else "dve", xo[:, kt, :W], tpk[:, :W], [("tp", kt % 2)], ["xo"])
                    DMA(xt_ap(t0, W), xo[:, :, :W], ["xo"], [("XT", t0)])
            S.barrier()

        def load_w_in(l, w_sb):
            for kt in range(8):
                for (c0, cw_) in ((0, 2048), (2048, 2048), (4096, 1040)):
                    DMA(w_sb[:, kt, c0:c0 + cw_], Wd["w_in"][l, kt * 128:(kt + 1) * 128, c0:c0 + cw_], [], ["w"], q="pool")

        def phase1(l, w_sb):
            with ExitStack() as ph:
                g1 = sb(ph, "p1_g", [128, 8], F32)
                xT = sb(ph, "p1_xT", [128, 8, 512], F32)
                sq = sb(ph, "p1_sq", [128, 8, 512], BF16)
                hT = sb(ph, "p1_hT", [128, 8, 512], BF16)
                rs = sb(ph, "p1_rs", [128, 512], F32)
                rs2 = sb(ph, "p1_rs2", [128, 512], F32)
                st_za = sb(ph, "p1_za", [128, 8, 512], BF16)
                st_xbc = [sb(ph, "p1_xbc%d" % i, [128, 2, 512], F32) for i in range(2)]
                st_us = sb(ph, "p1_us", [128, 8, 8, 64], BF16)
                st_ut = sb(ph, "p1_ut", [128, 8, 512], BF16)
                us32 = [sb(ph, "p1_us32_%d" % i, [128, 8, 64], F32) for i in range(2)]
                st_zb = sb(ph, "p1_zb", [128, 8, 512], BF16)
                st_dt = sb(ph, "p1_dt", [128, 4, 16], F32)
                mm = [ps(ph, "p1_mm%d" % i, [128, 512]) for i in range(4)]
                msp = ps(ph, "p1_ms", [128, 512])
                dtp = ps(ph, "p1_dtp", [128, 4, 16])
                DMA(g1[:], Wd["norm_g"][l].rearrange("(k p) -> p k", p=128), [], ["g1"])
                for kt in range(0 if "perm" in K_SKIP else 8):
                    tmpu = st_ut[:, (kt % 2) * 2:(kt % 2) * 2 + 2, :].rearrange("p a t -> p (a t)")
                    CP("dve", tmpu, w_sb[:, kt, C_U:C_U + 1024], ["w"], [("tmpu", kt % 2), "st_ut"])
                    CP("act", w_sb[:, kt, C_U:C_U + 1024].rearrange("p (j c g) -> p j c g", c=16, g=8),
                       tmpu.rearrange("p (j g c) -> p j c g", g=8, c=16), [("tmpu", kt % 2), "st_ut"], ["w"])
                mcount = [0]
                def pro_load(ti):
                    (t0_, W_, _, _, _) = TILES[ti]
                    DMA(xT[:, :, :W_], xt_ap(t0_, W_), [("XT", t0_)], ["xT"])

                def pro_sq(ti):
                    W_ = TILES[ti][1]
                    ACT(sq[:, :, :W_], xT[:, :, :W_], AF.Square, ["xT"], ["n1_sq"])

                def pro_ms(ti):
                    W_ = TILES[ti][1]
                    rms_rstd(msp, [sq[:, kt, :W_] for kt in range(8)], W_, 1.0 / 1024, rs, rs2, "n1")

                pro_load(0)
                pro_sq(0)
                pro_ms(0)
                for ti, (t0, W, seq, first, last) in enumerate(TILES):
                    Pb = min(128, W)
                    nb = W // Pb
                    nC = W // 8
                    n0 = t0 // 8
                    nxt = ti + 1 if ti + 1 < len(TILES) else None
                    for kt in range(8):
                        STT(hT[:, kt, :W], xT[:, kt, :W], g1[:, kt:kt + 1], rs2[:, :W], ALU.mult, ALU.mult, ["xT", "g1", "n1_rs2"], ["hT"])
                    if nxt is not None:
                        pro_load(nxt)

                    def mtile(lhs_fn, W=W):
                        bank = mm[mcount[0] % 4]
                        key = ("mm", mcount[0] % 4)
                        mcount[0] += 1
                        for kt in range(8):
                            MM(bank[:, :W], lhs_fn(kt), hT[:, kt, :W], kt == 0, kt == 7, ["w", "hT"], [key])
                        return bank, key

                    for m in range(0 if "za" in K_SKIP else 8):
                        bank, key = mtile(lambda kt, m=m: w_sb[:, kt, C_ZA + m * 128:C_ZA + (m + 1) * 128])
                        ACT(st_za[:, m, :W], bank[:, :W], AF.Silu, [key], ["st_za"])
                    if "za" not in K_SKIP:
                        DMA(SZA[:, :, t0:t0 + W].rearrange("m p t -> p m t"), st_za[:, :, :W], ["st_za"], [("SZA", t0)])
                    if nxt is not None:
                        pro_sq(nxt)
                    for m in range(0 if "xbc" in K_SKIP else 16):
                        bank, key = mtile(lambda kt, m=m: w_sb[:, kt, C_XBC + m * 128:C_XBC + (m + 1) * 128])
                        pr = m // 2
                        stx = st_xbc[pr % 2]
                        skey = ("st_xbc", pr % 2)
                        CP("act" if m % 2 == 0 else "dve", stx[:, m % 2, :W], bank[:, :W], [key], [skey])
                        if m % 2 == 1:
                            DMA(XBC[pr * 2:(pr + 1) * 2, :, t0:t0 + W].rearrange("m p t -> p m t"), stx[:, :, :W], [skey], [("XBC", t0, pr)])
                    if nxt is not None:
                        pro_ms(nxt)
                    for j in range(0 if "u" in K_SKIP else 8):
                        bank, key = mtile(lambda kt, j=j: w_sb[:, kt, C_U + j * 128:C_U + (j + 1) * 128])
                        u32 = us32[j % 2]
                        ACT(u32[:, :, :nC].rearrange("p s n -> p n s"), bank[:, :W].rearrange("p (n s) -> p n s", s=8), AF.Copy, [key], [("us32", j % 2), ("brd", key)])
                        CP("dve", st_us[:, :, j, :nC], u32[:, :, :nC], [("us32", j % 2)], ["st_us"])
                        CP("dve", st_ut[:, j, :W], bank[:, :W], [key, ("brd", key)], ["st_ut"])
                    if nC == 64:
                        DMA(UD[ti].rearrange("s c g j n -> (c g) s (j n)"), st_us[:].rearrange("q s j n -> q s (j n)"), ["st_us"], [("UD", t0)])
                    else:
                        for j in range(8):
                            DMA(UD[ti, :, :, :, j, 0:nC].rearrange("s c g n -> (c g) s n"), st_us[:, :, j, :nC], ["st_us"], [("UD", t0)])
                    if "u" not in K_SKIP and "utdma" not in K_SKIP:
                        DMA(UT[:, :, :, t0:t0 + W].rearrange("j c g t -> (c g) j t"), st_ut[:, :, :W], ["st_ut"], [("UT", t0)])
                    for m in range(0 if "zb" in K_SKIP else 8):
                        bank, key = mtile(lambda kt, m=m: w_sb[:, kt, C_ZB + m * 128:C_ZB + (m + 1) * 128])
                        ACT(st_zb[:, m, :W], bank[:, :W], AF.Silu, [key], ["st_zb"])
                    if "zb" not in K_SKIP:
                        DMA(SZB[:, :, t0:t0 + W].rearrange("m p t -> p m t"), st_zb[:, :, :W], ["st_zb"], [("SZB", t0)])
                    for b in range(0 if "dt" in K_SKIP else nb):
                        for kt in range(8):
                            MM(dtp[:Pb, b, :], hT[:, kt, b * Pb:(b + 1) * Pb], w_sb[:, kt, C_DT:C_DT + 16], kt == 0, kt == 7, ["w", "hT"], ["dtp"])
                    if "dt" not in K_SKIP:
                        ACT(st_dt[:Pb, :nb, :], dtp[:Pb, :nb, :], AF.Copy, ["dtp"], ["st_dt"])
                        DMA(DTR[t0:t0 + W, :].rearrange("(b p) h -> p b h", p=Pb), st_dt[:Pb, :nb, :], ["st_dt"], [("DTR", t0)])
            S.barrier()

        def phase2(l):
            with ExitStack() as ph:
                cw = sb(ph, "p2_cw", [128, 16, 4], F32)
                cbias = sb(ph, "p2_cb", [128, 16], F32)
                dtb = sb(ph, "p2_dtb", [128, 16], F32)
                arow = sb(ph, "p2_arow", [128, 16], F32)
                dvec = sb(ph, "p2_dvec", [128, 8], F32)
                ng = sb(ph, "p2_ng", [128, 8], F32)
                tails = sb(ph, "p2_tails", [128, 16, 3], F32)
                hstate = sb(ph, "p2_hst", [128, 1024], F32)
                hbf = sb(ph, "p2_hbf", [128, 1024], BF16)
                hio = sb(ph, "p2_hio", [128, 8, 128], F32)
                raw = sb(ph, "p2_raw", [128, 4, 515], F32)
                rawb = sb(ph, "p2_rawb", [128, 4, 515], BF16)
                dw = sb(ph, "p2_dw", [128, 16, 4, 128], BF16)
                xc2 = [sb(ph, "p2_xc%d" % i, [128, 16, 512], BF16) for i in range(2)]
                sza2 = [sb(ph, "p2_sza%d" % i, [128, 8, 512], BF16) for i in range(2)]
                dtr = sb(ph, "p2_dtr", [128, 4, 16], F32)
                x1 = sb(ph, "p2_x1", [128, 4, 16], F32)
                tA = sb(ph, "p2_tA", [128, 4, 16], F32)
                tB = sb(ph, "p2_tB", [128, 4, 16], F32)
                dtv2 = [sb(ph, "p2_dtv%d" % i, [128, 4, 16], F32) for i in range(2)]
                av = sb(ph, "p2_av", [128, 4, 16], F32)
                a_hi2 = [sb(ph, "p2_ahi%d" % i, [128, 4, 16], BF16) for i in range(2)]
                a_lo2 = [sb(ph, "p2_alo%d" % i, [128, 4, 16], BF16) for i in range(2)]
                xdt2 = [sb(ph, "p2_xdt%d" % i, [128, 1024], BF16) for i in range(2)]
                xe2 = [sb(ph, "p2_xe%d" % i, [128, 1024], BF16) for i in range(2)]
                btok2 = [sb(ph, "p2_btok%d" % i, [128, 512], BF16) for i in range(2)]
                acs2 = [sb(ph, "p2_acs%d" % i, [128, 16], F32) for i in range(2)]
                nacs2 = [sb(ph, "p2_nacs%d" % i, [128, 16], F32) for i in range(2)]
                tmd2 = [sb(ph, "p2_tmd%d" % i, [128, 16], F32) for i in range(2)]
                te2 = [sb(ph, "p2_te%d" % i, [128, 16], F32) for i in range(2)]
                etot2 = [sb(ph, "p2_etot%d" % i, [128, 16], F32) for i in range(2)]
                cbm2 = [sb(ph, "p2_cbm%d" % i, [128, 4, 128], BF16) for i in range(2)]
                Mh4 = [[sb(ph, "p2_Mh%d_%d" % (i, g), [128, 512], BF16) for g in range(4)] for i in range(2)]
                Cs4 = [[sb(ph, "p2_Cs%d_%d" % (i, g), [128, 512], BF16) for g in range(4)] for i in range(2)]
                E = [sb(ph, "p2_E%d" % i, [128, 512], BF16) for i in range(2)]
                dec = [sb(ph, "p2_dec%d" % i, [128, 512], BF16) for i in range(2)]
                yg = sb(ph, "p2_yg", [128, 8, 512], F32)
                sq = sb(ph, "p2_sq", [128, 8, 512], BF16)
                rs4 = [sb(ph, "p2_rs4_%d" % i, [128, 512], F32) for i in range(4)]
                htmp = sb(ph, "p2_htmp", [128, 512], F32)
                ya = sb(ph, "p2_ya", [128, 8, 512], BF16)
                smallp = ps(ph, "p2_small", [128, 512])
                tpp = ps(ph, "p2_tp", [128, 1024], BF16)
                cbp = ps(ph, "p2_cbp", [128, 512])
                acsb = [ps(ph, "p2_acsb%d" % i, [128, 512]) for i in range(2)]
                yps = [ps(ph, "p2_yps%d" % i, [128, 512]) for i in range(2)]
                hps = ps(ph, "p2_hps", [128, 512])

                for k in range(4):
                    DMA(cw[:, :, k], Wd["conv_w"][l, k].rearrange("(m p) -> p m", p=128), [], ["cw"])
                DMA(cbias[:], Wd["conv_b"][l].rearrange("(m p) -> p m", p=128), [], ["cbias"])
                DMA(dtb[:], Wd["dt_bias"][l].partition_broadcast(128), [], ["dtb"])
                DMA(arow[:], Wd["a_log"][l].partition_broadcast(128), [], ["arow"])
                for h2 in range(2):
                    DMA(dvec[h2 * 64:(h2 + 1) * 64, :], Wd["d_ssd"][l].rearrange("(m h) -> h m", h=2)[h2].partition_broadcast(64), [], ["dvec"])
                DMA(ng[:], Wd["ssd_norm_g"][l].rearrange("(k p) -> p k", p=128), [], ["ng"])
                for m in range(16):
                    for k in range(4):
                        TS("pool" if (m * 4 + k) % 3 == 0 else "dve", dw[:, m, k, :], ident_b[:], cw[:, m, k:k + 1], None, ALU.mult, ALU.bypass, ["cw", "ident_b"], ["dw"])
                ACT(arow[:], arow[:], AF.Exp, ["arow"], ["arow"])
                TSS("dve", arow[:], arow[:], -1.0, ALU.mult, ["arow"], ["arow"])

                def seq_state_init(ti):
                    (t0, W, seq, first, last) = TILES[ti]
                    if not first:
                        return
                    if seq < 2:
                        MSET("pool", hstate[:], 0.0, ["hstate"])
                        MSET("pool", hbf[:], 0.0, ["hbf"])
                    else:
                        DMA(hio[:], ssd0[l].rearrange("(m h) p n -> (h p) m n", h=2), [], ["hio"])
                        for m in range(8):
                            TR(hps[:, (m % 4) * 128:(m % 4 + 1) * 128], hio[:, m, :], ident_f[:], ["hio"], ["hps"])
                            if m % 4 == 3:
                                hh = m // 4
                                CP("dve", hstate[:, hh * 512:(hh + 1) * 512], hps[:], ["hps"], ["hstate"])
                        CP("act", hbf[:], hstate[:], ["hstate"], ["hbf"])

                def conv_blk(ti, gq):
                    (t0, W, seq, first, last) = TILES[ti]
                    Qc = min(128, W)
                    nck = W // Qc
                    tp_ = ti % 2
                    xc, sza, dtv, a_hi, a_lo = xc2[tp_], sza2[tp_], dtv2[tp_], a_hi2[tp_], a_lo2[tp_]
                    if gq == 0 and first:
                        if seq < 2:
                            MSET("pool", tails[:], 0.0, ["tails"])
                        else:
                            for k in range(3):
                                DMA(tails[:, :, k], conv0[l, k].rearrange("(m p) -> p m", p=128), [], ["tails"])
                    DMA(raw[:, :, 3:3 + W], XBC[gq * 4:(gq + 1) * 4, :, t0:t0 + W].rearrange("m p t -> p m t"),
                        [("XBC", t0, 2 * gq), ("XBC", t0, 2 * gq + 1)], ["raw"])
                    CP("dve", raw[:, :, 0:3], tails[:, gq * 4:(gq + 1) * 4, :], ["tails"], ["raw"])
                    ACT(rawb[:, :, 0:3 + W], raw[:, :, 0:3 + W], AF.Copy, ["raw"], ["rawb"])
                    for mi in range(4):
                        m = gq * 4 + mi
                        bnk, bkey = ((cbp, "cbp"), (hps, "hps"))[mi % 2]
                        for k in range(4):
                            MM(bnk[:, :W], dw[:, m, k, :], rawb[:, mi, k:k + W], k == 0, k == 3, ["rawb", "dw"], [bkey])
                        ACT(xc[:, m, :W], bnk[:, :W], AF.Silu, [bkey, "cbias"], [("xc", tp_, m)], bias=cbias[:, m:m + 1], scale=1.0)
                    CP("dve", tails[:, gq * 4:(gq + 1) * 4, :], raw[:, :, W:W + 3], ["raw"], ["tails"])
                    if gq == 3 and last:
                        for k in range(3):
                            DMA(conv_o[l, seq, k].rearrange("(m p) -> p m", p=128), tails[:, :, k], ["tails"], [])

                def pre(ti):
                    (t0, W, seq, first, last) = TILES[ti]
                    Qc = min(128, W)
                    nck = W // Qc
                    tp_ = ti % 2
                    xc, sza, dtv, a_hi, a_lo = xc2[tp_], sza2[tp_], dtv2[tp_], a_hi2[tp_], a_lo2[tp_]
                    DMA(sza[:, :, :W], SZA[:, :, t0:t0 + W].rearrange("m p t -> p m t"), [("SZA", t0)], [("sza", tp_)])
                    DMA(dtr[:Qc, :nck, :], DTR[t0:t0 + W, :].rearrange("(b p) h -> p b h", p=Qc), [("DTR", t0)], ["dtr"])
                    sl = lambda tl: tl[:Qc, :nck, :]
                    bc = lambda tl: tl[:Qc, :].unsqueeze(1).to_broadcast([Qc, nck, 16])
                    TT("dve", sl(x1), sl(dtr), bc(dtb), ALU.add, ["dtr", "dtb"], ["x1"])
                    STT(sl(tA), sl(x1), -1.0, sl(x1), ALU.mult, ALU.max, ["x1"], ["tA"])
                    ACT(sl(tA), sl(tA), AF.Exp, ["tA"], ["tA"], scale=-1.0)
                    ACT(sl(tA), sl(tA), AF.Ln, ["tA"], ["tA"], bias=1.0, scale=1.0)
                    TSS("dve", sl(tB), sl(x1), 0.0, ALU.max, ["x1"], ["tB"])
                    TT("dve", sl(dtv), sl(tA), sl(tB), ALU.add, ["tA", "tB"], [("dtv", tp_)])
                    TT("dve", sl(av), sl(dtv), bc(arow), ALU.mult, [("dtv", tp_), "arow"], ["av"])
                    CP("dve", sl(a_hi), sl(av), ["av"], [("a_hi", tp_)])
                    TT("dve", sl(a_lo), sl(av), sl(a_hi), ALU.subtract, ["av", ("a_hi", tp_)], [("a_lo", tp_)])

                def chunks(ti, hook):
                    (t0, W, seq, first, last) = TILES[ti]
                    Qc = min(128, W)
                    nck = W // Qc
                    tp_ = ti % 2
                    xc, sza, dtv, a_hi, a_lo = xc2[tp_], sza2[tp_], dtv2[tp_], a_hi2[tp_], a_lo2[tp_]
                    hd = lambda ap: ap.rearrange("p (h d) -> p h d", d=64)

                    def front_pre(c):
                        cp = c % 2
                        cs = c * Qc
                        xdt, xe, btok, acs, tmd, te, etot, cbm = xdt2[cp], xe2[cp], btok2[cp], acs2[cp], tmd2[cp], te2[cp], etot2[cp], cbm2[cp]
                        k = lambda n: (n, cp)
                        for mq in range(2):
                            for mi in range(4):
                                TR(tpp[:Qc, mq * 512 + mi * 128:mq * 512 + (mi + 1) * 128], xc[:, mq * 4 + mi, cs:cs + Qc], ident_b[:],
                                   [("xc", tp_, mq * 4 + mi)], ["tp"])
                            TT("dve", hd(xdt[:Qc, mq * 512:(mq + 1) * 512]), hd(tpp[:Qc, mq * 512:(mq + 1) * 512]),
                               dtv[:Qc, c, mq * 8:(mq + 1) * 8].unsqueeze(2).to_broadcast([Qc, 8, 64]), ALU.mult, ["tp", ("dtv", tp_)], [k("xdt%d" % mq)])
                        for g in range(4):
                            TR(tpp[:Qc, g * 128:(g + 1) * 128], xc[:, 8 + g, cs:cs + Qc], ident_b[:], [("xc", tp_, 8 + g)], ["tp"])
                        CP("act", btok[:Qc, :], tpp[:Qc, 0:512], ["tp"], [k("btok")])
                        for i, aa in enumerate((a_hi, a_lo)):
                            MM(smallp[:Qc, 0:16], tri_b[:Qc, :Qc], aa[:Qc, c, :], i == 0, i == 1, [("a_hi", tp_), ("a_lo", tp_), "tri_b"], ["small"])
                        for i, aa in enumerate((a_hi, a_lo)):
                            MM(smallp[:, 16:32], ones_b[:Qc, :], aa[:Qc, c, :], i == 0, i == 1, [("a_hi", tp_), ("a_lo", tp_)], ["small"])
                        CP("act", acs[:Qc, :], smallp[:Qc, 0:16], ["small"], [k("acs")])
                        ACT(nacs2[cp][:Qc, :], smallp[:Qc, 0:16], AF.Copy, ["small"], [k("nacs")], scale=-1.0)
                        ACT(etot[:], smallp[:, 16:32], AF.Exp, ["small"], [k("etot")])
                        TT("dve", tmd[:Qc, :], smallp[:Qc, 16:32], acs[:Qc, :], ALU.subtract, ["small", k("acs"), k("etot")], [k("tmd")])
                        ACT(te[:Qc, :], tmd[:Qc, :], AF.Exp, [k("tmd")], [k("te")])
                        TT("pool", hd(xe[:Qc, :]), hd(xdt[:Qc, :]), te[:Qc, :].unsqueeze(2).to_broadcast([Qc, 16, 64]), ALU.mult,
                           [k("xdt0"), k("xdt1"), k("te")], [k("xe")])
                        for g in range(4):
                            MM(cbp[:Qc, g * 128:g * 128 + Qc], xc[:, 8 + g, cs:cs + Qc], xc[:, 12 + g, cs:cs + Qc], True, True,
                               [("xc", tp_, 8 + g), ("xc", tp_, 12 + g)], ["cbp"])
                        TT("dve", cbm[:Qc, :, :Qc], cbp[:Qc, :].rearrange("p (g l) -> p g l", g=4)[:, :, :Qc],
                           tri_b[:Qc, :Qc].unsqueeze(1).to_broadcast([Qc, 4, Qc]), ALU.mult, ["cbp", "tri_b"], [k("cbm")])

                    def front_grp(c, g):
                        cp = c % 2
                        cs = c * Qc
                        par = g % 2
                        acs, cbm = acs2[cp], cbm2[cp]
                        k = lambda n: (n, cp)
                        ab = acsb[par]
                        abv = ab[:].rearrange("p (h l) -> p h l", h=4)
                        for hh in range(4):
                            h = g * 4 + hh
                            for i, aa in enumerate((a_hi, a_lo)):
                                MM(abv[:, hh, :Qc], aa[:Qc, c, h:h + 1].to_broadcast([Qc, 128]), tri_b[:Qc, :Qc], i == 0, i == 1,
                                   [("a_hi", tp_), ("a_lo", tp_)], [("acsb", par)])
                        Ev = E[par][:].rearrange("p (h l) -> p h l", h=4)
                        ACT(Ev[:, :, :Qc], abv[:, :, :Qc], AF.Exp, [("acsb", par)], [("E", par)])
                        dv = dec[par][:].rearrange("p (h l) -> p h l", h=4)
                        for hh in range(4):
                            h = g * 4 + hh
                            ACT(dv[:Qc, hh, :Qc], abv[:Qc, hh, :Qc], AF.Exp, [("acsb", par), k("nacs")], [("dec", par)], bias=nacs2[cp][:Qc, h:h + 1], scale=1.0)
                        Mv = Mh4[cp][g][:].rearrange("p (h l) -> p h l", h=4)
                        STT(Mv[:Qc, :, :Qc], dv[:Qc, :, :Qc], 1.0, cbm[:Qc, g, :Qc].unsqueeze(1).to_broadcast([Qc, 4, Qc]), ALU.min, ALU.mult,
                            [("dec", par), k("cbm")], [("Mh", cp, g)])
                        Cv = Cs4[cp][g][:].rearrange("p (h l) -> p h l", h=4)
                        TT("pool", Cv[:, :, :Qc], xc[:, 12 + g, cs:cs + Qc].unsqueeze(1).to_broadcast([128, 4, Qc]), Ev[:, :, :Qc], ALU.mult,
                           [("xc", tp_, 12 + g), ("E", par)], [("Cs", cp, g)])

                    def back_grp(c, g):
                        cp = c % 2
                        cs = c * Qc
                        par = g % 2
                        xdt = xdt2[cp]
                        k = lambda n: (n, cp)
                        Mv = Mh4[cp][g][:].rearrange("p (h l) -> p h l", h=4)
                        Cv = Cs4[cp][g][:].rearrange("p (h l) -> p h l", h=4)
                        yp = yps[par]
                        ykey = ("yps", par)
                        for hh in range(4):
                            h = g * 4 + hh
                            o = yp[(hh % 2) * 64:(hh % 2 + 1) * 64, (hh // 2) * 128:(hh // 2) * 128 + Qc]
                            MM(o, xdt[:Qc, h * 64:(h + 1) * 64], Mv[:Qc, hh, :Qc], True, False, [k("xdt%d" % (h // 8)), ("Mh", cp, g)], [ykey])
                            MM(o, hbf[:, h * 64:(h + 1) * 64], Cv[:, hh, :Qc], False, True, ["hbf", ("Cs", cp, g)], [ykey])
                        for mm_ in range(2):
                            m = 2 * g + mm_
                            STT(yg[:, m, cs:cs + Qc], xc[:, m, cs:cs + Qc], dvec[:, m:m + 1], yp[:, mm_ * 128:mm_ * 128 + Qc], ALU.mult, ALU.add,
                                [("xc", tp_, m), "dvec", ykey], ["yg"])

                    def back_post(c):
                        cp = c % 2
                        xe, btok, etot = xe2[cp], btok2[cp], etot2[cp]
                        k = lambda n: (n, cp)
                        for half in range(2):
                            for gi in range(2):
                                g = half * 2 + gi
                                MM(hps[:, gi * 256:(gi + 1) * 256], btok[:Qc, g * 128:(g + 1) * 128], xe[:Qc, g * 256:(g + 1) * 256], True, True,
                                   [k("btok"), k("xe")], ["hps"])
                            TT("pool", hd(htmp[:]), hd(hstate[:, half * 512:(half + 1) * 512]),
                               etot[:, half * 8:(half + 1) * 8].unsqueeze(2).to_broadcast([128, 8, 64]), ALU.mult, ["hstate", k("etot")], ["htmp"])
                            TT("dve", hstate[:, half * 512:(half + 1) * 512], htmp[:], hps[:], ALU.add, ["htmp", "hps"], ["hstate"])
                        CP("act", hbf[:], hstate[:], ["hstate"], ["hbf"])

                    front_pre(0)
                    for g in range(4):
                        front_grp(0, g)
                    for c in range(nck):
                        more = c + 1 < nck
                        if more:
                            front_pre(c + 1)
                        for g in range(4):
                            if more:
                                front_grp(c + 1, g)
                            back_grp(c, g)
                        back_post(c)
                        hook(c, nck)

                def post(ti):
                    (t0, W, seq, first, last) = TILES[ti]
                    Qc = min(128, W)
                    nck = W // Qc
                    tp_ = ti % 2
                    xc, sza, dtv, a_hi, a_lo = xc2[tp_], sza2[tp_], dtv2[tp_], a_hi2[tp_], a_lo2[tp_]
                    TT("pool", yg[:, 0:3, :W], yg[:, 0:3, :W], sza[:, 0:3, :W], ALU.mult, ["yg", ("sza", tp_)], ["yg"])
                    TT("dve", yg[:, 3:8, :W], yg[:, 3:8, :W], sza[:, 3:8, :W], ALU.mult, ["yg", ("sza", tp_)], ["yg"])
                    ACT(sq[:, 3:8, :W], yg[:, 3:8, :W], AF.Square, ["yg"], ["sq_b"])
                    ACT(sq[:, 0:3, :W], yg[:, 0:3, :W], AF.Square, ["yg"], ["sq_a"])
                    nbanks = [(cbp, "cbp"), (acsb[0], ("acsb", 0)), (acsb[1], ("acsb", 1)), (hps, "hps")]
                    for gg in (3, 2, 1, 0):
                        bnk, bkey = nbanks[gg]
                        for i in range(2):
                            MM(bnk[:, :W], ones_b[:], sq[:, 2 * gg + i, :W], i == 0, i == 1, ["sq_a", "sq_b"], [bkey])
                    for gg in (3, 2, 1, 0):
                        bnk, bkey = nbanks[gg]
                        ACT(rs4[gg][:, :W], bnk[:, :W], AF.Ln, [bkey], [("rs4", gg)], bias=EPS, scale=1.0 / 256)
                    for gg in (3, 2, 1, 0):
                        ACT(rs4[gg][:, :W], rs4[gg][:, :W], AF.Exp, [("rs4", gg)], [("rs4", gg)], scale=-0.5)
                    for gg in (3, 2, 1, 0):
                        for m in (2 * gg, 2 * gg + 1):
                            STT(ya[:, m, :W], yg[:, m, :W], ng[:, m:m + 1], rs4[gg][:, :W], ALU.mult, ALU.mult, ["yg", "ng", ("rs4", gg)], ["ya"])
                    DMA(YA[:, :, t0:t0 + W].rearrange("m p t -> p m t"), ya[:, :, :W], ["ya"], [("YA", t0)])
                    if last:
                        for m in range(8):
                            TR(hps[:, (m % 4) * 128:(m % 4 + 1) * 128], hstate[:, m * 128:(m + 1) * 128], ident_f[:], ["hstate"], ["hps"])
                            if m % 4 == 3:
                                hh = m // 4
                                CP("dve", hio[:, hh * 4:(hh + 1) * 4, :], hps[:].rearrange("p (m n) -> p m n", n=128), ["hps"], ["hio"])
                        DMA(ssd_o[l, seq].rearrange("(m h) p n -> (h p) m n", h=2), hio[:], ["hio"], [])

                for gq in range(4):
                    conv_blk(0, gq)
                pre(0)
                for ti in range(len(TILES)):
                    nxt = ti + 1 if ti + 1 < len(TILES) else None

                    def hook(c, nck, nxt=nxt):
                        if nxt is None:
                            return
                        per = 4 // nck if nck <= 4 else 1
                        for gq in range(c * per, (c + 1) * per):
                            conv_blk(nxt, gq)
                        if c == nck - 1:
                            pre(nxt)

                    seq_state_init(ti)
                    chunks(ti, hook)
                    post(ti)
            S.barrier()

        def phase3(l):
            with ExitStack() as ph:
                W_sb = sb(ph, "p3_W", [128, 64, 128], BF16)
                T0_sb = sb(ph, "p3_T0", [128, 64, 128], BF16)
                Cm_sb = sb(ph, "p3_Cm", [128, 32, 2, 128], BF16)
                L8 = sb(ph, "p3_L8", [128, 4, 32], F32)
                Hcar = sb(ph, "p3_Hcar", [128, 2, 32], F32)
                pa = ps(ph, "p3_pa", [128, 512])
                pb = ps(ph, "p3_pb", [128, 512])
                pc = ps(ph, "p3_pc", [128, 512])
                pd = ps(ph, "p3_pd", [128, 512])
                with ExitStack() as pp:
                    f64 = lambda nm, shp, dt=F32: sb(pp, nm, shp, dt)
                    lamr, lami, dtg, zr, zi = [f64("q_" + n, [64, 64]) for n in ("lamr", "lami", "dtg", "zr", "zi")]
                    kr, ki, t1, t2, den = [f64("q_" + n, [64, 64]) for n in ("kr", "ki", "t1", "t2", "den")]
                    erow_i = f64("q_erowi", [64, 25], I32)
                    erow = f64("q_erow", [64, 25])
                    ang, mag, s4, s2, Pr, Pi = [f64("q_" + n, [64, 64, 25]) for n in ("ang", "mag", "s4", "s2", "Pr", "Pi")]
                    kqi = f64("q_kqi", [64, 64, 25], I32)
                    PKr, PKi, u1 = [f64("q_" + n, [64, 64, 8]) for n in ("PKr", "PKi", "u1")]
                    Btr, Bti, Ctr, Cti = [f64("q_" + n, [64, 64, 16]) for n in ("Btr", "Bti", "Ctr", "Cti")]
                    Cn = sb(pp, "q_Cn", [128, 8, 64], F32)
                    Wtr, Wti, v1, v2, Cxr, Cxn = [f64("q_" + n, [64, 8, 8, 16]) for n in ("Wtr", "Wti", "v1", "v2", "Cxr", "Cxn")]

                    DMA(lamr[:], Wd["lam_re"][l].rearrange("g p -> p g"), [], ["lamr"])
                    DMA(lami[:], Wd["lam_im"][l].rearrange("g p -> p g"), [], ["lami"])
                    DMA(dtg[:], Wd["log_dt"][l].partition_broadcast(64), [], ["dtg"])
                    DMA(Btr[:], Wd["b_re"][l].rearrange("g p c -> p g c"), [], ["Btr"])
                    DMA(Bti[:], Wd["b_im"][l].rearrange("g p c -> p g c"), [], ["Bti"])
                    for (src, dst, nm) in ((Wd["c_re"], Ctr, "Ctr"), (Wd["c_im"], Cti, "Cti")):
                        DMA(Cn[:], src[l].rearrange("(j g) c p -> (g c) j p", g=8), [], ["Cn"])
                        for j in range(8):
                            TR(pa[:64, (j % 4) * 128:(j % 4 + 1) * 128], Cn[:, j, :], ident_f[:], ["Cn"], ["pa"])
                            if j % 4 == 3:
                                jj = j // 4
                                CP("dve", dst[:, jj * 32:(jj + 1) * 32, :], pa[:64, :].rearrange("p (g c) -> p g c", c=16), ["pa"], [nm])
                    ACT(dtg[:], dtg[:], AF.Exp, ["dtg"], ["dtg"])
                    TT("dve", zr[:], lamr[:], dtg[:], ALU.mult, ["lamr", "dtg"], ["zr"])
                    TT("dve", zi[:], lami[:], dtg[:], ALU.mult, ["lami", "dtg"], ["zi"])
                    for (a, b_, pat, base) in ((0, 8, [[-1, 8]], 7), (8, 16, [[1, 8]], 1), (16, 24, [[1, 8]], -7), (24, 25, [[1, 1]], 8)):
                        oap = erow_i[:, a:b_]
                        S.op("pool", lambda e, oap=oap, pat=pat, base=base: e.iota(out=oap, pattern=pat, base=base, channel_multiplier=0), (), ["erow_i"])
                    CP("dve", erow[:], erow_i[:], ["erow_i"], ["erow"])
                    ebc = erow[:, :].unsqueeze(1).to_broadcast([64, 64, 25])
                    gbc = lambda tl: tl[:, :].unsqueeze(2).to_broadcast([64, 64, 25])
                    fl = lambda tl: tl[:].rearrange("p g m -> p (g m)")
                    TT("dve", ang[:], gbc(zi), ebc, ALU.mult, ["zi", "erow"], ["ang"])
                    TT("dve", mag[:], gbc(zr), ebc, ALU.mult, ["zr", "erow"], ["mag"])
                    ACT(mag[:], mag[:], AF.Exp, ["mag"], ["mag"])
                    TSS("dve", s4[:], ang[:], 1.0 / (2 * math.pi), ALU.mult, ["ang"], ["s4"])
                    CP("dve", kqi[:], s4[:], ["s4"], ["kqi"])
                    CP("dve", s4[:], kqi[:], ["kqi"], ["s4"])
                    STT(fl(ang), fl(s4), -2 * math.pi, fl(ang), ALU.mult, ALU.add, ["s4", "ang"], ["ang"])
                    ACT(s4[:], ang[:], AF.Sin, ["ang"], ["s4"], scale=0.25)
                    ACT(s2[:], ang[:], AF.Sin, ["ang"], ["s2"], scale=0.5)
                    TT("dve", s4[:], s4[:], s4[:], ALU.mult, ["s4"], ["s4"])
                    TS("dve", s4[:], s4[:], -2.0, 1.0, ALU.mult, ALU.add, ["s4"], ["s4"])
                    TT("dve", Pi[:], s2[:], s4[:], ALU.mult, ["s2", "s4"], ["Pi"])
                    STT(fl(Pi), fl(Pi), 2.0, fl(mag), ALU.mult, ALU.mult, ["Pi", "mag"], ["Pi"])
                    TT("dve", s2[:], s2[:], s2[:], ALU.mult, ["s2"], ["s2"])
                    TS("dve", s2[:], s2[:], -2.0, 1.0, ALU.mult, ALU.add, ["s2"], ["s2"])
                    TT("dve", Pr[:], s2[:], mag[:], ALU.mult, ["s2", "mag"], ["Pr"])
                    TSS("dve", t1[:], Pr[:, :, 6], -1.0, ALU.add, ["Pr"], ["t1"])
                    TT("dve", den[:], lamr[:], lamr[:], ALU.mult, ["lamr"], ["den"])
                    TT("dve", t2[:], lami[:], lami[:], ALU.mult, ["lami"], ["t2"])
                    TT("dve", den[:], den[:], t2[:], ALU.add, ["den", "t2"], ["den"])
                    S.op("dve", lambda e: e.reciprocal(out=den[:], in_=den[:]), ["den"], ["den"])
                    TT("dve", kr[:], t1[:], lamr[:], ALU.mult, ["t1", "lamr"], ["kr"])
                    TT("dve", t2[:], Pi[:, :, 6], lami[:], ALU.mult, ["Pi", "lami"], ["t2"])
                    TT("dve", kr[:], kr[:], t2[:], ALU.add, ["kr", "t2"], ["kr"])
                    TT("dve", kr[:], kr[:], den[:], ALU.mult, ["kr", "den"], ["kr"])
                    TT("dve", ki[:], Pi[:, :, 6], lamr[:], ALU.mult, ["Pi", "lamr"], ["ki"])
                    TT("dve", t2[:], t1[:], lami[:], ALU.mult, ["t1", "lami"], ["t2"])
                    TT("dve", ki[:], ki[:], t2[:], ALU.subtract, ["ki", "t2"], ["ki"])
                    TT("dve", ki[:], ki[:], den[:], ALU.mult, ["ki", "den"], ["ki"])
                    k8 = lambda tl: tl[:, :].unsqueeze(2).to_broadcast([64, 64, 8])
                    TT("dve", PKr[:], Pr[:, :, 0:8], k8(kr), ALU.mult, ["Pr", "kr"], ["PKr"])
                    TT("dve", u1[:], Pi[:, :, 0:8], k8(ki), ALU.mult, ["Pi", "ki"], ["u1"])
                    TT("dve", PKr[:], PKr[:], u1[:], ALU.subtract, ["PKr", "u1"], ["PKr"])
                    TT("dve", PKi[:], Pr[:, :, 0:8], k8(ki), ALU.mult, ["Pr", "ki"], ["PKi"])
                    TT("dve", u1[:], Pi[:, :, 0:8], k8(kr), ALU.mult, ["Pi", "kr"], ["u1"])
                    TT("dve", PKi[:], PKi[:], u1[:], ALU.add, ["PKi", "u1"], ["PKi"])
                    for (ti, src, sc) in ((0, Pr, 1.0), (1, Pi, 1.0), (2, Pi, -1.0), (3, mag, 1.0)):
                        for gh in range(2):
                            ACT(L8[gh * 64:(gh + 1) * 64, ti, :].rearrange("p (g j) -> p g j", j=8),
                                src[:, :, 24].rearrange("p (j g) -> p g j", g=8)[:, gh * 4:(gh + 1) * 4, :], AF.Copy, ["Pr", "Pi", "mag"], ["L8"], scale=sc)

                    def cmul(outr, outi, PA_r, PA_i, Xr, Xi, jb, key, rd, neg):
                        pbc = lambda P_: P_.unsqueeze(3).to_broadcast([64, 8, 8, 16])
                        xbc = lambda X_: X_[:, jb * 8:(jb + 1) * 8, :].unsqueeze(2).to_broadcast([64, 8, 8, 16])
                        TT("dve", outr[:], pbc(PA_r), xbc(Xr), ALU.mult, rd, [key + "r"])
                        TT("pool", v1[:], pbc(PA_i), xbc(Xi), ALU.mult, rd, ["v1"])
                        TT("dve", outr[:], outr[:], v1[:], ALU.subtract, [key + "r", "v1"], [key + "r"])
                        TT("dve", outi[:], pbc(PA_r), xbc(Xi), ALU.mult, rd, [key + "i"])
                        TT("pool", v2[:], pbc(PA_i), xbc(Xr), ALU.mult, rd, ["v2"])
                        TT("dve", outi[:], outi[:], v2[:], ALU.add, [key + "i", "v2"], [key + "i"])
                        if neg:
                            TSS("dve", outi[:], outi[:], -1.0, ALU.mult, [key + "i"], [key + "i"])

                    base_rd = ["Pr", "Pi", "PKr", "PKi", "Btr", "Bti", "Ctr", "Cti"]
                    sc_ = lambda ap: ap.rearrange("p s c -> p (s c)")
                    for jb in range(8):
                        gs = slice(jb * 8, (jb + 1) * 8)
                        cmul(Wtr, Wti, PKr[:, gs, :], PKi[:, gs, :], Btr, Bti, jb, "Wt", base_rd, False)
                        for (ri, src, nm) in ((0, Wtr, "Wtr"), (1, Wti, "Wti")):
                            for g8 in range(8):
                                bank, bk = (pa, "pa") if g8 < 4 else (pb, "pb")
                                TR(bank[:, (g8 % 4) * 64:(g8 % 4 + 1) * 64], sc_(src[:, g8, :, :]), ident_f[:64, :64], [nm], [bk])
                            for hb, (bank, bk) in enumerate(((pa, "pa"), (pb, "pb"))):
                                ACT(W_sb[:, :, ri * 64:(ri + 1) * 64].rearrange("p (g j) x -> p g j x", j=8)[:, hb * 4:(hb + 1) * 4, jb, :],
                                    bank[:, 0:256].rearrange("p (g x) -> p g x", x=64), AF.Copy, [bk], ["W_sb"])
                        cmul(Cxr, Cxn, Pr[:, gs, 16:24], Pi[:, gs, 16:24], Ctr, Cti, jb, "Cx", base_rd, True)
                        for g8 in range(8):
                            bank, bk = (pc, "pc") if g8 < 4 else (pd, "pd")
                            o = bank[:, (g8 % 4) * 128:(g8 % 4 + 1) * 128]
                            MM(o, sc_(Wtr[:, g8, :, :]), sc_(Cxr[:, g8, :, :]), True, False, ["Wtr", "Cxr"], [bk])
                            MM(o, sc_(Wti[:, g8, :, :]), sc_(Cxn[:, g8, :, :]), False, True, ["Wti", "Cxi"], [bk])
                        for hb, (bank, bk) in enumerate(((pc, "pc"), (pd, "pd"))):
                            TT("dve", T0_sb[:].rearrange("p (g j) x -> p g j x", j=8)[:, hb * 4:(hb + 1) * 4, jb, :],
                               bank[:].rearrange("p (g x) -> p g x", x=128), bmask[:].unsqueeze(1).to_broadcast([128, 4, 128]), ALU.mult,
                               [bk, "bmask"], ["T0_sb"])
                        cmul(Cxr, Cxn, Pr[:, gs, 8:16], Pi[:, gs, 8:16], Ctr, Cti, jb, "Cx", base_rd, True)
                        for (ri, src, nm) in ((0, Cxr, "Cxr"), (1, Cxn, "Cxi")):
                            for gh in range(2):
                                ACT(Cm_sb[gh * 64:(gh + 1) * 64, :, ri, :].rearrange("p (g j) x -> p g j x", j=8)[:, :, jb, :],
                                    src[:, gh * 4:(gh + 1) * 4, :, :].rearrange("p g s c -> p g (s c)"), AF.Copy, [nm], ["Cm_sb"])
                S.barrier()
                with ExitStack() as pm:
                    Hb2 = [sb(pm, "p3_Hb%d" % i, [128, 2, 32, 64], F32) for i in range(2)]
                    U2 = [sb(pm, "p3_U%d" % i, [128, 64, 64], BF16) for i in range(3)]
                    Hbf = sb(pm, "p3_Hbf", [128, 2, 32, 64], BF16)
                    Ysth = [sb(pm, "p3_Yst%d" % i, [128, 32, 64], F32) for i in range(2)]
                    Fr = sb(pm, "p3_Fr", [128, 32, 65], F32)
                    Fi = sb(pm, "p3_Fi", [128, 32, 65], F32)
                    Rt = sb(pm, "p3_R", [128, 32, 65], F32)
                    Gir = sb(pm, "p3_Gir", [128, 32, 65], F32)
                    Gii = sb(pm, "p3_Gii", [128, 32, 65], F32)
                    Gsr = sb(pm, "p3_Gsr", [128, 32, 65], F32)
                    Gsi = sb(pm, "p3_Gsi", [128, 32, 65], F32)
                    c1 = sb(pm, "p3_c1", [128, 3, 32], F32)
                    S.op("dve", lambda e: e.reciprocal(out=c1[:, 2, :], in_=L8[:, 3, :]), ["L8"], ["c1"])
                    TT("dve", c1[:, 0, :], L8[:, 0, :], c1[:, 2, :], ALU.mult, ["L8", "c1"], ["c1"])
                    TT("dve", c1[:, 1, :], L8[:, 1, :], c1[:, 2, :], ALU.mult, ["L8", "c1"], ["c1"])
                    MSET("pool", Fr[:, :, 0:1], 1.0, ["Fr"])
                    MSET("pool", Fi[:, :, 0:1], 0.0, ["Fi"])
                    MSET("pool", Rt[:, :, 0:1], 0.0, ["Rt"])
                    MSET("pool", Gir[:], 0.0, ["Gir"])
                    MSET("pool", Gii[:], 0.0, ["Gii"])
                    CP("dve", Fr[:, :, 1], c1[:, 0, :], ["c1", "Fr"], ["Fr"])
                    CP("dve", Fi[:, :, 1], c1[:, 1, :], ["c1", "Fi"], ["Fi"])
                    CP("dve", Rt[:, :, 1:65], L8[:, 3, :].unsqueeze(2).to_broadcast([128, 32, 64]), ["L8", "Rt"], ["Rt"])
                    for k in range(6):
                        m_ = 1 << k
                        a_bc = Fr[:, :, m_:m_ + 1].to_broadcast([128, 32, m_])
                        b_bc = Fi[:, :, m_:m_ + 1].to_broadcast([128, 32, m_])
                        sr, si = Fr[:, :, 1:1 + m_], Fi[:, :, 1:1 + m_]
                        dr, di = Fr[:, :, m_ + 1:2 * m_ + 1], Fi[:, :, m_ + 1:2 * m_ + 1]
                        u_, v_ = Gsr[:, :, 0:m_], Gsi[:, :, 0:m_]
                        TT("dve", u_, sr, a_bc, ALU.mult, ["Fr"], ["Gsr"])
                        TT("dve", v_, si, b_bc, ALU.mult, ["Fi"], ["Gsi"])
                        TT("dve", dr, u_, v_, ALU.subtract, ["Gsr", "Gsi", "Fr"], ["Fr"])
                        TT("dve", u_, sr, b_bc, ALU.mult, ["Fr", "Fi"], ["Gsr"])
                        TT("dve", v_, si, a_bc, ALU.mult, ["Fr", "Fi"], ["Gsi"])
                        TT("dve", di, u_, v_, ALU.add, ["Gsr", "Gsi", "Fi"], ["Fi"])
                    lbanks = [pa, pb]
                    ybanks = [pc, pd]
                    def u_load(ti):
                        (t0_, W_, _, _, _) = TILES[ti]
                        DMA(U2[ti % 3][:, :, :W_ // 8], UD[ti].rearrange("s c g j n -> (s c) (g j) n")[:, :, 0:W_ // 8],
                            [("UD", t0_)], [("U_sb", ti % 3)])

                    def stage_L(ti):
                        (t0, W, seq, first, last) = TILES[ti]
                        nC = W // 8
                        U_sb = U2[ti % 3]
                        ukey = ("U_sb", ti % 3)
                        Lb = Hb2[ti % 2]
                        for gl4 in range(8):
                            bank = lbanks[gl4 % 2]
                            bk = ("lb", gl4 % 2)
                            bv = bank[:].rearrange("p (r g n) -> p r g n", r=2, g=4)
                            for gh in range(2):
                                for gi in range(4):
                                    gidx = gh * 32 + gl4 * 4 + gi
                                    for ri in range(2):
                                        MM(bv[gh * 64:(gh + 1) * 64, ri, gi, :nC], W_sb[:, gidx, ri * 64:(ri + 1) * 64], U_sb[:, gidx, :nC], True, True,
                                           [ukey], [bk])
                            CP("act", Lb[:, :, gl4 * 4:(gl4 + 1) * 4, :nC], bv[:, :, :, :nC], [bk], [("Lb", ti % 2)])

                    def stage_S(ti):
                        (t0, W, seq, first, last) = TILES[ti]
                        nC = W // 8
                        n1 = nC + 1
                        Lb = Hb2[ti % 2]
                        lk = ("Lb", ti % 2)
                        if first:
                            if seq < 2:
                                MSET("pool", Hcar[:], 0.0, ["Hcar"])
                            else:
                                for (ri, src) in ((0, s5r0), (1, s5i0)):
                                    for g8 in range(8):
                                        gh, g4 = g8 // 4, g8 % 4
                                        DMA(Hcar[gh * 64:(gh + 1) * 64, ri, g4 * 8:(g4 + 1) * 8],
                                            src[l].rearrange("(j g) p -> p g j", g=8)[:, g8, :], [], ["Hcar"])
                        Lr_, Li_ = Lb[:, 0, :, 0:nC], Lb[:, 1, :, 0:nC]
                        Frs, Fis = Fr[:, :, 1:n1], Fi[:, :, 1:n1]
                        CP("dve", Gir[:, :, 0], Hcar[:, 0, :], ["Hcar"], ["Gir"])
                        CP("dve", Gii[:, :, 0], Hcar[:, 1, :], ["Hcar"], ["Gii"])
                        TT("dve", Gsr[:, :, 1:n1], Lr_, Frs, ALU.mult, [lk, "Fr"], ["Gsr"])
                        TT("dve", Gsi[:, :, 1:n1], Li_, Fis, ALU.mult, [lk, "Fi"], ["Gsi"])
                        TT("dve", Gir[:, :, 1:n1], Gsr[:, :, 1:n1], Gsi[:, :, 1:n1], ALU.add, ["Gsr", "Gsi"], ["Gir"])
                        TT("dve", Gsr[:, :, 1:n1], Li_, Frs, ALU.mult, [lk, "Fr", "Gir"], ["Gsr"])
                        TT("dve", Gsi[:, :, 1:n1], Lr_, Fis, ALU.mult, [lk, "Fi", "Gir"], ["Gsi"])
                        TT("dve", Gii[:, :, 1:n1], Gsr[:, :, 1:n1], Gsi[:, :, 1:n1], ALU.subtract, ["Gsr", "Gsi"], ["Gii"])
                        fl_ = lambda tl: tl[:].rearrange("p g n -> p (g n)")
                        S.op("dve", lambda e, o=fl_(Gsr), d0=fl_(Rt), d1=fl_(Gir): e.tensor_tensor_scan(out=o, data0=d0, data1=d1, initial=0.0, op0=ALU.mult, op1=ALU.add),
                             ["Rt", "Gir", "Gii"], ["Gsr"])
                        S.op("dve", lambda e, o=fl_(Gsi), d0=fl_(Rt), d1=fl_(Gii): e.tensor_tensor_scan(out=o, data0=d0, data1=d1, initial=0.0, op0=ALU.mult, op1=ALU.add),
                             ["Rt", "Gii", "Gsr"], ["Gsi"])
                        Frn, Fin = Fr[:, :, 0:n1], Fi[:, :, 0:n1]
                        TT("dve", Gir[:, :, 0:n1], Gsr[:, :, 0:n1], Frn, ALU.mult, ["Gsr", "Gsi"], ["Gir"])
                        TT("dve", Gii[:, :, 0:n1], Gsi[:, :, 0:n1], Fin, ALU.mult, ["Gsr", "Gsi"], ["Gii"])
                        TT("dve", Gir[:, :, 0:n1], Gir[:, :, 0:n1], Gii[:, :, 0:n1], ALU.subtract, ["Gir", "Gii"], ["Gir"])
                        CP("act", Hbf[:, 0, :, :nC], Gir[:, :, 0:nC], ["Gir"], ["Hbf"])
                        CP("dve", Hcar[:, 0, :], Gir[:, :, nC], ["Gir", "Hcar"], ["Hcar"])
                        TT("dve", Gir[:, :, 0:n1], Gsr[:, :, 0:n1], Fin, ALU.mult, ["Gsr", "Gsi", "Hbf", "Hcar"], ["Gir"])
                        TT("dve", Gii[:, :, 0:n1], Gsi[:, :, 0:n1], Frn, ALU.mult, ["Gsr", "Gsi", "Gir"], ["Gii"])
                        TT("dve", Gir[:, :, 0:n1], Gir[:, :, 0:n1], Gii[:, :, 0:n1], ALU.add, ["Gir", "Gii"], ["Gir"])
                        CP("act", Hbf[:, 1, :, :nC], Gir[:, :, 0:nC], ["Gir"], ["Hbf"])
                        CP("dve", Hcar[:, 1, :], Gir[:, :, nC], ["Gir", "Hcar"], ["Hcar"])

                    def stage_Y(ti):
                        (t0, W, seq, first, last) = TILES[ti]
                        nC = W // 8
                        n0 = t0 // 8
                        U_sb = U2[ti % 3]
                        ukey = ("U_sb", ti % 3)
                        for g8b in range(8):
                            bank = ybanks[g8b % 2]
                            bk = ("yb", g8b % 2)
                            bv = bank[:].rearrange("p (g n) -> p g n", g=8)
                            for gi in range(8):
                                gidx = g8b * 8 + gi
                                gh, gl = gidx // 32, gidx % 32
                                MM(bv[:, gi, :nC], T0_sb[:, gidx, :], U_sb[:, gidx, :nC], True, False, [ukey], [bk])
                                for ri in range(2):
                                    MM(bv[:, gi, :nC], Cm_sb[gh * 64:(gh + 1) * 64, gl, ri, :], Hbf[gh * 64:(gh + 1) * 64, ri, gl, :nC], False, ri == 1,
                                       ["Hbf"], [bk])
                            hb_ = g8b // 4
                            CP("act" if g8b % 2 else "dve", Ysth[hb_][:, (g8b % 4) * 8:(g8b % 4 + 1) * 8, :nC], bv[:, :, :nC], [bk], [("Yst", hb_)])
                            if g8b % 4 == 3:
                                DMA(YD[ti, :, :, hb_ * 4:(hb_ + 1) * 4, :, 0:nC].rearrange("s c g j n -> (s c) (g j) n"),
                                    Ysth[hb_][:, :, :nC], [("Yst", hb_)], [("YD", t0)])
                        if last:
                            for (ri, dst) in ((0, s5r_o), (1, s5i_o)):
                                for g8 in range(8):
                                    gh, g4 = g8 // 4, g8 % 4
                                    DMA(dst[l, seq].rearrange("(j g) p -> p g j", g=8)[:, g8, :],
                                        Hcar[gh * 64:(gh + 1) * 64, ri, g4 * 8:(g4 + 1) * 8], ["Hcar"], [])

                    u_load(0)
                    u_load(1)
                    stage_L(0)
                    for ti in range(len(TILES)):
                        if ti + 2 < len(TILES):
                            u_load(ti + 2)
                        if ti + 1 < len(TILES):
                            stage_L(ti + 1)
                        stage_S(ti)
                        stage_Y(ti)
            S.barrier()

        def phase4(l):
            with ExitStack() as ph:
                wg = sb(ph, "p4_w", [128, 8, 1024], BF16)
                ds5 = sb(ph, "p4_ds5", [128, 8], F32)
                bgl = sb(ph, "p4_bgl", [128, 8], F32)
                yp4 = [sb(ph, "p4_yp%d" % i, [128, 8, 8, 64], F32) for i in range(2)]
                ut4 = [sb(ph, "p4_ut%d" % i, [128, 8, 512], BF16) for i in range(2)]
                szb4 = [sb(ph, "p4_szb%d" % i, [128, 8, 512], BF16) for i in range(2)]
                yb = sb(ph, "p4_yb", [128, 8, 512], F32)
                gf = sb(ph, "p4_gf", [128, 8, 512], F32)
                gb = sb(ph, "p4_gb", [128, 8, 512], BF16)
                sg = [sb(ph, "p4_sg%d" % i, [128, 512], F32) for i in range(2)]
                ybst = sb(ph, "p4_ybst", [128, 8, 512], BF16)
                mm = [ps(ph, "p4_mm%d" % i, [128, 512]) for i in range(2)]
                for kt in range(8):
                    DMA(wg[:, kt, :], Wd["w_glu"][l, kt * 128:(kt + 1) * 128, :], [], ["wg"], q="pool")
                DMA(ds5[:], Wd["d_s5"][l].rearrange("(k p) -> p k", p=128), [], ["ds5"])
                DMA(bgl[:], Wd["b_glu"][l].rearrange("(k p) -> p k", p=128), [], ["bgl"])
                def p4_load(ti):
                    (t0, W, seq, first, last) = TILES[ti]
                    nC = W // 8
                    n0 = t0 // 8
                    pr = ti % 2
                    for g8 in range(8):
                        if nC == 64:
                            DMA(yp4[pr][g8 * 16:(g8 + 1) * 16, :, :, :].rearrange("c s j n -> c s (j n)"),
                                YD[ti, :, :, g8, :, :].rearrange("s c j n -> c s (j n)"), [("YD", t0)], [("yp4", pr)])
                        else:
                            for s_ in range(8):
                                DMA(yp4[pr][g8 * 16:(g8 + 1) * 16, s_, :, :nC], YD[ti, s_, :, g8, :, 0:nC], [("YD", t0)], [("yp4", pr)])
                        DMA(ut4[pr][g8 * 16:(g8 + 1) * 16, :, :W], UT[:, :, g8, t0:t0 + W].rearrange("j c t -> c j t"), [("UT", t0)], [("ut4", pr)])
                    DMA(szb4[pr][:, :, :W], SZB[:, :, t0:t0 + W].rearrange("m p t -> p m t"), [("SZB", t0)], [("szb4", pr)])

                p4_load(0)
                for ti, (t0, W, seq, first, last) in enumerate(TILES):
                    nC = W // 8
                    n0 = t0 // 8
                    pr = ti % 2
                    if ti + 1 < len(TILES):
                        p4_load(ti + 1)
                    for j in range(8):
                        STT(yb[:, j, :W].rearrange("p (n s) -> p n s", s=8), ut4[pr][:, j, :W].rearrange("p (n s) -> p n s", s=8), ds5[:, j:j + 1],
                            yp4[pr][:, :, j, :nC].rearrange("p s n -> p n s"), ALU.mult, ALU.add, [("ut4", pr), ("yp4", pr), "ds5"], [("yb", j)])
                        ACT(gf[:, j, :W], yb[:, j, :W], AF.Gelu_apprx_tanh, [("yb", j)], [("gf", j)])
                        CP("act" if j % 2 else "dve", gb[:, j, :W], gf[:, j, :W], [("gf", j)], ["gb"])
                    for jo in range(8):
                        bank = mm[jo % 2]
                        bk = ("mm", jo % 2)
                        for ji in range(8):
                            MM(bank[:, :W], wg[:, ji, jo * 128:(jo + 1) * 128], gb[:, ji, :W], ji == 0, ji == 7, ["wg", "gb"], [bk])
                        sgt = sg[jo % 2]
                        sk = ("sg", jo % 2)
                        ACT(sgt[:, :W], bank[:, :W], AF.Sigmoid, [bk, "bgl"], [sk], bias=bgl[:, jo:jo + 1], scale=1.0)
                        TT("dve", sgt[:, :W], sgt[:, :W], gf[:, jo, :W], ALU.mult, [sk, ("gf", jo)], [sk])
                        TT("dve", ybst[:, jo, :W], sgt[:, :W], szb4[pr][:, jo, :W], ALU.mult, [sk, ("szb4", pr)], ["ybst"])
                    DMA(YB[:, :, t0:t0 + W].rearrange("m p t -> p m t"), ybst[:, :, :W], ["ybst"], [("YB", t0)])
            S.barrier()

        def phase5(l):
            with ExitStack() as ph:
                wo = sb(ph, "p5_w", [128, 16, 1024], BF16)
                y5 = [sb(ph, "p5_y%d" % i, [128, 16, 512], BF16) for i in range(2)]
                xT5 = sb(ph, "p5_x", [128, 8, 512], F32)
                mm = [ps(ph, "p5_mm%d" % i, [128, 512]) for i in range(4)]
                for kt in range(16):
                    DMA(wo[:, kt, :], Wd["w_out"][l, kt * 128:(kt + 1) * 128, :], [], ["wo"], q="pool")

                def y_load(ti):
                    (t0_, W_, _, _, _) = TILES[ti]
                    pr = ti % 2
                    DMA(y5[pr][:, 0:8, :W_], YA[:, :, t0_:t0_ + W_].rearrange("m p t -> p m t"), [("YA", t0_)], [("y5a", pr)])
                    DMA(y5[pr][:, 8:16, :W_], YB[:, :, t0_:t0_ + W_].rearrange("m p t -> p m t"), [("YB", t0_)], [("y5b", pr)])

                y_load(0)
                for ti, (t0, W, seq, first, last) in enumerate(TILES):
                    pr = ti % 2
                    if ti + 1 < len(TILES):
                        y_load(ti + 1)
                    DMA(xT5[:, :, :W], xt_ap(t0, W), [("XT", t0)], ["xT5"])
                    for dm in range(8):
                        bank = mm[dm % 4]
                        bk = ("mm", dm % 4)
                        for kt in range(16):
                            MM(bank[:, :W], wo[:, kt, dm * 128:(dm + 1) * 128], y5[pr][:, kt, :W], kt == 0, kt == 15, ["wo", ("y5a", pr), ("y5b", pr)], [bk])
                        TT("dve", xT5[:, dm, :W], bank[:, :W], xT5[:, dm, :W], ALU.add, [bk, "xT5"], ["xT5"])
                    DMA(xt_ap(t0, W), xT5[:, :, :W], ["xT5"], [("XT", t0)])
            S.barrier()

        def phase_final():
            with ExitStack() as ph:
                fg = sb(ph, "pf_g", [128, 8], F32)
                xT = sb(ph, "pf_xT", [128, 8, 512], F32)
                sq = sb(ph, "pf_sq", [128, 8, 512], BF16)
                hf = sb(ph, "pf_hf", [128, 8, 512], F32)
                rs = sb(ph, "pf_rs", [128, 512], F32)
                rs2 = sb(ph, "pf_rs2", [128, 512], F32)
                yst = sb(ph, "pf_yst", [128, 4, 1024], F32)
                msp = ps(ph, "pf_ms", [128, 512])
                tp = [ps(ph, "pf_tp%d" % i, [128, 512]) for i in range(4)]
                DMA(fg[:], Wd["final_norm_g"].rearrange("(k p) -> p k", p=128), [], ["fg"])
                for (t0, W, seq, first, last) in TILES:
                    Pb = min(128, W)
                    nb = W // Pb
                    DMA(xT[:, :, :W], xt_ap(t0, W), [("XT", t0)], ["xT"])
                    ACT(sq[:, :, :W], xT[:, :, :W], AF.Square, ["xT"], ["nf_sq"])
                    rms_rstd(msp, [sq[:, kt, :W] for kt in range(8)], W, 1.0 / 1024, rs, rs2, "nf")
                    for kt in range(8):
                        STT(hf[:, kt, :W], xT[:, kt, :W], fg[:, kt:kt + 1], rs2[:, :W], ALU.mult, ALU.mult, ["xT", "fg", "nf_rs2"], ["hf"])
                    for b in range(nb):
                        for half in range(2):
                            bank = tp[(b * 2 + half) % 4]
                            bk = ("tp", (b * 2 + half) % 4)
                            for k4 in range(4):
                                kt = half * 4 + k4
                                TR(bank[:Pb, k4 * 128:(k4 + 1) * 128], hf[:, kt, b * Pb:(b + 1) * Pb], ident_f[:], ["hf"], [bk])
                            CP("act" if half else "dve", yst[:Pb, b, half * 512:(half + 1) * 512], bank[:Pb, :], [bk], ["yst"])
                    DMA(y_out[t0:t0 + W, :].rearrange("(b p) d -> p b d", p=Pb), yst[:Pb, :nb, :], ["yst"], [])
            S.barrier()

        wst = ExitStack()
        w_cur = sb(wst, "w_in_sb", [128, 8, IN_COLS], BF16)
        load_w_in(0, w_cur)
        phase0()
        for l in range(n_layers):
            if 1 in phases:
                phase1(l, w_cur)
            wst.close()
            if 2 in phases:
                phase2(l)
            if 3 in phases:
                phase3(l)
            if 4 in phases:
                phase4(l)
            if l + 1 < n_layers:
                wst = ExitStack()
                w_cur = sb(wst, "w_in_sb", [128, 8, IN_COLS], BF16)
                load_w_in(l + 1, w_cur)
            if 5 in phases:
                phase5(l)
        if do_final:
            phase_final()
        S.finish(block)
        nc._n_instr = S.n_instr
    return nc


def make_in_maps(inputs):
    xp = np.ascontiguousarray(inputs["x_prompt"], dtype=np.float32)
    xs = np.ascontiguousarray(inputs["x_sample"], dtype=np.float32)
    maps = []
    for c in range(NCORES):
        m = {}
        m["x_core"] = np.ascontiguousarray(np.concatenate([xp[2 * c], xp[2 * c + 1], xs[c]], axis=0))
        m["conv0"] = np.ascontiguousarray(inputs["state_ssd_conv"][:, c])
        m["ssd0"] = np.ascontiguousarray(inputs["state_ssd"][:, c])
        m["s5r0"] = np.ascontiguousarray(inputs["state_s5_re"][:, c])
        m["s5i0"] = np.ascontiguousarray(inputs["state_s5_im"][:, c])
        for n in W_NAMES:
            m[n] = np.ascontiguousarray(inputs[n], dtype=np.float32)
        maps.append(m)
    return maps


_NC_CACHE = {}


def kernel(**inputs):
    inputs = {k: np.asarray(v) for k, v in inputs.items()}
    if "nc" not in _NC_CACHE:
        _NC_CACHE["nc"] = build_nc()
    nc = _NC_CACHE["nc"]
    maps = make_in_maps(inputs)
    res = run_bass_kernel_spmd(nc, maps, core_ids=list(range(NCORES)))
    R = res.results
    y_prompt = np.zeros((16, 2048, 1024), np.float32)
    y_sample = np.zeros((8, 64, 1024), np.float32)
    conv_p = np.zeros((4, 16, 3, 2048), np.float32)
    ssd_p = np.zeros((4, 16, 16, 64, 128), np.float32)
    s5r_p = np.zeros((4, 16, 64, 64), np.float32)
    s5i_p = np.zeros((4, 16, 64, 64), np.float32)
    conv_s = np.zeros((4, 8, 3, 2048), np.float32)
    ssd_s = np.zeros((4, 8, 16, 64, 128), np.float32)
    s5r_s = np.zeros((4, 8, 64, 64), np.float32)
    s5i_s = np.zeros((4, 8, 64, 64), np.float32)
    for c in range(NCORES):
        r = R[c]
        y = r["y_out"]
        y_prompt[2 * c] = y[0:2048]
        y_prompt[2 * c + 1] = y[2048:4096]
        y_sample[c] = y[4096:4160]
        for i in range(2):
            conv_p[:, 2 * c + i] = r["conv_o"][:, i]
            ssd_p[:, 2 * c + i] = r["ssd_o"][:, i]
            s5r_p[:, 2 * c + i] = r["s5r_o"][:, i]
            s5i_p[:, 2 * c + i] = r["s5i_o"][:, i]
        conv_s[:, c] = r["conv_o"][:, 2]
        ssd_s[:, c] = r["ssd_o"][:, 2]
        s5r_s[:, c] = r["s5r_o"][:, 2]
        s5i_s[:, c] = r["s5i_o"][:, 2]
    return (y_prompt, y_sample, conv_p, ssd_p, s5r_p, s5i_p, conv_s, ssd_s, s5r_s, s5i_s)
```

```python
import math
from contextlib import ExitStack

import numpy as np
import concourse.bass as bass
import concourse.mybir as mybir
from concourse.bass_utils import run_bass_kernel_spmd

F32 = mybir.dt.float32
BF16 = mybir.dt.bfloat16
I32 = mybir.dt.int32
AF = mybir.ActivationFunctionType
ALU = mybir.AluOpType

NCORES = 8
DEPTH = 4
T = 4160
NCH = T // 8
EPS = 1e-6
IN_COLS = 5136
C_ZA, C_XBC, C_DT, C_U, C_ZB = 0, 1024, 3072, 3088, 4112
TILES = [(s * 2048 + i * 512, 512, s, i == 0, i == 3) for s in range(2) for i in range(4)]
TILES.append((4096, 64, 2, True, True))
import os as _os
if _os.environ.get("K_TILES"):
    TILES = [TILES[int(i)] for i in _os.environ["K_TILES"].split(",")]
K_SKIP = set(_os.environ.get("K_SKIP", "").split(","))

ENGS = ("pe", "dve", "act", "pool", "sp")


class Sync:
    def __init__(self, nc, stack, n_dma_sems=40, same_engine_sync=True):
        self.nc = nc
        self.esem = {e: stack.enter_context(nc.semaphore("s_" + e)) for e in ENGS}
        self.cnt = {e: 0 for e in ENGS}
        self.prog = {e: [] for e in ENGS}
        self.waited = {e: {} for e in ENGS}
        self.res = {}
        self.same = same_engine_sync
        self.dsems = [stack.enter_context(nc.semaphore("s_dma%d" % i)) for i in range(n_dma_sems)]
        self.dval = [0] * n_dma_sems
        self.dnext = 0
        self.sems = {}
        for e in ENGS:
            self.sems[("e", e)] = self.esem[e]
        for i, s in enumerate(self.dsems):
            self.sems[("d", i)] = s
        self.n_instr = 0

    def _need(self, e, reads, writes):
        need = {}

        def add(ev):
            if ev is None:
                return
            k, v = ev
            if need.get(k, 0) < v:
                need[k] = v

        for r in reads:
            st = self.res.get(r)
            if st is not None:
                add(st[0])
        for w in writes:
            st = self.res.get(w)
            if st is not None:
                add(st[0])
                for ev in st[1]:
                    add(ev)
        out = []
        for k, v in need.items():
            if k == ("e", e) and (e == "pe" or not self.same):
                continue
            if self.waited[e].get(k, 0) >= v:
                continue
            self.waited[e][k] = v
            out.append((k, v))
        return out

    def _commit(self, ev, reads, writes):
        for r in reads:
            st = self.res.setdefault(r, [None, []])
            st[1].append(ev)
            if len(st[1]) > 16:
                mx = {}
                for k, v in st[1]:
                    if mx.get(k, 0) < v:
                        mx[k] = v
                st[1] = list(mx.items())
        for w in writes:
            self.res[w] = [ev, []]

    def op(self, e, fn, reads=(), writes=()):
        waits = self._need(e, reads, writes)
        self.cnt[e] += 1
        ev = (("e", e), self.cnt[e])
        sem = self.esem[e]
        sems = self.sems

        def emit(eng, waits=waits, fn=fn, sem=sem):
            for k, v in waits:
                eng.wait_ge(sems[k], v)
            fn(eng).then_inc(sem, 1)

        self.prog[e].append(emit)
        self._commit(ev, reads, writes)
        self.n_instr += 1
        return ev

    def dma(self, e, fn, reads=(), writes=()):
        i = self.dnext
        self.dnext = (self.dnext + 1) % len(self.dsems)
        k = ("d", i)
        waits = self._need(e, reads, writes)
        if self.dval[i] > 0 and self.waited[e].get(k, 0) < self.dval[i]:
            self.waited[e][k] = self.dval[i]
            waits.append((k, self.dval[i]))
        self.dval[i] += 16
        ev = (k, self.dval[i])
        sem = self.dsems[i]
        sems = self.sems

        def emit(eng, waits=waits, fn=fn, sem=sem):
            for kk, v in waits:
                eng.wait_ge(sems[kk], v)
            fn(eng).then_inc(sem, 16)

        self.prog[e].append(emit)
        self._commit(ev, reads, writes)
        self.n_instr += 1
        return ev

    def barrier(self):
        evs = [(("e", e), self.cnt[e]) for e in ENGS if self.cnt[e] > 0]
        evs += [(("d", i), v) for i, v in enumerate(self.dval) if v > 0]
        sems = self.sems
        for e in ENGS:
            waits = []
            for k, v in evs:
                if k == ("e", e):
                    continue
                if self.waited[e].get(k, 0) >= v:
                    continue
                self.waited[e][k] = v
                waits.append((k, v))

            def emit(eng, waits=waits):
                for kk, v in waits:
                    eng.wait_ge(sems[kk], v)

            self.prog[e].append(emit)
        self.res = {}

    def finish(self, block):
        self.barrier()
        prog = self.prog

        @block.tensor
        def _(eng):
            for f in prog["pe"]:
                f(eng)

        @block.vector
        def _(eng):
            for f in prog["dve"]:
                f(eng)

        @block.scalar
        def _(eng):
            for f in prog["act"]:
                f(eng)

        @block.gpsimd
        def _(eng):
            for f in prog["pool"]:
                f(eng)

        @block.sync
        def _(eng):
            for f in prog["sp"]:
                f(eng)


W_NAMES = ["norm_g", "w_in", "conv_w", "conv_b", "dt_bias", "a_log", "d_ssd", "ssd_norm_g",
           "lam_re", "lam_im", "log_dt", "b_re", "b_im", "c_re", "c_im", "d_s5", "w_glu", "b_glu",
           "w_out", "final_norm_g"]
W_SHAPES = {
    "norm_g": [4, 1024], "w_in": [4, 1024, 5136], "conv_w": [4, 4, 2048], "conv_b": [4, 2048],
    "dt_bias": [4, 16], "a_log": [4, 16], "d_ssd": [4, 16], "ssd_norm_g": [4, 1024],
    "lam_re": [4, 64, 64], "lam_im": [4, 64, 64], "log_dt": [4, 64], "b_re": [4, 64, 64, 16],
    "b_im": [4, 64, 64, 16], "c_re": [4, 64, 16, 64], "c_im": [4, 64, 16, 64], "d_s5": [4, 1024],
    "w_glu": [4, 1024, 1024], "b_glu": [4, 1024], "w_out": [4, 2048, 1024], "final_norm_g": [1024],
}


def build_nc(n_layers=DEPTH, dbg=False, phases=(1, 2, 3, 4, 5), do_final=True):
    nc = bass.Bass("TRN2", target_bir_lowering=False)
    skind = "ExternalOutput" if dbg else "Internal"

    def din(name, shape, dt=F32):
        return nc.dram_tensor(name, shape, dt, kind="ExternalInput").ap()

    def dout(name, shape, dt=F32):
        return nc.dram_tensor(name, shape, dt, kind="ExternalOutput").ap()

    def dscr(name, shape, dt=F32):
        return nc.dram_tensor(name, shape, dt, kind=skind).ap()

    x_in = din("x_core", [T, 1024])
    conv0 = din("conv0", [4, 3, 2048])
    ssd0 = din("ssd0", [4, 16, 64, 128])
    s5r0 = din("s5r0", [4, 64, 64])
    s5i0 = din("s5i0", [4, 64, 64])
    Wd = {n: din(n, W_SHAPES[n]) for n in W_NAMES}

    y_out = dout("y_out", [T, 1024])
    conv_o = dout("conv_o", [4, 3, 3, 2048])
    ssd_o = dout("ssd_o", [4, 3, 16, 64, 128])
    s5r_o = dout("s5r_o", [4, 3, 64, 64])
    s5i_o = dout("s5i_o", [4, 3, 64, 64])

    XT = dscr("XT", [8, 128, T])
    SZA = dscr("SZA", [8, 128, T], BF16)
    XBC = dscr("XBC", [16, 128, T])
    DTR = dscr("DTR", [T, 16])
    UD = dscr("UD", [len(TILES), 8, 16, 8, 8, 64], BF16)
    UT = dscr("UT", [8, 16, 8, T], BF16)
    SZB = dscr("SZB", [8, 128, T], BF16)
    YA = dscr("YA", [8, 128, T], BF16)
    YD = dscr("YD", [len(TILES), 8, 16, 8, 8, 64])
    YB = dscr("YB", [8, 128, T], BF16)

    with ExitStack() as st:
        S = Sync(nc, st)
        st.enter_context(nc.allow_non_contiguous_dma(reason="small strided parameter/state loads"))

        uniq = [0]

        def sb(stack, name, shape, dt):
            uniq[0] += 1
            return stack.enter_context(nc.sbuf_tensor("%s_%d" % (name, uniq[0]), shape, dt))

        def ps(stack, name, shape, dt=F32):
            uniq[0] += 1
            return stack.enter_context(nc.psum_tensor("%s_%d" % (name, uniq[0]), shape, dt))


        def TT(eng, out, in0, in1, op, reads, writes):
            return S.op(eng, lambda e: e.tensor_tensor(out=out, in0=in0, in1=in1, op=op), reads, writes)

        def TS(eng, out, in0, s1, s2, op0, op1, reads, writes):
            return S.op(eng, lambda e: e.tensor_scalar(out=out, in0=in0, scalar1=s1, scalar2=s2, op0=op0, op1=op1), reads, writes)

        def TSS(eng, out, in_, scalar, op, reads, writes):
            return S.op(eng, lambda e: e.tensor_single_scalar(out=out, in_=in_, scalar=scalar, op=op), reads, writes)

        def STT(out, in0, scalar, in1, op0, op1, reads, writes):
            return S.op("dve", lambda e: e.scalar_tensor_tensor(out=out, in0=in0, scalar=scalar, in1=in1, op0=op0, op1=op1), reads, writes)

        def ACT(out, in_, func, reads, writes, bias=None, scale=None):
            kw = {}
            if bias is not None:
                kw["bias"] = bias
            if scale is not None:
                kw["scale"] = scale
            return S.op("act", lambda e: e.activation(out=out, in_=in_, func=func, **kw), reads, writes)

        def CP(eng, out, in_, reads, writes):
            if eng == "act":
                return ACT(out, in_, AF.Copy, reads, writes)
            return S.op(eng, lambda e: e.tensor_copy(out=out, in_=in_), reads, writes)

        def MM(out, lhsT, rhs, start, stop, reads, writes):
            return S.op("pe", lambda e: e.matmul(out, lhsT=lhsT, rhs=rhs, start=start, stop=stop), reads, writes)

        def TR(out, in_, ident, reads, writes):
            return S.op("pe", lambda e: e.transpose(out=out, in_=in_, identity=ident), reads, writes)

        def DMA(out, in_, reads=(), writes=(), q="sp"):
            return S.dma(q, lambda e: e.dma_start(out=out, in_=in_), reads, writes)

        def MSET(eng, ap, val, writes):
            return S.op(eng, lambda e: e.memset(ap, val), (), writes)

        ones_f = sb(st, "ones_f", [128, 128], F32)
        ident_f = sb(st, "ident_f", [128, 128], F32)
        ident_b = sb(st, "ident_b", [128, 128], BF16)
        ones_b = sb(st, "ones_b", [128, 128], BF16)
        tri_f = sb(st, "tri_f", [128, 128], F32)
        tri_b = sb(st, "tri_b", [128, 128], BF16)
        bmask = sb(st, "bmask", [128, 128], F32)
        block = st.enter_context(nc.Block())

        S.op("pool", lambda e: e.memset(ones_f[:], 1.0), writes=["ones_f"])
        S.op("pool", lambda e: e.affine_select(out=ident_f[:], in_=ones_f[:], pattern=[[1, 128]],
                                               compare_op=ALU.is_equal, fill=0.0, base=0, channel_multiplier=-1),
             reads=["ones_f"], writes=["ident_f"])
        S.op("pool", lambda e: e.affine_select(out=tri_f[:], in_=ones_f[:], pattern=[[1, 128]],
                                               compare_op=ALU.is_ge, fill=0.0, base=0, channel_multiplier=-1),
             reads=["ones_f"], writes=["tri_f"])
        S.op("pool", lambda e: e.affine_select(out=bmask[:], in_=ones_f[:], pattern=[[16, 8], [0, 16]],
                                               compare_op=ALU.is_ge, fill=0.0, base=15, channel_multiplier=-1),
             reads=["ones_f"], writes=["bmask"])
        S.op("dve", lambda e: e.tensor_copy(out=ident_b[:], in_=ident_f[:]), reads=["ident_f"], writes=["ident_b"])
        S.op("dve", lambda e: e.tensor_copy(out=ones_b[:], in_=ones_f[:]), reads=["ones_f"], writes=["ones_b"])
        S.op("dve", lambda e: e.tensor_copy(out=tri_b[:], in_=tri_f[:]), reads=["tri_f"], writes=["tri_b"])
        S.barrier()

        def xt_ap(t0, W):
            return XT[:, :, t0:t0 + W].rearrange("k p t -> p k t")

        def rms_rstd(ph_ps, src_sq, W, scale, rs, rs2, key):
            nk = len(src_sq)
            for i, a in enumerate(src_sq):
                MM(ph_ps[:, :W], ones_b[:], a, i == 0, i == nk - 1, [key + "_sq"], [key + "_ms"])
            ACT(rs[:, :W], ph_ps[:, :W], AF.Ln, [key + "_ms"], [key + "_rs"], bias=EPS, scale=scale)
            ACT(rs2[:, :W], rs[:, :W], AF.Exp, [key + "_rs"], [key + "_rs2"], scale=-0.5)

        def phase0():
            with ExitStack() as ph:
                xs = sb(ph, "p0_xs", [128, 4, 1024], F32)
                xo = sb(ph, "p0_xo", [128, 8, 512], F32)
                tp = [ps(ph, "p0_tp%d" % i, [128, 512]) for i in range(2)]
                for (t0, W, seq, first, last) in TILES:
                    Pb = min(128, W)
                    nb = W // Pb
                    DMA(xs[:Pb, :nb, :], x_in[t0:t0 + W, :].rearrange("(b p) d -> p b d", p=Pb), [], ["xs"])
                    for kt in range(8):
                        tpk = tp[kt % 2]
                        for b in range(nb):
                            TR(tpk[:, b * Pb:(b + 1) * Pb], xs[:Pb, b, kt * 128:(kt + 1) * 128], ident_f[:Pb, :Pb], ["xs"], [("tp", kt % 2)])
                        CP("act" if kt % 2 else "dve", xo[:, kt, :W], tpk[:, :W], [("tp", kt % 2)], ["xo"])
                    DMA(xt_ap(t0, W), xo[:, :, :W], ["xo"], [("XT", t0)])
            S.barrier()

        def load_w_in(l, w_sb):
            for kt in range(8):
                for (c0, cw_) in ((0, 2048), (2048, 2048), (4096, 1040)):
                    DMA(w_sb[:, kt, c0:c0 + cw_], Wd["w_in"][l, kt * 128:(kt + 1) * 128, c0:c0 + cw_], [], ["w"], q="pool")

        def phase1(l, w_sb):
            with ExitStack() as ph:
                g1 = sb(ph, "p1_g", [128, 8], F32)
                xT = sb(ph, "p1_xT", [128, 8, 512], F32)
                sq = sb(ph, "p1_sq", [128, 8, 512], BF16)
                hT = sb(ph, "p1_hT", [128, 8, 512], BF16)
                rs = sb(ph, "p1_rs", [128, 512], F32)
                rs2 = sb(ph, "p1_rs2", [128, 512], F32)
                st_za = sb(ph, "p1_za", [128, 8, 512], BF16)
                st_xbc = [sb(ph, "p1_xbc%d" % i, [128, 2, 512], F32) for i in range(2)]
                st_us = sb(ph, "p1_us", [128, 8, 8, 64], BF16)
                st_ut = sb(ph, "p1_ut", [128, 8, 512], BF16)
                us32 = [sb(ph, "p1_us32_%d" % i, [128, 8, 64], F32) for i in range(2)]
                st_zb = sb(ph, "p1_zb", [128, 8, 512], BF16)
                st_dt = sb(ph, "p1_dt", [128, 4, 16], F32)
                mm = [ps(ph, "p1_mm%d" % i, [128, 512]) for i in range(4)]
                msp = ps(ph, "p1_ms", [128, 512])
                dtp = ps(ph, "p1_dtp", [128, 4, 16])
                DMA(g1[:], Wd["norm_g"][l].rearrange("(k p) -> p k", p=128), [], ["g1"])
                for kt in range(0 if "perm" in K_SKIP else 8):
                    tmpu = st_ut[:, (kt % 2) * 2:(kt % 2) * 2 + 2, :].rearrange("p a t -> p (a t)")
                    CP("dve", tmpu, w_sb[:, kt, C_U:C_U + 1024], ["w"], [("tmpu", kt % 2), "st_ut"])
                    CP("act", w_sb[:, kt, C_U:C_U + 1024].rearrange("p (j c g) -> p j c g", c=16, g=8),
                       tmpu.rearrange("p (j g c) -> p j c g", g=8, c=16), [("tmpu", kt % 2), "st_ut"], ["w"])
                mcount = [0]
                def pro_load(ti):
                    (t0_, W_, _, _, _) = TILES[ti]
                    DMA(xT[:, :, :W_], xt_ap(t0_, W_), [("XT", t0_)], ["xT"])

                def pro_sq(ti):
                    W_ = TILES[ti][1]
                    ACT(sq[:, :, :W_], xT[:, :, :W_], AF.Square, ["xT"], ["n1_sq"])

                def pro_ms(ti):
                    W_ = TILES[ti][1]
                    rms_rstd(msp, [sq[:, kt, :W_] for kt in range(8)], W_, 1.0 / 1024, rs, rs2, "n1")

                pro_load(0)
                pro_sq(0)
                pro_ms(0)
                for ti, (t0, W, seq, first, last) in enumerate(TILES):
                    Pb = min(128, W)
                    nb = W // Pb
                    nC = W // 8
                    n0 = t0 // 8
                    nxt = ti + 1 if ti + 1 < len(TILES) else None
                    for kt in range(8):
                        STT(hT[:, kt, :W], xT[:, kt, :W], g1[:, kt:kt + 1], rs2[:, :W], ALU.mult, ALU.mult, ["xT", "g1", "n1_rs2"], ["hT"])
                    if nxt is not None:
                        pro_load(nxt)

                    def mtile(lhs_fn, W=W):
                        bank = mm[mcount[0] % 4]
                        key = ("mm", mcount[0] % 4)
                        mcount[0] += 1
                        for kt in range(8):
                            MM(bank[:, :W], lhs_fn(kt), hT[:, kt, :W], kt == 0, kt == 7, ["w", "hT"], [key])
                        return bank, key

                    for m in range(0 if "za" in K_SKIP else 8):
                        bank, key = mtile(lambda kt, m=m: w_sb[:, kt, C_ZA + m * 128:C_ZA + (m + 1) * 128])
                        ACT(st_za[:, m, :W], bank[:, :W], AF.Silu, [key], ["st_za"])
                    if "za" not in K_SKIP:
                        DMA(SZA[:, :, t0:t0 + W].rearrange("m p t -> p m t"), st_za[:, :, :W], ["st_za"], [("SZA", t0)])
                    if nxt is not None:
                        pro_sq(nxt)
                    for m in range(0 if "xbc" in K_SKIP else 16):
                        bank, key = mtile(lambda kt, m=m: w_sb[:, kt, C_XBC + m * 128:C_XBC + (m + 1) * 128])
                        pr = m // 2
                        stx = st_xbc[pr % 2]
                        skey = ("st_xbc", pr % 2)
                        CP("act" if m % 2 == 0 else "dve", stx[:, m % 2, :W], bank[:, :W], [key], [skey])
                        if m % 2 == 1:
                            DMA(XBC[pr * 2:(pr + 1) * 2, :, t0:t0 + W].rearrange("m p t -> p m t"), stx[:, :, :W], [skey], [("XBC", t0, pr)])
                    if nxt is not None:
                        pro_ms(nxt)
                    for j in range(0 if "u" in K_SKIP else 8):
                        bank, key = mtile(lambda kt, j=j: w_sb[:, kt, C_U + j * 128:C_U + (j + 1) * 128])
                        u32 = us32[j % 2]
                        ACT(u32[:, :, :nC].rearrange("p s n -> p n s"), bank[:, :W].rearrange("p (n s) -> p n s", s=8), AF.Copy, [key], [("us32", j % 2), ("brd", key)])
                        CP("dve", st_us[:, :, j, :nC], u32[:, :, :nC], [("us32", j % 2)], ["st_us"])
                        CP("dve", st_ut[:, j, :W], bank[:, :W], [key, ("brd", key)], ["st_ut"])
                    if nC == 64:
                        DMA(UD[ti].rearrange("s c g j n -> (c g) s (j n)"), st_us[:].rearrange("q s j n -> q s (j n)"), ["st_us"], [("UD", t0)])
                    else:
                        for j in range(8):
                            DMA(UD[ti, :, :, :, j, 0:nC].rearrange("s c g n -> (c g) s n"), st_us[:, :, j, :nC], ["st_us"], [("UD", t0)])
                    if "u" not in K_SKIP and "utdma" not in K_SKIP:
                        DMA(UT[:, :, :, t0:t0 + W].rearrange("j c g t -> (c g) j t"), st_ut[:, :, :W], ["st_ut"], [("UT", t0)])
                    for m in range(0 if "zb" in K_SKIP else 8):
                        bank, key = mtile(lambda kt, m=m: w_sb[:, kt, C_ZB + m * 128:C_ZB + (m + 1) * 128])
                        ACT(st_zb[:, m, :W], bank[:, :W], AF.Silu, [key], ["st_zb"])
                    if "zb" not in K_SKIP:
                        DMA(SZB[:, :, t0:t0 + W].rearrange("m p t -> p m t"), st_zb[:, :, :W], ["st_zb"], [("SZB", t0)])
                    for b in range(0 if "dt" in K_SKIP else nb):
                        for kt in range(8):
                            MM(dtp[:Pb, b, :], hT[:, kt, b * Pb:(b + 1) * Pb], w_sb[:, kt, C_DT:C_DT + 16], kt == 0, kt == 7, ["w", "hT"], ["dtp"])
                    if "dt" not in K_SKIP:
                        ACT(st_dt[:Pb, :nb, :], dtp[:Pb, :nb, :], AF.Copy, ["dtp"], ["st_dt"])
                        DMA(DTR[t0:t0 + W, :].rearrange("(b p) h -> p b h", p=Pb), st_dt[:Pb, :nb, :], ["st_dt"], [("DTR", t0)])
            S.barrier()

        def phase2(l):
            with ExitStack() as ph:
                cw = sb(ph, "p2_cw", [128, 16, 4], F32)
                cbias = sb(ph, "p2_cb", [128, 16], F32)
                dtb = sb(ph, "p2_dtb", [128, 16], F32)
                arow = sb(ph, "p2_arow", [128, 16], F32)
                dvec = sb(ph, "p2_dvec", [128, 8], F32)
                ng = sb(ph, "p2_ng", [128, 8], F32)
                tails = sb(ph, "p2_tails", [128, 16, 3], F32)
                hstate = sb(ph, "p2_hst", [128, 1024], F32)
                hbf = sb(ph, "p2_hbf", [128, 1024], BF16)
                hio = sb(ph, "p2_hio", [128, 8, 128], F32)
                raw = sb(ph, "p2_raw", [128, 4, 515], F32)
                rawb = sb(ph, "p2_rawb", [128, 4, 515], BF16)
                dw = sb(ph, "p2_dw", [128, 16, 4, 128], BF16)
                xc2 = [sb(ph, "p2_xc%d" % i, [128, 16, 512], BF16) for i in range(2)]
                sza2 = [sb(ph, "p2_sza%d" % i, [128, 8, 512], BF16) for i in range(2)]
                dtr = sb(ph, "p2_dtr", [128, 4, 16], F32)
                x1 = sb(ph, "p2_x1", [128, 4, 16], F32)
                tA = sb(ph, "p2_tA", [128, 4, 16], F32)
                tB = sb(ph, "p2_tB", [128, 4, 16], F32)
                dtv2 = [sb(ph, "p2_dtv%d" % i, [128, 4, 16], F32) for i in range(2)]
                av = sb(ph, "p2_av", [128, 4, 16], F32)
                a_hi2 = [sb(ph, "p2_ahi%d" % i, [128, 4, 16], BF16) for i in range(2)]
                a_lo2 = [sb(ph, "p2_alo%d" % i, [128, 4, 16], BF16) for i in range(2)]
                xdt2 = [sb(ph, "p2_xdt%d" % i, [128, 1024], BF16) for i in range(2)]
                xe2 = [sb(ph, "p2_xe%d" % i, [128, 1024], BF16) for i in range(2)]
                btok2 = [sb(ph, "p2_btok%d" % i, [128, 512], BF16) for i in range(2)]
                acs2 = [sb(ph, "p2_acs%d" % i, [128, 16], F32) for i in range(2)]
                nacs2 = [sb(ph, "p2_nacs%d" % i, [128, 16], F32) for i in range(2)]
                tmd2 = [sb(ph, "p2_tmd%d" % i, [128, 16], F32) for i in range(2)]
                te2 = [sb(ph, "p2_te%d" % i, [128, 16], F32) for i in range(2)]
                etot2 = [sb(ph, "p2_etot%d" % i, [128, 16], F32) for i in range(2)]
                cbm2 = [sb(ph, "p2_cbm%d" % i, [128, 4, 128], BF16) for i in range(2)]
                Mh4 = [[sb(ph, "p2_Mh%d_%d" % (i, g), [128, 512], BF16) for g in range(4)] for i in range(2)]
                Cs4 = [[sb(ph, "p2_Cs%d_%d" % (i, g), [128, 512], BF16) for g in range(4)] for i in range(2)]
                E = [sb(ph, "p2_E%d" % i, [128, 512], BF16) for i in range(2)]
                dec = [sb(ph, "p2_dec%d" % i, [128, 512], BF16) for i in range(2)]
                yg = sb(ph, "p2_yg", [128, 8, 512], F32)
                sq = sb(ph, "p2_sq", [128, 8, 512], BF16)
                rs4 = [sb(ph, "p2_rs4_%d" % i, [128, 512], F32) for i in range(4)]
                htmp = sb(ph, "p2_htmp", [128, 512], F32)
                ya = sb(ph, "p2_ya", [128, 8, 512], BF16)
                smallp = ps(ph, "p2_small", [128, 512])
                tpp = ps(ph, "p2_tp", [128, 1024], BF16)
                cbp = ps(ph, "p2_cbp", [128, 512])
                acsb = [ps(ph, "p2_acsb%d" % i, [128, 512]) for i in range(2)]
                yps = [ps(ph, "p2_yps%d" % i, [128, 512]) for i in range(2)]
                hps = ps(ph, "p2_hps", [128, 512])

                for k in range(4):
                    DMA(cw[:, :, k], Wd["conv_w"][l, k].rearrange("(m p) -> p m", p=128), [], ["cw"])
                DMA(cbias[:], Wd["conv_b"][l].rearrange("(m p) -> p m", p=128), [], ["cbias"])
                DMA(dtb[:], Wd["dt_bias"][l].partition_broadcast(128), [], ["dtb"])
                DMA(arow[:], Wd["a_log"][l].partition_broadcast(128), [], ["arow"])
                for h2 in range(2):
                    DMA(dvec[h2 * 64:(h2 + 1) * 64, :], Wd["d_ssd"][l].rearrange("(m h) -> h m", h=2)[h2].partition_broadcast(64), [], ["dvec"])
                DMA(ng[:], Wd["ssd_norm_g"][l].rearrange("(k p) -> p k", p=128), [], ["ng"])
                for m in range(16):
                    for k in range(4):
                        TS("pool" if (m * 4 + k) % 3 == 0 else "dve", dw[:, m, k, :], ident_b[:], cw[:, m, k:k + 1], None, ALU.mult, ALU.bypass, ["cw", "ident_b"], ["dw"])
                ACT(arow[:], arow[:], AF.Exp, ["arow"], ["arow"])
                TSS("dve", arow[:], arow[:], -1.0, ALU.mult, ["arow"], ["arow"])

                def seq_state_init(ti):
                    (t0, W, seq, first, last) = TILES[ti]
                    if not first:
                        return
                    if seq < 2:
                        MSET("pool", hstate[:], 0.0, ["hstate"])
                        MSET("pool", hbf[:], 0.0, ["hbf"])
                    else:
                        DMA(hio[:], ssd0[l].rearrange("(m h) p n -> (h p) m n", h=2), [], ["hio"])
                        for m in range(8):
                            TR(hps[:, (m % 4) * 128:(m % 4 + 1) * 128], hio[:, m, :], ident_f[:], ["hio"], ["hps"])
                            if m % 4 == 3:
                                hh = m // 4
                                CP("dve", hstate[:, hh * 512:(hh + 1) * 512], hps[:], ["hps"], ["hstate"])
                        CP("act", hbf[:], hstate[:], ["hstate"], ["hbf"])

                def conv_blk(ti, gq):
                    (t0, W, seq, first, last) = TILES[ti]
                    Qc = min(128, W)
                    nck = W // Qc
                    tp_ = ti % 2
                    xc, sza, dtv, a_hi, a_lo = xc2[tp_], sza2[tp_], dtv2[tp_], a_hi2[tp_], a_lo2[tp_]
                    if gq == 0 and first:
                        if seq < 2:
                            MSET("pool", tails[:], 0.0, ["tails"])
                        else:
                            for k in range(3):
                                DMA(tails[:, :, k], conv0[l, k].rearrange("(m p) -> p m", p=128), [], ["tails"])
                    DMA(raw[:, :, 3:3 + W], XBC[gq * 4:(gq + 1) * 4, :, t0:t0 + W].rearrange("m p t -> p m t"),
                        [("XBC", t0, 2 * gq), ("XBC", t0, 2 * gq + 1)], ["raw"])
                    CP("dve", raw[:, :, 0:3], tails[:, gq * 4:(gq + 1) * 4, :], ["tails"], ["raw"])
                    ACT(rawb[:, :, 0:3 + W], raw[:, :, 0:3 + W], AF.Copy, ["raw"], ["rawb"])
                    for mi in range(4):
                        m = gq * 4 + mi
                        bnk, bkey = ((cbp, "cbp"), (hps, "hps"))[mi % 2]
                        for k in range(4):
                            MM(bnk[:, :W], dw[:, m, k, :], rawb[:, mi, k:k + W], k == 0, k == 3, ["rawb", "dw"], [bkey])
                        ACT(xc[:, m, :W], bnk[:, :W], AF.Silu, [bkey, "cbias"], [("xc", tp_, m)], bias=cbias[:, m:m + 1], scale=1.0)
                    CP("dve", tails[:, gq * 4:(gq + 1) * 4, :], raw[:, :, W:W + 3], ["raw"], ["tails"])
                    if gq == 3 and last:
                        for k in range(3):
                            DMA(conv_o[l, seq, k].rearrange("(m p) -> p m", p=128), tails[:, :, k], ["tails"], [])

                def pre(ti):
                    (t0, W, seq, first, last) = TILES[ti]
                    Qc = min(128, W)
                    nck = W // Qc
                    tp_ = ti % 2
                    xc, sza, dtv, a_hi, a_lo = xc2[tp_], sza2[tp_], dtv2[tp_], a_hi2[tp_], a_lo2[tp_]
                    DMA(sza[:, :, :W], SZA[:, :, t0:t0 + W].rearrange("m p t -> p m t"), [("SZA", t0)], [("sza", tp_)])
                    DMA(dtr[:Qc, :nck, :], DTR[t0:t0 + W, :].rearrange("(b p) h -> p b h", p=Qc), [("DTR", t0)], ["dtr"])
                    sl = lambda tl: tl[:Qc, :nck, :]
                    bc = lambda tl: tl[:Qc, :].unsqueeze(1).to_broadcast([Qc, nck, 16])
                    TT("dve", sl(x1), sl(dtr), bc(dtb), ALU.add, ["dtr", "dtb"], ["x1"])
                    STT(sl(tA), sl(x1), -1.0, sl(x1), ALU.mult, ALU.max, ["x1"], ["tA"])
                    ACT(sl(tA), sl(tA), AF.Exp, ["tA"], ["tA"], scale=-1.0)
                    ACT(sl(tA), sl(tA), AF.Ln, ["tA"], ["tA"], bias=1.0, scale=1.0)
                    TSS("dve", sl(tB), sl(x1), 0.0, ALU.max, ["x1"], ["tB"])
                    TT("dve", sl(dtv), sl(tA), sl(tB), ALU.add, ["tA", "tB"], [("dtv", tp_)])
                    TT("dve", sl(av), sl(dtv), bc(arow), ALU.mult, [("dtv", tp_), "arow"], ["av"])
                    CP("dve", sl(a_hi), sl(av), ["av"], [("a_hi", tp_)])
                    TT("dve", sl(a_lo), sl(av), sl(a_hi), ALU.subtract, ["av", ("a_hi", tp_)], [("a_lo", tp_)])

                def chunks(ti, hook):
                    (t0, W, seq, first, last) = TILES[ti]
                    Qc = min(128, W)
                    nck = W // Qc
                    tp_ = ti % 2
                    xc, sza, dtv, a_hi, a_lo = xc2[tp_], sza2[tp_], dtv2[tp_], a_hi2[tp_], a_lo2[tp_]
                    hd = lambda ap: ap.rearrange("p (h d) -> p h d", d=64)

                    def front_pre(c):
                        cp = c % 2
                        cs = c * Qc
                        xdt, xe, btok, acs, tmd, te, etot, cbm = xdt2[cp], xe2[cp], btok2[cp], acs2[cp], tmd2[cp], te2[cp], etot2[cp], cbm2[cp]
                        k = lambda n: (n, cp)
                        for mq in range(2):
                            for mi in range(4):
                                TR(tpp[:Qc, mq * 512 + mi * 128:mq * 512 + (mi + 1) * 128], xc[:, mq * 4 + mi, cs:cs + Qc], ident_b[:],
                                   [("xc", tp_, mq * 4 + mi)], ["tp"])
                            TT("dve", hd(xdt[:Qc, mq * 512:(mq + 1) * 512]), hd(tpp[:Qc, mq * 512:(mq + 1) * 512]),
                               dtv[:Qc, c, mq * 8:(mq + 1) * 8].unsqueeze(2).to_broadcast([Qc, 8, 64]), ALU.mult, ["tp", ("dtv", tp_)], [k("xdt%d" % mq)])
                        for g in range(4):
                            TR(tpp[:Qc, g * 128:(g + 1) * 128], xc[:, 8 + g, cs:cs + Qc], ident_b[:], [("xc", tp_, 8 + g)], ["tp"])
                        CP("act", btok[:Qc, :], tpp[:Qc, 0:512], ["tp"], [k("btok")])
                        for i, aa in enumerate((a_hi, a_lo)):
                            MM(smallp[:Qc, 0:16], tri_b[:Qc, :Qc], aa[:Qc, c, :], i == 0, i == 1, [("a_hi", tp_), ("a_lo", tp_), "tri_b"], ["small"])
                        for i, aa in enumerate((a_hi, a_lo)):
                            MM(smallp[:, 16:32], ones_b[:Qc, :], aa[:Qc, c, :], i == 0, i == 1, [("a_hi", tp_), ("a_lo", tp_)], ["small"])
                        CP("act", acs[:Qc, :], smallp[:Qc, 0:16], ["small"], [k("acs")])
                        ACT(nacs2[cp][:Qc, :], smallp[:Qc, 0:16], AF.Copy, ["small"], [k("nacs")], scale=-1.0)
                        ACT(etot[:], smallp[:, 16:32], AF.Exp, ["small"], [k("etot")])
                        TT("dve", tmd[:Qc, :], smallp[:Qc, 16:32], acs[:Qc, :], ALU.subtract, ["small", k("acs"), k("etot")], [k("tmd")])
                        ACT(te[:Qc, :], tmd[:Qc, :], AF.Exp, [k("tmd")], [k("te")])
                        TT("pool", hd(xe[:Qc, :]), hd(xdt[:Qc, :]), te[:Qc, :].unsqueeze(2).to_broadcast([Qc, 16, 64]), ALU.mult,
                           [k("xdt0"), k("xdt1"), k("te")], [k("xe")])
                        for g in range(4):
                            MM(cbp[:Qc, g * 128:g * 128 + Qc], xc[:, 8 + g, cs:cs + Qc], xc[:, 12 + g, cs:cs + Qc], True, True,
                               [("xc", tp_, 8 + g), ("xc", tp_, 12 + g)], ["cbp"])
                        TT("dve", cbm[:Qc, :, :Qc], cbp[:Qc, :].rearrange("p (g l) -> p g l", g=4)[:, :, :Qc],
                           tri_b[:Qc, :Qc].unsqueeze(1).to_broadcast([Qc, 4, Qc]), ALU.mult, ["cbp", "tri_b"], [k("cbm")])

                    def front_grp(c, g):
                        cp = c % 2
                        cs = c * Qc
                        par = g % 2
                        acs, cbm = acs2[cp], cbm2[cp]
                        k = lambda n: (n, cp)
                        ab = acsb[par]
                        abv = ab[:].rearrange("p (h l) -> p h l", h=4)
                        for hh in range(4):
                            h = g * 4 + hh
                            for i, aa in enumerate((a_hi, a_lo)):
                                MM(abv[:, hh, :Qc], aa[:Qc, c, h:h + 1].to_broadcast([Qc, 128]), tri_b[:Qc, :Qc], i == 0, i == 1,
                                   [("a_hi", tp_), ("a_lo", tp_)], [("acsb", par)])
                        Ev = E[par][:].rearrange("p (h l) -> p h l", h=4)
                        ACT(Ev[:, :, :Qc], abv[:, :, :Qc], AF.Exp, [("acsb", par)], [("E", par)])
                        dv = dec[par][:].rearrange("p (h l) -> p h l", h=4)
                        for hh in range(4):
                            h = g * 4 + hh
                            ACT(dv[:Qc, hh, :Qc], abv[:Qc, hh, :Qc], AF.Exp, [("acsb", par), k("nacs")], [("dec", par)], bias=nacs2[cp][:Qc, h:h + 1], scale=1.0)
                        Mv = Mh4[cp][g][:].rearrange("p (h l) -> p h l", h=4)
                        STT(Mv[:Qc, :, :Qc], dv[:Qc, :, :Qc], 1.0, cbm[:Qc, g, :Qc].unsqueeze(1).to_broadcast([Qc, 4, Qc]), ALU.min, ALU.mult,
                            [("dec", par), k("cbm")], [("Mh", cp, g)])
                        Cv = Cs4[cp][g][:].rearrange("p (h l) -> p h l", h=4)
                        TT("pool", Cv[:, :, :Qc], xc[:, 12 + g, cs:cs + Qc].unsqueeze(1).to_broadcast([128, 4, Qc]), Ev[:, :, :Qc], ALU.mult,
                           [("xc", tp_, 12 + g), ("E", par)], [("Cs", cp, g)])

                    def back_grp(c, g):
                        cp = c % 2
                        cs = c * Qc
                        par = g % 2
                        xdt = xdt2[cp]
                        k = lambda n: (n, cp)
                        Mv = Mh4[cp][g][:].rearrange("p (h l) -> p h l", h=4)
                        Cv = Cs4[cp][g][:].rearrange("p (h l) -> p h l", h=4)
                        yp = yps[par]
                        ykey = ("yps", par)
                        for hh in range(4):
                            h = g * 4 + hh
                            o = yp[(hh % 2) * 64:(hh % 2 + 1) * 64, (hh // 2) * 128:(hh // 2) * 128 + Qc]
                            MM(o, xdt[:Qc, h * 64:(h + 1) * 64], Mv[:Qc, hh, :Qc], True, False, [k("xdt%d" % (h // 8)), ("Mh", cp, g)], [ykey])
                            MM(o, hbf[:, h * 64:(h + 1) * 64], Cv[:, hh, :Qc], False, True, ["hbf", ("Cs", cp, g)], [ykey])
                        for mm_ in range(2):
                            m = 2 * g + mm_
                            STT(yg[:, m, cs:cs + Qc], xc[:, m, cs:cs + Qc], dvec[:, m:m + 1], yp[:, mm_ * 128:mm_ * 128 + Qc], ALU.mult, ALU.add,
                                [("xc", tp_, m), "dvec", ykey], ["yg"])

                    def back_post(c):
                        cp = c % 2
                        xe, btok, etot = xe2[cp], btok2[cp], etot2[cp]
                        k = lambda n: (n, cp)
                        for half in range(2):
                            for gi in range(2):
                                g = half * 2 + gi
                                MM(hps[:, gi * 256:(gi + 1) * 256], btok[:Qc, g * 128:(g + 1) * 128], xe[:Qc, g * 256:(g + 1) * 256], True, True,
                                   [k("btok"), k("xe")], ["hps"])
                            TT("pool", hd(htmp[:]), hd(hstate[:, half * 512:(half + 1) * 512]),
                               etot[:, half * 8:(half + 1) * 8].unsqueeze(2).to_broadcast([128, 8, 64]), ALU.mult, ["hstate", k("etot")], ["htmp"])
                            TT("dve", hstate[:, half * 512:(half + 1) * 512], htmp[:], hps[:], ALU.add, ["htmp", "hps"], ["hstate"])
                        CP("act", hbf[:], hstate[:], ["hstate"], ["hbf"])

                    front_pre(0)
                    for g in range(4):
                        front_grp(0, g)
                    for c in range(nck):
                        more = c + 1 < nck
                        if more:
                            front_pre(c + 1)
                        for g in range(4):
                            if more:
                                front_grp(c + 1, g)
                            back_grp(c, g)
                        back_post(c)
                        hook(c, nck)

                def post(ti):
                    (t0, W, seq, first, last) = TILES[ti]
                    Qc = min(128, W)
                    nck = W // Qc
                    tp_ = ti % 2
                    xc, sza, dtv, a_hi, a_lo = xc2[tp_], sza2[tp_], dtv2[tp_], a_hi2[tp_], a_lo2[tp_]
                    TT("pool", yg[:, 0:3, :W], yg[:, 0:3, :W], sza[:, 0:3, :W], ALU.mult, ["yg", ("sza", tp_)], ["yg"])
                    TT("dve", yg[:, 3:8, :W], yg[:, 3:8, :W], sza[:, 3:8, :W], ALU.mult, ["yg", ("sza", tp_)], ["yg"])
                    ACT(sq[:, 3:8, :W], yg[:, 3:8, :W], AF.Square, ["yg"], ["sq_b"])
                    ACT(sq[:, 0:3, :W], yg[:, 0:3, :W], AF.Square, ["yg"], ["sq_a"])
                    nbanks = [(cbp, "cbp"), (acsb[0], ("acsb", 0)), (acsb[1], ("acsb", 1)), (hps, "hps")]
                    for gg in (3, 2, 1, 0):
                        bnk, bkey = nbanks[gg]
                        for i in range(2):
                            MM(bnk[:, :W], ones_b[:], sq[:, 2 * gg + i, :W], i == 0, i == 1, ["sq_a", "sq_b"], [bkey])
                    for gg in (3, 2, 1, 0):
                        bnk, bkey = nbanks[gg]
                        ACT(rs4[gg][:, :W], bnk[:, :W], AF.Ln, [bkey], [("rs4", gg)], bias=EPS, scale=1.0 / 256)
                    for gg in (3, 2, 1, 0):
                        ACT(rs4[gg][:, :W], rs4[gg][:, :W], AF.Exp, [("rs4", gg)], [("rs4", gg)], scale=-0.5)
                    for gg in (3, 2, 1, 0):
                        for m in (2 * gg, 2 * gg + 1):
                            STT(ya[:, m, :W], yg[:, m, :W], ng[:, m:m + 1], rs4[gg][:, :W], ALU.mult, ALU.mult, ["yg", "ng", ("rs4", gg)], ["ya"])
                    DMA(YA[:, :, t0:t0 + W].rearrange("m p t -> p m t"), ya[:, :, :W], ["ya"], [("YA", t0)])
                    if last:
                        for m in range(8):
                            TR(hps[:, (m % 4) * 128:(m % 4 + 1) * 128], hstate[:, m * 128:(m + 1) * 128], ident_f[:], ["hstate"], ["hps"])
                            if m % 4 == 3:
                                hh = m // 4
                                CP("dve", hio[:, hh * 4:(hh + 1) * 4, :], hps[:].rearrange("p (m n) -> p m n", n=128), ["hps"], ["hio"])
                        DMA(ssd_o[l, seq].rearrange("(m h) p n -> (h p) m n", h=2), hio[:], ["hio"], [])

                for gq in range(4):
                    conv_blk(0, gq)
                pre(0)
                for ti in range(len(TILES)):
                    nxt = ti + 1 if ti + 1 < len(TILES) else None

                    def hook(c, nck, nxt=nxt):
                        if nxt is None:
                            return
                        per = 4 // nck if nck <= 4 else 1
                        for gq in range(c * per, (c + 1) * per):
                            conv_blk(nxt, gq)
                        if c == nck - 1:
                            pre(nxt)

                    seq_state_init(ti)
                    chunks(ti, hook)
                    post(ti)
            S.barrier()

        def phase3(l):
            with ExitStack() as ph:
                W_sb = sb(ph, "p3_W", [128, 64, 128], BF16)
                T0_sb = sb(ph, "p3_T0", [128, 64, 128], BF16)
                Cm_sb = sb(ph, "p3_Cm", [128, 32, 2, 128], BF16)
                L8 = sb(ph, "p3_L8", [128, 4, 32], F32)
                Hcar = sb(ph, "p3_Hcar", [128, 2, 32], F32)
                pa = ps(ph, "p3_pa", [128, 512])
                pb = ps(ph, "p3_pb", [128, 512])
                pc = ps(ph, "p3_pc", [128, 512])
                pd = ps(ph, "p3_pd", [128, 512])
                with ExitStack() as pp:
                    f64 = lambda nm, shp, dt=F32: sb(pp, nm, shp, dt)
                    lamr, lami, dtg, zr, zi = [f64("q_" + n, [64, 64]) for n in ("lamr", "lami", "dtg", "zr", "zi")]
                    kr, ki, t1, t2, den = [f64("q_" + n, [64, 64]) for n in ("kr", "ki", "t1", "t2", "den")]
                    erow_i = f64("q_erowi", [64, 25], I32)
                    erow = f64("q_erow", [64, 25])
                    ang, mag, s4, s2, Pr, Pi = [f64("q_" + n, [64, 64, 25]) for n in ("ang", "mag", "s4", "s2", "Pr", "Pi")]
                    kqi = f64("q_kqi", [64, 64, 25], I32)
                    PKr, PKi, u1 = [f64("q_" + n, [64, 64, 8]) for n in ("PKr", "PKi", "u1")]
                    Btr, Bti, Ctr, Cti = [f64("q_" + n, [64, 64, 16]) for n in ("Btr", "Bti", "Ctr", "Cti")]
                    Cn = sb(pp, "q_Cn", [128, 8, 64], F32)
                    Wtr, Wti, v1, v2, Cxr, Cxn = [f64("q_" + n, [64, 8, 8, 16]) for n in ("Wtr", "Wti", "v1", "v2", "Cxr", "Cxn")]

                    DMA(lamr[:], Wd["lam_re"][l].rearrange("g p -> p g"), [], ["lamr"])
                    DMA(lami[:], Wd["lam_im"][l].rearrange("g p -> p g"), [], ["lami"])
                    DMA(dtg[:], Wd["log_dt"][l].partition_broadcast(64), [], ["dtg"])
                    DMA(Btr[:], Wd["b_re"][l].rearrange("g p c -> p g c"), [], ["Btr"])
                    DMA(Bti[:], Wd["b_im"][l].rearrange("g p c -> p g c"), [], ["Bti"])
                    for (src, dst, nm) in ((Wd["c_re"], Ctr, "Ctr"), (Wd["c_im"], Cti, "Cti")):
                        DMA(Cn[:], src[l].rearrange("(j g) c p -> (g c) j p", g=8), [], ["Cn"])
                        for j in range(8):
                            TR(pa[:64, (j % 4) * 128:(j % 4 + 1) * 128], Cn[:, j, :], ident_f[:], ["Cn"], ["pa"])
                            if j % 4 == 3:
                                jj = j // 4
                                CP("dve", dst[:, jj * 32:(jj + 1) * 32, :], pa[:64, :].rearrange("p (g c) -> p g c", c=16), ["pa"], [nm])
                    ACT(dtg[:], dtg[:], AF.Exp, ["dtg"], ["dtg"])
                    TT("dve", zr[:], lamr[:], dtg[:], ALU.mult, ["lamr", "dtg"], ["zr"])
                    TT("dve", zi[:], lami[:], dtg[:], ALU.mult, ["lami", "dtg"], ["zi"])
                    for (a, b_, pat, base) in ((0, 8, [[-1, 8]], 7), (8, 16, [[1, 8]], 1), (16, 24, [[1, 8]], -7), (24, 25, [[1, 1]], 8)):
                        oap = erow_i[:, a:b_]
                        S.op("pool", lambda e, oap=oap, pat=pat, base=base: e.iota(out=oap, pattern=pat, base=base, channel_multiplier=0), (), ["erow_i"])
                    CP("dve", erow[:], erow_i[:], ["erow_i"], ["erow"])
                    ebc = erow[:, :].unsqueeze(1).to_broadcast([64, 64, 25])
                    gbc = lambda tl: tl[:, :].unsqueeze(2).to_broadcast([64, 64, 25])
                    fl = lambda tl: tl[:].rearrange("p g m -> p (g m)")
                    TT("dve", ang[:], gbc(zi), ebc, ALU.mult, ["zi", "erow"], ["ang"])
                    TT("dve", mag[:], gbc(zr), ebc, ALU.mult, ["zr", "erow"], ["mag"])
                    ACT(mag[:], mag[:], AF.Exp, ["mag"], ["mag"])
                    TSS("dve", s4[:], ang[:], 1.0 / (2 * math.pi), ALU.mult, ["ang"], ["s4"])
                    CP("dve", kqi[:], s4[:], ["s4"], ["kqi"])
                    CP("dve", s4[:], kqi[:], ["kqi"], ["s4"])
                    STT(fl(ang), fl(s4), -2 * math.pi, fl(ang), ALU.mult, ALU.add, ["s4", "ang"], ["ang"])
                    ACT(s4[:], ang[:], AF.Sin, ["ang"], ["s4"], scale=0.25)
                    ACT(s2[:], ang[:], AF.Sin, ["ang"], ["s2"], scale=0.5)
                    TT("dve", s4[:], s4[:], s4[:], ALU.mult, ["s4"], ["s4"])
                    TS("dve", s4[:], s4[:], -2.0, 1.0, ALU.mult, ALU.add, ["s4"], ["s4"])
                    TT("dve", Pi[:], s2[:], s4[:], ALU.mult, ["s2", "s4"], ["Pi"])
                    STT(fl(Pi), fl(Pi), 2.0, fl(mag), ALU.mult, ALU.mult, ["Pi", "mag"], ["Pi"])
                    TT("dve", s2[:], s2[:], s2[:], ALU.mult, ["s2"], ["s2"])
                    TS("dve", s2[:], s2[:], -2.0, 1.0, ALU.mult, ALU.add, ["s2"], ["s2"])
                    TT("dve", Pr[:], s2[:], mag[:], ALU.mult, ["s2", "mag"], ["Pr"])
                    TSS("dve", t1[:], Pr[:, :, 6], -1.0, ALU.add, ["Pr"], ["t1"])
                    TT("dve", den[:], lamr[:], lamr[:], ALU.mult, ["lamr"], ["den"])
                    TT("dve", t2[:], lami[:], lami[:], ALU.mult, ["lami"], ["t2"])
                    TT("dve", den[:], den[:], t2[:], ALU.add, ["den", "t2"], ["den"])
                    S.op("dve", lambda e: e.reciprocal(out=den[:], in_=den[:]), ["den"], ["den"])
                    TT("dve", kr[:], t1[:], lamr[:], ALU.mult, ["t1", "lamr"], ["kr"])
                    TT("dve", t2[:], Pi[:, :, 6], lami[:], ALU.mult, ["Pi", "lami"], ["t2"])
                    TT("dve", kr[:], kr[:], t2[:], ALU.add, ["kr", "t2"], ["kr"])
                    TT("dve", kr[:], kr[:], den[:], ALU.mult, ["kr", "den"], ["kr"])
                    TT("dve", ki[:], Pi[:, :, 6], lamr[:], ALU.mult, ["Pi", "lamr"], ["ki"])
                    TT("dve", t2[:], t1[:], lami[:], ALU.mult, ["t1", "lami"], ["t2"])
                    TT("dve", ki[:], ki[:], t2[:], ALU.subtract, ["ki", "t2"], ["ki"])
                    TT("dve", ki[:], ki[:], den[:], ALU.mult, ["ki", "den"], ["ki"])
                    k8 = lambda tl: tl[:, :].unsqueeze(2).to_broadcast([64, 64, 8])
                    TT("dve", PKr[:], Pr[:, :, 0:8], k8(kr), ALU.mult, ["Pr", "kr"], ["PKr"])
                    TT("dve", u1[:], Pi[:, :, 0:8], k8(ki), ALU.mult, ["Pi", "ki"], ["u1"])
                    TT("dve", PKr[:], PKr[:], u1[:], ALU.subtract, ["PKr", "u1"], ["PKr"])
                    TT("dve", PKi[:], Pr[:, :, 0:8], k8(ki), ALU.mult, ["Pr", "ki"], ["PKi"])
                    TT("dve", u1[:], Pi[:, :, 0:8], k8(kr), ALU.mult, ["Pi", "kr"], ["u1"])
                    TT("dve", PKi[:], PKi[:], u1[:], ALU.add, ["PKi", "u1"], ["PKi"])
                    for (ti, src, sc) in ((0, Pr, 1.0), (1, Pi, 1.0), (2, Pi, -1.0), (3, mag, 1.0)):
                        for gh in range(2):
                            ACT(L8[gh * 64:(gh + 1) * 64, ti, :].rearrange("p (g j) -> p g j", j=8),
                                src[:, :, 24].rearrange("p (j g) -> p g j", g=8)[:, gh * 4:(gh + 1) * 4, :], AF.Copy, ["Pr", "Pi", "mag"], ["L8"], scale=sc)

                    def cmul(outr, outi, PA_r, PA_i, Xr, Xi, jb, key, rd, neg):
                        pbc = lambda P_: P_.unsqueeze(3).to_broadcast([64, 8, 8, 16])
                        xbc = lambda X_: X_[:, jb * 8:(jb + 1) * 8, :].unsqueeze(2).to_broadcast([64, 8, 8, 16])
                        TT("dve", outr[:], pbc(PA_r), xbc(Xr), ALU.mult, rd, [key + "r"])
                        TT("pool", v1[:], pbc(PA_i), xbc(Xi), ALU.mult, rd, ["v1"])
                        TT("dve", outr[:], outr[:], v1[:], ALU.subtract, [key + "r", "v1"], [key + "r"])
                        TT("dve", outi[:], pbc(PA_r), xbc(Xi), ALU.mult, rd, [key + "i"])
                        TT("pool", v2[:], pbc(PA_i), xbc(Xr), ALU.mult, rd, ["v2"])
                        TT("dve", outi[:], outi[:], v2[:], ALU.add, [key + "i", "v2"], [key + "i"])
                        if neg:
                            TSS("dve", outi[:], outi[:], -1.0, ALU.mult, [key + "i"], [key + "i"])

                    base_rd = ["Pr", "Pi", "PKr", "PKi", "Btr", "Bti", "Ctr", "Cti"]
                    sc_ = lambda ap: ap.rearrange("p s c -> p (s c)")
                    for jb in range(8):
                        gs = slice(jb * 8, (jb + 1) * 8)
                        cmul(Wtr, Wti, PKr[:, gs, :], PKi[:, gs, :], Btr, Bti, jb, "Wt", base_rd, False)
                        for (ri, src, nm) in ((0, Wtr, "Wtr"), (1, Wti, "Wti")):
                            for g8 in range(8):
                                bank, bk = (pa, "pa") if g8 < 4 else (pb, "pb")
                                TR(bank[:, (g8 % 4) * 64:(g8 % 4 + 1) * 64], sc_(src[:, g8, :, :]), ident_f[:64, :64], [nm], [bk])
                            for hb, (bank, bk) in enumerate(((pa, "pa"), (pb, "pb"))):
                                ACT(W_sb[:, :, ri * 64:(ri + 1) * 64].rearrange("p (g j) x -> p g j x", j=8)[:, hb * 4:(hb + 1) * 4, jb, :],
                                    bank[:, 0:256].rearrange("p (g x) -> p g x", x=64), AF.Copy, [bk], ["W_sb"])
                        cmul(Cxr, Cxn, Pr[:, gs, 16:24], Pi[:, gs, 16:24], Ctr, Cti, jb, "Cx", base_rd, True)
                        for g8 in range(8):
                            bank, bk = (pc, "pc") if g8 < 4 else (pd, "pd")
                            o = bank[:, (g8 % 4) * 128:(g8 % 4 + 1) * 128]
                            MM(o, sc_(Wtr[:, g8, :, :]), sc_(Cxr[:, g8, :, :]), True, False, ["Wtr", "Cxr"], [bk])
                            MM(o, sc_(Wti[:, g8, :, :]), sc_(Cxn[:, g8, :, :]), False, True, ["Wti", "Cxi"], [bk])
                        for hb, (bank, bk) in enumerate(((pc, "pc"), (pd, "pd"))):
                            TT("dve", T0_sb[:].rearrange("p (g j) x -> p g j x", j=8)[:, hb * 4:(hb + 1) * 4, jb, :],
                               bank[:].rearrange("p (g x) -> p g x", x=128), bmask[:].unsqueeze(1).to_broadcast([128, 4, 128]), ALU.mult,
                               [bk, "bmask"], ["T0_sb"])
                        cmul(Cxr, Cxn, Pr[:, gs, 8:16], Pi[:, gs, 8:16], Ctr, Cti, jb, "Cx", base_rd, True)
                        for (ri, src, nm) in ((0, Cxr, "Cxr"), (1, Cxn, "Cxi")):
                            for gh in range(2):
                                ACT(Cm_sb[gh * 64:(gh + 1) * 64, :, ri, :].rearrange("p (g j) x -> p g j x", j=8)[:, :, jb, :],
                                    src[:, gh * 4:(gh + 1) * 4, :, :].rearrange("p g s c -> p g (s c)"), AF.Copy, [nm], ["Cm_sb"])
                S.barrier()
                with ExitStack() as pm:
                    Hb2 = [sb(pm, "p3_Hb%d" % i, [128, 2, 32, 64], F32) for i in range(2)]
                    U2 = [sb(pm, "p3_U%d" % i, [128, 64, 64], BF16) for i in range(3)]
                    Hbf2 = [sb(pm, "p3_Hbf%d" % i, [128, 2, 32, 64], BF16) for i in range(2)]
                    Ysth = [sb(pm, "p3_Yst%d" % i, [128, 32, 64], F32) for i in range(2)]
                    Fr = sb(pm, "p3_Fr", [128, 32, 65], F32)
                    Fi = sb(pm, "p3_Fi", [128, 32, 65], F32)
                    Rt = sb(pm, "p3_R", [128, 32, 65], F32)
                    Gir = sb(pm, "p3_Gir", [128, 32, 65], F32)
                    Gii = sb(pm, "p3_Gii", [128, 32, 65], F32)
                    Gsr = sb(pm, "p3_Gsr", [128, 32, 65], F32)
                    Gsi = sb(pm, "p3_Gsi", [128, 32, 65], F32)
                    c1 = sb(pm, "p3_c1", [128, 3, 32], F32)
                    S.op("dve", lambda e: e.reciprocal(out=c1[:, 2, :], in_=L8[:, 3, :]), ["L8"], ["c1"])
                    TT("dve", c1[:, 0, :], L8[:, 0, :], c1[:, 2, :], ALU.mult, ["L8", "c1"], ["c1"])
                    TT("dve", c1[:, 1, :], L8[:, 1, :], c1[:, 2, :], ALU.mult, ["L8", "c1"], ["c1"])
                    MSET("pool", Fr[:, :, 0:1], 1.0, ["Fr"])
                    MSET("pool", Fi[:, :, 0:1], 0.0, ["Fi"])
                    MSET("pool", Rt[:, :, 0:1], 0.0, ["Rt"])
                    MSET("pool", Gir[:], 0.0, ["Gir"])
                    MSET("pool", Gii[:], 0.0, ["Gii"])
                    CP("dve", Fr[:, :, 1], c1[:, 0, :], ["c1", "Fr"], ["Fr"])
                    CP("dve", Fi[:, :, 1], c1[:, 1, :], ["c1", "Fi"], ["Fi"])
                    CP("dve", Rt[:, :, 1:65], L8[:, 3, :].unsqueeze(2).to_broadcast([128, 32, 64]), ["L8", "Rt"], ["Rt"])
                    for k in range(6):
                        m_ = 1 << k
                        a_bc = Fr[:, :, m_:m_ + 1].to_broadcast([128, 32, m_])
                        b_bc = Fi[:, :, m_:m_ + 1].to_broadcast([128, 32, m_])
                        sr, si = Fr[:, :, 1:1 + m_], Fi[:, :, 1:1 + m_]
                        dr, di = Fr[:, :, m_ + 1:2 * m_ + 1], Fi[:, :, m_ + 1:2 * m_ + 1]
                        u_, v_ = Gsr[:, :, 0:m_], Gsi[:, :, 0:m_]
                        TT("dve", u_, sr, a_bc, ALU.mult, ["Fr"], ["Gsr"])
                        TT("dve", v_, si, b_bc, ALU.mult, ["Fi"], ["Gsi"])
                        TT("dve", dr, u_, v_, ALU.subtract, ["Gsr", "Gsi", "Fr"], ["Fr"])
                        TT("dve", u_, sr, b_bc, ALU.mult, ["Fr", "Fi"], ["Gsr"])
                        TT("dve", v_, si, a_bc, ALU.mult, ["Fr", "Fi"], ["Gsi"])
                        TT("dve", di, u_, v_, ALU.add, ["Gsr", "Gsi", "Fi"], ["Fi"])
                    lbanks = [pa, pb]
                    ybanks = [pc, pd]
                    def u_load(ti):
                        (t0_, W_, _, _, _) = TILES[ti]
                        DMA(U2[ti % 3][:, :, :W_ // 8], UD[ti].rearrange("s c g j n -> (s c) (g j) n")[:, :, 0:W_ // 8],
                            [("UD", t0_)], [("U_sb", ti % 3)])

                    def stage_L(ti):
                        (t0, W, seq, first, last) = TILES[ti]
                        nC = W // 8
                        U_sb = U2[ti % 3]
                        ukey = ("U_sb", ti % 3)
                        Lb = Hb2[ti % 2]
                        for gl4 in range(8):
                            bank = lbanks[gl4 % 2]
                            bk = ("lb", gl4 % 2)
                            bv = bank[:].rearrange("p (r g n) -> p r g n", r=2, g=4)
                            for gh in range(2):
                                for gi in range(4):
                                    gidx = gh * 32 + gl4 * 4 + gi
                                    for ri in range(2):
                                        MM(bv[gh * 64:(gh + 1) * 64, ri, gi, :nC], W_sb[:, gidx, ri * 64:(ri + 1) * 64], U_sb[:, gidx, :nC], True, True,
                                           [ukey], [bk])
                            CP("act", Lb[:, :, gl4 * 4:(gl4 + 1) * 4, :nC], bv[:, :, :, :nC], [bk], [("Lb", ti % 2)])

                    def stage_S(ti):
                        (t0, W, seq, first, last) = TILES[ti]
                        nC = W // 8
                        n1 = nC + 1
                        Lb = Hb2[ti % 2]
                        lk = ("Lb", ti % 2)
                        Hbf = Hbf2[ti % 2]
                        hk = ("Hbf", ti % 2)
                        if first:
                            if seq < 2:
                                MSET("pool", Hcar[:], 0.0, ["Hcar"])
                            else:
                                for (ri, src) in ((0, s5r0), (1, s5i0)):
                                    for g8 in range(8):
                                        gh, g4 = g8 // 4, g8 % 4
                                        DMA(Hcar[gh * 64:(gh + 1) * 64, ri, g4 * 8:(g4 + 1) * 8],
                                            src[l].rearrange("(j g) p -> p g j", g=8)[:, g8, :], [], ["Hcar"])
                        Lr_, Li_ = Lb[:, 0, :, 0:nC], Lb[:, 1, :, 0:nC]
                        Frs, Fis = Fr[:, :, 1:n1], Fi[:, :, 1:n1]
                        CP("dve", Gir[:, :, 0], Hcar[:, 0, :], ["Hcar"], ["Gir"])
                        CP("dve", Gii[:, :, 0], Hcar[:, 1, :], ["Hcar"], ["Gii"])
                        TT("dve", Gsr[:, :, 1:n1], Lr_, Frs, ALU.mult, [lk, "Fr"], ["Gsr"])
                        TT("dve", Gsi[:, :, 1:n1], Li_, Fis, ALU.mult, [lk, "Fi"], ["Gsi"])
                        TT("dve", Gir[:, :, 1:n1], Gsr[:, :, 1:n1], Gsi[:, :, 1:n1], ALU.add, ["Gsr", "Gsi"], ["Gir"])
                        TT("dve", Gsr[:, :, 1:n1], Li_, Frs, ALU.mult, [lk, "Fr", "Gir"], ["Gsr"])
                        TT("dve", Gsi[:, :, 1:n1], Lr_, Fis, ALU.mult, [lk, "Fi", "Gir"], ["Gsi"])
                        TT("dve", Gii[:, :, 1:n1], Gsr[:, :, 1:n1], Gsi[:, :, 1:n1], ALU.subtract, ["Gsr", "Gsi"], ["Gii"])
                        fl_ = lambda tl: tl[:].rearrange("p g n -> p (g n)")
                        S.op("dve", lambda e, o=fl_(Gsr), d0=fl_(Rt), d1=fl_(Gir): e.tensor_tensor_scan(out=o, data0=d0, data1=d1, initial=0.0, op0=ALU.mult, op1=ALU.add),
                             ["Rt", "Gir", "Gii"], ["Gsr"])
                        S.op("dve", lambda e, o=fl_(Gsi), d0=fl_(Rt), d1=fl_(Gii): e.tensor_tensor_scan(out=o, data0=d0, data1=d1, initial=0.0, op0=ALU.mult, op1=ALU.add),
                             ["Rt", "Gii", "Gsr"], ["Gsi"])
                        Frn, Fin = Fr[:, :, 0:n1], Fi[:, :, 0:n1]
                        TT("dve", Gir[:, :, 0:n1], Gsr[:, :, 0:n1], Frn, ALU.mult, ["Gsr", "Gsi"], ["Gir"])
                        TT("dve", Gii[:, :, 0:n1], Gsi[:, :, 0:n1], Fin, ALU.mult, ["Gsr", "Gsi"], ["Gii"])
                        TT("dve", Gir[:, :, 0:n1], Gir[:, :, 0:n1], Gii[:, :, 0:n1], ALU.subtract, ["Gir", "Gii"], ["Gir"])
                        CP("act", Hbf[:, 0, :, :nC], Gir[:, :, 0:nC], ["Gir"], [hk])
                        CP("dve", Hcar[:, 0, :], Gir[:, :, nC], ["Gir", "Hcar"], ["Hcar"])
                        TT("dve", Gir[:, :, 0:n1], Gsr[:, :, 0:n1], Fin, ALU.mult, ["Gsr", "Gsi", hk, "Hcar"], ["Gir"])
                        TT("dve", Gii[:, :, 0:n1], Gsi[:, :, 0:n1], Frn, ALU.mult, ["Gsr", "Gsi", "Gir"], ["Gii"])
                        TT("dve", Gir[:, :, 0:n1], Gir[:, :, 0:n1], Gii[:, :, 0:n1], ALU.add, ["Gir", "Gii"], ["Gir"])
                        CP("act", Hbf[:, 1, :, :nC], Gir[:, :, 0:nC], ["Gir"], [hk])
                        CP("dve", Hcar[:, 1, :], Gir[:, :, nC], ["Gir", "Hcar"], ["Hcar"])

                    def stage_Y(ti):
                        (t0, W, seq, first, last) = TILES[ti]
                        nC = W // 8
                        n0 = t0 // 8
                        U_sb = U2[ti % 3]
                        ukey = ("U_sb", ti % 3)
                        Hbf = Hbf2[ti % 2]
                        hk = ("Hbf", ti % 2)
                        for g8b in range(8):
                            bank = ybanks[g8b % 2]
                            bk = ("yb", g8b % 2)
                            bv = bank[:].rearrange("p (g n) -> p g n", g=8)
                            for gi in range(8):
                                gidx = g8b * 8 + gi
                                gh, gl = gidx // 32, gidx % 32
                                MM(bv[:, gi, :nC], T0_sb[:, gidx, :], U_sb[:, gidx, :nC], True, False, [ukey], [bk])
                                for ri in range(2):
                                    MM(bv[:, gi, :nC], Cm_sb[gh * 64:(gh + 1) * 64, gl, ri, :], Hbf[gh * 64:(gh + 1) * 64, ri, gl, :nC], False, ri == 1,
                                       [hk], [bk])
                            hb_ = g8b // 4
                            CP("act", Ysth[hb_][:, (g8b % 4) * 8:(g8b % 4 + 1) * 8, :nC], bv[:, :, :nC], [bk], [("Yst", hb_)])
                            if g8b % 4 == 3:
                                DMA(YD[ti, :, :, hb_ * 4:(hb_ + 1) * 4, :, 0:nC].rearrange("s c g j n -> (s c) (g j) n"),
                                    Ysth[hb_][:, :, :nC], [("Yst", hb_)], [("YD", t0)])
                        if last:
                            for (ri, dst) in ((0, s5r_o), (1, s5i_o)):
                                for g8 in range(8):
                                    gh, g4 = g8 // 4, g8 % 4
                                    DMA(dst[l, seq].rearrange("(j g) p -> p g j", g=8)[:, g8, :],
                                        Hcar[gh * 64:(gh + 1) * 64, ri, g4 * 8:(g4 + 1) * 8], ["Hcar"], [])

                    u_load(0)
                    u_load(1)
                    stage_L(0)
                    for ti in range(len(TILES)):
                        if ti + 2 < len(TILES):
                            u_load(ti + 2)
                        if ti + 1 < len(TILES):
                            stage_L(ti + 1)
                        stage_S(ti)
                        stage_Y(ti)
            S.barrier()

        def phase4(l):
            with ExitStack() as ph:
                wg = sb(ph, "p4_w", [128, 8, 1024], BF16)
                ds5 = sb(ph, "p4_ds5", [128, 8], F32)
                bgl = sb(ph, "p4_bgl", [128, 8], F32)
                yp4 = [sb(ph, "p4_yp%d" % i, [128, 8, 8, 64], F32) for i in range(2)]
                ut4 = [sb(ph, "p4_ut%d" % i, [128, 8, 512], BF16) for i in range(2)]
                szb4 = [sb(ph, "p4_szb%d" % i, [128, 8, 512], BF16) for i in range(2)]
                yb = sb(ph, "p4_yb", [128, 8, 512], F32)
                gf2 = [sb(ph, "p4_gf%d" % i, [128, 8, 512], F32) for i in range(2)]
                gb2 = [sb(ph, "p4_gb%d" % i, [128, 8, 512], BF16) for i in range(2)]
                sg = [sb(ph, "p4_sg%d" % i, [128, 512], F32) for i in range(2)]
                ybst = sb(ph, "p4_ybst", [128, 8, 512], BF16)
                mm = [ps(ph, "p4_mm%d" % i, [128, 512]) for i in range(2)]
                for kt in range(8):
                    DMA(wg[:, kt, :], Wd["w_glu"][l, kt * 128:(kt + 1) * 128, :], [], ["wg"], q="pool")
                DMA(ds5[:], Wd["d_s5"][l].rearrange("(k p) -> p k", p=128), [], ["ds5"])
                DMA(bgl[:], Wd["b_glu"][l].rearrange("(k p) -> p k", p=128), [], ["bgl"])
                def p4_load(ti):
                    (t0, W, seq, first, last) = TILES[ti]
                    nC = W // 8
                    n0 = t0 // 8
                    pr = ti % 2
                    for g8 in range(8):
                        if nC == 64:
                            DMA(yp4[pr][g8 * 16:(g8 + 1) * 16, :, :, :].rearrange("c s j n -> c s (j n)"),
                                YD[ti, :, :, g8, :, :].rearrange("s c j n -> c s (j n)"), [("YD", t0)], [("yp4", pr)])
                        else:
                            for s_ in range(8):
                                DMA(yp4[pr][g8 * 16:(g8 + 1) * 16, s_, :, :nC], YD[ti, s_, :, g8, :, 0:nC], [("YD", t0)], [("yp4", pr)])
                        DMA(ut4[pr][g8 * 16:(g8 + 1) * 16, :, :W], UT[:, :, g8, t0:t0 + W].rearrange("j c t -> c j t"), [("UT", t0)], [("ut4", pr)])

                def p4_loadS(ti):
                    (t0, W, seq, first, last) = TILES[ti]
                    pr = ti % 2
                    DMA(szb4[pr][:, :, :W], SZB[:, :, t0:t0 + W].rearrange("m p t -> p m t"), [("SZB", t0)], [("szb4", pr)])

                def p4_front(ti):
                    (t0, W, seq, first, last) = TILES[ti]
                    nC = W // 8
                    pr = ti % 2
                    for j in range(8):
                        STT(yb[:, j, :W].rearrange("p (n s) -> p n s", s=8), ut4[pr][:, j, :W].rearrange("p (n s) -> p n s", s=8), ds5[:, j:j + 1],
                            yp4[pr][:, :, j, :nC].rearrange("p s n -> p n s"), ALU.mult, ALU.add, [("ut4", pr), ("yp4", pr), "ds5"], [("yb", j)])
                        ACT(gf2[pr][:, j, :W], yb[:, j, :W], AF.Gelu_apprx_tanh, [("yb", j)], [("gf", pr, j)])
                        CP("dve", gb2[pr][:, j, :W], gf2[pr][:, j, :W], [("gf", pr, j)], [("gb", pr)])

                def p4_back(ti):
                    (t0, W, seq, first, last) = TILES[ti]
                    pr = ti % 2
                    gf, gb = gf2[pr], gb2[pr]
                    for jo in range(8):
                        bank = mm[jo % 2]
                        bk = ("mm", jo % 2)
                        for ji in range(8):
                            MM(bank[:, :W], wg[:, ji, jo * 128:(jo + 1) * 128], gb[:, ji, :W], ji == 0, ji == 7, ["wg", ("gb", pr)], [bk])
                        sgt = sg[jo % 2]
                        sk = ("sg", jo % 2)
                        ACT(sgt[:, :W], bank[:, :W], AF.Sigmoid, [bk, "bgl"], [sk], bias=bgl[:, jo:jo + 1], scale=1.0)
                        TT("dve", sgt[:, :W], sgt[:, :W], gf[:, jo, :W], ALU.mult, [sk, ("gf", pr, jo)], [sk])
                        TT("dve", ybst[:, jo, :W], sgt[:, :W], szb4[pr][:, jo, :W], ALU.mult, [sk, ("szb4", pr)], ["ybst"])
                    DMA(YB[:, :, t0:t0 + W].rearrange("m p t -> p m t"), ybst[:, :, :W], ["ybst"], [("YB", t0)])

                nT = len(TILES)
                p4_load(0)
                if nT > 1:
                    p4_load(1)
                p4_front(0)
                p4_loadS(0)
                for ti in range(nT):
                    if ti + 1 < nT:
                        p4_front(ti + 1)
                        p4_loadS(ti + 1)
                    if ti + 2 < nT:
                        p4_load(ti + 2)
                    p4_back(ti)
            S.barrier()

        def phase5(l):
            with ExitStack() as ph:
                wo = sb(ph, "p5_w", [128, 16, 1024], BF16)
                y5 = [sb(ph, "p5_y%d" % i, [128, 16, 512], BF16) for i in range(2)]
                xT5b = [sb(ph, "p5_x%d" % i, [128, 8, 512], F32) for i in range(2)]
                mm = [ps(ph, "p5_mm%d" % i, [128, 512]) for i in range(4)]
                for kt in range(16):
                    DMA(wo[:, kt, :], Wd["w_out"][l, kt * 128:(kt + 1) * 128, :], [], ["wo"], q="pool")

                def y_load(ti):
                    (t0_, W_, _, _, _) = TILES[ti]
                    pr = ti % 2
                    DMA(y5[pr][:, 0:8, :W_], YA[:, :, t0_:t0_ + W_].rearrange("m p t -> p m t"), [("YA", t0_)], [("y5a", pr)])
                    DMA(y5[pr][:, 8:16, :W_], YB[:, :, t0_:t0_ + W_].rearrange("m p t -> p m t"), [("YB", t0_)], [("y5b", pr)])

                y_load(0)
                for ti, (t0, W, seq, first, last) in enumerate(TILES):
                    pr = ti % 2
                    if ti + 1 < len(TILES):
                        y_load(ti + 1)
                    xT5 = xT5b[pr]
                    xk = ("xT5", pr)
                    DMA(xT5[:, :, :W], xt_ap(t0, W), [("XT", t0)], [xk])
                    for dm in range(8):
                        bank = mm[dm % 4]
                        bk = ("mm", dm % 4)
                        for kt in range(16):
                            MM(bank[:, :W], wo[:, kt, dm * 128:(dm + 1) * 128], y5[pr][:, kt, :W], kt == 0, kt == 15, ["wo", ("y5a", pr), ("y5b", pr)], [bk])
                        TT("dve", xT5[:, dm, :W], bank[:, :W], xT5[:, dm, :W], ALU.add, [bk, xk], [xk])
                    DMA(xt_ap(t0, W), xT5[:, :, :W], [xk], [("XT", t0)])
            S.barrier()

        def phase_final():
            with ExitStack() as ph:
                fg = sb(ph, "pf_g", [128, 8], F32)
                xT = sb(ph, "pf_xT", [128, 8, 512], F32)
                sq = sb(ph, "pf_sq", [128, 8, 512], BF16)
                hf = sb(ph, "pf_hf", [128, 8, 512], F32)
                rs = sb(ph, "pf_rs", [128, 512], F32)
                rs2 = sb(ph, "pf_rs2", [128, 512], F32)
                yst = sb(ph, "pf_yst", [128, 4, 1024], F32)
                msp = ps(ph, "pf_ms", [128, 512])
                tp = [ps(ph, "pf_tp%d" % i, [128, 512]) for i in range(4)]
                DMA(fg[:], Wd["final_norm_g"].rearrange("(k p) -> p k", p=128), [], ["fg"])
                for (t0, W, seq, first, last) in TILES:
                    Pb = min(128, W)
                    nb = W // Pb
                    DMA(xT[:, :, :W], xt_ap(t0, W), [("XT", t0)], ["xT"])
                    ACT(sq[:, :, :W], xT[:, :, :W], AF.Square, ["xT"], ["nf_sq"])
                    rms_rstd(msp, [sq[:, kt, :W] for kt in range(8)], W, 1.0 / 1024, rs, rs2, "nf")
                    for kt in range(8):
                        STT(hf[:, kt, :W], xT[:, kt, :W], fg[:, kt:kt + 1], rs2[:, :W], ALU.mult, ALU.mult, ["xT", "fg", "nf_rs2"], ["hf"])
                    for b in range(nb):
                        for half in range(2):
                            bank = tp[(b * 2 + half) % 4]
                            bk = ("tp", (b * 2 + half) % 4)
                            for k4 in range(4):
                                kt = half * 4 + k4
                                TR(bank[:Pb, k4 * 128:(k4 + 1) * 128], hf[:, kt, b * Pb:(b + 1) * Pb], ident_f[:], ["hf"], [bk])
                            CP("act" if half else "dve", yst[:Pb, b, half * 512:(half + 1) * 512], bank[:Pb, :], [bk], ["yst"])
                    DMA(y_out[t0:t0 + W, :].rearrange("(b p) d -> p b d", p=Pb), yst[:Pb, :nb, :], ["yst"], [])
            S.barrier()

        wst = ExitStack()
        w_cur = sb(wst, "w_in_sb", [128, 8, IN_COLS], BF16)
        load_w_in(0, w_cur)
        phase0()
        for l in range(n_layers):
            if 1 in phases:
                phase1(l, w_cur)
            wst.close()
            if 2 in phases:
                phase2(l)
            if 3 in phases:
                phase3(l)
            if 4 in phases:
                phase4(l)
            if l + 1 < n_layers:
                wst = ExitStack()
                w_cur = sb(wst, "w_in_sb", [128, 8, IN_COLS], BF16)
                load_w_in(l + 1, w_cur)
            if 5 in phases:
                phase5(l)
        if do_final:
            phase_final()
        S.finish(block)
        nc._n_instr = S.n_instr
    return nc


def make_in_maps(inputs):
    xp = np.ascontiguousarray(inputs["x_prompt"], dtype=np.float32)
    xs = np.ascontiguousarray(inputs["x_sample"], dtype=np.float32)
    maps = []
    for c in range(NCORES):
        m = {}
        m["x_core"] = np.ascontiguousarray(np.concatenate([xp[2 * c], xp[2 * c + 1], xs[c]], axis=0))
        m["conv0"] = np.ascontiguousarray(inputs["state_ssd_conv"][:, c])
        m["ssd0"] = np.ascontiguousarray(inputs["state_ssd"][:, c])
        m["s5r0"] = np.ascontiguousarray(inputs["state_s5_re"][:, c])
        m["s5i0"] = np.ascontiguousarray(inputs["state_s5_im"][:, c])
        for n in W_NAMES:
            m[n] = np.ascontiguousarray(inputs[n], dtype=np.float32)
        maps.append(m)
    return maps


_NC_CACHE = {}


def kernel(**inputs):
    inputs = {k: np.asarray(v) for k, v in inputs.items()}
    if "nc" not in _NC_CACHE:
        _NC_CACHE["nc"] = build_nc()
    nc = _NC_CACHE["nc"]
    maps = make_in_maps(inputs)
    res = run_bass_kernel_spmd(nc, maps, core_ids=list(range(NCORES)))
    R = res.results
    y_prompt = np.zeros((16, 2048, 1024), np.float32)
    y_sample = np.zeros((8, 64, 1024), np.float32)
    conv_p = np.zeros((4, 16, 3, 2048), np.float32)
    ssd_p = np.zeros((4, 16, 16, 64, 128), np.float32)
    s5r_p = np.zeros((4, 16, 64, 64), np.float32)
    s5i_p = np.zeros((4, 16, 64, 64), np.float32)
    conv_s = np.zeros((4, 8, 3, 2048), np.float32)
    ssd_s = np.zeros((4, 8, 16, 64, 128), np.float32)
    s5r_s = np.zeros((4, 8, 64, 64), np.float32)
    s5i_s = np.zeros((4, 8, 64, 64), np.float32)
    for c in range(NCORES):
        r = R[c]
        y = r["y_out"]
        y_prompt[2 * c] = y[0:2048]
        y_prompt[2 * c + 1] = y[2048:4096]
        y_sample[c] = y[4096:4160]
        for i in range(2):
            conv_p[:, 2 * c + i] = r["conv_o"][:, i]
            ssd_p[:, 2 * c + i] = r["ssd_o"][:, i]
            s5r_p[:, 2 * c + i] = r["s5r_o"][:, i]
            s5i_p[:, 2 * c + i] = r["s5i_o"][:, i]
        conv_s[:, c] = r["conv_o"][:, 2]
        ssd_s[:, c] = r["ssd_o"][:, 2]
        s5r_s[:, c] = r["s5r_o"][:, 2]
        s5i_s[:, c] = r["s5i_o"][:, 2]
    return (y_prompt, y_sample, conv_p, ssd_p, s5r_p, s5i_p, conv_s, ssd_s, s5r_s, s5i_s)
```
